# Optimizing a Trainium2 kernel written in Bass

```python
import math
import jax
import jax.numpy as jnp
from jax import lax
import numpy as np

D_MODEL = 2048
BATCH = 4
SEQ = 2048
DEPTH = 2
DEC_BATCH = 32
DEC_SEQ = 1
PAST_LEN = 16384
PAGE_SIZE = 128

HEAD_DIM = 64
ROPE_THETA = 10000.0
MIX_WIDTH = D_MODEL
D_FF = ((8 * D_MODEL // 3 + 127) // 128) * 128
NORM_EPS = 1e-6
NEG_BIG = -1e30
LB_FLOOR = 1e-30

SSM_GROUP = 16
SSM_STATE = 64
SSM_WIDTH = 7 * D_MODEL // 32
SSM_GROUPS = SSM_WIDTH // SSM_GROUP
DT_MIN = 0.001
DT_MAX = 0.1

SWA_HEADS = D_MODEL // 256
SWA_KV_HEADS = SWA_HEADS // 4
SWA_GROUP = SWA_HEADS // SWA_KV_HEADS
SWA_WIDTH = SWA_HEADS * HEAD_DIM
SWA_KV_WIDTH = SWA_KV_HEADS * HEAD_DIM
SWA_WINDOW = 128

DIL_PAIRS = ((128, 1), (512, 4), (2048, 16))
DIL_GROUPS = 3
DIL_HEADS_PER_GROUP = 3
DIL_HEADS = DIL_GROUPS * DIL_HEADS_PER_GROUP
DIL_WIDTH = DIL_HEADS * HEAD_DIM
DIL_KV_WIDTH = DIL_GROUPS * HEAD_DIM

HGRN_HEAD_DIM = 128
HGRN_WIDTH = MIX_WIDTH - SSM_WIDTH - SWA_WIDTH - DIL_WIDTH
HGRN_HEADS = HGRN_WIDTH // HGRN_HEAD_DIM
HGRN_CHUNK = 16

IN_COLS = SSM_WIDTH + SWA_WIDTH + 2 * SWA_KV_WIDTH + DIL_WIDTH + 2 * DIL_KV_WIDTH + 4 * HGRN_WIDTH

kernel_name = 'hymba_s5_swa_dilated_hgrn2_macaron_step'


def rms_norm(x, g):
    xf = x.astype(jnp.float32)
    y = xf * lax.rsqrt(jnp.mean(xf * xf, axis=-1, keepdims=True) + NORM_EPS)
    return (y * g.astype(jnp.float32)).astype(x.dtype)


def swiglu(x, wg, wu, wd):
    return (jax.nn.silu(x @ wg) * (x @ wu)) @ wd


def rotary(x, pos):
    half = HEAD_DIM // 2
    inv_freq = ROPE_THETA ** (-jnp.arange(half, dtype=jnp.float32) / half)
    ang = pos.astype(jnp.float32)[:, None] * inv_freq[None, :]
    cos = jnp.cos(ang)[None, :, None, :]
    sin = jnp.sin(ang)[None, :, None, :]
    xf = x.astype(jnp.float32)
    x1, x2 = xf[..., :half], xf[..., half:]
    return jnp.concatenate([x1 * cos - x2 * sin, x2 * cos + x1 * sin], axis=-1).astype(x.dtype)


def sink_softmax(s, sink):
    m = jnp.max(s, axis=-1)
    if sink is not None:
        m = jnp.maximum(m, sink)
    p = jnp.exp(s - m[..., None])
    l = jnp.sum(p, axis=-1)
    if sink is not None:
        l = l + jnp.exp(sink - m)
    return p, l, m + jnp.log(l)


def banded_attention(q, k, v, window, sink=None):
    n, t, hkv, grp, hd = q.shape
    nb = -(-t // window)
    pad = nb * window - t
    q = jnp.pad(q, ((0, 0), (0, pad), (0, 0), (0, 0), (0, 0)))
    k = jnp.pad(k, ((0, 0), (0, pad), (0, 0), (0, 0)))
    v = jnp.pad(v, ((0, 0), (0, pad), (0, 0), (0, 0)))
    qb = q.reshape(n, nb, window, hkv, grp, hd).astype(jnp.float32)
    kb = k.reshape(n, nb, window, hkv, hd).astype(jnp.float32)
    vb = v.reshape(n, nb, window, hkv, hd).astype(jnp.float32)
    prev = ((0, 0), (1, 0), (0, 0), (0, 0), (0, 0))
    kk = jnp.concatenate([jnp.pad(kb, prev)[:, :-1], kb], axis=2)
    vv = jnp.concatenate([jnp.pad(vb, prev)[:, :-1], vb], axis=2)
    s = jnp.einsum('nbqhgd,nbkhd->nbhgqk', qb, kk) * (hd ** -0.5)
    qi = jnp.arange(window)[:, None]
    kj = jnp.arange(2 * window)[None, :]
    dist = window + qi - kj
    kpos = (jnp.arange(nb)[:, None, None] - 1) * window + kj[None]
    mask = (dist >= 0) & (dist <= window) & (kpos >= 0)
    s = jnp.where(mask[None, :, None, None], s, NEG_BIG)
    sink_b = None if sink is None else sink[None, None, :, :, None]
    p, l, lse = sink_softmax(s, sink_b)
    o = jnp.einsum('nbhgqk,nbkhd->nbqhgd', p, vv) / l.transpose(0, 1, 4, 2, 3)[..., None]
    o = o.reshape(n, nb * window, hkv, grp, hd)[:, :t]
    lse = lse.transpose(0, 1, 4, 2, 3).reshape(n, nb * window, hkv, grp)[:, :t]
    return o, lse


def gathered_attention(q, k_all, v_all, q_start, window, dilation, sink=None):
    tq, hd = q.shape[1], q.shape[-1]
    nk = window // dilation + 1
    idx = q_start + jnp.arange(tq)[:, None] - dilation * jnp.arange(nk)[None, :]
    valid = idx >= 0
    idx = jnp.maximum(idx, 0)
    kg = k_all[:, idx].astype(jnp.float32)
    vg = v_all[:, idx].astype(jnp.float32)
    s = jnp.einsum('nqhgd,nqkhd->nhgqk', q.astype(jnp.float32), kg) * (hd ** -0.5)
    s = jnp.where(valid[None, None, None], s, NEG_BIG)
    sink_b = None if sink is None else sink[None, :, :, None]
    p, l, lse = sink_softmax(s, sink_b)
    o = jnp.einsum('nhgqk,nqkhd->nqhgd', p, vg) / l.transpose(0, 3, 1, 2)[..., None]
    return o, lse.transpose(0, 3, 1, 2)


def s5_scan(u, x0, a_re, a_im, log_dt, b_re, b_im, c_re, c_im, d_skip):
    f32 = jnp.float32
    a_re, a_im = a_re.astype(f32), a_im.astype(f32)
    dt = jnp.exp(log_dt.astype(f32))[:, None]
    mag = jnp.exp(a_re * dt)
    lam_re, lam_im = mag * jnp.cos(a_im * dt), mag * jnp.sin(a_im * dt)
    den = a_re * a_re + a_im * a_im
    z_re = ((lam_re - 1.0) * a_re + lam_im * a_im) / den
    z_im = (lam_im * a_re - (lam_re - 1.0) * a_im) / den
    b_re, b_im = b_re.astype(f32), b_im.astype(f32)
    bb_re = z_re[..., None] * b_re - z_im[..., None] * b_im
    bb_im = z_re[..., None] * b_im + z_im[..., None] * b_re
    bu_re = jnp.einsum('ntgj,gsj->ntgs', u, bb_re)
    bu_im = jnp.einsum('ntgj,gsj->ntgs', u, bb_im)
    if x0 is not None:
        bu_re = bu_re.at[:, 0].add(lam_re * x0[..., 0] - lam_im * x0[..., 1])
        bu_im = bu_im.at[:, 0].add(lam_re * x0[..., 1] + lam_im * x0[..., 0])
    ar = jnp.broadcast_to(lam_re, bu_re.shape)
    ai = jnp.broadcast_to(lam_im, bu_re.shape)

    def combine(e1, e2):
        ar1, ai1, br1, bi1 = e1
        ar2, ai2, br2, bi2 = e2
        return (ar1 * ar2 - ai1 * ai2, ar1 * ai2 + ai1 * ar2,
                ar2 * br1 - ai2 * bi1 + br2, ar2 * bi1 + ai2 * br1 + bi2)

    _, _, xr, xi = lax.associative_scan(combine, (ar, ai, bu_re, bu_im), axis=1)
    y = (jnp.einsum('ntgs,gjs->ntgj', xr, c_re.astype(f32))
         - jnp.einsum('ntgs,gjs->ntgj', xi, c_im.astype(f32))
         + d_skip.astype(f32) * u)
    return y, jnp.stack([xr[:, -1], xi[:, -1]], axis=-1)


def hgrn2_recurrence(q, k, v, logf, s0):
    n, t, h, _ = q.shape
    nc = -(-t // HGRN_CHUNK)
    pad = nc * HGRN_CHUNK - t

    def chunks(a):
        a = jnp.pad(a, ((0, 0), (0, pad), (0, 0), (0, 0)))
        return jnp.moveaxis(a.reshape(n, nc, HGRN_CHUNK, h, a.shape[-1]), 1, 0)

    tri = jnp.tril(jnp.ones((HGRN_CHUNK, HGRN_CHUNK), dtype=bool))

    def step(S, inp):
        qc, kc, vc, gc = inp
        b = jnp.cumsum(gc, axis=1)
        diff = b[:, :, None] - b[:, None, :]
        dec = jnp.exp(jnp.where(tri[None, :, :, None, None], diff, NEG_BIG))
        att = jnp.einsum('nthk,nshk,ntshk->nhts', qc, kc, dec)
        o = (jnp.einsum('nhts,nshv->nthv', att, vc)
             + jnp.einsum('nthk,nhkv->nthv', qc * jnp.exp(b), S))
        bl = b[:, -1]
        S = (jnp.exp(bl)[..., None] * S
             + jnp.einsum('nshk,nshv->nhkv', kc * jnp.exp(bl[:, None] - b), vc))
        return S, o

    S, o = lax.scan(step, s0, (chunks(q), chunks(k), chunks(v), chunks(logf)))
    o = jnp.moveaxis(o, 0, 1).reshape(n, nc * HGRN_CHUNK, h, v.shape[-1])[:, :t]
    return o, S


def token_mixing(xn, pos, lp, lower_bound, cache):
    n, t, _ = xn.shape
    f32 = jnp.float32
    sizes = (SSM_WIDTH, SWA_WIDTH, SWA_KV_WIDTH, SWA_KV_WIDTH, DIL_WIDTH, DIL_KV_WIDTH, DIL_KV_WIDTH,
             HGRN_WIDTH, HGRN_WIDTH, HGRN_WIDTH, HGRN_WIDTH)
    cuts, acc = [], 0
    for s in sizes[:-1]:
        acc += s
        cuts.append(acc)
    u, q_b, k_b, v_b, q_c, k_c, v_c, q_d, f_d, i_d, g_d = jnp.split(xn @ lp['w_in'], cuts, axis=-1)

    y_a, ssm_new = s5_scan(u.astype(f32).reshape(n, t, SSM_GROUPS, SSM_GROUP),
                           None if cache is None else cache[0].astype(f32),
                           lp['ssm_a_re'], lp['ssm_a_im'], lp['ssm_log_dt'], lp['ssm_b_re'], lp['ssm_b_im'],
                           lp['ssm_c_re'], lp['ssm_c_im'], lp['ssm_d'])
    z = jax.nn.gelu(y_a.reshape(n, t, SSM_WIDTH))
    out_a = rms_norm(z * jax.nn.sigmoid(z @ lp['ssm_w_glu'].astype(f32) + lp['ssm_b_glu'].astype(f32)),
                     lp['out_norm_a'])

    qb = rotary(q_b.reshape(n, t, SWA_HEADS, HEAD_DIM), pos).reshape(n, t, SWA_KV_HEADS, SWA_GROUP, HEAD_DIM)
    kb = rotary(k_b.reshape(n, t, SWA_KV_HEADS, HEAD_DIM), pos)
    vb = v_b.reshape(n, t, SWA_KV_HEADS, HEAD_DIM)
    sink = lp['swa_sinks'].astype(f32).reshape(SWA_KV_HEADS, SWA_GROUP)
    if cache is None:
        ob, _ = banded_attention(qb, kb, vb, SWA_WINDOW, sink)
        keep = min(SWA_WINDOW, t)
        swa_new = jnp.stack([kb[:, t - keep:], vb[:, t - keep:]], axis=2)
    else:
        buf = cache[1]
        k_all = jnp.concatenate([buf[:, :, 0].astype(kb.dtype), kb], axis=1)
        v_all = jnp.concatenate([buf[:, :, 1].astype(vb.dtype), vb], axis=1)
        ob, _ = gathered_attention(qb, k_all, v_all, buf.shape[1], SWA_WINDOW, 1, sink)
        swa_new = jnp.stack([kb, vb], axis=2)
    out_b = rms_norm(ob.reshape(n, t, SWA_WIDTH), lp['out_norm_b'])

    qc = rotary(q_c.reshape(n, t, DIL_HEADS, HEAD_DIM), pos).reshape(n, t, DIL_GROUPS, DIL_HEADS_PER_GROUP, HEAD_DIM)
    kc = rotary(k_c.reshape(n, t, DIL_GROUPS, HEAD_DIM), pos)
    vc = v_c.reshape(n, t, DIL_GROUPS, HEAD_DIM)
    outs, lses, dil_new = [], [], []
    for gi, (win, dil) in enumerate(DIL_PAIRS):
        qg, kg, vg = qc[:, :, gi:gi + 1], kc[:, :, gi:gi + 1], vc[:, :, gi:gi + 1]
        if cache is None:
            def sub(a):
                a = a.reshape((n, t // dil, dil) + a.shape[2:])
                return jnp.moveaxis(a, 2, 1).reshape((n * dil, t // dil) + a.shape[3:])

            def unsub(a):
                a = a.reshape((n, dil, t // dil) + a.shape[2:])
                return jnp.moveaxis(a, 1, 2).reshape((n, t) + a.shape[3:])

            o, lse = banded_attention(sub(qg), sub(kg), sub(vg), win // dil)
            o, lse = unsub(o), unsub(lse)
            keep = min(win, t)
            dil_new.append(jnp.stack([kg[:, t - keep:, 0], vg[:, t - keep:, 0]], axis=2))
        else:
            buf = cache[2 + gi]
            k_all = jnp.concatenate([buf[:, :, 0][:, :, None].astype(kg.dtype), kg], axis=1)
            v_all = jnp.concatenate([buf[:, :, 1][:, :, None].astype(vg.dtype), vg], axis=1)
            o, lse = gathered_attention(qg, k_all, v_all, buf.shape[1], win, dil)
            dil_new.append(jnp.stack([kg[:, :, 0], vg[:, :, 0]], axis=2))
        outs.append(o[:, :, 0])
        lses.append(lse[:, :, 0])
    wts = jax.nn.softmax(jnp.stack(lses, axis=0), axis=0)
    oc = jnp.moveaxis(jnp.stack(outs, axis=0) * wts[..., None], 0, 2).reshape(n, t, DIL_WIDTH)
    out_c = rms_norm(oc, lp['out_norm_c'])

    lb = lower_bound.astype(f32).reshape(HGRN_HEADS, HGRN_HEAD_DIM)
    fpre = f_d.astype(f32).reshape(n, t, HGRN_HEADS, HGRN_HEAD_DIM)
    log_f = jnp.logaddexp(jnp.log(jnp.maximum(lb, LB_FLOOR)), jnp.log1p(-lb) + jax.nn.log_sigmoid(fpre))
    k_d = (1.0 - lb) * jax.nn.sigmoid(-fpre)
    q_dd = jax.nn.silu(q_d.astype(f32)).reshape(n, t, HGRN_HEADS, HGRN_HEAD_DIM)
    v_d = i_d.astype(f32).reshape(n, t, HGRN_HEADS, HGRN_HEAD_DIM)
    if cache is None:
        s0 = jnp.zeros((n, HGRN_HEADS, HGRN_HEAD_DIM, HGRN_HEAD_DIM), f32)
    else:
        s0 = cache[5].astype(f32)
    o_d, hgrn_new = hgrn2_recurrence(q_dd, k_d, v_d, log_f, s0)
    o_d = o_d * lax.rsqrt(jnp.mean(o_d * o_d, axis=-1, keepdims=True) + NORM_EPS)
    out_d = o_d.reshape(n, t, HGRN_WIDTH) * lp['out_norm_d'].astype(f32) * jax.nn.silu(g_d.astype(f32))

    mixed = jnp.concatenate([out_a.astype(xn.dtype), out_b.astype(xn.dtype),
                             out_c.astype(xn.dtype), out_d.astype(xn.dtype)], axis=-1)
    new_state = (ssm_new, swa_new, dil_new[0], dil_new[1], dil_new[2], hgrn_new)
    return mixed @ lp['w_out'], new_state


def trunk_layer(h, pos, lp, lower_bound, cache):
    h = h + 0.5 * rms_norm(swiglu(rms_norm(h, lp['ffn1_norm_pre']), lp['ffn1_w_gate'], lp['ffn1_w_up'],
                                  lp['ffn1_w_down']), lp['ffn1_norm_post'])
    mix, new_state = token_mixing(rms_norm(h, lp['mix_norm_pre']), pos, lp, lower_bound, cache)
    h = h + rms_norm(mix, lp['mix_norm_post'])
    h = h + 0.5 * rms_norm(swiglu(rms_norm(h, lp['ffn2_norm_pre']), lp['ffn2_w_gate'], lp['ffn2_w_up'],
                                  lp['ffn2_w_down']), lp['ffn2_norm_post'])
    return h, new_state


def setup_inputs(seed: int = 0) -> dict:
    key = jax.random.key(seed)
    keys = iter(jax.random.split(key, 64))
    f32 = jnp.float32

    def nrm(shape, scale):
        return scale * jax.random.normal(next(keys), shape, f32)

    def gain(width):
        return 1.0 + nrm((DEPTH, width), 0.05)

    inp = {}
    inp['x_prompt'] = nrm((BATCH, SEQ, D_MODEL), 1.0)
    inp['x_sample'] = nrm((DEC_BATCH, DEC_SEQ, D_MODEL), 1.0)
    inp['state_ssm'] = nrm((DEPTH, DEC_BATCH, SSM_GROUPS, SSM_STATE, 2), 0.1)
    inp['cache_swa_kv'] = nrm((DEPTH, DEC_BATCH, min(SWA_WINDOW, PAST_LEN), 2, SWA_KV_HEADS, HEAD_DIM), 1.0)
    inp['cache_dil0_kv'] = nrm((DEPTH, DEC_BATCH, min(DIL_PAIRS[0][0], PAST_LEN), 2, HEAD_DIM), 1.0)
    inp['cache_dil1_kv'] = nrm((DEPTH, DEC_BATCH, min(DIL_PAIRS[1][0], PAST_LEN), 2, HEAD_DIM), 1.0)
    inp['cache_dil2_kv'] = nrm((DEPTH, DEC_BATCH, min(DIL_PAIRS[2][0], PAST_LEN), 2, HEAD_DIM), 1.0)
    inp['state_hgrn'] = nrm((DEPTH, DEC_BATCH, HGRN_HEADS, HGRN_HEAD_DIM, HGRN_HEAD_DIM), 0.5)
    inp['ffn1_norm_pre'] = gain(D_MODEL)
    inp['ffn1_w_gate'] = nrm((DEPTH, D_MODEL, D_FF), D_MODEL ** -0.5)
    inp['ffn1_w_up'] = nrm((DEPTH, D_MODEL, D_FF), D_MODEL ** -0.5)
    inp['ffn1_w_down'] = nrm((DEPTH, D_FF, D_MODEL), D_FF ** -0.5)
    inp['ffn1_norm_post'] = gain(D_MODEL)
    inp['mix_norm_pre'] = gain(D_MODEL)
    inp['w_in'] = nrm((DEPTH, D_MODEL, IN_COLS), D_MODEL ** -0.5)
    inp['ssm_a_re'] = -0.5 + nrm((DEPTH, SSM_GROUPS, SSM_STATE), 0.01)
    inp['ssm_a_im'] = (math.pi * jnp.arange(SSM_STATE, dtype=f32))[None, None, :] + nrm((DEPTH, SSM_GROUPS, SSM_STATE), 0.01)
    inp['ssm_log_dt'] = jax.random.uniform(next(keys), (DEPTH, SSM_GROUPS), f32, math.log(DT_MIN), math.log(DT_MAX))
    inp['ssm_b_re'] = nrm((DEPTH, SSM_GROUPS, SSM_STATE, SSM_GROUP), SSM_GROUP ** -0.5)
    inp['ssm_b_im'] = nrm((DEPTH, SSM_GROUPS, SSM_STATE, SSM_GROUP), SSM_GROUP ** -0.5)
    inp['ssm_c_re'] = nrm((DEPTH, SSM_GROUPS, SSM_GROUP, SSM_STATE), SSM_STATE ** -0.5)
    inp['ssm_c_im'] = nrm((DEPTH, SSM_GROUPS, SSM_GROUP, SSM_STATE), SSM_STATE ** -0.5)
    inp['ssm_d'] = nrm((DEPTH, SSM_GROUPS, SSM_GROUP), 1.0)
    inp['ssm_w_glu'] = nrm((DEPTH, SSM_WIDTH, SSM_WIDTH), SSM_WIDTH ** -0.5)
    inp['ssm_b_glu'] = nrm((DEPTH, SSM_WIDTH), 0.01)
    inp['swa_sinks'] = nrm((DEPTH, SWA_HEADS), 0.5)
    inp['hgrn_lower_bounds'] = nrm((DEPTH, HGRN_WIDTH), 0.5)
    inp['out_norm_a'] = gain(SSM_WIDTH)
    inp['out_norm_b'] = gain(SWA_WIDTH)
    inp['out_norm_c'] = gain(DIL_WIDTH)
    inp['out_norm_d'] = gain(HGRN_WIDTH)
    inp['w_out'] = nrm((DEPTH, MIX_WIDTH, D_MODEL), MIX_WIDTH ** -0.5)
    inp['mix_norm_post'] = gain(D_MODEL)
    inp['ffn2_norm_pre'] = gain(D_MODEL)
    inp['ffn2_w_gate'] = nrm((DEPTH, D_MODEL, D_FF), D_MODEL ** -0.5)
    inp['ffn2_w_up'] = nrm((DEPTH, D_MODEL, D_FF), D_MODEL ** -0.5)
    inp['ffn2_w_down'] = nrm((DEPTH, D_FF, D_MODEL), D_FF ** -0.5)
    inp['ffn2_norm_post'] = gain(D_MODEL)
    return inp


def reference(x_prompt, x_sample, state_ssm, cache_swa_kv, cache_dil0_kv, cache_dil1_kv, cache_dil2_kv,
              state_hgrn, ffn1_norm_pre, ffn1_w_gate, ffn1_w_up, ffn1_w_down, ffn1_norm_post, mix_norm_pre,
              w_in, ssm_a_re, ssm_a_im, ssm_log_dt, ssm_b_re, ssm_b_im, ssm_c_re, ssm_c_im, ssm_d, ssm_w_glu,
              ssm_b_glu, swa_sinks, hgrn_lower_bounds, out_norm_a, out_norm_b, out_norm_c, out_norm_d, w_out,
              mix_norm_post, ffn2_norm_pre, ffn2_w_gate, ffn2_w_up, ffn2_w_down, ffn2_norm_post):
    lbp = jax.nn.softmax(hgrn_lower_bounds.astype(jnp.float32), axis=0)
    lower = jnp.cumsum(lbp, axis=0) - lbp[0:1]
    pos_p = jnp.arange(SEQ, dtype=jnp.int32)
    pos_s = PAST_LEN + jnp.arange(DEC_SEQ, dtype=jnp.int32)
    h_p, h_s = x_prompt, x_sample
    new_p = [[] for _ in range(6)]
    new_s = [[] for _ in range(6)]
    for l in range(DEPTH):
        lp = {
            'ffn1_norm_pre': ffn1_norm_pre[l], 'ffn1_w_gate': ffn1_w_gate[l], 'ffn1_w_up': ffn1_w_up[l],
            'ffn1_w_down': ffn1_w_down[l], 'ffn1_norm_post': ffn1_norm_post[l], 'mix_norm_pre': mix_norm_pre[l],
            'w_in': w_in[l], 'ssm_a_re': ssm_a_re[l], 'ssm_a_im': ssm_a_im[l], 'ssm_log_dt': ssm_log_dt[l],
            'ssm_b_re': ssm_b_re[l], 'ssm_b_im': ssm_b_im[l], 'ssm_c_re': ssm_c_re[l], 'ssm_c_im': ssm_c_im[l],
            'ssm_d': ssm_d[l], 'ssm_w_glu': ssm_w_glu[l], 'ssm_b_glu': ssm_b_glu[l], 'swa_sinks': swa_sinks[l],
            'out_norm_a': out_norm_a[l], 'out_norm_b': out_norm_b[l], 'out_norm_c': out_norm_c[l],
            'out_norm_d': out_norm_d[l], 'w_out': w_out[l], 'mix_norm_post': mix_norm_post[l],
            'ffn2_norm_pre': ffn2_norm_pre[l], 'ffn2_w_gate': ffn2_w_gate[l], 'ffn2_w_up': ffn2_w_up[l],
            'ffn2_w_down': ffn2_w_down[l], 'ffn2_norm_post': ffn2_norm_post[l],
        }
        h_p, st_p = trunk_layer(h_p, pos_p, lp, lower[l], None)
        cache = (state_ssm[l], cache_swa_kv[l], cache_dil0_kv[l], cache_dil1_kv[l], cache_dil2_kv[l], state_hgrn[l])
        h_s, st_s = trunk_layer(h_s, pos_s, lp, lower[l], cache)
        for i in range(6):
            new_p[i].append(st_p[i])
            new_s[i].append(st_s[i])
    new_p = [jnp.stack(a, axis=0) for a in new_p]
    new_s = [jnp.stack(a, axis=0) for a in new_s]
    return (h_p, h_s, new_p[0], new_s[0], new_p[1], new_s[1], new_p[2], new_s[2],
            new_p[3], new_s[3], new_p[4], new_s[4], new_p[5], new_s[5])
```

```python
import numpy as np
from contextlib import ExitStack
import concourse.bass as bass
import concourse.mybir as mybir
from concourse.bass_utils import run_bass_kernel_spmd

F32 = mybir.dt.float32
BF16 = mybir.dt.bfloat16
ALU = mybir.AluOpType
AF = mybir.ActivationFunctionType
AX = mybir.AxisListType

D = 2048
KT = D // 128
DFF = 5504
FT = DFF // 128
FH = (22, 21)
L = 2
SEQ = 2048
SEG = 512
NSEG = SEQ // SEG
NS = 8
EPS = 1e-6
NEG = -1e30


class Buf:
    __slots__ = ("t", "name", "w", "r", "dsem", "dcnt", "psum")

    def __init__(self, t, name):
        self.t = t
        self.name = name
        self.psum = False
        self.w = []
        self.r = []
        self.dsem = None
        self.dcnt = 0

    def __getitem__(self, k):
        return self.t[k]


class EngS:
    def __init__(self, name, eng, sem):
        self.name = name
        self.eng = eng
        self.sem = sem
        self.cnt = 0
        self.waited = {}
        self.pend_r = []
        self.pend_w = []


class Ctx:
    def __init__(self, nc, stack):
        self.nc = nc
        self.stack = stack
        self.E = {}
        for name, eng in (("pe", nc.tensor), ("act", nc.scalar), ("dve", nc.vector),
                          ("pool", nc.gpsimd), ("sp", nc.sync)):
            sem = stack.enter_context(nc.semaphore("s_" + name))
            self.E[name] = EngS(name, eng, sem)
        self.nbuf = 0
        self.out_tokens = []
        self.dma_tokens = {}
        self.nins = 0

    def sbuf(self, shape, dt, name=None):
        self.nbuf += 1
        name = f"{name or 'b'}_{self.nbuf}"
        t = self.stack.enter_context(self.nc.sbuf_tensor(name, list(shape), dt))
        return Buf(t, name)

    def psum(self, shape, dt, name=None):
        self.nbuf += 1
        name = f"{name or 'p'}_{self.nbuf}"
        t = self.stack.enter_context(self.nc.psum_tensor(name, list(shape), dt))
        bb = Buf(t, name)
        bb.psum = True
        return bb

    def view(self, buf, name=None):
        self.nbuf += 1
        return Buf(buf.t, f"{name or 'v'}_{self.nbuf}")

    def _wait(self, es, tokens):
        best = {}
        for (sem, val) in tokens:
            k = id(sem)
            if k not in best or best[k][1] < val:
                best[k] = (sem, val)
        for k, (sem, val) in best.items():
            if es.name == "pe" and sem is es.sem:
                continue
            if es.waited.get(k, 0) >= val:
                continue
            es.eng.wait_ge(sem, val)
            es.waited[k] = val

    def _deps(self, reads, writes, en=None):
        toks = []
        for e in self.E.values():
            if e.name == en:
                continue
            for b in writes:
                if any(b is x for x in e.pend_r) or any(b is x for x in e.pend_w):
                    raise RuntimeError(f"hazard: {b.name} is written while engine {e.name} has un-signalled accesses to it")
            for b in reads:
                if any(b is x for x in e.pend_w):
                    raise RuntimeError(f"hazard: {b.name} is read while engine {e.name} has un-signalled writes to it")
        own = self.E[en].sem if en in self.E else None
        for b in reads:
            toks += b.w
            if b.psum:
                toks += [t for t in b.r if t[0] is not own]
        for b in writes:
            toks += b.w
            toks += b.r
        return toks

    def op(self, en, fn, reads=(), writes=(), inc=True):
        es = self.E[en]
        self._wait(es, self._deps(reads, writes, en))
        ins = fn(es.eng)
        self.nins += 1
        es.pend_r += list(reads)
        es.pend_w += list(writes)
        if inc:
            es.cnt += 1
            ins.then_inc(es.sem, 1)
            tok = (es.sem, es.cnt)
            for b in es.pend_r:
                b.r.append(tok)
                if len(b.r) > 12:
                    b.r = self._compact(b.r)
            for b in es.pend_w:
                b.w = [tok]
                b.r = []
            es.pend_r = []
            es.pend_w = []
        return ins

    @staticmethod
    def _compact(toks):
        best = {}
        for (sem, val) in toks:
            k = id(sem)
            if k not in best or best[k][1] < val:
                best[k] = (sem, val)
        return list(best.values())

    def dma(self, qn, out_ap, in_ap, reads=(), writes=(), is_output=False, owner=None):
        es = self.E[qn]
        self._wait(es, self._deps(reads, writes, qn))
        bufs = list(writes) + list(reads)
        owner = owner or bufs[0]
        if owner.dsem is None:
            owner.dsem = self.stack.enter_context(self.nc.semaphore("d_" + owner.name))
        owner.dcnt += 16
        tok = (owner.dsem, owner.dcnt)
        ins = es.eng.dma_start(out=out_ap, in_=in_ap)
        ins.then_inc(owner.dsem, 16)
        self.nins += 1
        for b in reads:
            b.r.append(tok)
        for b in writes:
            b.w = [tok]
            b.r = []
        self.dma_tokens[id(owner.dsem)] = tok
        if is_output:
            self.out_tokens.append(tok)
        return ins

    def barrier(self, engines=("pe", "act", "dve", "pool", "sp")):
        toks = [(e.sem, e.cnt) for e in self.E.values() if e.cnt > 0]
        toks += list(self.dma_tokens.values())
        for n in engines:
            self._wait(self.E[n], toks)

    def finish(self):
        es = self.E["sp"]
        toks = list(self.out_tokens) + list(self.dma_tokens.values())
        for n, e in self.E.items():
            if e.cnt > 0:
                toks.append((e.sem, e.cnt))
        self._wait(es, toks)


class Prog:
    def __init__(self, nseg=NSEG, nlayer=L, with_mix=True, dbg=(), parts=("mix",)):
        self.parts = set(parts)
        self.nseg = nseg
        self.nlayer = nlayer
        self.with_mix = with_mix
        self.dbg = set(dbg)
        self.ntok = nseg * SEG + NS
        self.nc = bass.Bass("TRN2", target_bir_lowering=False)
        self.ins = {}
        self.outs = {}

    def din(self, name, shape):
        t = self.nc.dram_tensor(name, list(shape), F32, kind="ExternalInput").ap()
        self.ins[name] = t
        return t

    def dout(self, name, shape):
        t = self.nc.dram_tensor(name, list(shape), F32, kind="ExternalOutput").ap()
        self.outs[name] = t
        return t

    def build(self):
        nc = self.nc
        nl = self.nlayer
        self.xT = self.din("xT", [D, self.ntok])
        self.yT = self.dout("yT", [D, self.ntok])
        self.wgu = [self.din(f"wgu{i}", [nl, FT, 128, 2, KT, 128]) for i in (1, 2)]
        self.wdn = [self.din(f"wdn{i}", [nl, KT, 128, FT, 128]) for i in (1, 2)]
        self.gains = self.din("gains", [128, nl * 6 * KT])
        if self.with_mix:
            self.declare_mix()
        with ExitStack() as st:
            self.c = c = Ctx(nc, st)
            self.st = st
            self.setup_consts()
            if self.with_mix:
                self.setup_mix_consts()
            for seg in range(self.nseg):
                self.run_segment(seg)
            c.finish()
        return nc

    def setup_consts(self):
        c = self.c
        nl = self.nlayer
        self.G = c.sbuf([128, nl * 6 * KT], F32, "gains")
        c.dma("sp", self.G[:], self.gains, writes=[self.G])
        for l in range(nl):
            for n in (1, 5):
                o = (l * 6 + n) * KT
                c.op("dve", lambda e, o=o: e.tensor_scalar(out=self.G[:, o:o + KT], in0=self.G[:, o:o + KT],
                                                           scalar1=0.5, scalar2=None, op0=ALU.mult),
                     reads=[self.G], writes=[self.G])
        self.ones_bf = c.sbuf([128, 128], BF16, "ones")
        c.op("dve", lambda e: e.memset(self.ones_bf[:], 1.0), writes=[self.ones_bf])
        self.NT = SEG + NS
        self.H = c.sbuf([128, KT, self.NT], F32, "H")
        xn = c.sbuf([128, KT, self.NT], BF16, "XN")
        self.XN = (xn, xn.t)
        self.YB = KT * self.NT * 4 + 1024
        self.AB = max(FH[0], 18) * self.NT * 2
        self.ARENA = c.sbuf([128, (self.YB + self.AB) // 4], F32, "ARENA")
        yb = Buf(self.ARENA.t, "Y")
        self.Y = (yb, self.ARENA.t[:, 0:KT * self.NT].rearrange("p (k t) -> p k t", k=KT))
        self.ACTB = Buf(self.ARENA.t[:, self.YB // 4:(self.YB + FH[0] * self.NT * 2) // 4].bitcast(BF16).rearrange(
            "p (k t) -> p k t", k=FH[0]), "ACT")
        self.WA = [c.sbuf([128, 2, KT, 128], BF16, f"WA{i}") for i in range(3)]
        self.WD = [c.sbuf([128, FH[0], 128], BF16, f"WD{i}") for i in range(2)]
        self.wa_i = 0
        self.wd_i = 0
        self.PS = [c.psum([128, 512], F32, f"PS{i}") for i in range(8)]
        self.tmp = [c.sbuf([128, 512], F32, f"tmp{i}") for i in range(2)]
        self.tmp_i = 0
        self.sq = [c.sbuf([128, 512], BF16, f"sq{i}") for i in range(2)]
        self.sq_i = 0
        self.rstd = c.sbuf([128, self.NT], F32, "rstd")

    def dump(self, name, buf, ap, shape, dt=F32):
        if name not in self.dbg:
            return
        c = self.c
        t = c.sbuf(shape, F32, "dbg_" + name)
        c.op("act", lambda e: e.activation(out=t[:], in_=ap, func=AF.Copy), reads=[buf], writes=[t])
        o = self.dout("dbg_" + name, shape)
        c.dma("sp", o, t[:], reads=[t], is_output=True)
        self.dbg.discard(name)

    def tiles(self, seg):
        t = [(0, SEG)]
        if seg == self.nseg - 1:
            t.append((SEG, NS))
        return t

    def run_segment(self, seg):
        c = self.c
        c.dma("sp", self.H[:, :, 0:SEG], self.xT[:, seg * SEG:(seg + 1) * SEG].rearrange("(k p) t -> p k t", p=128),
              writes=[self.H])
        if seg == self.nseg - 1:
            c.dma("sp", self.H[:, :, SEG:SEG + NS],
                  self.xT[:, self.nseg * SEG:self.nseg * SEG + NS].rearrange("(k p) t -> p k t", p=128),
                  writes=[self.H])
        for l in range(self.nlayer):
            self.ffn(seg, l, 0)
            if self.with_mix:
                self.mixer(seg, l)
            self.ffn(seg, l, 1)
        c.dma("sp", self.yT[:, seg * SEG:(seg + 1) * SEG].rearrange("(k p) t -> p k t", p=128), self.H[:, :, 0:SEG],
              reads=[self.H], is_output=True)
        if seg == self.nseg - 1:
            c.dma("sp", self.yT[:, self.nseg * SEG:self.nseg * SEG + NS].rearrange("(k p) t -> p k t", p=128),
                  self.H[:, :, SEG:SEG + NS], reads=[self.H], is_output=True)

    def sumsq_rstd(self, src_buf, src_ap_fn, nchunk, n_feat, c0, n, ps, eps=EPS):
        c = self.c
        for k in range(nchunk):
            sq = self.sq[self.sq_i]
            self.sq_i ^= 1
            c.op("act", lambda e, k=k, sq=sq: e.activation(out=sq[:, 0:n], in_=src_ap_fn(k), func=AF.Square),
                 reads=[src_buf], writes=[sq])
            c.op("pe", lambda e, k=k, sq=sq: e.matmul(ps[:, 0:n], self.ones_bf[:, :], sq[:, 0:n],
                                                      start=(k == 0), stop=(k == nchunk - 1)),
                 reads=[sq, self.ones_bf], writes=[ps], inc=True)
        c.op("act", lambda e: e.activation(out=self.rstd[:, c0:c0 + n], in_=ps[:, 0:n], func=AF.Sqrt,
                                           bias=self.eps_ap(), scale=1.0 / n_feat),
             reads=[ps, self.epsb], writes=[self.rstd])
        c.op("dve", lambda e: e.reciprocal(out=self.rstd[:, c0:c0 + n], in_=self.rstd[:, c0:c0 + n]),
             reads=[self.rstd], writes=[self.rstd])

    def eps_ap(self):
        if not hasattr(self, "epsb"):
            self.epsb = self.c.sbuf([128, 1], F32, "eps")
            self.c.op("dve", lambda e: e.memset(self.epsb[:], EPS), writes=[self.epsb])
        return self.epsb[:, 0:1]

    def pre_norm(self, seg, gofs):
        c = self.c
        self.eps_ap()
        xnb, xn = self.XN
        for (c0, n) in self.tiles(seg):
            ps = self.PS[7]
            self.sumsq_rstd(self.H, lambda k: self.H[:, k, c0:c0 + n], KT, D, c0, n, ps)
            for k in range(KT):
                c.op("dve", lambda e, k=k: e.scalar_tensor_tensor(
                    out=xn[:, k, c0:c0 + n], in0=self.H[:, k, c0:c0 + n], scalar=self.G[:, gofs + k:gofs + k + 1],
                    in1=self.rstd[:, c0:c0 + n], op0=ALU.mult, op1=ALU.mult),
                    reads=[self.H, self.G, self.rstd], writes=[xnb])

    def post_norm_add(self, seg, gofs):
        c = self.c
        yb, y = self.Y
        for (c0, n) in self.tiles(seg):
            ps = self.PS[7]
            self.sumsq_rstd(yb, lambda k: y[:, k, c0:c0 + n], KT, D, c0, n, ps)
            for k in range(KT):
                t = self.tmp[self.tmp_i]
                self.tmp_i ^= 1
                c.op("dve", lambda e, k=k, t=t: e.scalar_tensor_tensor(
                    out=t[:, 0:n], in0=y[:, k, c0:c0 + n], scalar=self.G[:, gofs + k:gofs + k + 1],
                    in1=self.rstd[:, c0:c0 + n], op0=ALU.mult, op1=ALU.mult),
                    reads=[yb, self.G, self.rstd], writes=[t])
                c.op("pool", lambda e, k=k, t=t: e.tensor_tensor(
                    out=self.H[:, k, c0:c0 + n], in0=self.H[:, k, c0:c0 + n], in1=t[:, 0:n], op=ALU.add),
                    reads=[t, self.H], writes=[self.H])

    def ffn(self, seg, l, which):
        c = self.c
        tiles = self.tiles(seg)
        gbase = (l * 6 + (0 if which == 0 else 4)) * KT
        self.pre_norm(seg, gbase)
        xnb, xn = self.XN
        yb, y = self.Y
        self.dump("rstd", self.rstd, self.rstd[:, 0:512], [128, 512])
        self.dump("xn", xnb, xn[:, 3, 0:512], [128, 512])
        wgu = self.wgu[which]
        wdn = self.wdn[which]
        m0 = 0
        for half in range(2):
            nm = FH[half]
            for mi in range(nm):
                m = m0 + mi
                W = self.WA[self.wa_i]
                self.wa_i = (self.wa_i + 1) % len(self.WA)
                c.dma("pool", W[:, :, :, :], wgu[l, m], writes=[W])
                for ti, (c0, n) in enumerate(tiles):
                    pg = self.PS[0 + (mi % 2)] if ti == 0 else self.PS[6]
                    pu = self.PS[2 + (mi % 2)] if ti == 0 else self.PS[6]
                    og = 0 if ti == 0 else 0
                    ou = 0 if ti == 0 else 16
                    for k in range(KT):
                        c.op("pe", lambda e, k=k, pg=pg, og=og: e.matmul(pg[:, og:og + n], W[:, 0, k, :], xn[:, k, c0:c0 + n],
                                                                      start=(k == 0), stop=(k == KT - 1)),
                             reads=[W, xnb], writes=[pg], inc=(k == KT - 1))
                    for k in range(KT):
                        c.op("pe", lambda e, k=k, pu=pu, ou=ou: e.matmul(pu[:, ou:ou + n], W[:, 1, k, :], xn[:, k, c0:c0 + n],
                                                                      start=(k == 0), stop=(k == KT - 1)),
                             reads=[W, xnb], writes=[pu], inc=(k == KT - 1))
                    t = self.tmp[self.tmp_i]
                    self.tmp_i ^= 1
                    c.op("act", lambda e, t=t, pg=pg, og=og: e.activation(out=t[:, 0:n], in_=pg[:, og:og + n], func=AF.Silu),
                         reads=[pg], writes=[t])
                    c.op("dve", lambda e, t=t, pu=pu, ou=ou, mi=mi: e.tensor_tensor(
                        out=self.ACTB[:, mi, c0:c0 + n], in0=t[:, 0:n], in1=pu[:, ou:ou + n], op=ALU.mult),
                        reads=[t, pu], writes=[self.ACTB])
            self.dump("act", self.ACTB, self.ACTB[:, 1, 0:512], [128, 512])
            for mo in range(KT):
                W = self.WD[self.wd_i]
                self.wd_i = (self.wd_i + 1) % len(self.WD)
                c.dma("pool", W[:, 0:nm, :], wdn[l, mo, :, m0:m0 + nm, :], writes=[W])
                for ti, (c0, n) in enumerate(tiles):
                    pd = self.PS[4 + (mo % 2)] if ti == 0 else self.PS[6]
                    od = 0 if ti == 0 else 32
                    for k in range(nm):
                        c.op("pe", lambda e, k=k, pd=pd, od=od: e.matmul(pd[:, od:od + n], W[:, k, :], self.ACTB[:, k, c0:c0 + n],
                                                                      start=(k == 0), stop=(k == nm - 1)),
                             reads=[W, self.ACTB], writes=[pd], inc=(k == nm - 1))
                    if half == 0:
                        c.op("act", lambda e, pd=pd, od=od, mo=mo: e.activation(out=y[:, mo, c0:c0 + n], in_=pd[:, od:od + n], func=AF.Copy),
                             reads=[pd, xnb], writes=[yb])
                    else:
                        c.op("dve", lambda e, pd=pd, od=od, mo=mo: e.tensor_tensor(out=y[:, mo, c0:c0 + n], in0=y[:, mo, c0:c0 + n],
                                                                                  in1=pd[:, od:od + n], op=ALU.add),
                             reads=[pd, yb], writes=[yb])
            m0 += nm
        self.dump("y", yb, y[:, 2, 0:512], [128, 512])
        self.post_norm_add(seg, gbase + KT)

    NCB = 258 + 512 + 128 + 128 + 512 + 512
    NCF = 128 + 512 + 512
    CB_BAND, CB_M2, CB_BD, CB_ID, CB_MB, CB_MC = 0, 258, 770, 898, 1026, 1538
    CF_ID, CF_IOTA, CF_RST = 0, 128, 640

    def declare_mix(self):
        nl = self.nlayer
        self.win_f = self.din("win_f", [nl, NPAIR, 128, 2, KT, 128])
        self.wt = self.din("wt", [nl, 4, 128, KT, 256])
        self.wout = self.din("wout", [nl, KT, 128, MIXK, 128])
        self.wglu = self.din("wglu", [nl, 4, 128, 4, 128])
        self.rot = self.din("rot", [128, 2, self.ntok])
        self.cbf = self.din("cbf", [128, self.NCB])
        self.cf32 = self.din("cf32", [128, self.NCF])
        self.plA = self.din("plA", [128, nl, 46])
        self.pmix = self.din("pmix", [128, nl, 30])
        self.hlb = self.din("hlb", [128, nl, 4])
        self.s5r = self.din("s5r", [128, nl, 1728])
        self.x0s = self.din("x0s", [128, nl, NS, 14, 2])
        self.hs0 = self.din("hs0", [128, nl, NS, 4, 128])
        self.kct = self.din("kct", [128, nl, NS, 4, 128])
        self.vct = self.din("vct", [128, nl, NS, 320])
        self.o_ssm_p = self.dout("o_ssm_p", [128, nl, 14, 2])
        self.o_ssm_s = self.dout("o_ssm_s", [128, nl, NS, 14, 2])
        self.o_swa_k = self.dout("o_swa_k", [nl, 128, 128 + NS])
        self.o_swa_v = self.dout("o_swa_v", [nl, 128, 128])
        self.o_sv = self.dout("o_sv", [nl, NS, 320])
        self.o_d0_k = self.dout("o_d0_k", [nl, 64, 128 + NS])
        self.o_d0_v = self.dout("o_d0_v", [nl, 128, 64])
        self.o_d1_k = self.dout("o_d1_k", [nl, 64, 512 + NS])
        self.o_d1_v = self.dout("o_d1_v", [nl, 4, 128, 64])
        self.o_d2_k = self.dout("o_d2_k", [nl, 64, self.nseg * SEG + NS])
        self.o_d2_v = self.dout("o_d2_v", [nl, self.nseg, 4, 128, 64])
        self.o_hg_p = self.dout("o_hg_p", [nl, 4, 128, 128])
        self.o_hg_s = self.dout("o_hg_s", [nl, NS, 4, 128, 128])

    def setup_mix_consts(self):
        c = self.c
        nl = self.nlayer
        NT = self.NT
        self.CB = c.sbuf([128, self.NCB], BF16, "CB")
        c.dma("pool", self.CB[:], self.cbf, writes=[self.CB])
        self.CF = c.sbuf([128, self.NCF], F32, "CF")
        c.dma("sp", self.CF[:], self.cf32, writes=[self.CF])
        self.idb = self.CB[:, self.CB_ID:self.CB_ID + 128]
        self.idf = self.CF[:, self.CF_ID:self.CF_ID + 128]
        self.PLA = c.sbuf([128, nl, 46], F32, "PLA")
        c.dma("sp", self.PLA[:], self.plA, writes=[self.PLA])
        self.PMX = c.sbuf([128, nl, 30], F32, "PMX")
        c.dma("sp", self.PMX[:], self.pmix, writes=[self.PMX])
        self.HLB = c.sbuf([128, nl, 4], F32, "HLB")
        c.dma("sp", self.HLB[:], self.hlb, writes=[self.HLB])
        self.SK8 = c.sbuf([1, 8], BF16, "SK8")
        self.LBT = c.sbuf([128, nl, 4, 3], F32, "LBT")
        mx = c.sbuf([128, 4], F32, "lbmx")
        ex = c.sbuf([128, nl, 4], F32, "lbex")
        sm = c.sbuf([128, 4], F32, "lbsm")
        c.op("dve", lambda e: e.tensor_copy(out=mx[:], in_=self.HLB[:, 0, :]), reads=[self.HLB], writes=[mx])
        for l in range(1, nl):
            c.op("dve", lambda e, l=l: e.tensor_tensor(out=mx[:], in0=mx[:], in1=self.HLB[:, l, :], op=ALU.max),
                 reads=[self.HLB, mx], writes=[mx])
        for l in range(nl):
            c.op("dve", lambda e, l=l: e.tensor_tensor(out=ex[:, l, :], in0=self.HLB[:, l, :], in1=mx[:], op=ALU.subtract),
                 reads=[self.HLB, mx], writes=[ex])
        c.op("act", lambda e: e.activation(out=ex[:], in_=ex[:], func=AF.Exp), reads=[ex], writes=[ex])
        c.op("dve", lambda e: e.tensor_copy(out=sm[:], in_=ex[:, 0, :]), reads=[ex], writes=[sm])
        for l in range(1, nl):
            c.op("dve", lambda e, l=l: e.tensor_tensor(out=sm[:], in0=sm[:], in1=ex[:, l, :], op=ALU.add),
                 reads=[ex, sm], writes=[sm])
        c.op("dve", lambda e: e.reciprocal(out=sm[:], in_=sm[:]), reads=[sm], writes=[sm])
        for l in range(nl):
            c.op("dve", lambda e, l=l: e.tensor_tensor(out=ex[:, l, :], in0=ex[:, l, :], in1=sm[:], op=ALU.mult),
                 reads=[ex, sm], writes=[ex])
        cum = c.sbuf([128, 4], F32, "lbcum")
        c.op("dve", lambda e: e.memset(cum[:], 0.0), writes=[cum])
        for l in range(nl):
            if l > 0:
                c.op("dve", lambda e, l=l: e.tensor_tensor(out=cum[:], in0=cum[:], in1=ex[:, l, :], op=ALU.add),
                     reads=[ex, cum], writes=[cum])
            c.op("dve", lambda e, l=l: e.tensor_scalar(out=self.LBT[:, l, :, 0], in0=cum[:], scalar1=-1.0, scalar2=1.0,
                                                       op0=ALU.mult, op1=ALU.add), reads=[cum], writes=[self.LBT])
            c.op("dve", lambda e, l=l: e.tensor_scalar(out=self.LBT[:, l, :, 1], in0=cum[:], scalar1=1e-30, scalar2=None,
                                                       op0=ALU.max), reads=[cum], writes=[self.LBT])
            c.op("dve", lambda e, l=l: e.tensor_scalar(out=self.LBT[:, l, :, 2], in0=cum[:], scalar1=1.0, scalar2=-1.0,
                                                       op0=ALU.mult, op1=ALU.add), reads=[cum], writes=[self.LBT])
        self.KB = [[c.sbuf([128, NT], BF16, f"KB{l}{p}") for p in range(2)] for l in range(nl)]
        self.KC0 = [[c.sbuf([128, NT], BF16, f"KC0{l}{p}") for p in range(2)] for l in range(nl)]
        self.KC1 = [[c.sbuf([128, NT], BF16, f"KC1{l}{p}") for p in range(2)] for l in range(nl)]
        self.KC2 = [c.sbuf([128, self.nseg * SEG + NS], BF16, f"KC2{l}") for l in range(nl)]
        self.VBh = [[c.sbuf([128, 4, 128], BF16, f"VB{l}{p}") for p in range(2)] for l in range(nl)]
        self.VC0h = [[c.sbuf([128, 4, 64], BF16, f"VC0{l}{p}") for p in range(2)] for l in range(nl)]
        self.VC1h = [[c.sbuf([128, 4, 64], BF16, f"VC1{l}{p}") for p in range(2)] for l in range(nl)]
        self.VC2 = [c.sbuf([128, 4, self.nseg, 64], BF16, f"VC2{l}") for l in range(nl)]
        self.X0 = [c.sbuf([128, 14, 2], F32, f"X0{l}") for l in range(nl)]
        self.SH = [[c.sbuf([128, 4, 128], F32, f"SH{l}{p}") for p in range(2)] for l in range(nl)]
        for l in range(nl):
            c.op("pool", lambda e, l=l: e.memset(self.X0[l][:], 0.0), writes=[self.X0[l]])
            c.op("pool", lambda e, l=l: e.memset(self.SH[l][0][:], 0.0), writes=[self.SH[l][0]])
        self.sh_par = [0] * nl
        self.MIX = Buf(self.ARENA.t[:, self.YB // 4:(self.YB + MIXK * NT * 2) // 4].bitcast(BF16).rearrange(
            "p (k t) -> p k t", k=MIXK), "MIX")
        self.stat = [c.sbuf([128, 8], F32, f"stat{i}") for i in range(4)]

    def ar_reset(self):
        self.ar_off = [0, 0]
        self.rings = {}

    def ta(self, shape, dt, name, pool=0):
        esz = 2 if dt == BF16 else 4
        n = 1
        for s in shape[1:]:
            n *= s
        nbytes = (n * esz + 3) // 4 * 4
        off = self.ar_off[pool]
        lim = self.YB if pool == 0 else KT * self.NT * 2
        assert off + nbytes <= lim, f"arena pool {pool} overflow at {name}: {off}+{nbytes} > {lim}"
        self.ar_off[pool] = off + nbytes
        if pool == 0:
            base = self.ARENA.t[:, off // 4:(off + nbytes) // 4]
        else:
            base = self.XN[0].t.rearrange("p k t -> p (k t)")[:, off // 2:(off + nbytes) // 2].bitcast(F32)
        P = shape[0]
        ap = base[0:P, :]
        if dt == BF16:
            ap = ap.bitcast(BF16)
        ap = ap[:, 0:n]
        if len(shape) == 3:
            ap = ap.rearrange("p (a b) -> p a b", a=shape[1])
        elif len(shape) == 4:
            ap = ap.rearrange("p (a b c) -> p a b c", a=shape[1], b=shape[2])
        self.c.nbuf += 1
        return Buf(ap, f"{name}_{self.c.nbuf}")

    def next_wa(self):
        W = self.WA[self.wa_i]
        self.wa_i = (self.wa_i + 1) % len(self.WA)
        return W

    def proj_pair(self, l, pr, tiles, consume, which=(0, 1)):
        c = self.c
        xnb, xn = self.XN
        W = self.next_wa()
        c.dma("pool", W[:, :, :, :], self.win_f[l, pr], writes=[W])
        for ti, (c0, n) in enumerate(tiles):
            res = []
            for s in which:
                if ti == 0:
                    pb = self.PS[2 * s + self.pp[s]]
                    self.pp[s] ^= 1
                    o = 0
                else:
                    pb = self.PS[6]
                    o = 64 * s
                for k in range(KT):
                    c.op("pe", lambda e, k=k, pb=pb, o=o, s=s: e.matmul(pb[:, o:o + n], W[:, s, k, :], xn[:, k, c0:c0 + n],
                                                                      start=(k == 0), stop=(k == KT - 1)),
                         reads=[W, xnb], writes=[pb], inc=(k == KT - 1))
                res.append((pb, pb[:, o:o + n]))
            consume(c0, n, res)

    def rotary(self, c0, n, res, dst_buf, dst_ap, f32_buf=None, f32_ap=None):
        c = self.c
        (b0, x), (b1, xs) = res
        t1 = self.tmp[0]
        t2 = self.tmp[1]
        c.op("dve", lambda e: e.tensor_tensor(out=t1[:, 0:n], in0=x, in1=self.ROT[:, 0, c0:c0 + n], op=ALU.mult),
             reads=[b0, self.ROT], writes=[t1])
        c.op("dve", lambda e: e.tensor_tensor(out=t2[:, 0:n], in0=xs, in1=self.ROT[:, 1, c0:c0 + n], op=ALU.mult),
             reads=[b1, self.ROT], writes=[t2])
        if f32_buf is None:
            c.op("pool", lambda e: e.tensor_tensor(out=dst_ap, in0=t1[:, 0:n], in1=t2[:, 0:n], op=ALU.add),
                 reads=[t1, t2], writes=[dst_buf])
        else:
            c.op("pool", lambda e: e.tensor_tensor(out=f32_ap, in0=t1[:, 0:n], in1=t2[:, 0:n], op=ALU.add),
                 reads=[t1, t2], writes=[f32_buf])
            c.op("act", lambda e: e.activation(out=dst_ap, in_=f32_ap, func=AF.Copy), reads=[f32_buf], writes=[dst_buf])

    def mix_norm(self, seg, src_buf, src_fn, nchunk, n_feat, l, gofs, mbase):
        c = self.c
        for (c0, n) in self.tiles(seg):
            ps = self.PS[7]
            self.sumsq_rstd(src_buf, lambda k: src_fn(k, c0, n), nchunk, n_feat, c0, n, ps)
            for k in range(nchunk):
                c.op("dve", lambda e, k=k: e.scalar_tensor_tensor(
                    out=self.MIX[:, mbase + k, c0:c0 + n], in0=src_fn(k, c0, n),
                    scalar=self.PMX[:, l, gofs + k:gofs + k + 1], in1=self.rstd[:, c0:c0 + n],
                    op0=ALU.mult, op1=ALU.mult), reads=[src_buf, self.PMX, self.rstd], writes=[self.MIX])

    def mixer(self, seg, l):
        c = self.c
        last = (seg == self.nseg - 1)
        NC = SEG + NS if last else SEG
        gm = (l * 6 + 2) * KT
        c.barrier()
        self.ar_reset()
        self.pp = [0, 0]
        self.pre_norm(seg, gm)
        c.op("act", lambda e: e.activation(out=self.SK8[0:1, :], in_=self.PMX[0:1, l, 22:30], func=AF.Copy, scale=8.0),
             reads=[self.PMX], writes=[self.SK8])
        if "mix" in self.parts or "D" in self.parts:
            self.mix_D(seg, l, NC)
        if "mix" in self.parts or "A" in self.parts:
            self.mix_A(seg, l, NC)
        if "mix" in self.parts or "BC" in self.parts:
            self.mix_BC(seg, l, NC)
        if "mix" in self.parts:
            self.mix_out(seg, l, NC)
        c.barrier()
    def attn_units(self, units):
        for i in range(0, len(units), 2):
            batch = units[i:i + 2]
            for s, u in enumerate(batch):
                self.au_scores(u, s)
            for s, u in enumerate(batch):
                self.au_softmax(u, s)
            for s, u in enumerate(batch):
                self.au_transpose(u, s)
            for s, u in enumerate(batch):
                self.au_pv(u, s)
            for s, u in enumerate(batch):
                self.au_out(u, s)

    def au_scores(self, u, s):
        c = self.c
        nq, ncols = u["nq"], u["ncols"]
        ps = self.PS[3 * s]
        if u.get("sink") is not None:
            hh = u["sink"]
            nkeys = u["nkeys"]
            c.op("pe", lambda e: e.matmul(ps[0:nq, nkeys:nkeys + 1], self.ones_bf[0:1, 0:nq], self.SK8[0:1, hh:hh + 1], start=True, stop=True),
                 reads=[self.ones_bf, self.SK8], writes=[ps], inc=False)
        first = True
        if u["mask"] is not None:
            mb, map_ = u["mask"]
            nkk = u["nkeys"]
            c.op("pe", lambda e: e.matmul(ps[0:nq, 0:nkk], self.idb[0:nq, 0:nq], map_, start=True, stop=False),
                 reads=[self.CB, mb], writes=[ps], inc=False)
            first = False
        col = 0
        nkb = len(u["kblocks"])
        for i, (kb, kap, nk, rr) in enumerate(u["kblocks"]):
            o = ps[0:nq, col:col + nk]
            if rr:
                o = o.rearrange("p (s r j) -> p s r j", s=rr[0], r=rr[1])
            stp = True if first else (i == nkb - 1)
            c.op("pe", lambda e, o=o, kap=kap, stp=stp: e.matmul(o, u["q"][1], kap, start=first, stop=stp),
                 reads=[u["q"][0], kb], writes=[ps], inc=(i == nkb - 1))
            col += nk

    def au_softmax(self, u, s):
        c = self.c
        nq, ncols, nkeys = u["nq"], u["ncols"], u["nkeys"]
        ps = self.PS[3 * s]
        st = self.stat[s]
        P = self.P32[s]
        c.op("dve", lambda e: e.reduce_max(out=st[0:nq, 0:1], in_=ps[0:nq, 0:ncols], axis=AX.X), reads=[ps], writes=[st])
        c.op("dve", lambda e: e.tensor_scalar(out=st[0:nq, 1:2], in0=st[0:nq, 0:1], scalar1=-0.125, scalar2=None, op0=ALU.mult),
             reads=[st], writes=[st])
        c.op("act", lambda e: e.activation(out=P[0:nq, 0:ncols], in_=ps[0:nq, 0:ncols], func=AF.Exp, bias=st[0:nq, 1:2],
                                           scale=0.125, accum_out=st[0:nq, 2:3]), reads=[ps, st], writes=[P, st])
        c.op("dve", lambda e: e.reciprocal(out=st[0:nq, 3:4], in_=st[0:nq, 2:3]), reads=[st], writes=[st])
        c.op("dve", lambda e: e.tensor_scalar(out=P[0:nq, 0:nkeys], in0=P[0:nq, 0:nkeys], scalar1=st[0:nq, 3:4], scalar2=None,
                                              op0=ALU.mult), reads=[P, st], writes=[P])
        if u.get("lse") is not None:
            c.op("act", lambda e: e.activation(out=st[0:nq, 4:5], in_=st[0:nq, 2:3], func=AF.Ln), reads=[st], writes=[st])
            c.op("dve", lambda e: e.scalar_tensor_tensor(out=st[0:nq, 5:6], in0=st[0:nq, 0:1], scalar=0.125, in1=st[0:nq, 4:5],
                                                         op0=ALU.mult, op1=ALU.add), reads=[st], writes=[st])

    def au_transpose(self, u, s):
        c = self.c
        nq = u["nq"]
        P = self.P32[s]
        pst = self.PS[3 * s + 1]
        PT = self.PTb[s]
        for i, vb in enumerate(u["vblocks"]):
            if vb["kind"] != "T":
                continue
            nk, k0 = vb["nk"], vb["k0"]
            c.op("pe", lambda e, i=i, nk=nk, k0=k0: e.transpose(pst[0:nk, i * 128:i * 128 + nq], P[0:nq, k0:k0 + nk],
                                                                   self.idf[0:nq, 0:nq]),
                 reads=[P, self.CF], writes=[pst])
            eng = "act" if s == 0 else "dve"
            if eng == "act":
                c.op("act", lambda e, i=i, nk=nk: e.activation(out=PT[0:nk, i, 0:nq], in_=pst[0:nk, i * 128:i * 128 + nq],
                                                               func=AF.Copy), reads=[pst], writes=[PT])
            else:
                c.op("dve", lambda e, i=i, nk=nk: e.tensor_copy(out=PT[0:nk, i, 0:nq], in_=pst[0:nk, i * 128:i * 128 + nq]),
                     reads=[pst], writes=[PT])

    def au_pv(self, u, s):
        c = self.c
        nq, hf = u["nq"], u["half"]
        P = self.P32[s]
        PT = self.PTb[s]
        pso = self.PS[3 * s + 2]
        st = self.stat[s]
        rows = slice(64 * hf, 64 * hf + 64)
        nvb = len(u["vblocks"])
        for i, vb in enumerate(u["vblocks"]):
            if vb["kind"] == "T":
                rhs = PT[0:vb["nk"], i, 0:nq]
                rd = [PT, vb["buf"]]
            else:
                rhs = P[0:1, vb["k0"]:vb["k0"] + 1]
                rd = [P, vb["buf"]]
            c.op("pe", lambda e, vb=vb, rhs=rhs, i=i: e.matmul(pso[rows, 0:nq], vb["ap"], rhs, start=(i == 0), stop=(i == nvb - 1)),
                 reads=rd, writes=[pso], inc=(i == nvb - 1))
        if u.get("lse") is not None:
            c.op("pe", lambda e: e.matmul(pso[:, 128:128 + nq], st[0:nq, 5:6].to_broadcast([nq, 128]), self.idf[0:nq, 0:nq],
                                          start=True, stop=True), reads=[st, self.CF], writes=[pso])

    def au_out(self, u, s):
        c = self.c
        nq, hf = u["nq"], u["half"]
        pso = self.PS[3 * s + 2]
        rows = slice(64 * hf, 64 * hf + 64)
        ob, oap = u["out"]
        src = pso[rows, 0:nq]
        if u.get("orr"):
            src = src.rearrange("p (r j) -> p r j", r=u["orr"])
        c.op("act", lambda e: e.activation(out=oap, in_=src, func=AF.Copy), reads=[pso], writes=[ob])
        if u.get("lse") is not None:
            lb, lap = u["lse"]
            src2 = pso[rows, 128:128 + nq]
            if u.get("orr"):
                src2 = src2.rearrange("p (r j) -> p r j", r=u["orr"])
            c.op("act", lambda e: e.activation(out=lap, in_=src2, func=AF.Copy), reads=[pso], writes=[lb])

    def mix_BC(self, seg, l, NC):
        c = self.c
        last = (seg == self.nseg - 1)
        NT = self.NT
        tiles = self.tiles(seg)
        xnb, xn = self.XN
        par = seg % 2
        KBc, KBp = self.KB[l][par], self.KB[l][1 - par]
        KC0c, KC0p = self.KC0[l][par], self.KC0[l][1 - par]
        KC1c, KC1p = self.KC1[l][par], self.KC1[l][1 - par]
        KC2 = self.KC2[l]
        VBc, VBp = self.VBh[l][par], self.VBh[l][1 - par]
        VC0c, VC0p = self.VC0h[l][par], self.VC0h[l][1 - par]
        VC1c, VC1p = self.VC1h[l][par], self.VC1h[l][1 - par]
        VC2 = self.VC2[l]
        QCT = self.ta([128, 6, NT], BF16, "QCT")
        V8 = self.ta([NS, 320], F32, "V8") if last else None
        self.P32 = [self.ta([128, 514], F32, f"P32{i}") for i in range(2)]
        self.PTb = [self.ta([128, 4, 128], BF16, f"PT{i}") for i in range(2)]
        if last:
            for nm, shp, dt in (("KCTb", [128, 4, 128], BF16), ("VCTb", [128, 320], BF16), ("VSb", [1, 320], F32)):
                self.ta_ring(nm, shp, dt)
        mark0 = self.ar_off[0]
        QBT = self.ta([128, 4, NT], BF16, "QBT")
        mark1 = self.ar_off[0]
        KF = self.ta([128, NT], F32, "KF")
        VF = self.ta([128, 192], F32, "VF")
        self.ROT = self.ta([128, 2, NT], F32, "ROT")
        c.dma("sp", self.ROT[:, :, 0:SEG], self.rot[:, :, seg * SEG:(seg + 1) * SEG], writes=[self.ROT])
        if last:
            c.dma("sp", self.ROT[:, :, SEG:SEG + NS], self.rot[:, :, self.nseg * SEG:self.nseg * SEG + NS], writes=[self.ROT])
        for j in range(4):
            self.proj_pair(l, 2 + j, tiles, lambda c0, n, res, j=j: self.rotary(c0, n, res, QBT, QBT[:, j, c0:c0 + n]))
        for j in range(6):
            self.proj_pair(l, 7 + j, tiles, lambda c0, n, res, j=j: self.rotary(c0, n, res, QCT, QCT[:, j, c0:c0 + n]))
        import os as _os
        _stopat = _os.environ.get("STOPAT", "")
        if _stopat == "q":
            return
        self.proj_pair(l, 6, tiles, lambda c0, n, res: self.rotary(c0, n, res, KBc, KBc[:, c0:c0 + n], KF, KF[:, c0:c0 + n]))
        if last:
            c.dma("sp", self.o_swa_k[l, :, 0:128], KF[:, SEG - 128:SEG], reads=[KF], is_output=True)
            c.dma("sp", self.o_swa_k[l, :, 128:128 + NS], KF[:, SEG:SEG + NS], reads=[KF], is_output=True)
        self.proj_pair(l, 13, tiles, lambda c0, n, res: self.rotary(c0, n, res, KC0c, KC0c[:, c0:c0 + n], KF, KF[:, c0:c0 + n]))
        if last:
            c.dma("sp", self.o_d0_k[l, :, 0:128], KF[0:64, SEG - 128:SEG], reads=[KF], is_output=True)
            c.dma("sp", self.o_d0_k[l, :, 128:128 + NS], KF[0:64, SEG:SEG + NS], reads=[KF], is_output=True)
        self.proj_pair(l, 14, tiles, lambda c0, n, res: self.rotary(c0, n, res, KC1c, KC1c[:, c0:c0 + n], KF, KF[:, c0:c0 + n]))
        if last:
            c.dma("sp", self.o_d1_k[l, :, 0:SEG + NS], KF[0:64, 0:SEG + NS], reads=[KF], is_output=True)

        def k2(c0, n, res):
            g0 = seg * SEG + c0 if c0 < SEG else self.nseg * SEG + (c0 - SEG)
            self.rotary(c0, n, res, KC2, KC2[:, g0:g0 + n], KF, KF[:, c0:c0 + n])
        self.proj_pair(l, 15, tiles, k2)
        c.dma("sp", self.o_d2_k[l, :, seg * SEG:(seg + 1) * SEG], KF[0:64, 0:SEG], reads=[KF], is_output=True)
        if last:
            c.dma("sp", self.o_d2_k[l, :, self.nseg * SEG:self.nseg * SEG + NS], KF[0:64, SEG:SEG + NS], reads=[KF],
                  is_output=True)
        if _stopat == "k":
            return
        W2 = self.next_wa()
        W2v = W2[:, :, :, :].rearrange("p a k c -> p (a k c)").rearrange("p (k c) -> p k c", k=KT)
        c.dma("pool", W2v, self.wt[l, 2], writes=[W2])
        for blk in range(4):
            ps = self.PS[blk % 2]
            for k in range(KT):
                c.op("pe", lambda e, k=k, ps=ps, blk=blk: e.matmul(ps[:, 0:192], xn[:, k, blk * 128:(blk + 1) * 128], W2v[:, k, 0:192],
                                                                  start=(k == 0), stop=(k == KT - 1)),
                     reads=[W2, xnb], writes=[ps], inc=(k == KT - 1))
            c.op("act", lambda e, ps=ps, blk=blk: e.activation(out=VBc[:, blk, :], in_=ps[:, 0:128], func=AF.Copy), reads=[ps], writes=[VBc])
            c.op("act", lambda e, ps=ps, blk=blk: e.activation(out=VC0c[:, blk, :], in_=ps[:, 128:192], func=AF.Copy), reads=[ps], writes=[VC0c])
            if last and blk == 3 and not _os.environ.get("NOVOUT"):
                c.op("act", lambda e, ps=ps: e.activation(out=VF[:, 0:192], in_=ps[:, 0:192], func=AF.Copy), reads=[ps], writes=[VF])
                c.dma("sp", self.o_swa_v[l], VF[:, 0:128], reads=[VF], is_output=True)
                c.dma("sp", self.o_d0_v[l], VF[:, 128:192], reads=[VF], is_output=True)
        if _stopat == "v1":
            return
        for r in range(4):
            ps = self.PS[r % 2]
            for k in range(KT):
                c.op("pe", lambda e, k=k, ps=ps, r=r: e.matmul(ps[:, 0:64], xn[:, k, r:SEG:4], W2v[:, k, 192:256],
                                                              start=(k == 0), stop=(k == KT - 1)),
                     reads=[W2, xnb], writes=[ps], inc=(k == KT - 1))
            c.op("dve", lambda e, ps=ps, r=r: e.tensor_copy(out=VC1c[:, r, :], in_=ps[:, 0:64]), reads=[ps], writes=[VC1c])
            if last:
                c.op("act", lambda e, ps=ps: e.activation(out=VF[:, 0:64], in_=ps[:, 0:64], func=AF.Copy), reads=[ps], writes=[VF])
                c.dma("sp", self.o_d1_v[l, r], VF[:, 0:64], reads=[VF], is_output=True)
        if last:
            ps8 = self.PS[6]
            for k in range(KT):
                c.op("pe", lambda e, k=k: e.matmul(ps8[0:NS, 0:256], xn[:, k, SEG:SEG + NS], W2v[:, k, 0:256],
                                                   start=(k == 0), stop=(k == KT - 1)),
                     reads=[W2, xnb], writes=[ps8], inc=(k == KT - 1))
        if _stopat == "v2":
            return
        W3 = self.next_wa()
        W3v = W3[:, :, :, :].rearrange("p a k c -> p (a k c)").rearrange("p (k c) -> p k c", k=KT)
        c.dma("pool", W3v, self.wt[l, 3], writes=[W3])
        for r0 in range(4):
            ps = self.PS[2 + r0 % 2]
            for rr in range(4):
                for k in range(KT):
                    c.op("pe", lambda e, k=k, ps=ps, rr=rr, r0=r0: e.matmul(
                        ps[32 * rr:32 * rr + 32, 0:64], xn[:, k, 4 * r0 + rr:SEG:16], W3v[:, k, 0:64],
                        start=(k == 0), stop=(k == KT - 1), tile_position=(0, 32 * rr)),
                        reads=[W3, xnb], writes=[ps], inc=(k == KT - 1))
            c.op("dve", lambda e, ps=ps, r0=r0: e.tensor_copy(out=VC2[:, r0, seg, :], in_=ps[:, 0:64]), reads=[ps], writes=[VC2])
            c.op("act", lambda e, ps=ps: e.activation(out=VF[:, 64:128], in_=ps[:, 0:64], func=AF.Copy), reads=[ps], writes=[VF])
            c.dma("sp", self.o_d2_v[l, seg, r0], VF[:, 64:128], reads=[VF], is_output=True)
        if last:
            for k in range(KT):
                c.op("pe", lambda e, k=k: e.matmul(ps8[0:NS, 256:320], xn[:, k, SEG:SEG + NS], W3v[:, k, 0:64],
                                                   start=(k == 0), stop=(k == KT - 1)),
                     reads=[W3, xnb], writes=[ps8], inc=(k == KT - 1))
            c.op("act", lambda e: e.activation(out=V8[:, :], in_=ps8[0:NS, 0:320], func=AF.Copy), reads=[ps8], writes=[V8])
            c.dma("sp", self.o_sv[l], V8[:, :], reads=[V8], is_output=True)
        if "stop_proj" in self.parts:
            return
        c.barrier()
        self.ar_off[0] = mark1
        OBT = self.ta([128, 4, NT], BF16, "OBT")
        first_seq = (seg == 0)
        units = []
        for h in range(8):
            hf, j = h // 4, h % 4
            rows = slice(64 * hf, 64 * hf + 64)
            for qb in range(4):
                u = dict(nq=128, half=hf, q=(QBT, QBT[rows, j, qb * 128:(qb + 1) * 128]))
                vcol = slice(64 * hf, 64 * hf + 64)
                if qb > 0:
                    u["kblocks"] = [(KBc, KBc[rows, (qb - 1) * 128:(qb + 1) * 128], 256, 0)]
                    u["vblocks"] = [dict(kind="T", buf=VBc, ap=VBc[:, qb - 1, vcol], nk=128, k0=0),
                                    dict(kind="T", buf=VBc, ap=VBc[:, qb, vcol], nk=128, k0=128)]
                    u["mask"] = (self.CB, self.CB[:, 0:256])
                    u["nkeys"], u["ncols"] = 256, 257
                elif not first_seq:
                    u["kblocks"] = [(KBp, KBp[rows, SEG - 128:SEG], 128, 0), (KBc, KBc[rows, 0:128], 128, 0)]
                    u["vblocks"] = [dict(kind="T", buf=VBp, ap=VBp[:, 3, vcol], nk=128, k0=0),
                                    dict(kind="T", buf=VBc, ap=VBc[:, 0, vcol], nk=128, k0=128)]
                    u["mask"] = (self.CB, self.CB[:, 0:256])
                    u["nkeys"], u["ncols"] = 256, 257
                else:
                    u["kblocks"] = [(KBc, KBc[rows, 0:128], 128, 0)]
                    u["vblocks"] = [dict(kind="T", buf=VBc, ap=VBc[:, 0, vcol], nk=128, k0=0)]
                    u["mask"] = (self.CB, self.CB[:, 128:256])
                    u["nkeys"], u["ncols"] = 128, 129
                u["out"] = (OBT, OBT[rows, j, qb * 128:(qb + 1) * 128])
                u["sink"] = h
                units.append(u)
        self.attn_units(units)
        if last:
            self.sample_attn(l, QBT, QCT, KBc, (KC0c, KC1c, KC2), V8, OBT, None, None, which="B")
        self.dump(f"obt{seg}", OBT, OBT[:, 1, 0:512], [128, 512])
        if last:
            self.dump("sob", OBT, OBT[:, 1, 512:520], [128, 8])
        self.mix_norm(seg, OBT, lambda k, c0, n: OBT[:, k, c0:c0 + n], 4, 512, l, 4, 4)
        if "stop_B" in self.parts:
            return
        for chn in (4, 5):
            tq = self.tmp[chn % 2]
            c.op("dve", lambda e, chn=chn, tq=tq: e.tensor_copy(
                out=tq[:, 0:SEG].rearrange("p (a r j) -> p a r j", a=4, r=4),
                in_=QCT[:, chn, 0:SEG].rearrange("p (j a r) -> p a r j", a=4, r=4)), reads=[QCT], writes=[tq])
            c.op("pool", lambda e, chn=chn, tq=tq: e.tensor_copy(out=QCT[:, chn, 0:SEG], in_=tq[:, 0:SEG]), reads=[tq], writes=[QCT])
        c.barrier()
        self.ar_off[0] = mark0
        OCT = self.ta([128, 6, NT], F32, "OCT", pool=1)
        LST = self.ta([128, 6, NT], F32, "LST", pool=0)
        c.op("pool", lambda e: e.memset(OCT[:, :, :], 0.0), writes=[OCT])
        c.op("pool", lambda e: e.memset(LST[:, :, :], 0.0), writes=[LST])
        units = []
        band = (self.CB, self.CB[:, 0:256])
        bandf = (self.CB, self.CB[:, 128:256])
        for i in range(3):
            hf = 1 if i == 1 else 0
            rows = slice(64 * hf, 64 * hf + 64)
            ch = 0 + (1 if i == 2 else 0)
            for qb in range(4):
                u = dict(nq=128, half=hf, q=(QCT, QCT[rows, ch, qb * 128:(qb + 1) * 128]))
                if qb > 0:
                    u["kblocks"] = [(KC0c, KC0c[rows, (qb - 1) * 128:(qb + 1) * 128], 256, 0)]
                    u["vblocks"] = [dict(kind="T", buf=VC0c, ap=VC0c[:, qb - 1, :], nk=128, k0=0),
                                    dict(kind="T", buf=VC0c, ap=VC0c[:, qb, :], nk=128, k0=128)]
                    u["mask"], u["nkeys"], u["ncols"] = band, 256, 256
                elif not first_seq:
                    u["kblocks"] = [(KC0p, KC0p[rows, SEG - 128:SEG], 128, 0), (KC0c, KC0c[rows, 0:128], 128, 0)]
                    u["vblocks"] = [dict(kind="T", buf=VC0p, ap=VC0p[:, 3, :], nk=128, k0=0),
                                    dict(kind="T", buf=VC0c, ap=VC0c[:, 0, :], nk=128, k0=128)]
                    u["mask"], u["nkeys"], u["ncols"] = band, 256, 256
                else:
                    u["kblocks"] = [(KC0c, KC0c[rows, 0:128], 128, 0)]
                    u["vblocks"] = [dict(kind="T", buf=VC0c, ap=VC0c[:, 0, :], nk=128, k0=0)]
                    u["mask"], u["nkeys"], u["ncols"] = bandf, 128, 128
                u["out"] = (OCT, OCT[rows, ch, qb * 128:(qb + 1) * 128])
                u["lse"] = (LST, LST[rows, ch, qb * 128:(qb + 1) * 128])
                units.append(u)
            ch = 2 + (1 if i == 2 else 0)
            for r in range(4):
                u = dict(nq=128, half=hf, q=(QCT, QCT[rows, ch, r:SEG:4]))
                if not first_seq:
                    u["kblocks"] = [(KC1p, KC1p[rows, r:SEG:4], 128, 0), (KC1c, KC1c[rows, r:SEG:4], 128, 0)]
                    u["vblocks"] = [dict(kind="T", buf=VC1p, ap=VC1p[:, r, :], nk=128, k0=0),
                                    dict(kind="T", buf=VC1c, ap=VC1c[:, r, :], nk=128, k0=128)]
                    u["mask"], u["nkeys"], u["ncols"] = band, 256, 256
                else:
                    u["kblocks"] = [(KC1c, KC1c[rows, r:SEG:4], 128, 0)]
                    u["vblocks"] = [dict(kind="T", buf=VC1c, ap=VC1c[:, r, :], nk=128, k0=0)]
                    u["mask"], u["nkeys"], u["ncols"] = bandf, 128, 128
                u["out"] = (OCT, OCT[rows, ch, r:SEG:4])
                u["lse"] = (LST, LST[rows, ch, r:SEG:4])
                units.append(u)
            ch = 4 + (1 if i == 2 else 0)
            nkt = 128 * (seg + 1)
            for r0 in range(4):
                u = dict(nq=128, half=hf, q=(QCT, QCT[rows, ch, 128 * r0:128 * r0 + 128]))
                kap = KC2[rows, 0:SEG * (seg + 1)].rearrange("p (s j r) -> p s r j", s=seg + 1, r=16)[:, :, 4 * r0:4 * r0 + 4, :]
                u["kblocks"] = [(KC2, kap, nkt, (seg + 1, 4))]
                u["vblocks"] = [dict(kind="T", buf=VC2, ap=VC2[:, r0, sg, :], nk=128, k0=128 * sg) for sg in range(seg + 1)]
                u["mask"] = (self.CB, self.CB[:, self.CB_M2 + 512 - nkt:self.CB_M2 + 512])
                u["nkeys"], u["ncols"] = nkt, nkt
                u["out"] = (OCT, OCT[rows, ch, 0:SEG].rearrange("p (j r) -> p r j", r=16)[:, 4 * r0:4 * r0 + 4, :])
                u["lse"] = (LST, LST[rows, ch, 0:SEG].rearrange("p (j r) -> p r j", r=16)[:, 4 * r0:4 * r0 + 4, :])
                u["orr"] = 4
                units.append(u)
        self.attn_units(units)
        if last:
            self.sample_attn(l, QBT, QCT, KBc, (KC0c, KC1c, KC2), V8, None, OCT, LST, which="C")
        self.dump(f"oct_raw{seg}", OCT, OCT[:, 2, 0:512], [128, 512])
        self.dump(f"lst{seg}", LST, LST[:, 2, 0:512], [128, 512])
        ta_, tb_ = self.tmp[0], self.tmp[1]
        E = [self.ta([128, 256], F32, f"E{g}", pool=1) for g in range(3)]
        ctiles = [(0, 256), (256, 256)] + ([(SEG, NS)] if last else [])
        for ls in range(2):
            chs = [2 * g + ls for g in range(3)]
            for (c0, n) in ctiles:
                cs = slice(c0, c0 + n)
                c.op("dve", lambda e, cs=cs, n=n: e.tensor_tensor(out=ta_[:, 0:n], in0=LST[:, chs[0], cs], in1=LST[:, chs[1], cs], op=ALU.max),
                     reads=[LST], writes=[ta_])
                c.op("dve", lambda e, cs=cs, n=n: e.tensor_tensor(out=ta_[:, 0:n], in0=ta_[:, 0:n], in1=LST[:, chs[2], cs], op=ALU.max),
                     reads=[LST, ta_], writes=[ta_])
                for g in range(3):
                    c.op("dve", lambda e, g=g, cs=cs, n=n: e.tensor_tensor(out=E[g][:, 0:n], in0=LST[:, chs[g], cs], in1=ta_[:, 0:n], op=ALU.subtract),
                         reads=[LST, ta_], writes=[E[g]])
                    c.op("act", lambda e, g=g, n=n: e.activation(out=E[g][:, 0:n], in_=E[g][:, 0:n], func=AF.Exp), reads=[E[g]], writes=[E[g]])
                c.op("pool", lambda e, n=n: e.tensor_tensor(out=tb_[:, 0:n], in0=E[0][:, 0:n], in1=E[1][:, 0:n], op=ALU.add),
                     reads=[E[0], E[1]], writes=[tb_])
                c.op("pool", lambda e, n=n: e.tensor_tensor(out=tb_[:, 0:n], in0=tb_[:, 0:n], in1=E[2][:, 0:n], op=ALU.add),
                     reads=[E[2], tb_], writes=[tb_])
                c.op("dve", lambda e, n=n: e.reciprocal(out=tb_[:, 0:n], in_=tb_[:, 0:n]), reads=[tb_], writes=[tb_])
                for g in range(3):
                    c.op("pool", lambda e, g=g, n=n: e.tensor_tensor(out=E[g][:, 0:n], in0=E[g][:, 0:n], in1=tb_[:, 0:n], op=ALU.mult),
                         reads=[E[g], tb_], writes=[E[g]])
                    c.op("dve", lambda e, g=g, cs=cs, n=n: e.tensor_tensor(out=OCT[:, chs[g], cs], in0=OCT[:, chs[g], cs], in1=E[g][:, 0:n], op=ALU.mult),
                         reads=[OCT, E[g]], writes=[OCT])
        self.dump(f"oct{seg}", OCT, OCT[:, 2, 0:512], [128, 512])
        if last:
            self.dump("soc", OCT, OCT[:, 2, 512:520], [128, 8])
        self.mix_norm(seg, OCT, lambda k, c0, n: OCT[:, k, c0:c0 + n], 6, 576, l, 8, 8)

    def sample_attn(self, l, QBT, QCT, KBc, KCs, V8, OBT, OCT, LST, which):
        c = self.c
        for b in range(NS):
            col = SEG + b
            KCTb = self.ta_ring("KCTb", [128, 4, 128], BF16)
            VCTb = self.ta_ring("VCTb", [128, 320], BF16)
            VSb = self.ta_ring("VSb", [1, 320], F32)
            c.dma("pool", KCTb[:, :, :], self.kct[:, l, b], writes=[KCTb])
            c.dma("pool", VCTb[:, :], self.vct[:, l, b], writes=[VCTb])
            psr = self.PS[6]
            c.op("pe", lambda e, b=b: e.matmul(psr[0:1, 0:320], self.idf[0:NS, b:b + 1], V8[0:NS, 0:320], start=True, stop=True),
                 reads=[self.CF, V8], writes=[psr])
            c.op("act", lambda e: e.activation(out=VSb[0:1, :], in_=psr[0:1, 0:320], func=AF.Copy), reads=[psr], writes=[VSb])
            units = []
            if which == "B":
                for h in range(8):
                    hf, j = h // 4, h % 4
                    rows = slice(64 * hf, 64 * hf + 64)
                    u = dict(nq=1, half=hf, q=(QBT, QBT[rows, j, col:col + 1]))
                    u["kblocks"] = [(KCTb, KCTb[rows, 0, :], 128, 0), (KBc, KBc[rows, col:col + 1], 1, 0)]
                    u["vblocks"] = [dict(kind="T", buf=VCTb, ap=VCTb[:, 64 * hf:64 * hf + 64], nk=128, k0=0),
                                    dict(kind="D", buf=VSb, ap=VSb[0:1, 64 * hf:64 * hf + 64], nk=1, k0=128)]
                    u["mask"] = None
                    u["sink"] = h
                    u["nkeys"], u["ncols"] = 129, 130
                    u["out"] = (OBT, OBT[rows, j, col:col + 1])
                    units.append(u)
            else:
                for g in range(3):
                    Kg = KCs[g]
                    kcol = col if g < 2 else self.nseg * SEG + b
                    for i in range(3):
                        hf = 1 if i == 1 else 0
                        rows = slice(64 * hf, 64 * hf + 64)
                        ch = 2 * g + (1 if i == 2 else 0)
                        u = dict(nq=1, half=hf, q=(QCT, QCT[rows, ch, col:col + 1]))
                        u["kblocks"] = [(KCTb, KCTb[rows, 1 + g, :], 128, 0), (Kg, Kg[rows, kcol:kcol + 1], 1, 0)]
                        u["vblocks"] = [dict(kind="T", buf=VCTb, ap=VCTb[:, 128 + 64 * g:192 + 64 * g], nk=128, k0=0),
                                        dict(kind="D", buf=VSb, ap=VSb[0:1, 128 + 64 * g:192 + 64 * g], nk=1, k0=128)]
                        u["mask"] = None
                        u["nkeys"], u["ncols"] = 129, 129
                        u["out"] = (OCT, OCT[rows, ch, col:col + 1])
                        u["lse"] = (LST, LST[rows, ch, col:col + 1])
                        units.append(u)
            self.attn_units(units)

    def ta_ring(self, name, shape, dt, nbuf=2):
        key = ("ring", name)
        if key not in self.rings:
            self.rings[key] = [[self.ta(shape, dt, f"{name}{i}") for i in range(nbuf)], 0]
        r = self.rings[key]
        b = r[0][r[1]]
        r[1] = (r[1] + 1) % nbuf
        return b
    def mix_D(self, seg, l, NC):
        c = self.c
        last = (seg == self.nseg - 1)
        NT = self.NT
        tiles = self.tiles(seg)
        xnb, xn = self.XN
        m_start = self.ar_off[0]
        VD = self.ta([128, 4, 512], BF16, "VD")
        V8d = self.ta([NS, 512], F32, "V8d") if last else None
        for grp in range(2):
            W = self.next_wa()
            Wv = W[:, :, :, :].rearrange("p a k c -> p (a k c)").rearrange("p (k c) -> p k c", k=KT)
            c.dma("pool", Wv, self.wt[l, grp], writes=[W])
            for blk in range(4):
                ps = self.PS[blk % 2]
                for k in range(KT):
                    c.op("pe", lambda e, k=k, ps=ps, blk=blk: e.matmul(ps[:, 0:256], xn[:, k, blk * 128:(blk + 1) * 128], Wv[:, k, :],
                                                                      start=(k == 0), stop=(k == KT - 1)),
                         reads=[W, xnb], writes=[ps], inc=(k == KT - 1))
                if blk % 2 == 0:
                    c.op("act", lambda e, ps=ps, blk=blk, grp=grp: e.activation(out=VD[:, blk, 256 * grp:256 * grp + 256], in_=ps[:, 0:256],
                                                                             func=AF.Copy), reads=[ps], writes=[VD])
                else:
                    c.op("dve", lambda e, ps=ps, blk=blk, grp=grp: e.tensor_copy(out=VD[:, blk, 256 * grp:256 * grp + 256], in_=ps[:, 0:256]),
                         reads=[ps], writes=[VD])
            if last:
                ps8 = self.PS[6]
                for k in range(KT):
                    c.op("pe", lambda e, k=k: e.matmul(ps8[0:NS, 0:256], xn[:, k, SEG:SEG + NS], Wv[:, k, :],
                                                       start=(k == 0), stop=(k == KT - 1)),
                         reads=[W, xnb], writes=[ps8], inc=(k == KT - 1))
                c.op("act", lambda e, grp=grp: e.activation(out=V8d[:, 256 * grp:256 * grp + 256], in_=ps8[0:NS, 0:256], func=AF.Copy),
                     reads=[ps8], writes=[V8d])
        A = [self.ta([128, NT], F32, f"A{i}") for i in range(6)]
        QTb = self.ta([128, SEG], BF16, "QTb")
        KTb = self.ta([128, SEG], BF16, "KTb")
        KHT = self.ta([128, 4, 128], BF16, "KHT")
        OD = self.ta([128, NT], F32, "OD")
        ATm = [self.ta([128, 128], BF16, f"ATm{i}") for i in range(2)]
        FS = self.ta([128, NS], F32, "FS")
        SGp = self.ta([128, 2, NT], F32, "SGp")
        S0 = self.ta([128, 4, 128], F32, "S0") if last else None
        SN = self.ta([128, 128], F32, "SN") if last else None
        T1 = self.ta([128, 128], F32, "T1") if last else None
        BD = self.CB[:, self.CB_BD:self.CB_BD + 128]
        RST = self.CF[:, self.CF_RST:self.CF_RST + SEG]
        cur = self.sh_par[l]
        for h in range(4):
            if h % 2 == 0:
                def gcons(c0, n, res):
                    for s_, (pb, pap) in enumerate(res):
                        c.op("act", lambda e, s_=s_, pap=pap: e.activation(out=SGp[:, s_, c0:c0 + n], in_=pap, func=AF.Silu),
                             reads=[pb], writes=[SGp])
                self.proj_pair(l, 20 + h // 2, tiles, gcons)
            lb1 = self.LBT[:, l, h, 0:1]
            lbp = self.LBT[:, l, h, 1:2]
            lbn = self.LBT[:, l, h, 2:3]

            def qf(c0, n, res):
                (bq, qp), (bf_, fp) = res
                cs = slice(c0, c0 + n)
                c.op("act", lambda e: e.activation(out=A[0][:, cs], in_=qp, func=AF.Silu), reads=[bq], writes=[A[0]])
                c.op("act", lambda e: e.activation(out=A[1][:, cs], in_=fp, func=AF.Sigmoid), reads=[bf_], writes=[A[1]])
                c.op("dve", lambda e: e.tensor_scalar(out=A[2][:, cs], in0=A[1][:, cs], scalar1=lb1, scalar2=lbp, op0=ALU.mult, op1=ALU.add),
                     reads=[A[1], self.LBT], writes=[A[2]])
                c.op("dve", lambda e: e.tensor_scalar(out=A[3][:, cs], in0=A[1][:, cs], scalar1=lbn, scalar2=lb1, op0=ALU.mult, op1=ALU.add),
                     reads=[A[1], self.LBT], writes=[A[3]])
                if c0 >= SEG:
                    c.op("dve", lambda e: e.tensor_copy(out=FS[:, 0:n], in_=A[2][:, cs]), reads=[A[2]], writes=[FS])
                else:
                    c.op("act", lambda e: e.activation(out=A[2][:, cs], in_=A[2][:, cs], func=AF.Ln), reads=[A[2]], writes=[A[2]])
            self.proj_pair(l, 16 + h, tiles, qf)
            P_ = slice(0, SEG)
            c.op("dve", lambda e: e.tensor_tensor_scan(out=A[1][:, P_], data0=RST, data1=A[2][:, P_], initial=0.0, op0=ALU.mult, op1=ALU.add),
                 reads=[A[2], self.CF], writes=[A[1]])
            c.op("dve", lambda e: e.tensor_scalar(out=A[1][:, P_], in0=A[1][:, P_], scalar1=-80.0, scalar2=None, op0=ALU.max),
                 reads=[A[1]], writes=[A[1]])
            c.op("act", lambda e: e.activation(out=A[2][:, P_], in_=A[1][:, P_], func=AF.Exp), reads=[A[1]], writes=[A[2]])
            c.op("act", lambda e: e.activation(out=A[4][:, P_], in_=A[1][:, P_], func=AF.Exp, scale=-1.0), reads=[A[1]], writes=[A[4]])
            c.op("pool", lambda e: e.tensor_tensor(out=A[5][:, P_], in0=A[0][:, P_], in1=A[2][:, P_], op=ALU.mult),
                 reads=[A[0], A[2]], writes=[A[5]])
            c.op("act", lambda e: e.activation(out=QTb[:, :], in_=A[5][:, P_], func=AF.Copy), reads=[A[5]], writes=[QTb])
            c.op("pool", lambda e: e.tensor_tensor(out=KTb[:, :], in0=A[3][:, P_], in1=A[4][:, P_], op=ALU.mult),
                 reads=[A[3], A[4]], writes=[KTb])
            b3 = A[1][:, P_].rearrange("p (c t) -> p c t", t=32)
            c.op("pool", lambda e: e.tensor_tensor(out=A[4][:, P_].rearrange("p (c t) -> p c t", t=32),
                                                   in0=b3[:, :, 31:32].to_broadcast([128, 16, 32]), in1=b3, op=ALU.subtract),
                 reads=[A[1]], writes=[A[4]])
            c.op("act", lambda e: e.activation(out=A[4][:, P_], in_=A[4][:, P_], func=AF.Exp), reads=[A[4]], writes=[A[4]])
            c.op("pool", lambda e: e.tensor_tensor(out=A[4][:, P_], in0=A[3][:, P_], in1=A[4][:, P_], op=ALU.mult),
                 reads=[A[3], A[4]], writes=[A[4]])
            for blk in range(4):
                pst = self.PS[2 + blk % 2]
                c.op("pe", lambda e, blk=blk, pst=pst: e.transpose(pst[:, 0:128], A[4][:, blk * 128:(blk + 1) * 128], self.idf),
                     reads=[A[4], self.CF], writes=[pst])
                if blk % 2 == 0:
                    c.op("act", lambda e, blk=blk, pst=pst: e.activation(out=KHT[:, blk, :], in_=pst[:, 0:128], func=AF.Copy),
                         reads=[pst], writes=[KHT])
                else:
                    c.op("dve", lambda e, blk=blk, pst=pst: e.tensor_copy(out=KHT[:, blk, :], in_=pst[:, 0:128]), reads=[pst], writes=[KHT])
            hc = slice(128 * h, 128 * h + 128)
            for blk in range(4):
                bs = slice(blk * 128, (blk + 1) * 128)
                pa = self.PS[4]
                po = self.PS[5]
                at = ATm[blk % 2]
                c.op("pe", lambda e, bs=bs: e.matmul(pa[:, 0:128], KTb[:, bs], QTb[:, bs], start=True, stop=True),
                     reads=[KTb, QTb], writes=[pa])
                c.op("dve", lambda e, at=at: e.tensor_tensor(out=at[:, :], in0=pa[:, 0:128], in1=BD, op=ALU.mult),
                     reads=[pa, self.CB], writes=[at])
                c.op("pe", lambda e, at=at, blk=blk: e.matmul(po[:, 0:128], VD[:, blk, hc], at[:, :], start=True, stop=False),
                     reads=[VD, at], writes=[po], inc=True)
                for cc in range(4):
                    ch = 4 * blk + cc
                    Sc = self.SH[l][cur]
                    Sn_ = self.SH[l][1 - cur]
                    c.op("pe", lambda e, cc=cc, ch=ch, Sc=Sc: e.matmul(po[:, 32 * cc:32 * cc + 32], Sc[:, h, :], A[5][:, 32 * ch:32 * ch + 32],
                                                                     start=False, stop=(cc == 3)),
                         reads=[Sc, A[5]], writes=[po], inc=True)
                    pd = self.PS[cc % 2]
                    tp = dict(tile_position=(96, 0)) if cc == 3 else {}
                    c.op("pe", lambda e, cc=cc, blk=blk, pd=pd, tp=tp: e.matmul(
                        pd[:, 0:128], KHT[32 * cc:32 * cc + 32, blk, :], VD[32 * cc:32 * cc + 32, blk, hc], start=True, stop=True, **tp),
                        reads=[KHT, VD], writes=[pd])
                    c.op("dve", lambda e, ch=ch, pd=pd, Sc=Sc, Sn_=Sn_: e.scalar_tensor_tensor(
                        out=Sn_[:, h, :], in0=Sc[:, h, :], scalar=A[2][:, 32 * ch + 31:32 * ch + 32], in1=pd[:, 0:128],
                        op0=ALU.mult, op1=ALU.add), reads=[Sc, A[2], pd], writes=[Sn_])
                    cur = 1 - cur
                c.op("act", lambda e, bs=bs: e.activation(out=OD[:, bs], in_=po[:, 0:128], func=AF.Copy), reads=[po], writes=[OD])
            c.op("pool", lambda e, cur=cur: e.tensor_copy(out=self.SH[l][1 - cur][:, h, :], in_=self.SH[l][cur][:, h, :]),
                 reads=[self.SH[l][cur]], writes=[self.SH[l][1 - cur]])
            if last:
                c.dma("sp", self.o_hg_p[l, h], self.SH[l][cur][:, h, :], reads=[self.SH[l][cur]], is_output=True)
                for b in range(NS):
                    col = SEG + b
                    if h == 0 or True:
                        c.dma("sp", S0[:, h, :], self.hs0[:, l, b, h, :], writes=[S0])
                    pv = self.PS[6]
                    c.op("pe", lambda e, b=b: e.matmul(pv[:, 256:384], self.idf[0:NS, b:b + 1].to_broadcast([NS, 128]), V8d[0:NS, hc],
                                                       start=True, stop=True), reads=[self.CF, V8d], writes=[pv])
                    c.op("dve", lambda e, b=b: e.tensor_scalar(out=T1[:, :], in0=S0[:, h, :], scalar1=FS[:, b:b + 1], scalar2=None, op0=ALU.mult),
                         reads=[S0, FS], writes=[T1])
                    c.op("dve", lambda e, col=col: e.scalar_tensor_tensor(out=SN[:, :], in0=pv[:, 256:384], scalar=A[3][:, col:col + 1],
                                                                          in1=T1[:, :], op0=ALU.mult, op1=ALU.add),
                         reads=[pv, A[3], T1], writes=[SN])
                    c.dma("sp", self.o_hg_s[l, b, h], SN[:, :], reads=[SN], is_output=True)
                    c.op("pe", lambda e, col=col: e.matmul(pv[:, 384:385], SN[:, :], A[0][:, col:col + 1], start=True, stop=True),
                         reads=[SN, A[0]], writes=[pv])
                    c.op("act", lambda e, col=col: e.activation(out=OD[:, col:col + 1], in_=pv[:, 384:385], func=AF.Copy),
                         reads=[pv], writes=[OD])
            self.dump(f"od{seg}_{h}", OD, OD[:, 0:512], [128, 512])
            if last:
                self.dump(f"sod{h}", OD, OD[:, 512:520], [128, 8])
            for (c0, n) in tiles:
                cs = slice(c0, c0 + n)
                self.sumsq_rstd(OD, lambda k: OD[:, cs], 1, 128, c0, n, self.PS[7])
                tt = self.tmp[self.tmp_i]
                self.tmp_i ^= 1
                c.op("dve", lambda e, tt=tt, cs=cs, n=n: e.scalar_tensor_tensor(out=tt[:, 0:n], in0=OD[:, cs], scalar=self.PMX[:, l, 14 + h:15 + h],
                                                                        in1=self.rstd[:, cs], op0=ALU.mult, op1=ALU.mult),
                     reads=[OD, self.PMX, self.rstd], writes=[tt])
                c.op("pool", lambda e, tt=tt, cs=cs, n=n: e.tensor_tensor(out=self.MIX[:, 14 + h, cs], in0=tt[:, 0:n], in1=SGp[:, h % 2, cs], op=ALU.mult),
                     reads=[tt, SGp], writes=[self.MIX])
        self.sh_par[l] = cur
        c.barrier()
        self.ar_off[0] = m_start
    MAGIC = 12582912.0
    TWO_PI = 6.283185307179586

    def range_reduce(self, eng, out, src, kt, n=None, reads=(), outb=None, srcb=None, ktb=None):
        c = self.c
        c.op(eng, lambda e: e.tensor_scalar(out=kt, in0=src, scalar1=1.0 / self.TWO_PI, scalar2=self.MAGIC, op0=ALU.mult, op1=ALU.add),
             reads=[srcb], writes=[ktb])
        c.op(eng, lambda e: e.tensor_scalar(out=kt, in0=kt, scalar1=-self.MAGIC, scalar2=None, op0=ALU.add), reads=[ktb], writes=[ktb])
        c.op(eng, lambda e: e.scalar_tensor_tensor(out=out, in0=kt, scalar=-self.TWO_PI, in1=src, op0=ALU.mult, op1=ALU.add),
             reads=[ktb, srcb], writes=[outb])
        c.op(eng, lambda e: e.tensor_scalar(out=out, in0=out, scalar1=-3.14159, scalar2=3.14159, op0=ALU.max, op1=ALU.min),
             reads=[outb], writes=[outb])

    def lam_calc(self, are, aim, ldt, srcb, W, T, want_z):
        c = self.c
        dt, r, th, k, sn, cs, lre, lim = T[:8]
        w = slice(0, W)
        c.op("act", lambda e: e.activation(out=dt[:, w], in_=ldt, func=AF.Exp), reads=[srcb], writes=[dt])
        c.op("pool", lambda e: e.tensor_tensor(out=r[:, w], in0=are, in1=dt[:, w], op=ALU.mult), reads=[srcb, dt], writes=[r])
        c.op("pool", lambda e: e.tensor_tensor(out=th[:, w], in0=aim, in1=dt[:, w], op=ALU.mult), reads=[srcb, dt], writes=[th])
        c.op("act", lambda e: e.activation(out=r[:, w], in_=r[:, w], func=AF.Exp), reads=[r], writes=[r])
        self.range_reduce("dve", th[:, w], th[:, w], k[:, w], outb=th, srcb=th, ktb=k)
        c.op("dve", lambda e: e.tensor_scalar(out=dt[:, w], in0=th[:, w], scalar1=3.141592653589793 / 2, scalar2=None, op0=ALU.add),
             reads=[th], writes=[dt])
        self.range_reduce("dve", dt[:, w], dt[:, w], k[:, w], outb=dt, srcb=dt, ktb=k)
        c.op("act", lambda e: e.activation(out=sn[:, w], in_=th[:, w], func=AF.Sin), reads=[th], writes=[sn])
        c.op("act", lambda e: e.activation(out=cs[:, w], in_=dt[:, w], func=AF.Sin), reads=[dt], writes=[cs])
        c.op("pool", lambda e: e.tensor_tensor(out=lre[:, w], in0=r[:, w], in1=cs[:, w], op=ALU.mult), reads=[r, cs], writes=[lre])
        c.op("pool", lambda e: e.tensor_tensor(out=lim[:, w], in0=r[:, w], in1=sn[:, w], op=ALU.mult), reads=[r, sn], writes=[lim])
        res = dict(r=r, th=th, lre=lre, lim=lim)
        if want_z:
            den, l1, zr, zi = dt, k, sn, cs
            c.op("pool", lambda e: e.tensor_tensor(out=den[:, w], in0=are, in1=are, op=ALU.mult), reads=[srcb], writes=[den])
            c.op("pool", lambda e: e.tensor_tensor(out=l1[:, w], in0=aim, in1=aim, op=ALU.mult), reads=[srcb], writes=[l1])
            c.op("pool", lambda e: e.tensor_tensor(out=den[:, w], in0=den[:, w], in1=l1[:, w], op=ALU.add), reads=[den, l1], writes=[den])
            c.op("dve", lambda e: e.reciprocal(out=den[:, w], in_=den[:, w]), reads=[den], writes=[den])
            c.op("dve", lambda e: e.tensor_scalar(out=l1[:, w], in0=lre[:, w], scalar1=-1.0, scalar2=None, op0=ALU.add), reads=[lre], writes=[l1])
            t1, t2 = T[8], T[9]
            c.op("pool", lambda e: e.tensor_tensor(out=t1[:, w], in0=l1[:, w], in1=are, op=ALU.mult), reads=[l1, srcb], writes=[t1])
            c.op("pool", lambda e: e.tensor_tensor(out=t2[:, w], in0=lim[:, w], in1=aim, op=ALU.mult), reads=[lim, srcb], writes=[t2])
            c.op("pool", lambda e: e.tensor_tensor(out=t1[:, w], in0=t1[:, w], in1=t2[:, w], op=ALU.add), reads=[t1, t2], writes=[t1])
            c.op("pool", lambda e: e.tensor_tensor(out=zr[:, w], in0=t1[:, w], in1=den[:, w], op=ALU.mult), reads=[t1, den], writes=[zr])
            c.op("pool", lambda e: e.tensor_tensor(out=t1[:, w], in0=lim[:, w], in1=are, op=ALU.mult), reads=[lim, srcb], writes=[t1])
            c.op("pool", lambda e: e.tensor_tensor(out=t2[:, w], in0=l1[:, w], in1=aim, op=ALU.mult), reads=[l1, srcb], writes=[t2])
            c.op("pool", lambda e: e.tensor_tensor(out=t1[:, w], in0=t1[:, w], in1=t2[:, w], op=ALU.subtract), reads=[t1, t2], writes=[t1])
            c.op("pool", lambda e: e.tensor_tensor(out=zi[:, w], in0=t1[:, w], in1=den[:, w], op=ALU.mult), reads=[t1, den], writes=[zi])
            res.update(zr=zr, zi=zi)
        return res

    def mix_A(self, seg, l, NC):
        c = self.c
        last = (seg == self.nseg - 1)
        NT = self.NT
        tiles = self.tiles(seg)
        m_start = self.ar_off[0]
        UT = self.ta([128, 4, NT], BF16, "UT")
        ZB = self.ta([128, 4, NT], BF16, "ZB")
        BR = [self.ta([128, 4, 64], F32, f"BR{i}") for i in range(2)]
        CRW = self.ta([128, 448], F32, "CRW")
        LP = [self.ta([128, 14], F32, f"LP{i}") for i in range(10)]
        NLI = self.ta([128, 14], F32, "NLI")
        X0S = self.ta([128, NS, 14, 2], F32, "X0S") if last else None
        XSN = self.ta([128, NS, 14, 2], F32, "XSN") if last else None
        for j in range(2):
            def ucons(c0, n, res, j=j):
                for s_, (pb, pap) in enumerate(res):
                    c.op("act", lambda e, s_=s_, pap=pap: e.activation(out=UT[:, 2 * j + s_, c0:c0 + n], in_=pap, func=AF.Copy),
                         reads=[pb], writes=[UT])
            self.proj_pair(l, j, tiles, ucons)
        m_setup = self.ar_off[0]
        R = self.ta([128, 1728], F32, "R")
        c.dma("sp", R[:, :], self.s5r[:, l, :], writes=[R])
        if last:
            c.dma("sp", X0S[:, :, :, :], self.x0s[:, l], writes=[X0S])
        c.op("dve", lambda e: e.tensor_copy(out=CRW[:, :], in_=R[:, 1280:1728]), reads=[R], writes=[CRW])
        TR = [self.ta([128, 256], F32, f"TR{i}") for i in range(10)]
        zz = self.lam_calc(R[:, 0:256], R[:, 256:512], R[:, 512:768], R, 256, TR, True)
        zr, zi = zz["zr"], zz["zi"]
        t1, t2 = TR[8], TR[9]
        bre, bim = R[:, 768:1024], R[:, 1024:1280]
        br0 = BR[0][:, :, :].rearrange("p a b -> p (a b)")
        br1 = BR[1][:, :, :].rearrange("p a b -> p (a b)")
        c.op("pool", lambda e: e.tensor_tensor(out=t1[:, :], in0=zr[:, :], in1=bre, op=ALU.mult), reads=[zr, R], writes=[t1])
        c.op("pool", lambda e: e.tensor_tensor(out=t2[:, :], in0=zi[:, :], in1=bim, op=ALU.mult), reads=[zi, R], writes=[t2])
        c.op("pool", lambda e: e.tensor_tensor(out=br0, in0=t1[:, :], in1=t2[:, :], op=ALU.subtract), reads=[t1, t2], writes=[BR[0]])
        c.op("pool", lambda e: e.tensor_tensor(out=t1[:, :], in0=zr[:, :], in1=bim, op=ALU.mult), reads=[zr, R], writes=[t1])
        c.op("pool", lambda e: e.tensor_tensor(out=t2[:, :], in0=zi[:, :], in1=bre, op=ALU.mult), reads=[zi, R], writes=[t2])
        c.op("pool", lambda e: e.tensor_tensor(out=br1, in0=t1[:, :], in1=t2[:, :], op=ALU.add), reads=[t1, t2], writes=[BR[1]])
        lp = self.lam_calc(self.PLA[:, l, 0:14], self.PLA[:, l, 14:28], self.PLA[:, l, 28:42], self.PLA, 14, LP, False)
        RR, TH, LRE, LIM = lp["r"], lp["th"], lp["lre"], lp["lim"]
        c.op("dve", lambda e: e.tensor_scalar(out=NLI[:, :], in0=LIM[:, 0:14], scalar1=-1.0, scalar2=None, op0=ALU.mult), reads=[LIM], writes=[NLI])
        c.barrier()
        self.ar_off[0] = m_setup
        CS = self.ta([128, SEG], F32, "CS")
        SNT = self.ta([128, SEG], F32, "SNT")
        G = [self.ta([128, SEG], F32, f"G{i}") for i in range(4)]
        XR = self.ta([128, NT], BF16, "XR")
        XI = self.ta([128, NT], BF16, "XI")
        BBD = [[self.ta([128, 128], BF16, f"BBD{q}{i}") for i in range(2)] for q in range(4)]
        CBD = [[self.ta([128, 128], BF16, f"CBD{q}{i}") for i in range(2)] for q in range(4)]
        DD = self.ta([128, 128], BF16, "DD")
        SM = self.ta([128, 16], F32, "SM")
        IOTA = self.CF[:, self.CF_IOTA:self.CF_IOTA + SEG]
        MB_ = self.CB[:, self.CB_MB:self.CB_MB + 512].rearrange("p (q g s) -> p q g s", q=4, g=2)
        MC_ = self.CB[:, self.CB_MC:self.CB_MC + 512].rearrange("p (q g j) -> p q g j", q=4, g=8)
        X0 = self.X0[l]
        P_ = slice(0, SEG)
        for ci in range(4):
            npair = 4 if ci < 3 else 2
            for q in range(npair):
                p = 4 * ci + q
                for i in range(2):
                    c.op("pool", lambda e, q=q, i=i: e.tensor_tensor(
                        out=BBD[q][i][:, :].rearrange("p (g s) -> p g s", g=2), in0=MB_[:, q, :, :],
                        in1=BR[i][:, ci, :].unsqueeze(1).to_broadcast([128, 2, 64]), op=ALU.mult),
                        reads=[self.CB, BR[i]], writes=[BBD[q][i]])
                crp = CRW[:, 16 * p:16 * p + 16].unsqueeze(1).to_broadcast([128, 8, 16])
                cip = CRW[:, 224 + 16 * p:224 + 16 * p + 16].unsqueeze(1).to_broadcast([128, 8, 16])
                c.op("pool", lambda e, q=q, crp=crp: e.tensor_tensor(out=CBD[q][0][:, :].rearrange("p (g j) -> p g j", g=8),
                                                                    in0=MC_[:, q, :, :], in1=crp, op=ALU.mult),
                     reads=[self.CB, CRW], writes=[CBD[q][0]])
                c.op("dve", lambda e, q=q, cip=cip: e.scalar_tensor_tensor(out=CBD[q][1][:, :].rearrange("p (g j) -> p g j", g=8),
                                                                         in0=cip, scalar=-1.0, in1=MC_[:, q, :, :], op0=ALU.mult, op1=ALU.mult),
                     reads=[self.CB, CRW], writes=[CBD[q][1]])
            c.op("dve", lambda e: e.tensor_scalar(out=DD[:, :], in0=self.idb, scalar1=self.PLA[:, l, 42 + ci:43 + ci], scalar2=None, op0=ALU.mult),
                 reads=[self.CB, self.PLA], writes=[DD])
            yps = [(self.PS[4], self.PS[4][:, 0:SEG])] + ([(self.PS[7], self.PS[7][:, 0:NS])] if last else [])
            for ti, (c0, n) in enumerate(tiles):
                yb_, yap = yps[ti]
                c.op("pe", lambda e, yap=yap, c0=c0, n=n: e.matmul(yap, DD[:, :], UT[:, ci, c0:c0 + n], start=True, stop=False),
                     reads=[DD, UT], writes=[yb_])
            for q in range(npair):
                p = 4 * ci + q
                bps = []
                for ti, (c0, n) in enumerate(tiles):
                    for i in range(2):
                        pb = self.PS[i] if ti == 0 else self.PS[6]
                        pap = pb[:, 0:n] if ti == 0 else pb[:, 16 * i:16 * i + n]
                        c.op("pe", lambda e, pap=pap, i=i, q=q, c0=c0, n=n: e.matmul(pap, BBD[q][i][:, :], UT[:, ci, c0:c0 + n], start=True, stop=True),
                             reads=[BBD[q][i], UT], writes=[pb])
                        bps.append((pb, pap))
                (bre_b, bre_p), (bim_b, bim_p) = bps[0], bps[1]
                thp = TH[:, p:p + 1]
                c.op("dve", lambda e: e.tensor_scalar(out=G[0][:, :], in0=IOTA, scalar1=thp, scalar2=None, op0=ALU.mult), reads=[self.CF, TH], writes=[G[0]])
                self.range_reduce("dve", G[2][:, :], G[0][:, :], G[1][:, :], outb=G[2], srcb=G[0], ktb=G[1])
                c.op("act", lambda e: e.activation(out=SNT[:, :], in_=G[2][:, :], func=AF.Sin), reads=[G[2]], writes=[SNT])
                c.op("dve", lambda e: e.tensor_scalar(out=G[2][:, :], in0=G[2][:, :], scalar1=3.141592653589793 / 2, scalar2=None, op0=ALU.add),
                     reads=[G[2]], writes=[G[2]])
                self.range_reduce("dve", G[3][:, :], G[2][:, :], G[1][:, :], outb=G[3], srcb=G[2], ktb=G[1])
                c.op("act", lambda e: e.activation(out=CS[:, :], in_=G[3][:, :], func=AF.Sin), reads=[G[3]], writes=[CS])
                c.op("dve", lambda e: e.tensor_tensor(out=G[0][:, :], in0=bre_p, in1=CS[:, :], op=ALU.mult), reads=[bre_b, CS], writes=[G[0]])
                c.op("dve", lambda e: e.tensor_tensor(out=G[1][:, :], in0=bim_p, in1=SNT[:, :], op=ALU.mult), reads=[bim_b, SNT], writes=[G[1]])
                c.op("pool", lambda e: e.tensor_tensor(out=G[2][:, :], in0=G[0][:, :], in1=G[1][:, :], op=ALU.add), reads=[G[0], G[1]], writes=[G[2]])
                c.op("dve", lambda e: e.tensor_tensor(out=G[0][:, :], in0=bim_p, in1=CS[:, :], op=ALU.mult), reads=[bim_b, CS], writes=[G[0]])
                c.op("dve", lambda e: e.tensor_tensor(out=G[1][:, :], in0=bre_p, in1=SNT[:, :], op=ALU.mult), reads=[bre_b, SNT], writes=[G[1]])
                c.op("pool", lambda e: e.tensor_tensor(out=G[3][:, :], in0=G[0][:, :], in1=G[1][:, :], op=ALU.subtract), reads=[G[0], G[1]], writes=[G[3]])
                rb = RR[:, p:p + 1].to_broadcast([128, SEG])
                c.op("dve", lambda e: e.tensor_tensor_scan(out=G[0][:, :], data0=rb, data1=G[2][:, :], initial=X0[:, p, 0:1], op0=ALU.mult, op1=ALU.add),
                     reads=[RR, G[2], X0], writes=[G[0]])
                c.op("dve", lambda e: e.tensor_tensor_scan(out=G[1][:, :], data0=rb, data1=G[3][:, :], initial=X0[:, p, 1:2], op0=ALU.mult, op1=ALU.add),
                     reads=[RR, G[3], X0], writes=[G[1]])
                c.op("dve", lambda e: e.tensor_tensor(out=G[2][:, :], in0=G[0][:, :], in1=CS[:, :], op=ALU.mult), reads=[G[0], CS], writes=[G[2]])
                c.op("pool", lambda e: e.tensor_tensor(out=G[3][:, :], in0=G[1][:, :], in1=SNT[:, :], op=ALU.mult), reads=[G[1], SNT], writes=[G[3]])
                c.op("pool", lambda e: e.tensor_tensor(out=XR[:, P_], in0=G[2][:, :], in1=G[3][:, :], op=ALU.subtract), reads=[G[2], G[3]], writes=[XR])
                c.op("dve", lambda e: e.tensor_tensor(out=X0[:, p, 0:1], in0=G[2][:, SEG - 1:SEG], in1=G[3][:, SEG - 1:SEG], op=ALU.subtract),
                     reads=[G[2], G[3]], writes=[X0])
                c.op("dve", lambda e: e.tensor_tensor(out=G[2][:, :], in0=G[0][:, :], in1=SNT[:, :], op=ALU.mult), reads=[G[0], SNT], writes=[G[2]])
                c.op("pool", lambda e: e.tensor_tensor(out=G[3][:, :], in0=G[1][:, :], in1=CS[:, :], op=ALU.mult), reads=[G[1], CS], writes=[G[3]])
                c.op("pool", lambda e: e.tensor_tensor(out=XI[:, P_], in0=G[2][:, :], in1=G[3][:, :], op=ALU.add), reads=[G[2], G[3]], writes=[XI])
                c.op("dve", lambda e: e.tensor_tensor(out=X0[:, p, 1:2], in0=G[2][:, SEG - 1:SEG], in1=G[3][:, SEG - 1:SEG], op=ALU.add),
                     reads=[G[2], G[3]], writes=[X0])
                if last:
                    (sre_b, sre_p), (sim_b, sim_p) = bps[2], bps[3]
                    a_ = SM[:, 0:NS]
                    c.op("dve", lambda e: e.tensor_scalar(out=a_, in0=X0S[:, :, p, 0], scalar1=LRE[:, p:p + 1], scalar2=None, op0=ALU.mult),
                         reads=[X0S, LRE], writes=[SM])
                    c.op("dve", lambda e: e.scalar_tensor_tensor(out=a_, in0=X0S[:, :, p, 1], scalar=NLI[:, p:p + 1], in1=a_, op0=ALU.mult, op1=ALU.add),
                         reads=[X0S, NLI, SM], writes=[SM])
                    c.op("dve", lambda e: e.tensor_tensor(out=XSN[:, :, p, 0], in0=a_, in1=sre_p, op=ALU.add), reads=[SM, sre_b], writes=[XSN])
                    b_ = SM[:, 8:8 + NS]
                    c.op("dve", lambda e: e.tensor_scalar(out=b_, in0=X0S[:, :, p, 1], scalar1=LRE[:, p:p + 1], scalar2=None, op0=ALU.mult),
                         reads=[X0S, LRE], writes=[SM])
                    c.op("dve", lambda e: e.scalar_tensor_tensor(out=b_, in0=X0S[:, :, p, 0], scalar=LIM[:, p:p + 1], in1=b_, op0=ALU.mult, op1=ALU.add),
                         reads=[X0S, LIM, SM], writes=[SM])
                    c.op("dve", lambda e: e.tensor_tensor(out=XSN[:, :, p, 1], in0=b_, in1=sim_p, op=ALU.add), reads=[SM, sim_b], writes=[XSN])
                    c.op("act", lambda e: e.activation(out=XR[:, SEG:SEG + NS], in_=XSN[:, :, p, 0], func=AF.Copy), reads=[XSN], writes=[XR])
                    c.op("act", lambda e: e.activation(out=XI[:, SEG:SEG + NS], in_=XSN[:, :, p, 1], func=AF.Copy), reads=[XSN], writes=[XI])
                for ti, (c0, n) in enumerate(tiles):
                    yb_, yap = yps[ti]
                    lastmm = (q == npair - 1)
                    c.op("pe", lambda e, yap=yap, q=q, c0=c0, n=n: e.matmul(yap, CBD[q][0][:, :], XR[:, c0:c0 + n], start=False, stop=False),
                         reads=[CBD[q][0], XR], writes=[yb_])
                    c.op("pe", lambda e, yap=yap, q=q, c0=c0, n=n, lastmm=lastmm: e.matmul(yap, CBD[q][1][:, :], XI[:, c0:c0 + n], start=False, stop=lastmm),
                         reads=[CBD[q][1], XI], writes=[yb_])
            for ti, (c0, n) in enumerate(tiles):
                yb_, yap = yps[ti]
                if "ya" in self.dbg and ci == 1 and ti == 0:
                    self.dump("ya", yb_, yap, [128, 512])
                c.op("act", lambda e, yap=yap, c0=c0, n=n: e.activation(out=ZB[:, ci, c0:c0 + n], in_=yap, func=AF.Gelu_apprx_tanh),
                     reads=[yb_], writes=[ZB])
        if last:
            c.dma("sp", self.o_ssm_p[:, l], X0[:, :, :], reads=[X0], is_output=True)
            c.dma("sp", self.o_ssm_s[:, l], XSN[:, :, :, :], reads=[XSN], is_output=True)
        W = self.next_wa()
        Wg = W[:, :, :, :].rearrange("p a k c -> p (a k c)")[:, 0:2048].rearrange("p (m k c) -> p m k c", m=4, k=4)
        c.dma("pool", Wg, self.wglu[l].rearrange("m p k c -> p m k c"), writes=[W])
        ZS = self.ta([128, 4, NS], F32, "ZS") if last else None
        for mo in range(4):
            for ti, (c0, n) in enumerate(tiles):
                pb = self.PS[mo % 2] if ti == 0 else self.PS[6]
                pap = pb[:, 0:n]
                for k in range(4):
                    c.op("pe", lambda e, k=k, pap=pap, c0=c0, n=n: e.matmul(pap, Wg[:, mo, k, :], ZB[:, k, c0:c0 + n], start=(k == 0), stop=(k == 3)),
                         reads=[W, ZB], writes=[pb], inc=(k == 3))
                gt = self.tmp[self.tmp_i]
                self.tmp_i ^= 1
                c.op("act", lambda e, gt=gt, pap=pap, n=n: e.activation(out=gt[:, 0:n], in_=pap, func=AF.Sigmoid, bias=self.PMX[:, l, 18 + mo:19 + mo]),
                     reads=[pb, self.PMX], writes=[gt])
                dst_b, dst = (G[mo], G[mo][:, 0:n]) if ti == 0 else (ZS, ZS[:, mo, 0:n])
                c.op("dve", lambda e, gt=gt, dst=dst, c0=c0, n=n: e.tensor_tensor(out=dst, in0=ZB[:, mo, c0:c0 + n], in1=gt[:, 0:n], op=ALU.mult),
                     reads=[ZB, gt], writes=[dst_b])

        class _Multi:
            pass
        srcs = list(G) + ([ZS] if last else [])
        self.mix_norm_multi(seg, srcs, lambda k, c0, n: (G[k][:, c0:c0 + n] if c0 < SEG else ZS[:, k, 0:n]), 4, 448, l, 0, 0)
        c.barrier()
        self.ar_off[0] = m_start

    def mix_norm_multi(self, seg, src_bufs, src_fn, nchunk, n_feat, l, gofs, mbase):
        c = self.c
        for (c0, n) in self.tiles(seg):
            ps = self.PS[7]
            for k in range(nchunk):
                sq = self.sq[self.sq_i]
                self.sq_i ^= 1
                c.op("act", lambda e, k=k, sq=sq: e.activation(out=sq[:, 0:n], in_=src_fn(k, c0, n), func=AF.Square),
                     reads=src_bufs, writes=[sq])
                c.op("pe", lambda e, k=k, sq=sq: e.matmul(ps[:, 0:n], self.ones_bf[:, :], sq[:, 0:n], start=(k == 0), stop=(k == nchunk - 1)),
                     reads=[sq, self.ones_bf], writes=[ps])
            c.op("act", lambda e: e.activation(out=self.rstd[:, c0:c0 + n], in_=ps[:, 0:n], func=AF.Sqrt, bias=self.eps_ap(), scale=1.0 / n_feat),
                 reads=[ps, self.epsb], writes=[self.rstd])
            c.op("dve", lambda e: e.reciprocal(out=self.rstd[:, c0:c0 + n], in_=self.rstd[:, c0:c0 + n]), reads=[self.rstd], writes=[self.rstd])
            for k in range(nchunk):
                c.op("dve", lambda e, k=k: e.scalar_tensor_tensor(
                    out=self.MIX[:, mbase + k, c0:c0 + n], in0=src_fn(k, c0, n), scalar=self.PMX[:, l, gofs + k:gofs + k + 1],
                    in1=self.rstd[:, c0:c0 + n], op0=ALU.mult, op1=ALU.mult), reads=src_bufs + [self.PMX, self.rstd], writes=[self.MIX])
    def mix_out(self, seg, l, NC):
        c = self.c
        tiles = self.tiles(seg)
        yb, y = self.Y
        c.barrier()
        for k in range(MIXK):
            self.dump(f"mix{k}_{seg}", self.MIX, self.MIX[:, k, 0:512], [128, 512])
        for mo in range(KT):
            W = self.next_wa()
            Wv = W[:, :, :, :].rearrange("p a k c -> p (a k c)")[:, 0:MIXK * 128].rearrange("p (k c) -> p k c", k=MIXK)
            c.dma("pool", Wv, self.wout[l, mo], writes=[W])
            for ti, (c0, n) in enumerate(tiles):
                pd = self.PS[4 + (mo % 2)] if ti == 0 else self.PS[6]
                od = 0 if ti == 0 else 32
                for k in range(MIXK):
                    c.op("pe", lambda e, k=k, pd=pd, od=od: e.matmul(pd[:, od:od + n], Wv[:, k, :], self.MIX[:, k, c0:c0 + n],
                                                                  start=(k == 0), stop=(k == MIXK - 1)),
                         reads=[W, self.MIX], writes=[pd], inc=(k == MIXK - 1))
                c.op("act", lambda e, pd=pd, od=od, mo=mo: e.activation(out=y[:, mo, c0:c0 + n], in_=pd[:, od:od + n], func=AF.Copy),
                     reads=[pd], writes=[yb])
        self.dump("ymix", yb, y[:, 3, 0:512], [128, 512])
        self.post_norm_add(seg, (l * 6 + 3) * KT)


U_OFF, QB_OFF, KB_OFF, VB_OFF, QC_OFF, KC_OFF, VC_OFF, QD_OFF, FD_OFF, ID_OFF, GD_OFF = (
    0, 448, 960, 1088, 1216, 1792, 1984, 2176, 2688, 3200, 3712)
NPAIR = 22
MIXK = 18


def _prep_ffn_weights(wg, wu, wd):
    nl = wg.shape[0]
    g = wg.reshape(nl, KT, 128, FT, 128).transpose(0, 3, 2, 1, 4)
    u = wu.reshape(nl, KT, 128, FT, 128).transpose(0, 3, 2, 1, 4)
    gu = np.ascontiguousarray(np.stack([g, u], axis=3))
    d = np.ascontiguousarray(wd.reshape(nl, FT, 128, KT, 128).transpose(0, 3, 2, 1, 4))
    return gu, d


def _gain_layout(vs):
    nl = vs[0].shape[0]
    a = np.stack(vs, axis=1)
    a = a.reshape(nl, 6, KT, 128).transpose(3, 0, 1, 2).reshape(128, nl * 6 * KT)
    return np.ascontiguousarray(a)


def _head(off, h):
    return list(range(off + 64 * h, off + 64 * h + 64))


def _swap(cols):
    return cols[32:] + cols[:32]


def _win_chunks():
    Z = [-1] * 64
    ch = []
    for j in range(4):
        ch.append([cc if cc < 448 else -1 for cc in range(128 * j, 128 * j + 128)])
    for j in range(4):
        ch.append(_head(QB_OFF, j) + _head(QB_OFF, j + 4))
        ch.append(_swap(_head(QB_OFF, j)) + _swap(_head(QB_OFF, j + 4)))
    ch.append(_head(KB_OFF, 0) + _head(KB_OFF, 1))
    ch.append(_swap(_head(KB_OFF, 0)) + _swap(_head(KB_OFF, 1)))
    for g in range(3):
        a, b, cc = _head(QC_OFF, 3 * g), _head(QC_OFF, 3 * g + 1), _head(QC_OFF, 3 * g + 2)
        ch.append(a + b)
        ch.append(_swap(a) + _swap(b))
        ch.append(cc + Z)
        ch.append(_swap(cc) + Z)
    for g in range(3):
        k = _head(KC_OFF, g)
        ch.append(k + k)
        ch.append(_swap(k) + _swap(k))
    for h in range(4):
        ch.append(list(range(QD_OFF + 128 * h, QD_OFF + 128 * h + 128)))
        ch.append(list(range(FD_OFF + 128 * h, FD_OFF + 128 * h + 128)))
    for h in range(4):
        ch.append(list(range(GD_OFF + 128 * h, GD_OFF + 128 * h + 128)))
    assert len(ch) == 2 * NPAIR
    return ch


def _gather_cols(w, cols):
    cols = np.asarray(cols)
    out = w[:, :, np.maximum(cols, 0)]
    out[:, :, cols < 0] = 0.0
    return out


def _mix_rows():
    Z = [-1] * 64
    rows = []
    for j in range(4):
        rows += [r if r < 448 else -1 for r in range(128 * j, 128 * j + 128)]
    ob = 448
    for j in range(4):
        rows += _head(ob, j) + _head(ob, j + 4)
    oc = 448 + 512
    for g in range(3):
        rows += _head(oc, 3 * g) + _head(oc, 3 * g + 1)
        rows += _head(oc, 3 * g + 2) + Z
    od = 448 + 512 + 576
    rows += list(range(od, od + 512))
    assert len(rows) == MIXK * 128
    return np.asarray(rows)


PAST_LEN = 16384
ROPE_THETA = 10000.0


def _consts(nseg):
    ntok = nseg * SEG + NS
    cb = np.zeros((128, Prog.NCB), np.float32)
    qi = np.arange(128)[:, None]
    kj = np.arange(256)[None, :]
    dist = 128 + qi - kj
    cb[:, 0:256] = np.where((dist >= 0) & (dist <= 128), 0.0, NEG)
    cb[:, 256] = 0.0
    cb[:, 257] = NEG
    q = np.arange(128)[:, None]
    col = np.arange(512)[None, :]
    rrq, iq = q // 32, q % 32
    jt, rr = 32 * (col // 128) + (col % 32), (col % 128) // 32
    cb[:, Prog.CB_M2:Prog.CB_M2 + 512] = np.where((rr == rrq) & (jt <= 96 + iq), 0.0, NEG)
    s_ = np.arange(128)[:, None]
    t_ = np.arange(128)[None, :]
    cb[:, Prog.CB_BD:Prog.CB_BD + 128] = ((s_ // 32 == t_ // 32) & (s_ <= t_)).astype(np.float32)
    cb[:, Prog.CB_ID:Prog.CB_ID + 128] = np.eye(128, dtype=np.float32)
    row = np.arange(128)[:, None, None]
    qq = np.arange(4)[None, :, None]
    cc = np.arange(128)[None, None, :]
    cb[:, Prog.CB_MB:Prog.CB_MB + 512] = ((row // 16) == 2 * qq + (cc // 64)).astype(np.float32).reshape(128, 512)
    cb[:, Prog.CB_MC:Prog.CB_MC + 512] = ((cc // 16) == 2 * qq + (row // 64)).astype(np.float32).reshape(128, 512)
    cf = np.zeros((128, Prog.NCF), np.float32)
    cf[:, 0:128] = np.eye(128, dtype=np.float32)
    cf[:, Prog.CF_IOTA:Prog.CF_IOTA + 512] = np.arange(1, 513, dtype=np.float32)[None, :]
    cf[:, Prog.CF_RST:Prog.CF_RST + 512] = (np.arange(512) % 32 != 0).astype(np.float32)[None, :]
    pos = np.concatenate([np.arange(nseg * SEG), np.full(NS, PAST_LEN)]).astype(np.float32)
    inv_freq = (np.float32(ROPE_THETA) ** (-(np.arange(32, dtype=np.float32) / np.float32(32)))).astype(np.float32)
    ang = (pos[:, None] * inv_freq[None, :]).astype(np.float32)
    cos = np.cos(ang).astype(np.float32).T
    sin = np.sin(ang).astype(np.float32).T
    rot = np.zeros((128, 2, ntok), np.float32)
    for p in range(128):
        rot[p, 0] = cos[p % 32]
        rot[p, 1] = sin[p % 32] * (-1.0 if (p % 64) < 32 else 1.0)
    return cb, cf, rot


def _prep_shared(inp, nl):
    sh = {}
    for i, nm in ((1, "ffn1"), (2, "ffn2")):
        gu, d = _prep_ffn_weights(inp[nm + "_w_gate"][:nl], inp[nm + "_w_up"][:nl], inp[nm + "_w_down"][:nl])
        sh[f"wgu{i}"], sh[f"wdn{i}"] = gu, d
    sh["gains"] = _gain_layout([inp[k][:nl] for k in ("ffn1_norm_pre", "ffn1_norm_post", "mix_norm_pre", "mix_norm_post",
                                                     "ffn2_norm_pre", "ffn2_norm_post")])
    w_in = inp["w_in"][:nl]
    ch = _win_chunks()
    wf = np.stack([_gather_cols(w_in, cc) for cc in ch], axis=1)
    wf = wf.reshape(nl, NPAIR, 2, KT, 128, 128).transpose(0, 1, 4, 2, 3, 5)
    sh["win_f"] = np.ascontiguousarray(wf)
    Z = [-1] * 64
    tg = [list(range(ID_OFF, ID_OFF + 256)), list(range(ID_OFF + 256, ID_OFF + 512)),
          list(range(VB_OFF, VB_OFF + 128)) + list(range(VC_OFF, VC_OFF + 128)),
          list(range(VC_OFF + 128, VC_OFF + 192)) + Z * 3]
    wt = np.stack([_gather_cols(w_in, cc) for cc in tg], axis=1)
    sh["wt"] = np.ascontiguousarray(wt.reshape(nl, 4, KT, 128, 256).transpose(0, 1, 3, 2, 4))
    rows = _mix_rows()
    wo = inp["w_out"][:nl][:, np.maximum(rows, 0), :].copy()
    wo[:, rows < 0, :] = 0.0
    sh["wout"] = np.ascontiguousarray(wo.reshape(nl, MIXK, 128, KT, 128).transpose(0, 3, 2, 1, 4))
    wg = np.zeros((nl, 512, 512), np.float32)
    wg[:, :448, :448] = inp["ssm_w_glu"][:nl]
    sh["wglu"] = np.ascontiguousarray(wg.reshape(nl, 4, 128, 4, 128).transpose(0, 3, 2, 1, 4))
    a_re, a_im, ldt = inp["ssm_a_re"][:nl], inp["ssm_a_im"][:nl], inp["ssm_log_dt"][:nl]
    plA = np.zeros((128, nl, 46), np.float32)
    plA[:, :, 0:14] = a_re.reshape(nl, 14, 2, 64).transpose(2, 3, 0, 1).reshape(128, nl, 14)
    plA[:, :, 14:28] = a_im.reshape(nl, 14, 2, 64).transpose(2, 3, 0, 1).reshape(128, nl, 14)
    plA[:, :, 28:42] = np.broadcast_to(ldt.reshape(nl, 14, 2, 1), (nl, 14, 2, 64)).transpose(2, 3, 0, 1).reshape(128, nl, 14)
    dsk = np.zeros((nl, 32, 16), np.float32)
    dsk[:, :28] = inp["ssm_d"][:nl]
    plA[:, :, 42:46] = dsk.reshape(nl, 4, 8, 16).transpose(2, 3, 0, 1).reshape(128, nl, 4)
    sh["plA"] = plA
    pm = np.zeros((128, nl, 30), np.float32)
    gcat = np.concatenate([inp["out_norm_a"][:nl], inp["out_norm_b"][:nl], inp["out_norm_c"][:nl], inp["out_norm_d"][:nl]], axis=1)
    gm = gcat[:, np.maximum(rows, 0)].copy()
    gm[:, rows < 0] = 0.0
    pm[:, :, 0:18] = gm.reshape(nl, MIXK, 128).transpose(2, 0, 1)
    bg = np.zeros((nl, 512), np.float32)
    bg[:, :448] = inp["ssm_b_glu"][:nl]
    pm[:, :, 18:22] = bg.reshape(nl, 4, 128).transpose(2, 0, 1)
    pm[:, :, 22:30] = inp["swa_sinks"][:nl][None, :, :]
    sh["pmix"] = pm
    sh["hlb"] = np.ascontiguousarray(inp["hgrn_lower_bounds"][:nl].reshape(nl, 4, 128).transpose(2, 0, 1))
    s5 = np.zeros((128, nl, 1728), np.float32)

    def rowlay(a, fill=0.0):
        z = np.full((nl, 32, 64), fill, np.float32)
        z[:, :28] = a
        z = z.reshape(nl, 4, 8, 1, 64)
        z = np.broadcast_to(z, (nl, 4, 8, 16, 64))
        return z.transpose(2, 3, 0, 1, 4).reshape(128, nl, 256)
    s5[:, :, 0:256] = rowlay(a_re, -1.0)
    s5[:, :, 256:512] = rowlay(a_im, 1.0)
    s5[:, :, 512:768] = rowlay(np.broadcast_to(ldt[:, :, None], (nl, 28, 64)))

    def rowlay_b(b):
        z = np.zeros((nl, 32, 64, 16), np.float32)
        z[:, :28] = b
        return z.reshape(nl, 4, 8, 64, 16).transpose(2, 4, 0, 1, 3).reshape(128, nl, 256)
    s5[:, :, 768:1024] = rowlay_b(inp["ssm_b_re"][:nl])
    s5[:, :, 1024:1280] = rowlay_b(inp["ssm_b_im"][:nl])

    def crow(cm):
        return cm.reshape(nl, 14, 2, 16, 64).transpose(2, 4, 0, 1, 3).reshape(128, nl, 224)
    s5[:, :, 1280:1504] = crow(inp["ssm_c_re"][:nl])
    s5[:, :, 1504:1728] = crow(inp["ssm_c_im"][:nl])
    sh["s5r"] = s5
    return sh


def _prep_core(inp, nl, nseg, seq, sb):
    d = {}
    x = np.concatenate([inp["x_prompt"][seq, :nseg * SEG], inp["x_sample"][sb:sb + NS, 0]], axis=0)
    d["xT"] = np.ascontiguousarray(x.T)
    ss = inp["state_ssm"][:nl, sb:sb + NS]
    d["x0s"] = np.ascontiguousarray(ss.reshape(nl, NS, 14, 2, 64, 2).transpose(3, 4, 0, 1, 2, 5).reshape(128, nl, NS, 14, 2))
    d["hs0"] = np.ascontiguousarray(inp["state_hgrn"][:nl, sb:sb + NS].transpose(3, 0, 1, 2, 4))
    cs = inp["cache_swa_kv"][:nl, sb:sb + NS]
    kct = np.zeros((128, nl, NS, 4, 128), np.float32)
    vct = np.zeros((128, nl, NS, 320), np.float32)
    kct[:, :, :, 0, :] = cs[:, :, :, 0].reshape(nl, NS, 128, 128).transpose(3, 0, 1, 2)
    vct[:, :, :, 0:128] = cs[:, :, :, 1].reshape(nl, NS, 128, 128).transpose(2, 0, 1, 3)
    for g, (nm, st) in enumerate((("cache_dil0_kv", 1), ("cache_dil1_kv", 4), ("cache_dil2_kv", 16))):
        cg = inp[nm][:nl, sb:sb + NS, ::st]
        kk = cg[:, :, :, 0].transpose(3, 0, 1, 2)
        kct[0:64, :, :, 1 + g, :] = kk
        kct[64:128, :, :, 1 + g, :] = kk
        vct[:, :, :, 128 + 64 * g:192 + 64 * g] = cg[:, :, :, 1].transpose(2, 0, 1, 3)
    d["kct"], d["vct"] = kct, vct
    return d


_CACHE = {}


def _build(nseg, nl, **kw):
    key = (nseg, nl, tuple(sorted(kw.items())))
    if key not in _CACHE:
        p = Prog(nseg=nseg, nlayer=nl, **kw)
        p.build()
        _CACHE[key] = p
    return _CACHE[key]


def kernel(**inputs):
    inp = {k: np.asarray(v) for k, v in inputs.items()}
    nl, nseg = L, NSEG
    prog = _build(nseg, nl)
    cb, cf, rot = _consts(nseg)
    sh = _prep_shared(inp, nl)
    sh.update(cbf=cb, cf32=cf, rot=rot)
    in_maps = []
    for core in range(8):
        seq = core // 2
        d = dict(sh)
        d.update(_prep_core(inp, nl, nseg, seq, 8 * seq))
        in_maps.append(d)
    res = run_bass_kernel_spmd(prog.nc, in_maps, core_ids=list(range(8)))
    R = [res.results[2 * s] for s in range(4)]
    return _assemble(R, nl, nseg)


def _assemble(R, nl, nseg):
    nb = len(R)
    T = nseg * SEG
    f32 = np.float32
    y_p = np.stack([r["yT"][:, :T].T for r in R]).astype(f32)
    y_s = np.concatenate([r["yT"][:, T:T + NS].T for r in R])[:, None, :].astype(f32)
    ssm_p = np.stack([r["o_ssm_p"].reshape(2, 64, nl, 14, 2).transpose(2, 3, 0, 1, 4).reshape(nl, 28, 64, 2) for r in R], axis=1)
    ssm_s = np.concatenate([r["o_ssm_s"].reshape(2, 64, nl, NS, 14, 2).transpose(2, 3, 4, 0, 1, 5).reshape(nl, NS, 28, 64, 2)
                            for r in R], axis=1)
    swa_p = np.stack([np.stack([r["o_swa_k"][:, :, 0:128].transpose(0, 2, 1).reshape(nl, 128, 2, 64),
                                r["o_swa_v"].reshape(nl, 128, 2, 64)], axis=2) for r in R], axis=1)
    swa_s = np.concatenate([np.stack([r["o_swa_k"][:, :, 128:128 + NS].transpose(0, 2, 1).reshape(nl, NS, 2, 64),
                                      r["o_sv"][:, :, 0:128].reshape(nl, NS, 2, 64)], axis=2)[:, :, None] for r in R], axis=1)

    def dil(kname, vfun, npr, vcol):
        p = np.stack([np.stack([r[kname][:, :, 0:npr].transpose(0, 2, 1), vfun(r)], axis=2) for r in R], axis=1)
        s = np.concatenate([np.stack([r[kname][:, :, npr:npr + NS].transpose(0, 2, 1),
                                      r["o_sv"][:, :, vcol:vcol + 64]], axis=2)[:, :, None] for r in R], axis=1)
        return p.astype(f32), s.astype(f32)
    d0_p, d0_s = dil("o_d0_k", lambda r: r["o_d0_v"], 128, 128)
    d1_p, d1_s = dil("o_d1_k", lambda r: r["o_d1_v"].transpose(0, 2, 1, 3).reshape(nl, 512, 64), 512, 192)
    d2_p, d2_s = dil("o_d2_k", lambda r: r["o_d2_v"].reshape(nl, nseg, 4, 4, 32, 64).transpose(0, 1, 4, 2, 3, 5).reshape(nl, T, 64),
                     T, 256)
    hg_p = np.stack([r["o_hg_p"] for r in R], axis=1)
    hg_s = np.concatenate([r["o_hg_s"] for r in R], axis=1)
    outs = (y_p, y_s, ssm_p, ssm_s, swa_p, swa_s, d0_p, d0_s, d1_p, d1_s, d2_p, d2_s, hg_p, hg_s)
    return tuple(np.ascontiguousarray(o, dtype=f32) for o in outs)
```

```python
import numpy as np
from contextlib import ExitStack
import concourse.bass as bass
import concourse.mybir as mybir
from concourse.bass_utils import run_bass_kernel_spmd

F32 = mybir.dt.float32
BF16 = mybir.dt.bfloat16
ALU = mybir.AluOpType
AF = mybir.ActivationFunctionType
AX = mybir.AxisListType

D = 2048
KT = D // 128
DFF = 5504
FT = DFF // 128
FH = (22, 21)
L = 2
SEQ = 2048
SEG = 512
NSEG = SEQ // SEG
NS = 8
EPS = 1e-6
NEG = -1e30


class Buf:
    __slots__ = ("t", "name", "w", "r", "dsem", "dcnt", "psum")

    def __init__(self, t, name):
        self.t = t
        self.name = name
        self.psum = False
        self.w = []
        self.r = []
        self.dsem = None
        self.dcnt = 0

    def __getitem__(self, k):
        return self.t[k]


class EngS:
    def __init__(self, name, eng, sem):
        self.name = name
        self.eng = eng
        self.sem = sem
        self.cnt = 0
        self.waited = {}
        self.pend_r = []
        self.pend_w = []


class Ctx:
    def __init__(self, nc, stack):
        self.nc = nc
        self.stack = stack
        self.E = {}
        for name, eng in (("pe", nc.tensor), ("act", nc.scalar), ("dve", nc.vector),
                          ("pool", nc.gpsimd), ("sp", nc.sync)):
            sem = stack.enter_context(nc.semaphore("s_" + name))
            self.E[name] = EngS(name, eng, sem)
        self.nbuf = 0
        self.out_tokens = []
        self.dma_tokens = {}
        self.nins = 0

    def sbuf(self, shape, dt, name=None):
        self.nbuf += 1
        name = f"{name or 'b'}_{self.nbuf}"
        t = self.stack.enter_context(self.nc.sbuf_tensor(name, list(shape), dt))
        return Buf(t, name)

    def psum(self, shape, dt, name=None):
        self.nbuf += 1
        name = f"{name or 'p'}_{self.nbuf}"
        t = self.stack.enter_context(self.nc.psum_tensor(name, list(shape), dt))
        bb = Buf(t, name)
        bb.psum = True
        return bb

    def view(self, buf, name=None):
        self.nbuf += 1
        return Buf(buf.t, f"{name or 'v'}_{self.nbuf}")

    def _wait(self, es, tokens):
        best = {}
        for (sem, val) in tokens:
            k = id(sem)
            if k not in best or best[k][1] < val:
                best[k] = (sem, val)
        for k, (sem, val) in best.items():
            if es.name == "pe" and sem is es.sem:
                continue
            if es.waited.get(k, 0) >= val:
                continue
            es.eng.wait_ge(sem, val)
            es.waited[k] = val

    def _deps(self, reads, writes, en=None):
        toks = []
        for e in self.E.values():
            if e.name == en:
                continue
            for b in writes:
                if any(b is x for x in e.pend_r) or any(b is x for x in e.pend_w):
                    raise RuntimeError(f"hazard: {b.name} is written while engine {e.name} has un-signalled accesses to it")
            for b in reads:
                if any(b is x for x in e.pend_w):
                    raise RuntimeError(f"hazard: {b.name} is read while engine {e.name} has un-signalled writes to it")
        own = self.E[en].sem if en in self.E else None
        for b in reads:
            toks += b.w
            if b.psum:
                toks += [t for t in b.r if t[0] is not own]
        for b in writes:
            toks += b.w
            toks += b.r
        return toks

    def op(self, en, fn, reads=(), writes=(), inc=True):
        es = self.E[en]
        self._wait(es, self._deps(reads, writes, en))
        ins = fn(es.eng)
        self.nins += 1
        es.pend_r += list(reads)
        es.pend_w += list(writes)
        if inc:
            es.cnt += 1
            ins.then_inc(es.sem, 1)
            tok = (es.sem, es.cnt)
            for b in es.pend_r:
                b.r.append(tok)
                if len(b.r) > 12:
                    b.r = self._compact(b.r)
            for b in es.pend_w:
                b.w = [tok]
                b.r = []
            es.pend_r = []
            es.pend_w = []
        return ins

    @staticmethod
    def _compact(toks):
        best = {}
        for (sem, val) in toks:
            k = id(sem)
            if k not in best or best[k][1] < val:
                best[k] = (sem, val)
        return list(best.values())

    def dma(self, qn, out_ap, in_ap, reads=(), writes=(), is_output=False, owner=None):
        es = self.E[qn]
        self._wait(es, self._deps(reads, writes, qn))
        bufs = list(writes) + list(reads)
        owner = owner or bufs[0]
        if owner.dsem is None:
            owner.dsem = self.stack.enter_context(self.nc.semaphore("d_" + owner.name))
        owner.dcnt += 16
        tok = (owner.dsem, owner.dcnt)
        ins = es.eng.dma_start(out=out_ap, in_=in_ap)
        ins.then_inc(owner.dsem, 16)
        self.nins += 1
        for b in reads:
            b.r.append(tok)
        for b in writes:
            b.w = [tok]
            b.r = []
        self.dma_tokens[id(owner.dsem)] = tok
        if is_output:
            self.out_tokens.append(tok)
        return ins

    def barrier(self, engines=("pe", "act", "dve", "sp")):
        toks = [(e.sem, e.cnt) for e in self.E.values() if e.cnt > 0]
        toks += list(self.dma_tokens.values())
        for n in engines:
            self._wait(self.E[n], toks)

    def finish(self):
        es = self.E["sp"]
        toks = list(self.out_tokens) + list(self.dma_tokens.values())
        for n, e in self.E.items():
            if e.cnt > 0:
                toks.append((e.sem, e.cnt))
        self._wait(es, toks)


class Prog:
    def __init__(self, nseg=NSEG, nlayer=L, with_mix=True, dbg=(), parts=("mix",)):
        self.parts = set(parts)
        self.nseg = nseg
        self.nlayer = nlayer
        self.with_mix = with_mix
        self.dbg = set(dbg)
        self.ntok = nseg * SEG + NS
        self.nc = bass.Bass("TRN2", target_bir_lowering=False)
        self.ins = {}
        self.outs = {}

    def din(self, name, shape):
        t = self.nc.dram_tensor(name, list(shape), F32, kind="ExternalInput").ap()
        self.ins[name] = t
        return t

    def dout(self, name, shape):
        t = self.nc.dram_tensor(name, list(shape), F32, kind="ExternalOutput").ap()
        self.outs[name] = t
        return t

    def build(self):
        nc = self.nc
        nl = self.nlayer
        self.xT = self.din("xT", [D, self.ntok])
        self.yT = self.dout("yT", [D, self.ntok])
        self.wgu = [self.din(f"wgu{i}", [nl, FT, 128, 2, KT, 128]) for i in (1, 2)]
        self.wdn = [self.din(f"wdn{i}", [nl, KT, 128, FT, 128]) for i in (1, 2)]
        self.gains = self.din("gains", [128, nl * 6 * KT])
        if self.with_mix:
            self.declare_mix()
        with ExitStack() as st:
            self.c = c = Ctx(nc, st)
            self.st = st
            self.setup_consts()
            if self.with_mix:
                self.setup_mix_consts()
            for seg in range(self.nseg):
                self.run_segment(seg)
            c.finish()
        return nc

    def setup_consts(self):
        c = self.c
        nl = self.nlayer
        self.G = c.sbuf([128, nl * 6 * KT], F32, "gains")
        c.dma("sp", self.G[:], self.gains, writes=[self.G])
        for l in range(nl):
            for n in (1, 5):
                o = (l * 6 + n) * KT
                c.op("dve", lambda e, o=o: e.tensor_scalar(out=self.G[:, o:o + KT], in0=self.G[:, o:o + KT],
                                                           scalar1=0.5, scalar2=None, op0=ALU.mult),
                     reads=[self.G], writes=[self.G])
        self.ones_bf = c.sbuf([128, 128], BF16, "ones")
        c.op("dve", lambda e: e.memset(self.ones_bf[:], 1.0), writes=[self.ones_bf])
        self.NT = SEG + NS
        self.H = c.sbuf([128, KT, self.NT], F32, "H")
        xn = c.sbuf([128, KT, self.NT], BF16, "XN")
        self.XN = (xn, xn.t)
        self.YB = KT * self.NT * 4 + 1024
        self.AB = max(FH[0], 18) * self.NT * 2
        self.ARENA = c.sbuf([128, (self.YB + self.AB) // 4], F32, "ARENA")
        yb = Buf(self.ARENA.t, "Y")
        self.Y = (yb, self.ARENA.t[:, 0:KT * self.NT].rearrange("p (k t) -> p k t", k=KT))
        self.ACTB = Buf(self.ARENA.t[:, self.YB // 4:(self.YB + FH[0] * self.NT * 2) // 4].bitcast(BF16).rearrange(
            "p (k t) -> p k t", k=FH[0]), "ACT")
        self.WA = [c.sbuf([128, 2, KT, 128], BF16, f"WA{i}") for i in range(3)]
        self.WD = [c.sbuf([128, FH[0], 128], BF16, f"WD{i}") for i in range(2)]
        self.wa_i = 0
        self.wd_i = 0
        self.PS = [c.psum([128, 512], F32, f"PS{i}") for i in range(8)]
        self.tmp = [c.sbuf([128, 512], F32, f"tmp{i}") for i in range(2)]
        self.tmp_i = 0
        self.sq = [c.sbuf([128, 512], BF16, f"sq{i}") for i in range(2)]
        self.sq_i = 0
        self.rstd = c.sbuf([128, self.NT], F32, "rstd")

    def dump(self, name, buf, ap, shape, dt=F32):
        if name not in self.dbg:
            return
        c = self.c
        t = c.sbuf(shape, F32, "dbg_" + name)
        c.op("act", lambda e: e.activation(out=t[:], in_=ap, func=AF.Copy), reads=[buf], writes=[t])
        o = self.dout("dbg_" + name, shape)
        c.dma("sp", o, t[:], reads=[t], is_output=True)
        self.dbg.discard(name)

    def tiles(self, seg):
        t = [(0, SEG)]
        if seg == self.nseg - 1:
            t.append((SEG, NS))
        return t

    def run_segment(self, seg):
        c = self.c
        c.dma("sp", self.H[:, :, 0:SEG], self.xT[:, seg * SEG:(seg + 1) * SEG].rearrange("(k p) t -> p k t", p=128),
              writes=[self.H])
        if seg == self.nseg - 1:
            c.dma("sp", self.H[:, :, SEG:SEG + NS],
                  self.xT[:, self.nseg * SEG:self.nseg * SEG + NS].rearrange("(k p) t -> p k t", p=128),
                  writes=[self.H])
        for l in range(self.nlayer):
            self.ffn(seg, l, 0)
            if self.with_mix:
                self.mixer(seg, l)
            self.ffn(seg, l, 1)
        c.dma("sp", self.yT[:, seg * SEG:(seg + 1) * SEG].rearrange("(k p) t -> p k t", p=128), self.H[:, :, 0:SEG],
              reads=[self.H], is_output=True)
        if seg == self.nseg - 1:
            c.dma("sp", self.yT[:, self.nseg * SEG:self.nseg * SEG + NS].rearrange("(k p) t -> p k t", p=128),
                  self.H[:, :, SEG:SEG + NS], reads=[self.H], is_output=True)

    def sumsq_rstd(self, src_buf, src_ap_fn, nchunk, n_feat, c0, n, ps, eps=EPS):
        c = self.c
        for k in range(nchunk):
            sq = self.sq[self.sq_i]
            self.sq_i ^= 1
            c.op("act", lambda e, k=k, sq=sq: e.activation(out=sq[:, 0:n], in_=src_ap_fn(k), func=AF.Square),
                 reads=[src_buf], writes=[sq])
            c.op("pe", lambda e, k=k, sq=sq: e.matmul(ps[:, 0:n], self.ones_bf[:, :], sq[:, 0:n],
                                                      start=(k == 0), stop=(k == nchunk - 1)),
                 reads=[sq, self.ones_bf], writes=[ps], inc=True)
        c.op("act", lambda e: e.activation(out=self.rstd[:, c0:c0 + n], in_=ps[:, 0:n], func=AF.Sqrt,
                                           bias=self.eps_ap(), scale=1.0 / n_feat),
             reads=[ps, self.epsb], writes=[self.rstd])
        c.op("dve", lambda e: e.reciprocal(out=self.rstd[:, c0:c0 + n], in_=self.rstd[:, c0:c0 + n]),
             reads=[self.rstd], writes=[self.rstd])

    def eps_ap(self):
        if not hasattr(self, "epsb"):
            self.epsb = self.c.sbuf([128, 1], F32, "eps")
            self.c.op("dve", lambda e: e.memset(self.epsb[:], EPS), writes=[self.epsb])
        return self.epsb[:, 0:1]

    def pre_norm(self, seg, gofs):
        c = self.c
        self.eps_ap()
        xnb, xn = self.XN
        for (c0, n) in self.tiles(seg):
            ps = self.PS[7]
            self.sumsq_rstd(self.H, lambda k: self.H[:, k, c0:c0 + n], KT, D, c0, n, ps)
            for k in range(KT):
                c.op("dve", lambda e, k=k: e.scalar_tensor_tensor(
                    out=xn[:, k, c0:c0 + n], in0=self.H[:, k, c0:c0 + n], scalar=self.G[:, gofs + k:gofs + k + 1],
                    in1=self.rstd[:, c0:c0 + n], op0=ALU.mult, op1=ALU.mult),
                    reads=[self.H, self.G, self.rstd], writes=[xnb])

    def post_norm_add(self, seg, gofs):
        c = self.c
        yb, y = self.Y
        for (c0, n) in self.tiles(seg):
            ps = self.PS[7]
            self.sumsq_rstd(yb, lambda k: y[:, k, c0:c0 + n], KT, D, c0, n, ps)
            for k in range(KT):
                t = self.tmp[self.tmp_i]
                self.tmp_i ^= 1
                c.op("dve", lambda e, k=k, t=t: e.scalar_tensor_tensor(
                    out=t[:, 0:n], in0=y[:, k, c0:c0 + n], scalar=self.G[:, gofs + k:gofs + k + 1],
                    in1=self.rstd[:, c0:c0 + n], op0=ALU.mult, op1=ALU.mult),
                    reads=[yb, self.G, self.rstd], writes=[t])
                c.op("dve", lambda e, k=k, t=t: e.tensor_tensor(
                    out=self.H[:, k, c0:c0 + n], in0=self.H[:, k, c0:c0 + n], in1=t[:, 0:n], op=ALU.add),
                    reads=[t, self.H], writes=[self.H])

    def ffn(self, seg, l, which):
        c = self.c
        tiles = self.tiles(seg)
        gbase = (l * 6 + (0 if which == 0 else 4)) * KT
        self.pre_norm(seg, gbase)
        xnb, xn = self.XN
        yb, y = self.Y
        self.dump("rstd", self.rstd, self.rstd[:, 0:512], [128, 512])
        self.dump("xn", xnb, xn[:, 3, 0:512], [128, 512])
        wgu = self.wgu[which]
        wdn = self.wdn[which]
        m0 = 0
        for half in range(2):
            nm = FH[half]
            for mi in range(nm):
                m = m0 + mi
                W = self.WA[self.wa_i]
                self.wa_i = (self.wa_i + 1) % len(self.WA)
                c.dma("pool", W[:, :, :, :], wgu[l, m], writes=[W])
                for ti, (c0, n) in enumerate(tiles):
                    pg = self.PS[0 + (mi % 2)] if ti == 0 else self.PS[6]
                    pu = self.PS[2 + (mi % 2)] if ti == 0 else self.PS[6]
                    og = 0 if ti == 0 else 0
                    ou = 0 if ti == 0 else 16
                    for k in range(KT):
                        c.op("pe", lambda e, k=k, pg=pg, og=og: e.matmul(pg[:, og:og + n], W[:, 0, k, :], xn[:, k, c0:c0 + n],
                                                                      start=(k == 0), stop=(k == KT - 1)),
                             reads=[W, xnb], writes=[pg], inc=(k == KT - 1))
                    for k in range(KT):
                        c.op("pe", lambda e, k=k, pu=pu, ou=ou: e.matmul(pu[:, ou:ou + n], W[:, 1, k, :], xn[:, k, c0:c0 + n],
                                                                      start=(k == 0), stop=(k == KT - 1)),
                             reads=[W, xnb], writes=[pu], inc=(k == KT - 1))
                    t = self.tmp[self.tmp_i]
                    self.tmp_i ^= 1
                    c.op("act", lambda e, t=t, pg=pg, og=og: e.activation(out=t[:, 0:n], in_=pg[:, og:og + n], func=AF.Silu),
                         reads=[pg], writes=[t])
                    c.op("dve", lambda e, t=t, pu=pu, ou=ou, mi=mi: e.tensor_tensor(
                        out=self.ACTB[:, mi, c0:c0 + n], in0=t[:, 0:n], in1=pu[:, ou:ou + n], op=ALU.mult),
                        reads=[t, pu], writes=[self.ACTB])
            self.dump("act", self.ACTB, self.ACTB[:, 1, 0:512], [128, 512])
            for mo in range(KT):
                W = self.WD[self.wd_i]
                self.wd_i = (self.wd_i + 1) % len(self.WD)
                c.dma("pool", W[:, 0:nm, :], wdn[l, mo, :, m0:m0 + nm, :], writes=[W])
                for ti, (c0, n) in enumerate(tiles):
                    pd = self.PS[4 + (mo % 2)] if ti == 0 else self.PS[6]
                    od = 0 if ti == 0 else 32
                    for k in range(nm):
                        c.op("pe", lambda e, k=k, pd=pd, od=od: e.matmul(pd[:, od:od + n], W[:, k, :], self.ACTB[:, k, c0:c0 + n],
                                                                      start=(k == 0), stop=(k == nm - 1)),
                             reads=[W, self.ACTB], writes=[pd], inc=(k == nm - 1))
                    if half == 0:
                        c.op("act", lambda e, pd=pd, od=od, mo=mo: e.activation(out=y[:, mo, c0:c0 + n], in_=pd[:, od:od + n], func=AF.Copy),
                             reads=[pd, xnb], writes=[yb])
                    else:
                        c.op("dve", lambda e, pd=pd, od=od, mo=mo: e.tensor_tensor(out=y[:, mo, c0:c0 + n], in0=y[:, mo, c0:c0 + n],
                                                                                  in1=pd[:, od:od + n], op=ALU.add),
                             reads=[pd, yb], writes=[yb])
            m0 += nm
        self.dump("y", yb, y[:, 2, 0:512], [128, 512])
        self.post_norm_add(seg, gbase + KT)

    NCB = 258 + 512 + 128 + 128 + 512 + 512
    NCF = 128 + 512 + 512
    CB_BAND, CB_M2, CB_BD, CB_ID, CB_MB, CB_MC = 0, 258, 770, 898, 1026, 1538
    CF_ID, CF_IOTA, CF_RST = 0, 128, 640

    def declare_mix(self):
        nl = self.nlayer
        self.win_f = self.din("win_f", [nl, NPAIR, 128, 2, KT, 128])
        self.wt = self.din("wt", [nl, 4, 128, KT, 256])
        self.wout = self.din("wout", [nl, KT, 128, MIXK, 128])
        self.wglu = self.din("wglu", [nl, 4, 128, 4, 128])
        self.rot = self.din("rot", [128, 2, self.ntok])
        self.cbf = self.din("cbf", [128, self.NCB])
        self.cf32 = self.din("cf32", [128, self.NCF])
        self.plA = self.din("plA", [128, nl, 46])
        self.pmix = self.din("pmix", [128, nl, 30])
        self.hlb = self.din("hlb", [128, nl, 4])
        self.s5r = self.din("s5r", [128, nl, 1728])
        self.x0s = self.din("x0s", [128, nl, NS, 14, 2])
        self.hs0 = self.din("hs0", [128, nl, NS, 4, 128])
        self.kct = self.din("kct", [128, nl, NS, 4, 128])
        self.vct = self.din("vct", [128, nl, NS, 320])
        self.o_ssm_p = self.dout("o_ssm_p", [128, nl, 14, 2])
        self.o_ssm_s = self.dout("o_ssm_s", [128, nl, NS, 14, 2])
        self.o_swa_k = self.dout("o_swa_k", [nl, 128, 128 + NS])
        self.o_swa_v = self.dout("o_swa_v", [nl, 128, 128])
        self.o_sv = self.dout("o_sv", [nl, NS, 320])
        self.o_d0_k = self.dout("o_d0_k", [nl, 64, 128 + NS])
        self.o_d0_v = self.dout("o_d0_v", [nl, 128, 64])
        self.o_d1_k = self.dout("o_d1_k", [nl, 64, 512 + NS])
        self.o_d1_v = self.dout("o_d1_v", [nl, 4, 128, 64])
        self.o_d2_k = self.dout("o_d2_k", [nl, 64, self.nseg * SEG + NS])
        self.o_d2_v = self.dout("o_d2_v", [nl, self.nseg, 4, 128, 64])
        self.o_hg_p = self.dout("o_hg_p", [nl, 4, 128, 128])
        self.o_hg_s = self.dout("o_hg_s", [nl, NS, 4, 128, 128])

    def setup_mix_consts(self):
        c = self.c
        nl = self.nlayer
        NT = self.NT
        self.CB = c.sbuf([128, self.NCB], BF16, "CB")
        c.dma("pool", self.CB[:], self.cbf, writes=[self.CB])
        self.CF = c.sbuf([128, self.NCF], F32, "CF")
        c.dma("sp", self.CF[:], self.cf32, writes=[self.CF])
        self.idb = self.CB[:, self.CB_ID:self.CB_ID + 128]
        self.idf = self.CF[:, self.CF_ID:self.CF_ID + 128]
        self.PLA = c.sbuf([128, nl, 46], F32, "PLA")
        c.dma("sp", self.PLA[:], self.plA, writes=[self.PLA])
        self.PMX = c.sbuf([128, nl, 30], F32, "PMX")
        c.dma("sp", self.PMX[:], self.pmix, writes=[self.PMX])
        self.HLB = c.sbuf([128, nl, 4], F32, "HLB")
        c.dma("sp", self.HLB[:], self.hlb, writes=[self.HLB])
        self.SK8 = c.sbuf([1, 8], BF16, "SK8")
        self.LBT = c.sbuf([128, nl, 4, 3], F32, "LBT")
        mx = c.sbuf([128, 4], F32, "lbmx")
        ex = c.sbuf([128, nl, 4], F32, "lbex")
        sm = c.sbuf([128, 4], F32, "lbsm")
        c.op("dve", lambda e: e.tensor_copy(out=mx[:], in_=self.HLB[:, 0, :]), reads=[self.HLB], writes=[mx])
        for l in range(1, nl):
            c.op("dve", lambda e, l=l: e.tensor_tensor(out=mx[:], in0=mx[:], in1=self.HLB[:, l, :], op=ALU.max),
                 reads=[self.HLB, mx], writes=[mx])
        for l in range(nl):
            c.op("dve", lambda e, l=l: e.tensor_tensor(out=ex[:, l, :], in0=self.HLB[:, l, :], in1=mx[:], op=ALU.subtract),
                 reads=[self.HLB, mx], writes=[ex])
        c.op("act", lambda e: e.activation(out=ex[:], in_=ex[:], func=AF.Exp), reads=[ex], writes=[ex])
        c.op("dve", lambda e: e.tensor_copy(out=sm[:], in_=ex[:, 0, :]), reads=[ex], writes=[sm])
        for l in range(1, nl):
            c.op("dve", lambda e, l=l: e.tensor_tensor(out=sm[:], in0=sm[:], in1=ex[:, l, :], op=ALU.add),
                 reads=[ex, sm], writes=[sm])
        c.op("dve", lambda e: e.reciprocal(out=sm[:], in_=sm[:]), reads=[sm], writes=[sm])
        for l in range(nl):
            c.op("dve", lambda e, l=l: e.tensor_tensor(out=ex[:, l, :], in0=ex[:, l, :], in1=sm[:], op=ALU.mult),
                 reads=[ex, sm], writes=[ex])
        cum = c.sbuf([128, 4], F32, "lbcum")
        c.op("dve", lambda e: e.memset(cum[:], 0.0), writes=[cum])
        for l in range(nl):
            if l > 0:
                c.op("dve", lambda e, l=l: e.tensor_tensor(out=cum[:], in0=cum[:], in1=ex[:, l, :], op=ALU.add),
                     reads=[ex, cum], writes=[cum])
            c.op("dve", lambda e, l=l: e.tensor_scalar(out=self.LBT[:, l, :, 0], in0=cum[:], scalar1=-1.0, scalar2=1.0,
                                                       op0=ALU.mult, op1=ALU.add), reads=[cum], writes=[self.LBT])
            c.op("dve", lambda e, l=l: e.tensor_scalar(out=self.LBT[:, l, :, 1], in0=cum[:], scalar1=1e-30, scalar2=None,
                                                       op0=ALU.max), reads=[cum], writes=[self.LBT])
            c.op("dve", lambda e, l=l: e.tensor_scalar(out=self.LBT[:, l, :, 2], in0=cum[:], scalar1=1.0, scalar2=-1.0,
                                                       op0=ALU.mult, op1=ALU.add), reads=[cum], writes=[self.LBT])
        self.KB = [[c.sbuf([128, NT], BF16, f"KB{l}{p}") for p in range(2)] for l in range(nl)]
        self.KC0 = [[c.sbuf([128, NT], BF16, f"KC0{l}{p}") for p in range(2)] for l in range(nl)]
        self.KC1 = [[c.sbuf([128, NT], BF16, f"KC1{l}{p}") for p in range(2)] for l in range(nl)]
        self.KC2 = [c.sbuf([128, self.nseg * SEG + NS], BF16, f"KC2{l}") for l in range(nl)]
        self.VBh = [[c.sbuf([128, 4, 128], BF16, f"VB{l}{p}") for p in range(2)] for l in range(nl)]
        self.VC0h = [[c.sbuf([128, 4, 64], BF16, f"VC0{l}{p}") for p in range(2)] for l in range(nl)]
        self.VC1h = [[c.sbuf([128, 4, 64], BF16, f"VC1{l}{p}") for p in range(2)] for l in range(nl)]
        self.VC2 = [c.sbuf([128, 4, self.nseg, 64], BF16, f"VC2{l}") for l in range(nl)]
        self.X0 = [c.sbuf([128, 14, 2], F32, f"X0{l}") for l in range(nl)]
        self.SH = [[c.sbuf([128, 4, 128], F32, f"SH{l}{p}") for p in range(2)] for l in range(nl)]
        for l in range(nl):
            c.op("dve", lambda e, l=l: e.memset(self.X0[l][:], 0.0), writes=[self.X0[l]])
            c.op("dve", lambda e, l=l: e.memset(self.SH[l][0][:], 0.0), writes=[self.SH[l][0]])
        self.sh_par = [0] * nl
        self.MIX = Buf(self.ARENA.t[:, self.YB // 4:(self.YB + MIXK * NT * 2) // 4].bitcast(BF16).rearrange(
            "p (k t) -> p k t", k=MIXK), "MIX")
        self.stat = [c.sbuf([128, 8], F32, f"stat{i}") for i in range(4)]

    def ar_reset(self):
        self.ar_off = [0, 0]
        self.rings = {}

    def ta(self, shape, dt, name, pool=0):
        esz = 2 if dt == BF16 else 4
        n = 1
        for s in shape[1:]:
            n *= s
        nbytes = (n * esz + 3) // 4 * 4
        off = self.ar_off[pool]
        lim = self.YB if pool == 0 else KT * self.NT * 2
        assert off + nbytes <= lim, f"arena pool {pool} overflow at {name}: {off}+{nbytes} > {lim}"
        self.ar_off[pool] = off + nbytes
        if pool == 0:
            base = self.ARENA.t[:, off // 4:(off + nbytes) // 4]
        else:
            base = self.XN[0].t.rearrange("p k t -> p (k t)")[:, off // 2:(off + nbytes) // 2].bitcast(F32)
        P = shape[0]
        ap = base[0:P, :]
        if dt == BF16:
            ap = ap.bitcast(BF16)
        ap = ap[:, 0:n]
        if len(shape) == 3:
            ap = ap.rearrange("p (a b) -> p a b", a=shape[1])
        elif len(shape) == 4:
            ap = ap.rearrange("p (a b c) -> p a b c", a=shape[1], b=shape[2])
        self.c.nbuf += 1
        return Buf(ap, f"{name}_{self.c.nbuf}")

    def next_wa(self):
        W = self.WA[self.wa_i]
        self.wa_i = (self.wa_i + 1) % len(self.WA)
        return W

    def proj_pair(self, l, pr, tiles, consume, which=(0, 1)):
        c = self.c
        xnb, xn = self.XN
        W = self.next_wa()
        c.dma("pool", W[:, :, :, :], self.win_f[l, pr], writes=[W])
        for ti, (c0, n) in enumerate(tiles):
            res = []
            for s in which:
                if ti == 0:
                    pb = self.PS[2 * s + self.pp[s]]
                    self.pp[s] ^= 1
                    o = 0
                else:
                    pb = self.PS[6]
                    o = 64 * s
                for k in range(KT):
                    c.op("pe", lambda e, k=k, pb=pb, o=o, s=s: e.matmul(pb[:, o:o + n], W[:, s, k, :], xn[:, k, c0:c0 + n],
                                                                      start=(k == 0), stop=(k == KT - 1)),
                         reads=[W, xnb], writes=[pb], inc=(k == KT - 1))
                res.append((pb, pb[:, o:o + n]))
            consume(c0, n, res)

    def rotary(self, c0, n, res, dst_buf, dst_ap, f32_buf=None, f32_ap=None):
        c = self.c
        (b0, x), (b1, xs) = res
        t1 = self.tmp[0]
        t2 = self.tmp[1]
        c.op("dve", lambda e: e.tensor_tensor(out=t1[:, 0:n], in0=x, in1=self.ROT[:, 0, c0:c0 + n], op=ALU.mult),
             reads=[b0, self.ROT], writes=[t1])
        c.op("dve", lambda e: e.tensor_tensor(out=t2[:, 0:n], in0=xs, in1=self.ROT[:, 1, c0:c0 + n], op=ALU.mult),
             reads=[b1, self.ROT], writes=[t2])
        if f32_buf is None:
            c.op("dve", lambda e: e.tensor_tensor(out=dst_ap, in0=t1[:, 0:n], in1=t2[:, 0:n], op=ALU.add),
                 reads=[t1, t2], writes=[dst_buf])
        else:
            c.op("dve", lambda e: e.tensor_tensor(out=f32_ap, in0=t1[:, 0:n], in1=t2[:, 0:n], op=ALU.add),
                 reads=[t1, t2], writes=[f32_buf])
            c.op("act", lambda e: e.activation(out=dst_ap, in_=f32_ap, func=AF.Copy), reads=[f32_buf], writes=[dst_buf])

    def mix_norm(self, seg, src_buf, src_fn, nchunk, n_feat, l, gofs, mbase):
        c = self.c
        for (c0, n) in self.tiles(seg):
            ps = self.PS[7]
            self.sumsq_rstd(src_buf, lambda k: src_fn(k, c0, n), nchunk, n_feat, c0, n, ps)
            for k in range(nchunk):
                c.op("dve", lambda e, k=k: e.scalar_tensor_tensor(
                    out=self.MIX[:, mbase + k, c0:c0 + n], in0=src_fn(k, c0, n),
                    scalar=self.PMX[:, l, gofs + k:gofs + k + 1], in1=self.rstd[:, c0:c0 + n],
                    op0=ALU.mult, op1=ALU.mult), reads=[src_buf, self.PMX, self.rstd], writes=[self.MIX])

    def mixer(self, seg, l):
        c = self.c
        last = (seg == self.nseg - 1)
        NC = SEG + NS if last else SEG
        gm = (l * 6 + 2) * KT
        c.barrier()
        self.ar_reset()
        self.pp = [0, 0]
        self.pre_norm(seg, gm)
        c.op("act", lambda e: e.activation(out=self.SK8[0:1, :], in_=self.PMX[0:1, l, 22:30], func=AF.Copy, scale=8.0),
             reads=[self.PMX], writes=[self.SK8])
        if "mix" in self.parts or "D" in self.parts:
            self.mix_D(seg, l, NC)
        if "mix" in self.parts or "A" in self.parts:
            self.mix_A(seg, l, NC)
        if "mix" in self.parts or "BC" in self.parts:
            self.mix_BC(seg, l, NC)
        if "mix" in self.parts:
            self.mix_out(seg, l, NC)
        c.barrier()
    def set_slots(self, mode):
        if getattr(self, "_slot_mode", None) not in (None, mode):
            self.c.barrier()
        self._slot_mode = mode
        R, T = self.P32R, self.PTR
        sl = []
        if mode == "small":
            for s in range(4):
                P = Buf(R[:, 260 * s:260 * s + 260], f"P32s{s}")
                PT = Buf(T[:, 256 * s:256 * s + 256].rearrange("p (a b) -> p a b", a=2), f"PTs{s}")
                sl.append(dict(ps=self.PS[s], pst=self.PS[4 + s // 2], tc0=256 * (s % 2), pso=self.PS[6 + s // 2], oc0=256 * (s % 2),
                               P=P, PT=PT, st=self.stat[s], eng=("act" if s % 2 == 0 else "dve")))
        else:
            for s in range(2):
                P = Buf(R[:, 520 * s:520 * s + 514], f"P32b{s}")
                PT = Buf(T[:, 512 * s:512 * s + 512].rearrange("p (a b) -> p a b", a=4), f"PTb{s}")
                sl.append(dict(ps=self.PS[3 * s], pst=self.PS[3 * s + 1], tc0=0, pso=self.PS[3 * s + 2], oc0=0,
                               P=P, PT=PT, st=self.stat[s], eng=("act" if s == 0 else "dve")))
        self.AS = sl

    def attn_units(self, units):
        n = len(self.AS)
        for i in range(0, len(units), n):
            batch = units[i:i + n]
            for s, u in enumerate(batch):
                self.au_scores(u, s)
            for s, u in enumerate(batch):
                self.au_softmax(u, s)
            for s, u in enumerate(batch):
                self.au_transpose(u, s)
            for s, u in enumerate(batch):
                self.au_pv(u, s)
            for s, u in enumerate(batch):
                self.au_out(u, s)

    def au_scores(self, u, s):
        c = self.c
        nq, ncols = u["nq"], u["ncols"]
        ps = self.AS[s]["ps"]
        if u.get("sink") is not None:
            hh = u["sink"]
            nkeys = u["nkeys"]
            c.op("pe", lambda e: e.matmul(ps[0:nq, nkeys:nkeys + 1], self.ones_bf[0:1, 0:nq], self.SK8[0:1, hh:hh + 1], start=True, stop=True),
                 reads=[self.ones_bf, self.SK8], writes=[ps], inc=False)
        first = True
        if u["mask"] is not None:
            mb, map_ = u["mask"]
            nkk = u["nkeys"]
            c.op("pe", lambda e: e.matmul(ps[0:nq, 0:nkk], self.idb[0:nq, 0:nq], map_, start=True, stop=False),
                 reads=[self.CB, mb], writes=[ps], inc=False)
            first = False
        col = 0
        nkb = len(u["kblocks"])
        for i, (kb, kap, nk, rr) in enumerate(u["kblocks"]):
            o = ps[0:nq, col:col + nk]
            if rr:
                o = o.rearrange("p (s r j) -> p s r j", s=rr[0], r=rr[1])
            stp = True if first else (i == nkb - 1)
            c.op("pe", lambda e, o=o, kap=kap, stp=stp: e.matmul(o, u["q"][1], kap, start=first, stop=stp),
                 reads=[u["q"][0], kb], writes=[ps], inc=(i == nkb - 1))
            col += nk

    def au_softmax(self, u, s):
        c = self.c
        nq, ncols, nkeys = u["nq"], u["ncols"], u["nkeys"]
        A_ = self.AS[s]
        ps, st, P = A_["ps"], A_["st"], A_["P"]
        c.op("dve", lambda e: e.reduce_max(out=st[0:nq, 0:1], in_=ps[0:nq, 0:ncols], axis=AX.X), reads=[ps], writes=[st])
        c.op("dve", lambda e: e.tensor_scalar(out=st[0:nq, 1:2], in0=st[0:nq, 0:1], scalar1=-0.125, scalar2=None, op0=ALU.mult),
             reads=[st], writes=[st])
        c.op("act", lambda e: e.activation(out=P[0:nq, 0:ncols], in_=ps[0:nq, 0:ncols], func=AF.Exp, bias=st[0:nq, 1:2],
                                           scale=0.125, accum_out=st[0:nq, 2:3]), reads=[ps, st], writes=[P, st])
        c.op("dve", lambda e: e.reciprocal(out=st[0:nq, 3:4], in_=st[0:nq, 2:3]), reads=[st], writes=[st])
        c.op("dve", lambda e: e.tensor_scalar(out=P[0:nq, 0:nkeys], in0=P[0:nq, 0:nkeys], scalar1=st[0:nq, 3:4], scalar2=None,
                                              op0=ALU.mult), reads=[P, st], writes=[P])
        if u.get("lse") is not None:
            c.op("act", lambda e: e.activation(out=st[0:nq, 4:5], in_=st[0:nq, 2:3], func=AF.Ln), reads=[st], writes=[st])
            c.op("dve", lambda e: e.scalar_tensor_tensor(out=st[0:nq, 5:6], in0=st[0:nq, 0:1], scalar=0.125, in1=st[0:nq, 4:5],
                                                         op0=ALU.mult, op1=ALU.add), reads=[st], writes=[st])

    def au_transpose(self, u, s):
        c = self.c
        nq = u["nq"]
        A_ = self.AS[s]
        P, pst, PT, t0 = A_["P"], A_["pst"], A_["PT"], A_["tc0"]
        for i, vb in enumerate(u["vblocks"]):
            if vb["kind"] != "T":
                continue
            nk, k0 = vb["nk"], vb["k0"]
            dst = pst[0:nk, t0 + i * 128:t0 + i * 128 + nq]
            c.op("pe", lambda e, dst=dst, nk=nk, k0=k0: e.transpose(dst, P[0:nq, k0:k0 + nk], self.idf[0:nq, 0:nq]),
                 reads=[P, self.CF], writes=[pst])
            if A_["eng"] == "act":
                c.op("act", lambda e, i=i, nk=nk, dst=dst: e.activation(out=PT[0:nk, i, 0:nq], in_=dst, func=AF.Copy), reads=[pst], writes=[PT])
            else:
                c.op("dve", lambda e, i=i, nk=nk, dst=dst: e.tensor_copy(out=PT[0:nk, i, 0:nq], in_=dst), reads=[pst], writes=[PT])

    def au_pv(self, u, s):
        c = self.c
        nq, hf = u["nq"], u["half"]
        A_ = self.AS[s]
        P, PT, pso, st, o0 = A_["P"], A_["PT"], A_["pso"], A_["st"], A_["oc0"]
        rows = slice(64 * hf, 64 * hf + 64)
        nvb = len(u["vblocks"])
        for i, vb in enumerate(u["vblocks"]):
            if vb["kind"] == "T":
                rhs = PT[0:vb["nk"], i, 0:nq]
                rd = [PT, vb["buf"]]
            else:
                rhs = P[0:1, vb["k0"]:vb["k0"] + 1]
                rd = [P, vb["buf"]]
            c.op("pe", lambda e, vb=vb, rhs=rhs, i=i: e.matmul(pso[rows, o0:o0 + nq], vb["ap"], rhs, start=(i == 0), stop=(i == nvb - 1)),
                 reads=rd, writes=[pso], inc=(i == nvb - 1))
        if u.get("lse") is not None:
            c.op("pe", lambda e: e.matmul(pso[:, o0 + 128:o0 + 128 + nq], st[0:nq, 5:6].to_broadcast([nq, 128]), self.idf[0:nq, 0:nq],
                                          start=True, stop=True), reads=[st, self.CF], writes=[pso])

    def au_out(self, u, s):
        c = self.c
        nq, hf = u["nq"], u["half"]
        pso, o0 = self.AS[s]["pso"], self.AS[s]["oc0"]
        rows = slice(64 * hf, 64 * hf + 64)
        ob, oap = u["out"]
        src = pso[rows, o0:o0 + nq]
        if u.get("orr"):
            src = src.rearrange("p (r j) -> p r j", r=u["orr"])
        c.op("act", lambda e: e.activation(out=oap, in_=src, func=AF.Copy), reads=[pso], writes=[ob])
        if u.get("lse") is not None:
            lb, lap = u["lse"]
            src2 = pso[rows, o0 + 128:o0 + 128 + nq]
            if u.get("orr"):
                src2 = src2.rearrange("p (r j) -> p r j", r=u["orr"])
            c.op("act", lambda e: e.activation(out=lap, in_=src2, func=AF.Copy), reads=[pso], writes=[lb])

    def mix_BC(self, seg, l, NC):
        c = self.c
        last = (seg == self.nseg - 1)
        NT = self.NT
        tiles = self.tiles(seg)
        xnb, xn = self.XN
        par = seg % 2
        KBc, KBp = self.KB[l][par], self.KB[l][1 - par]
        KC0c, KC0p = self.KC0[l][par], self.KC0[l][1 - par]
        KC1c, KC1p = self.KC1[l][par], self.KC1[l][1 - par]
        KC2 = self.KC2[l]
        VBc, VBp = self.VBh[l][par], self.VBh[l][1 - par]
        VC0c, VC0p = self.VC0h[l][par], self.VC0h[l][1 - par]
        VC1c, VC1p = self.VC1h[l][par], self.VC1h[l][1 - par]
        VC2 = self.VC2[l]
        QCT = self.ta([128, 6, NT], BF16, "QCT")
        V8 = self.ta([NS, 320], F32, "V8") if last else None
        self.P32R = self.ta([128, 1040], F32, "P32R")
        self.PTR = self.ta([128, 1024], BF16, "PTR")
        self._slot_mode = None
        if last:
            for nm, shp, dt in (("KCTb", [128, 4, 128], BF16), ("VCTb", [128, 320], BF16), ("VSb", [1, 320], F32)):
                self.ta_ring(nm, shp, dt)
        mark0 = self.ar_off[0]
        QBT = self.ta([128, 4, NT], BF16, "QBT")
        mark1 = self.ar_off[0]
        KF = self.ta([128, NT], F32, "KF")
        VF = self.ta([128, 192], F32, "VF")
        self.ROT = self.ta([128, 2, NT], F32, "ROT")
        c.dma("sp", self.ROT[:, :, 0:SEG], self.rot[:, :, seg * SEG:(seg + 1) * SEG], writes=[self.ROT])
        if last:
            c.dma("sp", self.ROT[:, :, SEG:SEG + NS], self.rot[:, :, self.nseg * SEG:self.nseg * SEG + NS], writes=[self.ROT])
        for j in range(4):
            self.proj_pair(l, 2 + j, tiles, lambda c0, n, res, j=j: self.rotary(c0, n, res, QBT, QBT[:, j, c0:c0 + n]))
        for j in range(6):
            self.proj_pair(l, 7 + j, tiles, lambda c0, n, res, j=j: self.rotary(c0, n, res, QCT, QCT[:, j, c0:c0 + n]))
        import os as _os
        _stopat = _os.environ.get("STOPAT", "")
        if _stopat == "q":
            return
        self.proj_pair(l, 6, tiles, lambda c0, n, res: self.rotary(c0, n, res, KBc, KBc[:, c0:c0 + n], KF, KF[:, c0:c0 + n]))
        if last:
            c.dma("sp", self.o_swa_k[l, :, 0:128], KF[:, SEG - 128:SEG], reads=[KF], is_output=True)
            c.dma("sp", self.o_swa_k[l, :, 128:128 + NS], KF[:, SEG:SEG + NS], reads=[KF], is_output=True)
        self.proj_pair(l, 13, tiles, lambda c0, n, res: self.rotary(c0, n, res, KC0c, KC0c[:, c0:c0 + n], KF, KF[:, c0:c0 + n]))
        if last:
            c.dma("sp", self.o_d0_k[l, :, 0:128], KF[0:64, SEG - 128:SEG], reads=[KF], is_output=True)
            c.dma("sp", self.o_d0_k[l, :, 128:128 + NS], KF[0:64, SEG:SEG + NS], reads=[KF], is_output=True)
        self.proj_pair(l, 14, tiles, lambda c0, n, res: self.rotary(c0, n, res, KC1c, KC1c[:, c0:c0 + n], KF, KF[:, c0:c0 + n]))
        if last:
            c.dma("sp", self.o_d1_k[l, :, 0:SEG + NS], KF[0:64, 0:SEG + NS], reads=[KF], is_output=True)

        def k2(c0, n, res):
            g0 = seg * SEG + c0 if c0 < SEG else self.nseg * SEG + (c0 - SEG)
            self.rotary(c0, n, res, KC2, KC2[:, g0:g0 + n], KF, KF[:, c0:c0 + n])
        self.proj_pair(l, 15, tiles, k2)
        c.dma("sp", self.o_d2_k[l, :, seg * SEG:(seg + 1) * SEG], KF[0:64, 0:SEG], reads=[KF], is_output=True)
        if last:
            c.dma("sp", self.o_d2_k[l, :, self.nseg * SEG:self.nseg * SEG + NS], KF[0:64, SEG:SEG + NS], reads=[KF],
                  is_output=True)
        if _stopat == "k":
            return
        W2 = self.next_wa()
        W2v = W2[:, :, :, :].rearrange("p a k c -> p (a k c)").rearrange("p (k c) -> p k c", k=KT)
        c.dma("pool", W2v, self.wt[l, 2], writes=[W2])
        for blk in range(4):
            ps = self.PS[blk % 2]
            for k in range(KT):
                c.op("pe", lambda e, k=k, ps=ps, blk=blk: e.matmul(ps[:, 0:192], xn[:, k, blk * 128:(blk + 1) * 128], W2v[:, k, 0:192],
                                                                  start=(k == 0), stop=(k == KT - 1)),
                     reads=[W2, xnb], writes=[ps], inc=(k == KT - 1))
            c.op("act", lambda e, ps=ps, blk=blk: e.activation(out=VBc[:, blk, :], in_=ps[:, 0:128], func=AF.Copy), reads=[ps], writes=[VBc])
            c.op("act", lambda e, ps=ps, blk=blk: e.activation(out=VC0c[:, blk, :], in_=ps[:, 128:192], func=AF.Copy), reads=[ps], writes=[VC0c])
            if last and blk == 3 and not _os.environ.get("NOVOUT"):
                c.op("act", lambda e, ps=ps: e.activation(out=VF[:, 0:192], in_=ps[:, 0:192], func=AF.Copy), reads=[ps], writes=[VF])
                c.dma("sp", self.o_swa_v[l], VF[:, 0:128], reads=[VF], is_output=True)
                c.dma("sp", self.o_d0_v[l], VF[:, 128:192], reads=[VF], is_output=True)
        if _stopat == "v1":
            return
        for r in range(4):
            ps = self.PS[r % 2]
            for k in range(KT):
                c.op("pe", lambda e, k=k, ps=ps, r=r: e.matmul(ps[:, 0:64], xn[:, k, r:SEG:4], W2v[:, k, 192:256],
                                                              start=(k == 0), stop=(k == KT - 1)),
                     reads=[W2, xnb], writes=[ps], inc=(k == KT - 1))
            c.op("dve", lambda e, ps=ps, r=r: e.tensor_copy(out=VC1c[:, r, :], in_=ps[:, 0:64]), reads=[ps], writes=[VC1c])
            if last:
                c.op("act", lambda e, ps=ps: e.activation(out=VF[:, 0:64], in_=ps[:, 0:64], func=AF.Copy), reads=[ps], writes=[VF])
                c.dma("sp", self.o_d1_v[l, r], VF[:, 0:64], reads=[VF], is_output=True)
        if last:
            ps8 = self.PS[6]
            for k in range(KT):
                c.op("pe", lambda e, k=k: e.matmul(ps8[0:NS, 0:256], xn[:, k, SEG:SEG + NS], W2v[:, k, 0:256],
                                                   start=(k == 0), stop=(k == KT - 1)),
                     reads=[W2, xnb], writes=[ps8], inc=(k == KT - 1))
        if _stopat == "v2":
            return
        W3 = self.next_wa()
        W3v = W3[:, :, :, :].rearrange("p a k c -> p (a k c)").rearrange("p (k c) -> p k c", k=KT)
        c.dma("pool", W3v, self.wt[l, 3], writes=[W3])
        for r0 in range(4):
            ps = self.PS[2 + r0 % 2]
            for rr in range(4):
                for k in range(KT):
                    c.op("pe", lambda e, k=k, ps=ps, rr=rr, r0=r0: e.matmul(
                        ps[32 * rr:32 * rr + 32, 0:64], xn[:, k, 4 * r0 + rr:SEG:16], W3v[:, k, 0:64],
                        start=(k == 0), stop=(k == KT - 1), tile_position=(0, 32 * rr)),
                        reads=[W3, xnb], writes=[ps], inc=(k == KT - 1))
            c.op("dve", lambda e, ps=ps, r0=r0: e.tensor_copy(out=VC2[:, r0, seg, :], in_=ps[:, 0:64]), reads=[ps], writes=[VC2])
            c.op("act", lambda e, ps=ps: e.activation(out=VF[:, 64:128], in_=ps[:, 0:64], func=AF.Copy), reads=[ps], writes=[VF])
            c.dma("sp", self.o_d2_v[l, seg, r0], VF[:, 64:128], reads=[VF], is_output=True)
        if last:
            for k in range(KT):
                c.op("pe", lambda e, k=k: e.matmul(ps8[0:NS, 256:320], xn[:, k, SEG:SEG + NS], W3v[:, k, 0:64],
                                                   start=(k == 0), stop=(k == KT - 1)),
                     reads=[W3, xnb], writes=[ps8], inc=(k == KT - 1))
            c.op("act", lambda e: e.activation(out=V8[:, :], in_=ps8[0:NS, 0:320], func=AF.Copy), reads=[ps8], writes=[V8])
            c.dma("sp", self.o_sv[l], V8[:, :], reads=[V8], is_output=True)
        if "stop_proj" in self.parts:
            return
        c.barrier()
        self.ar_off[0] = mark1
        OBT = self.ta([128, 4, NT], BF16, "OBT")
        first_seq = (seg == 0)
        units = []
        for h in range(8):
            hf, j = h // 4, h % 4
            rows = slice(64 * hf, 64 * hf + 64)
            for qb in range(4):
                u = dict(nq=128, half=hf, q=(QBT, QBT[rows, j, qb * 128:(qb + 1) * 128]))
                vcol = slice(64 * hf, 64 * hf + 64)
                if qb > 0:
                    u["kblocks"] = [(KBc, KBc[rows, (qb - 1) * 128:(qb + 1) * 128], 256, 0)]
                    u["vblocks"] = [dict(kind="T", buf=VBc, ap=VBc[:, qb - 1, vcol], nk=128, k0=0),
                                    dict(kind="T", buf=VBc, ap=VBc[:, qb, vcol], nk=128, k0=128)]
                    u["mask"] = (self.CB, self.CB[:, 0:256])
                    u["nkeys"], u["ncols"] = 256, 257
                elif not first_seq:
                    u["kblocks"] = [(KBp, KBp[rows, SEG - 128:SEG], 128, 0), (KBc, KBc[rows, 0:128], 128, 0)]
                    u["vblocks"] = [dict(kind="T", buf=VBp, ap=VBp[:, 3, vcol], nk=128, k0=0),
                                    dict(kind="T", buf=VBc, ap=VBc[:, 0, vcol], nk=128, k0=128)]
                    u["mask"] = (self.CB, self.CB[:, 0:256])
                    u["nkeys"], u["ncols"] = 256, 257
                else:
                    u["kblocks"] = [(KBc, KBc[rows, 0:128], 128, 0)]
                    u["vblocks"] = [dict(kind="T", buf=VBc, ap=VBc[:, 0, vcol], nk=128, k0=0)]
                    u["mask"] = (self.CB, self.CB[:, 128:256])
                    u["nkeys"], u["ncols"] = 128, 129
                u["out"] = (OBT, OBT[rows, j, qb * 128:(qb + 1) * 128])
                u["sink"] = h
                units.append(u)
        self.set_slots("small")
        self.attn_units(units)
        if last:
            self.sample_attn(l, QBT, QCT, KBc, (KC0c, KC1c, KC2), V8, OBT, None, None, which="B")
        self.dump(f"obt{seg}", OBT, OBT[:, 1, 0:512], [128, 512])
        if last:
            self.dump("sob", OBT, OBT[:, 1, 512:520], [128, 8])
        self.mix_norm(seg, OBT, lambda k, c0, n: OBT[:, k, c0:c0 + n], 4, 512, l, 4, 4)
        if "stop_B" in self.parts:
            return
        for chn in (4, 5):
            tq = self.tmp[chn % 2]
            c.op("dve", lambda e, chn=chn, tq=tq: e.tensor_copy(
                out=tq[:, 0:SEG].rearrange("p (a r j) -> p a r j", a=4, r=4),
                in_=QCT[:, chn, 0:SEG].rearrange("p (j a r) -> p a r j", a=4, r=4)), reads=[QCT], writes=[tq])
            c.op("dve", lambda e, chn=chn, tq=tq: e.tensor_copy(out=QCT[:, chn, 0:SEG], in_=tq[:, 0:SEG]), reads=[tq], writes=[QCT])
        c.barrier()
        self.ar_off[0] = mark0
        OCT = self.ta([128, 6, NT], F32, "OCT", pool=1)
        LST = self.ta([128, 6, NT], F32, "LST", pool=0)
        c.op("dve", lambda e: e.memset(OCT[:, :, :], 0.0), writes=[OCT])
        c.op("dve", lambda e: e.memset(LST[:, :, :], 0.0), writes=[LST])
        units = []
        band = (self.CB, self.CB[:, 0:256])
        bandf = (self.CB, self.CB[:, 128:256])
        for i in range(3):
            hf = 1 if i == 1 else 0
            rows = slice(64 * hf, 64 * hf + 64)
            ch = 0 + (1 if i == 2 else 0)
            for qb in range(4):
                u = dict(nq=128, half=hf, q=(QCT, QCT[rows, ch, qb * 128:(qb + 1) * 128]))
                if qb > 0:
                    u["kblocks"] = [(KC0c, KC0c[rows, (qb - 1) * 128:(qb + 1) * 128], 256, 0)]
                    u["vblocks"] = [dict(kind="T", buf=VC0c, ap=VC0c[:, qb - 1, :], nk=128, k0=0),
                                    dict(kind="T", buf=VC0c, ap=VC0c[:, qb, :], nk=128, k0=128)]
                    u["mask"], u["nkeys"], u["ncols"] = band, 256, 256
                elif not first_seq:
                    u["kblocks"] = [(KC0p, KC0p[rows, SEG - 128:SEG], 128, 0), (KC0c, KC0c[rows, 0:128], 128, 0)]
                    u["vblocks"] = [dict(kind="T", buf=VC0p, ap=VC0p[:, 3, :], nk=128, k0=0),
                                    dict(kind="T", buf=VC0c, ap=VC0c[:, 0, :], nk=128, k0=128)]
                    u["mask"], u["nkeys"], u["ncols"] = band, 256, 256
                else:
                    u["kblocks"] = [(KC0c, KC0c[rows, 0:128], 128, 0)]
                    u["vblocks"] = [dict(kind="T", buf=VC0c, ap=VC0c[:, 0, :], nk=128, k0=0)]
                    u["mask"], u["nkeys"], u["ncols"] = bandf, 128, 128
                u["out"] = (OCT, OCT[rows, ch, qb * 128:(qb + 1) * 128])
                u["lse"] = (LST, LST[rows, ch, qb * 128:(qb + 1) * 128])
                units.append(u)
            ch = 2 + (1 if i == 2 else 0)
            for r in range(4):
                u = dict(nq=128, half=hf, q=(QCT, QCT[rows, ch, r:SEG:4]))
                if not first_seq:
                    u["kblocks"] = [(KC1p, KC1p[rows, r:SEG:4], 128, 0), (KC1c, KC1c[rows, r:SEG:4], 128, 0)]
                    u["vblocks"] = [dict(kind="T", buf=VC1p, ap=VC1p[:, r, :], nk=128, k0=0),
                                    dict(kind="T", buf=VC1c, ap=VC1c[:, r, :], nk=128, k0=128)]
                    u["mask"], u["nkeys"], u["ncols"] = band, 256, 256
                else:
                    u["kblocks"] = [(KC1c, KC1c[rows, r:SEG:4], 128, 0)]
                    u["vblocks"] = [dict(kind="T", buf=VC1c, ap=VC1c[:, r, :], nk=128, k0=0)]
                    u["mask"], u["nkeys"], u["ncols"] = bandf, 128, 128
                u["out"] = (OCT, OCT[rows, ch, r:SEG:4])
                u["lse"] = (LST, LST[rows, ch, r:SEG:4])
                units.append(u)
            ch = 4 + (1 if i == 2 else 0)
            nkt = 128 * (seg + 1)
            for r0 in range(4):
                u = dict(nq=128, half=hf, q=(QCT, QCT[rows, ch, 128 * r0:128 * r0 + 128]))
                kap = KC2[rows, 0:SEG * (seg + 1)].rearrange("p (s j r) -> p s r j", s=seg + 1, r=16)[:, :, 4 * r0:4 * r0 + 4, :]
                u["kblocks"] = [(KC2, kap, nkt, (seg + 1, 4))]
                u["vblocks"] = [dict(kind="T", buf=VC2, ap=VC2[:, r0, sg, :], nk=128, k0=128 * sg) for sg in range(seg + 1)]
                u["mask"] = (self.CB, self.CB[:, self.CB_M2 + 512 - nkt:self.CB_M2 + 512])
                u["nkeys"], u["ncols"] = nkt, nkt
                u["out"] = (OCT, OCT[rows, ch, 0:SEG].rearrange("p (j r) -> p r j", r=16)[:, 4 * r0:4 * r0 + 4, :])
                u["lse"] = (LST, LST[rows, ch, 0:SEG].rearrange("p (j r) -> p r j", r=16)[:, 4 * r0:4 * r0 + 4, :])
                u["orr"] = 4
                units.append(u)
        self.set_slots("small")
        self.attn_units([u for u in units if not u.get("orr")])
        if last:
            self.sample_attn(l, QBT, QCT, KBc, (KC0c, KC1c, KC2), V8, None, OCT, LST, which="C")
        self.set_slots("big")
        self.attn_units([u for u in units if u.get("orr")])
        self.dump(f"oct_raw{seg}", OCT, OCT[:, 2, 0:512], [128, 512])
        self.dump(f"lst{seg}", LST, LST[:, 2, 0:512], [128, 512])
        ta_, tb_ = self.tmp[0], self.tmp[1]
        E = [self.ta([128, 256], F32, f"E{g}", pool=1) for g in range(3)]
        ctiles = [(0, 256), (256, 256)] + ([(SEG, NS)] if last else [])
        for ls in range(2):
            chs = [2 * g + ls for g in range(3)]
            for (c0, n) in ctiles:
                cs = slice(c0, c0 + n)
                c.op("dve", lambda e, cs=cs, n=n: e.tensor_tensor(out=ta_[:, 0:n], in0=LST[:, chs[0], cs], in1=LST[:, chs[1], cs], op=ALU.max),
                     reads=[LST], writes=[ta_])
                c.op("dve", lambda e, cs=cs, n=n: e.tensor_tensor(out=ta_[:, 0:n], in0=ta_[:, 0:n], in1=LST[:, chs[2], cs], op=ALU.max),
                     reads=[LST, ta_], writes=[ta_])
                for g in range(3):
                    c.op("dve", lambda e, g=g, cs=cs, n=n: e.tensor_tensor(out=E[g][:, 0:n], in0=LST[:, chs[g], cs], in1=ta_[:, 0:n], op=ALU.subtract),
                         reads=[LST, ta_], writes=[E[g]])
                    c.op("act", lambda e, g=g, n=n: e.activation(out=E[g][:, 0:n], in_=E[g][:, 0:n], func=AF.Exp), reads=[E[g]], writes=[E[g]])
                c.op("dve", lambda e, n=n: e.tensor_tensor(out=tb_[:, 0:n], in0=E[0][:, 0:n], in1=E[1][:, 0:n], op=ALU.add),
                     reads=[E[0], E[1]], writes=[tb_])
                c.op("dve", lambda e, n=n: e.tensor_tensor(out=tb_[:, 0:n], in0=tb_[:, 0:n], in1=E[2][:, 0:n], op=ALU.add),
                     reads=[E[2], tb_], writes=[tb_])
                c.op("dve", lambda e, n=n: e.reciprocal(out=tb_[:, 0:n], in_=tb_[:, 0:n]), reads=[tb_], writes=[tb_])
                for g in range(3):
                    c.op("dve", lambda e, g=g, n=n: e.tensor_tensor(out=E[g][:, 0:n], in0=E[g][:, 0:n], in1=tb_[:, 0:n], op=ALU.mult),
                         reads=[E[g], tb_], writes=[E[g]])
                    c.op("dve", lambda e, g=g, cs=cs, n=n: e.tensor_tensor(out=OCT[:, chs[g], cs], in0=OCT[:, chs[g], cs], in1=E[g][:, 0:n], op=ALU.mult),
                         reads=[OCT, E[g]], writes=[OCT])
        self.dump(f"oct{seg}", OCT, OCT[:, 2, 0:512], [128, 512])
        if last:
            self.dump("soc", OCT, OCT[:, 2, 512:520], [128, 8])
        self.mix_norm(seg, OCT, lambda k, c0, n: OCT[:, k, c0:c0 + n], 6, 576, l, 8, 8)

    def sample_attn(self, l, QBT, QCT, KBc, KCs, V8, OBT, OCT, LST, which):
        c = self.c
        for b in range(NS):
            col = SEG + b
            KCTb = self.ta_ring("KCTb", [128, 4, 128], BF16)
            VCTb = self.ta_ring("VCTb", [128, 320], BF16)
            VSb = self.ta_ring("VSb", [1, 320], F32)
            kf, vf = self.tmp[0], self.tmp[1]
            c.dma("sp", kf[:, 0:512].rearrange("p (a b) -> p a b", a=4), self.kct[:, l, b], writes=[kf])
            c.dma("sp", vf[:, 0:320], self.vct[:, l, b], writes=[vf])
            c.op("act", lambda e: e.activation(out=KCTb[:, :, :], in_=kf[:, 0:512].rearrange("p (a b) -> p a b", a=4), func=AF.Copy),
                 reads=[kf], writes=[KCTb])
            c.op("dve", lambda e: e.tensor_copy(out=VCTb[:, :], in_=vf[:, 0:320]), reads=[vf], writes=[VCTb])
            psr = self.PS[6]
            c.op("pe", lambda e, b=b: e.matmul(psr[0:1, 0:320], self.idf[0:NS, b:b + 1], V8[0:NS, 0:320], start=True, stop=True),
                 reads=[self.CF, V8], writes=[psr])
            c.op("act", lambda e: e.activation(out=VSb[0:1, :], in_=psr[0:1, 0:320], func=AF.Copy), reads=[psr], writes=[VSb])
            units = []
            if which == "B":
                for h in range(8):
                    hf, j = h // 4, h % 4
                    rows = slice(64 * hf, 64 * hf + 64)
                    u = dict(nq=1, half=hf, q=(QBT, QBT[rows, j, col:col + 1]))
                    u["kblocks"] = [(KCTb, KCTb[rows, 0, :], 128, 0), (KBc, KBc[rows, col:col + 1], 1, 0)]
                    u["vblocks"] = [dict(kind="T", buf=VCTb, ap=VCTb[:, 64 * hf:64 * hf + 64], nk=128, k0=0),
                                    dict(kind="D", buf=VSb, ap=VSb[0:1, 64 * hf:64 * hf + 64], nk=1, k0=128)]
                    u["mask"] = None
                    u["sink"] = h
                    u["nkeys"], u["ncols"] = 129, 130
                    u["out"] = (OBT, OBT[rows, j, col:col + 1])
                    units.append(u)
            else:
                for g in range(3):
                    Kg = KCs[g]
                    kcol = col if g < 2 else self.nseg * SEG + b
                    for i in range(3):
                        hf = 1 if i == 1 else 0
                        rows = slice(64 * hf, 64 * hf + 64)
                        ch = 2 * g + (1 if i == 2 else 0)
                        u = dict(nq=1, half=hf, q=(QCT, QCT[rows, ch, col:col + 1]))
                        u["kblocks"] = [(KCTb, KCTb[rows, 1 + g, :], 128, 0), (Kg, Kg[rows, kcol:kcol + 1], 1, 0)]
                        u["vblocks"] = [dict(kind="T", buf=VCTb, ap=VCTb[:, 128 + 64 * g:192 + 64 * g], nk=128, k0=0),
                                        dict(kind="D", buf=VSb, ap=VSb[0:1, 128 + 64 * g:192 + 64 * g], nk=1, k0=128)]
                        u["mask"] = None
                        u["nkeys"], u["ncols"] = 129, 129
                        u["out"] = (OCT, OCT[rows, ch, col:col + 1])
                        u["lse"] = (LST, LST[rows, ch, col:col + 1])
                        units.append(u)
            self.attn_units(units)

    def ta_ring(self, name, shape, dt, nbuf=2):
        key = ("ring", name)
        if key not in self.rings:
            self.rings[key] = [[self.ta(shape, dt, f"{name}{i}") for i in range(nbuf)], 0]
        r = self.rings[key]
        b = r[0][r[1]]
        r[1] = (r[1] + 1) % nbuf
        return b
    def mix_D(self, seg, l, NC):
        c = self.c
        last = (seg == self.nseg - 1)
        NT = self.NT
        tiles = self.tiles(seg)
        xnb, xn = self.XN
        m_start = self.ar_off[0]
        VD = self.ta([128, 4, 512], BF16, "VD")
        V8d = self.ta([NS, 512], F32, "V8d") if last else None
        for grp in range(2):
            W = self.next_wa()
            Wv = W[:, :, :, :].rearrange("p a k c -> p (a k c)").rearrange("p (k c) -> p k c", k=KT)
            c.dma("pool", Wv, self.wt[l, grp], writes=[W])
            for blk in range(4):
                ps = self.PS[blk % 2]
                for k in range(KT):
                    c.op("pe", lambda e, k=k, ps=ps, blk=blk: e.matmul(ps[:, 0:256], xn[:, k, blk * 128:(blk + 1) * 128], Wv[:, k, :],
                                                                      start=(k == 0), stop=(k == KT - 1)),
                         reads=[W, xnb], writes=[ps], inc=(k == KT - 1))
                if blk % 2 == 0:
                    c.op("act", lambda e, ps=ps, blk=blk, grp=grp: e.activation(out=VD[:, blk, 256 * grp:256 * grp + 256], in_=ps[:, 0:256],
                                                                             func=AF.Copy), reads=[ps], writes=[VD])
                else:
                    c.op("dve", lambda e, ps=ps, blk=blk, grp=grp: e.tensor_copy(out=VD[:, blk, 256 * grp:256 * grp + 256], in_=ps[:, 0:256]),
                         reads=[ps], writes=[VD])
            if last:
                ps8 = self.PS[6]
                for k in range(KT):
                    c.op("pe", lambda e, k=k: e.matmul(ps8[0:NS, 0:256], xn[:, k, SEG:SEG + NS], Wv[:, k, :],
                                                       start=(k == 0), stop=(k == KT - 1)),
                         reads=[W, xnb], writes=[ps8], inc=(k == KT - 1))
                c.op("act", lambda e, grp=grp: e.activation(out=V8d[:, 256 * grp:256 * grp + 256], in_=ps8[0:NS, 0:256], func=AF.Copy),
                     reads=[ps8], writes=[V8d])
        A = [self.ta([128, NT], F32, f"A{i}") for i in range(6)]
        QTb = self.ta([128, SEG], BF16, "QTb")
        KTb = self.ta([128, SEG], BF16, "KTb")
        KHT = self.ta([128, 4, 128], BF16, "KHT")
        OD = self.ta([128, NT], F32, "OD")
        ATm = [self.ta([128, 128], BF16, f"ATm{i}") for i in range(2)]
        FS = self.ta([128, NS], F32, "FS")
        SGp = self.ta([128, 2, NT], F32, "SGp")
        S0 = self.ta([128, 4, 128], F32, "S0") if last else None
        SN = self.ta([128, 128], F32, "SN") if last else None
        T1 = self.ta([128, 128], F32, "T1") if last else None
        BD = self.CB[:, self.CB_BD:self.CB_BD + 128]
        RST = self.CF[:, self.CF_RST:self.CF_RST + SEG]
        cur = self.sh_par[l]
        for h in range(4):
            if h % 2 == 0:
                def gcons(c0, n, res):
                    for s_, (pb, pap) in enumerate(res):
                        c.op("act", lambda e, s_=s_, pap=pap: e.activation(out=SGp[:, s_, c0:c0 + n], in_=pap, func=AF.Silu),
                             reads=[pb], writes=[SGp])
                self.proj_pair(l, 20 + h // 2, tiles, gcons)
            lb1 = self.LBT[:, l, h, 0:1]
            lbp = self.LBT[:, l, h, 1:2]
            lbn = self.LBT[:, l, h, 2:3]

            def qf(c0, n, res):
                (bq, qp), (bf_, fp) = res
                cs = slice(c0, c0 + n)
                c.op("act", lambda e: e.activation(out=A[0][:, cs], in_=qp, func=AF.Silu), reads=[bq], writes=[A[0]])
                c.op("act", lambda e: e.activation(out=A[1][:, cs], in_=fp, func=AF.Sigmoid), reads=[bf_], writes=[A[1]])
                c.op("dve", lambda e: e.tensor_scalar(out=A[2][:, cs], in0=A[1][:, cs], scalar1=lb1, scalar2=lbp, op0=ALU.mult, op1=ALU.add),
                     reads=[A[1], self.LBT], writes=[A[2]])
                c.op("dve", lambda e: e.tensor_scalar(out=A[3][:, cs], in0=A[1][:, cs], scalar1=lbn, scalar2=lb1, op0=ALU.mult, op1=ALU.add),
                     reads=[A[1], self.LBT], writes=[A[3]])
                if c0 >= SEG:
                    c.op("dve", lambda e: e.tensor_copy(out=FS[:, 0:n], in_=A[2][:, cs]), reads=[A[2]], writes=[FS])
                else:
                    c.op("act", lambda e: e.activation(out=A[2][:, cs], in_=A[2][:, cs], func=AF.Ln), reads=[A[2]], writes=[A[2]])
            self.proj_pair(l, 16 + h, tiles, qf)
            P_ = slice(0, SEG)
            c.op("dve", lambda e: e.tensor_tensor_scan(out=A[1][:, P_], data0=RST, data1=A[2][:, P_], initial=0.0, op0=ALU.mult, op1=ALU.add),
                 reads=[A[2], self.CF], writes=[A[1]])
            c.op("dve", lambda e: e.tensor_scalar(out=A[1][:, P_], in0=A[1][:, P_], scalar1=-80.0, scalar2=None, op0=ALU.max),
                 reads=[A[1]], writes=[A[1]])
            c.op("act", lambda e: e.activation(out=A[2][:, P_], in_=A[1][:, P_], func=AF.Exp), reads=[A[1]], writes=[A[2]])
            c.op("act", lambda e: e.activation(out=A[4][:, P_], in_=A[1][:, P_], func=AF.Exp, scale=-1.0), reads=[A[1]], writes=[A[4]])
            c.op("dve", lambda e: e.tensor_tensor(out=A[5][:, P_], in0=A[0][:, P_], in1=A[2][:, P_], op=ALU.mult),
                 reads=[A[0], A[2]], writes=[A[5]])
            c.op("act", lambda e: e.activation(out=QTb[:, :], in_=A[5][:, P_], func=AF.Copy), reads=[A[5]], writes=[QTb])
            c.op("dve", lambda e: e.tensor_tensor(out=KTb[:, :], in0=A[3][:, P_], in1=A[4][:, P_], op=ALU.mult),
                 reads=[A[3], A[4]], writes=[KTb])
            b3 = A[1][:, P_].rearrange("p (c t) -> p c t", t=32)
            c.op("dve", lambda e: e.tensor_tensor(out=A[4][:, P_].rearrange("p (c t) -> p c t", t=32),
                                                   in0=b3[:, :, 31:32].to_broadcast([128, 16, 32]), in1=b3, op=ALU.subtract),
                 reads=[A[1]], writes=[A[4]])
            c.op("act", lambda e: e.activation(out=A[4][:, P_], in_=A[4][:, P_], func=AF.Exp), reads=[A[4]], writes=[A[4]])
            c.op("dve", lambda e: e.tensor_tensor(out=A[4][:, P_], in0=A[3][:, P_], in1=A[4][:, P_], op=ALU.mult),
                 reads=[A[3], A[4]], writes=[A[4]])
            for blk in range(4):
                pst = self.PS[2 + blk % 2]
                c.op("pe", lambda e, blk=blk, pst=pst: e.transpose(pst[:, 0:128], A[4][:, blk * 128:(blk + 1) * 128], self.idf),
                     reads=[A[4], self.CF], writes=[pst])
                if blk % 2 == 0:
                    c.op("act", lambda e, blk=blk, pst=pst: e.activation(out=KHT[:, blk, :], in_=pst[:, 0:128], func=AF.Copy),
                         reads=[pst], writes=[KHT])
                else:
                    c.op("dve", lambda e, blk=blk, pst=pst: e.tensor_copy(out=KHT[:, blk, :], in_=pst[:, 0:128]), reads=[pst], writes=[KHT])
            hc = slice(128 * h, 128 * h + 128)
            for blk in range(4):
                bs = slice(blk * 128, (blk + 1) * 128)
                pa = self.PS[4]
                po = self.PS[5]
                at = ATm[blk % 2]
                c.op("pe", lambda e, bs=bs: e.matmul(pa[:, 0:128], KTb[:, bs], QTb[:, bs], start=True, stop=True),
                     reads=[KTb, QTb], writes=[pa])
                c.op("dve", lambda e, at=at: e.tensor_tensor(out=at[:, :], in0=pa[:, 0:128], in1=BD, op=ALU.mult),
                     reads=[pa, self.CB], writes=[at])
                c.op("pe", lambda e, at=at, blk=blk: e.matmul(po[:, 0:128], VD[:, blk, hc], at[:, :], start=True, stop=False),
                     reads=[VD, at], writes=[po], inc=True)
                for cc in range(4):
                    pd = self.PS[cc]
                    tp = dict(tile_position=(96, 0)) if cc == 3 else {}
                    c.op("pe", lambda e, cc=cc, blk=blk, pd=pd, tp=tp: e.matmul(
                        pd[:, 0:128], KHT[32 * cc:32 * cc + 32, blk, :], VD[32 * cc:32 * cc + 32, blk, hc],
                        start=True, stop=True, **tp), reads=[KHT, VD], writes=[pd], inc=True)
                for cc in range(4):
                    ch = 4 * blk + cc
                    Sc = self.SH[l][cur]
                    Sn_ = self.SH[l][1 - cur]
                    c.op("pe", lambda e, cc=cc, ch=ch, Sc=Sc: e.matmul(po[:, 32 * cc:32 * cc + 32], Sc[:, h, :], A[5][:, 32 * ch:32 * ch + 32],
                                                                     start=False, stop=(cc == 3)),
                         reads=[Sc, A[5]], writes=[po], inc=True)
                    pd = self.PS[cc]
                    c.op("dve", lambda e, ch=ch, cc=cc, pd=pd, Sc=Sc, Sn_=Sn_: e.scalar_tensor_tensor(
                        out=Sn_[:, h, :], in0=Sc[:, h, :], scalar=A[2][:, 32 * ch + 31:32 * ch + 32], in1=pd[:, 0:128],
                        op0=ALU.mult, op1=ALU.add), reads=[Sc, A[2], pd], writes=[Sn_])
                    cur = 1 - cur
                c.op("act", lambda e, bs=bs: e.activation(out=OD[:, bs], in_=po[:, 0:128], func=AF.Copy), reads=[po], writes=[OD])
            c.op("dve", lambda e, cur=cur: e.tensor_copy(out=self.SH[l][1 - cur][:, h, :], in_=self.SH[l][cur][:, h, :]),
                 reads=[self.SH[l][cur]], writes=[self.SH[l][1 - cur]])
            if last:
                c.dma("sp", self.o_hg_p[l, h], self.SH[l][cur][:, h, :], reads=[self.SH[l][cur]], is_output=True)
                for b in range(NS):
                    col = SEG + b
                    if h == 0 or True:
                        c.dma("sp", S0[:, h, :], self.hs0[:, l, b, h, :], writes=[S0])
                    pv = self.PS[6]
                    c.op("pe", lambda e, b=b: e.matmul(pv[:, 256:384], self.idf[0:NS, b:b + 1].to_broadcast([NS, 128]), V8d[0:NS, hc],
                                                       start=True, stop=True), reads=[self.CF, V8d], writes=[pv])
                    c.op("dve", lambda e, b=b: e.tensor_scalar(out=T1[:, :], in0=S0[:, h, :], scalar1=FS[:, b:b + 1], scalar2=None, op0=ALU.mult),
                         reads=[S0, FS], writes=[T1])
                    c.op("dve", lambda e, col=col: e.scalar_tensor_tensor(out=SN[:, :], in0=pv[:, 256:384], scalar=A[3][:, col:col + 1],
                                                                          in1=T1[:, :], op0=ALU.mult, op1=ALU.add),
                         reads=[pv, A[3], T1], writes=[SN])
                    c.dma("sp", self.o_hg_s[l, b, h], SN[:, :], reads=[SN], is_output=True)
                    c.op("pe", lambda e, col=col: e.matmul(pv[:, 384:385], SN[:, :], A[0][:, col:col + 1], start=True, stop=True),
                         reads=[SN, A[0]], writes=[pv])
                    c.op("act", lambda e, col=col: e.activation(out=OD[:, col:col + 1], in_=pv[:, 384:385], func=AF.Copy),
                         reads=[pv], writes=[OD])
            self.dump(f"od{seg}_{h}", OD, OD[:, 0:512], [128, 512])
            if last:
                self.dump(f"sod{h}", OD, OD[:, 512:520], [128, 8])
            for (c0, n) in tiles:
                cs = slice(c0, c0 + n)
                self.sumsq_rstd(OD, lambda k: OD[:, cs], 1, 128, c0, n, self.PS[7])
                tt = self.tmp[self.tmp_i]
                self.tmp_i ^= 1
                c.op("dve", lambda e, tt=tt, cs=cs, n=n: e.scalar_tensor_tensor(out=tt[:, 0:n], in0=OD[:, cs], scalar=self.PMX[:, l, 14 + h:15 + h],
                                                                        in1=self.rstd[:, cs], op0=ALU.mult, op1=ALU.mult),
                     reads=[OD, self.PMX, self.rstd], writes=[tt])
                c.op("dve", lambda e, tt=tt, cs=cs, n=n: e.tensor_tensor(out=self.MIX[:, 14 + h, cs], in0=tt[:, 0:n], in1=SGp[:, h % 2, cs], op=ALU.mult),
                     reads=[tt, SGp], writes=[self.MIX])
        self.sh_par[l] = cur
        c.barrier()
        self.ar_off[0] = m_start
    MAGIC = 12582912.0
    TWO_PI = 6.283185307179586

    def range_reduce(self, eng, out, src, kt, n=None, reads=(), outb=None, srcb=None, ktb=None):
        c = self.c
        c.op(eng, lambda e: e.tensor_scalar(out=kt, in0=src, scalar1=1.0 / self.TWO_PI, scalar2=self.MAGIC, op0=ALU.mult, op1=ALU.add),
             reads=[srcb], writes=[ktb])
        c.op(eng, lambda e: e.tensor_scalar(out=kt, in0=kt, scalar1=-self.MAGIC, scalar2=None, op0=ALU.add), reads=[ktb], writes=[ktb])
        c.op(eng, lambda e: e.scalar_tensor_tensor(out=out, in0=kt, scalar=-self.TWO_PI, in1=src, op0=ALU.mult, op1=ALU.add),
             reads=[ktb, srcb], writes=[outb])
        c.op(eng, lambda e: e.tensor_scalar(out=out, in0=out, scalar1=-3.14159, scalar2=3.14159, op0=ALU.max, op1=ALU.min),
             reads=[outb], writes=[outb])

    def lam_calc(self, are, aim, ldt, srcb, W, T, want_z):
        c = self.c
        dt, r, th, k, sn, cs, lre, lim = T[:8]
        w = slice(0, W)
        c.op("act", lambda e: e.activation(out=dt[:, w], in_=ldt, func=AF.Exp), reads=[srcb], writes=[dt])
        c.op("dve", lambda e: e.tensor_tensor(out=r[:, w], in0=are, in1=dt[:, w], op=ALU.mult), reads=[srcb, dt], writes=[r])
        c.op("dve", lambda e: e.tensor_tensor(out=th[:, w], in0=aim, in1=dt[:, w], op=ALU.mult), reads=[srcb, dt], writes=[th])
        c.op("act", lambda e: e.activation(out=r[:, w], in_=r[:, w], func=AF.Exp), reads=[r], writes=[r])
        self.range_reduce("dve", th[:, w], th[:, w], k[:, w], outb=th, srcb=th, ktb=k)
        c.op("dve", lambda e: e.tensor_scalar(out=dt[:, w], in0=th[:, w], scalar1=3.141592653589793 / 2, scalar2=None, op0=ALU.add),
             reads=[th], writes=[dt])
        self.range_reduce("dve", dt[:, w], dt[:, w], k[:, w], outb=dt, srcb=dt, ktb=k)
        c.op("act", lambda e: e.activation(out=sn[:, w], in_=th[:, w], func=AF.Sin), reads=[th], writes=[sn])
        c.op("act", lambda e: e.activation(out=cs[:, w], in_=dt[:, w], func=AF.Sin), reads=[dt], writes=[cs])
        c.op("dve", lambda e: e.tensor_tensor(out=lre[:, w], in0=r[:, w], in1=cs[:, w], op=ALU.mult), reads=[r, cs], writes=[lre])
        c.op("dve", lambda e: e.tensor_tensor(out=lim[:, w], in0=r[:, w], in1=sn[:, w], op=ALU.mult), reads=[r, sn], writes=[lim])
        res = dict(r=r, th=th, lre=lre, lim=lim)
        if want_z:
            den, l1, zr, zi = dt, k, sn, cs
            c.op("dve", lambda e: e.tensor_tensor(out=den[:, w], in0=are, in1=are, op=ALU.mult), reads=[srcb], writes=[den])
            c.op("dve", lambda e: e.tensor_tensor(out=l1[:, w], in0=aim, in1=aim, op=ALU.mult), reads=[srcb], writes=[l1])
            c.op("dve", lambda e: e.tensor_tensor(out=den[:, w], in0=den[:, w], in1=l1[:, w], op=ALU.add), reads=[den, l1], writes=[den])
            c.op("dve", lambda e: e.reciprocal(out=den[:, w], in_=den[:, w]), reads=[den], writes=[den])
            c.op("dve", lambda e: e.tensor_scalar(out=l1[:, w], in0=lre[:, w], scalar1=-1.0, scalar2=None, op0=ALU.add), reads=[lre], writes=[l1])
            t1, t2 = T[8], T[9]
            c.op("dve", lambda e: e.tensor_tensor(out=t1[:, w], in0=l1[:, w], in1=are, op=ALU.mult), reads=[l1, srcb], writes=[t1])
            c.op("dve", lambda e: e.tensor_tensor(out=t2[:, w], in0=lim[:, w], in1=aim, op=ALU.mult), reads=[lim, srcb], writes=[t2])
            c.op("dve", lambda e: e.tensor_tensor(out=t1[:, w], in0=t1[:, w], in1=t2[:, w], op=ALU.add), reads=[t1, t2], writes=[t1])
            c.op("dve", lambda e: e.tensor_tensor(out=zr[:, w], in0=t1[:, w], in1=den[:, w], op=ALU.mult), reads=[t1, den], writes=[zr])
            c.op("dve", lambda e: e.tensor_tensor(out=t1[:, w], in0=lim[:, w], in1=are, op=ALU.mult), reads=[lim, srcb], writes=[t1])
            c.op("dve", lambda e: e.tensor_tensor(out=t2[:, w], in0=l1[:, w], in1=aim, op=ALU.mult), reads=[l1, srcb], writes=[t2])
            c.op("dve", lambda e: e.tensor_tensor(out=t1[:, w], in0=t1[:, w], in1=t2[:, w], op=ALU.subtract), reads=[t1, t2], writes=[t1])
            c.op("dve", lambda e: e.tensor_tensor(out=zi[:, w], in0=t1[:, w], in1=den[:, w], op=ALU.mult), reads=[t1, den], writes=[zi])
            res.update(zr=zr, zi=zi)
        return res

    def mix_A(self, seg, l, NC):
        c = self.c
        last = (seg == self.nseg - 1)
        NT = self.NT
        tiles = self.tiles(seg)
        m_start = self.ar_off[0]
        UT = self.ta([128, 4, NT], BF16, "UT")
        ZB = self.ta([128, 4, NT], BF16, "ZB")
        BR = [self.ta([128, 4, 64], F32, f"BR{i}") for i in range(2)]
        CRW = self.ta([128, 448], F32, "CRW")
        LP = [self.ta([128, 14], F32, f"LP{i}") for i in range(10)]
        NLI = self.ta([128, 14], F32, "NLI")
        X0S = self.ta([128, NS, 14, 2], F32, "X0S") if last else None
        XSN = self.ta([128, NS, 14, 2], F32, "XSN") if last else None
        for j in range(2):
            def ucons(c0, n, res, j=j):
                for s_, (pb, pap) in enumerate(res):
                    c.op("act", lambda e, s_=s_, pap=pap: e.activation(out=UT[:, 2 * j + s_, c0:c0 + n], in_=pap, func=AF.Copy),
                         reads=[pb], writes=[UT])
            self.proj_pair(l, j, tiles, ucons)
        m_setup = self.ar_off[0]
        R = self.ta([128, 1728], F32, "R")
        c.dma("sp", R[:, :], self.s5r[:, l, :], writes=[R])
        if last:
            c.dma("sp", X0S[:, :, :, :], self.x0s[:, l], writes=[X0S])
        c.op("dve", lambda e: e.tensor_copy(out=CRW[:, :], in_=R[:, 1280:1728]), reads=[R], writes=[CRW])
        TR = [self.ta([128, 256], F32, f"TR{i}") for i in range(10)]
        zz = self.lam_calc(R[:, 0:256], R[:, 256:512], R[:, 512:768], R, 256, TR, True)
        zr, zi = zz["zr"], zz["zi"]
        t1, t2 = TR[8], TR[9]
        bre, bim = R[:, 768:1024], R[:, 1024:1280]
        br0 = BR[0][:, :, :].rearrange("p a b -> p (a b)")
        br1 = BR[1][:, :, :].rearrange("p a b -> p (a b)")
        c.op("dve", lambda e: e.tensor_tensor(out=t1[:, :], in0=zr[:, :], in1=bre, op=ALU.mult), reads=[zr, R], writes=[t1])
        c.op("dve", lambda e: e.tensor_tensor(out=t2[:, :], in0=zi[:, :], in1=bim, op=ALU.mult), reads=[zi, R], writes=[t2])
        c.op("dve", lambda e: e.tensor_tensor(out=br0, in0=t1[:, :], in1=t2[:, :], op=ALU.subtract), reads=[t1, t2], writes=[BR[0]])
        c.op("dve", lambda e: e.tensor_tensor(out=t1[:, :], in0=zr[:, :], in1=bim, op=ALU.mult), reads=[zr, R], writes=[t1])
        c.op("dve", lambda e: e.tensor_tensor(out=t2[:, :], in0=zi[:, :], in1=bre, op=ALU.mult), reads=[zi, R], writes=[t2])
        c.op("dve", lambda e: e.tensor_tensor(out=br1, in0=t1[:, :], in1=t2[:, :], op=ALU.add), reads=[t1, t2], writes=[BR[1]])
        lp = self.lam_calc(self.PLA[:, l, 0:14], self.PLA[:, l, 14:28], self.PLA[:, l, 28:42], self.PLA, 14, LP, False)
        RR, TH, LRE, LIM = lp["r"], lp["th"], lp["lre"], lp["lim"]
        c.op("dve", lambda e: e.tensor_scalar(out=NLI[:, :], in0=LIM[:, 0:14], scalar1=-1.0, scalar2=None, op0=ALU.mult), reads=[LIM], writes=[NLI])
        c.barrier()
        self.ar_off[0] = m_setup
        CS = self.ta([128, SEG], F32, "CS")
        SNT = self.ta([128, SEG], F32, "SNT")
        G = [self.ta([128, SEG], F32, f"G{i}") for i in range(4)]
        XR = self.ta([128, NT], BF16, "XR")
        XI = self.ta([128, NT], BF16, "XI")
        BBD = [[self.ta([128, 128], BF16, f"BBD{q}{i}") for i in range(2)] for q in range(4)]
        CBD = [[self.ta([128, 128], BF16, f"CBD{q}{i}") for i in range(2)] for q in range(4)]
        DD = self.ta([128, 128], BF16, "DD")
        SM = self.ta([128, 16], F32, "SM")
        IOTA = self.CF[:, self.CF_IOTA:self.CF_IOTA + SEG]
        MB_ = self.CB[:, self.CB_MB:self.CB_MB + 512].rearrange("p (q g s) -> p q g s", q=4, g=2)
        MC_ = self.CB[:, self.CB_MC:self.CB_MC + 512].rearrange("p (q g j) -> p q g j", q=4, g=8)
        X0 = self.X0[l]
        P_ = slice(0, SEG)
        for ci in range(4):
            npair = 4 if ci < 3 else 2
            for q in range(npair):
                p = 4 * ci + q
                for i in range(2):
                    c.op("dve", lambda e, q=q, i=i: e.tensor_tensor(
                        out=BBD[q][i][:, :].rearrange("p (g s) -> p g s", g=2), in0=MB_[:, q, :, :],
                        in1=BR[i][:, ci, :].unsqueeze(1).to_broadcast([128, 2, 64]), op=ALU.mult),
                        reads=[self.CB, BR[i]], writes=[BBD[q][i]])
                crp = CRW[:, 16 * p:16 * p + 16].unsqueeze(1).to_broadcast([128, 8, 16])
                cip = CRW[:, 224 + 16 * p:224 + 16 * p + 16].unsqueeze(1).to_broadcast([128, 8, 16])
                c.op("dve", lambda e, q=q, crp=crp: e.tensor_tensor(out=CBD[q][0][:, :].rearrange("p (g j) -> p g j", g=8),
                                                                    in0=MC_[:, q, :, :], in1=crp, op=ALU.mult),
                     reads=[self.CB, CRW], writes=[CBD[q][0]])
                c.op("dve", lambda e, q=q, cip=cip: e.scalar_tensor_tensor(out=CBD[q][1][:, :].rearrange("p (g j) -> p g j", g=8),
                                                                         in0=cip, scalar=-1.0, in1=MC_[:, q, :, :], op0=ALU.mult, op1=ALU.mult),
                     reads=[self.CB, CRW], writes=[CBD[q][1]])
            c.op("dve", lambda e: e.tensor_scalar(out=DD[:, :], in0=self.idb, scalar1=self.PLA[:, l, 42 + ci:43 + ci], scalar2=None, op0=ALU.mult),
                 reads=[self.CB, self.PLA], writes=[DD])
            yps = [(self.PS[4], self.PS[4][:, 0:SEG])] + ([(self.PS[7], self.PS[7][:, 0:NS])] if last else [])
            for ti, (c0, n) in enumerate(tiles):
                yb_, yap = yps[ti]
                c.op("pe", lambda e, yap=yap, c0=c0, n=n: e.matmul(yap, DD[:, :], UT[:, ci, c0:c0 + n], start=True, stop=False),
                     reads=[DD, UT], writes=[yb_])
            for q in range(npair):
                p = 4 * ci + q
                bps = []
                for ti, (c0, n) in enumerate(tiles):
                    for i in range(2):
                        pb = self.PS[i] if ti == 0 else self.PS[6]
                        pap = pb[:, 0:n] if ti == 0 else pb[:, 16 * i:16 * i + n]
                        c.op("pe", lambda e, pap=pap, i=i, q=q, c0=c0, n=n: e.matmul(pap, BBD[q][i][:, :], UT[:, ci, c0:c0 + n], start=True, stop=True),
                             reads=[BBD[q][i], UT], writes=[pb])
                        bps.append((pb, pap))
                (bre_b, bre_p), (bim_b, bim_p) = bps[0], bps[1]
                thp = TH[:, p:p + 1]
                c.op("dve", lambda e: e.tensor_scalar(out=G[0][:, :], in0=IOTA, scalar1=thp, scalar2=None, op0=ALU.mult), reads=[self.CF, TH], writes=[G[0]])
                self.range_reduce("dve", G[2][:, :], G[0][:, :], G[1][:, :], outb=G[2], srcb=G[0], ktb=G[1])
                c.op("act", lambda e: e.activation(out=SNT[:, :], in_=G[2][:, :], func=AF.Sin), reads=[G[2]], writes=[SNT])
                c.op("dve", lambda e: e.tensor_scalar(out=G[2][:, :], in0=G[2][:, :], scalar1=3.141592653589793 / 2, scalar2=None, op0=ALU.add),
                     reads=[G[2]], writes=[G[2]])
                self.range_reduce("dve", G[3][:, :], G[2][:, :], G[1][:, :], outb=G[3], srcb=G[2], ktb=G[1])
                c.op("act", lambda e: e.activation(out=CS[:, :], in_=G[3][:, :], func=AF.Sin), reads=[G[3]], writes=[CS])
                c.op("dve", lambda e: e.tensor_tensor(out=G[0][:, :], in0=bre_p, in1=CS[:, :], op=ALU.mult), reads=[bre_b, CS], writes=[G[0]])
                c.op("dve", lambda e: e.tensor_tensor(out=G[1][:, :], in0=bim_p, in1=SNT[:, :], op=ALU.mult), reads=[bim_b, SNT], writes=[G[1]])
                c.op("dve", lambda e: e.tensor_tensor(out=G[2][:, :], in0=G[0][:, :], in1=G[1][:, :], op=ALU.add), reads=[G[0], G[1]], writes=[G[2]])
                c.op("dve", lambda e: e.tensor_tensor(out=G[0][:, :], in0=bim_p, in1=CS[:, :], op=ALU.mult), reads=[bim_b, CS], writes=[G[0]])
                c.op("dve", lambda e: e.tensor_tensor(out=G[1][:, :], in0=bre_p, in1=SNT[:, :], op=ALU.mult), reads=[bre_b, SNT], writes=[G[1]])
                c.op("dve", lambda e: e.tensor_tensor(out=G[3][:, :], in0=G[0][:, :], in1=G[1][:, :], op=ALU.subtract), reads=[G[0], G[1]], writes=[G[3]])
                rb = RR[:, p:p + 1].to_broadcast([128, SEG])
                c.op("dve", lambda e: e.tensor_tensor_scan(out=G[0][:, :], data0=rb, data1=G[2][:, :], initial=X0[:, p, 0:1], op0=ALU.mult, op1=ALU.add),
                     reads=[RR, G[2], X0], writes=[G[0]])
                c.op("dve", lambda e: e.tensor_tensor_scan(out=G[1][:, :], data0=rb, data1=G[3][:, :], initial=X0[:, p, 1:2], op0=ALU.mult, op1=ALU.add),
                     reads=[RR, G[3], X0], writes=[G[1]])
                c.op("dve", lambda e: e.tensor_tensor(out=G[2][:, :], in0=G[0][:, :], in1=CS[:, :], op=ALU.mult), reads=[G[0], CS], writes=[G[2]])
                c.op("dve", lambda e: e.tensor_tensor(out=G[3][:, :], in0=G[1][:, :], in1=SNT[:, :], op=ALU.mult), reads=[G[1], SNT], writes=[G[3]])
                c.op("dve", lambda e: e.tensor_tensor(out=XR[:, P_], in0=G[2][:, :], in1=G[3][:, :], op=ALU.subtract), reads=[G[2], G[3]], writes=[XR])
                c.op("dve", lambda e: e.tensor_tensor(out=X0[:, p, 0:1], in0=G[2][:, SEG - 1:SEG], in1=G[3][:, SEG - 1:SEG], op=ALU.subtract),
                     reads=[G[2], G[3]], writes=[X0])
                c.op("dve", lambda e: e.tensor_tensor(out=G[2][:, :], in0=G[0][:, :], in1=SNT[:, :], op=ALU.mult), reads=[G[0], SNT], writes=[G[2]])
                c.op("dve", lambda e: e.tensor_tensor(out=G[3][:, :], in0=G[1][:, :], in1=CS[:, :], op=ALU.mult), reads=[G[1], CS], writes=[G[3]])
                c.op("dve", lambda e: e.tensor_tensor(out=XI[:, P_], in0=G[2][:, :], in1=G[3][:, :], op=ALU.add), reads=[G[2], G[3]], writes=[XI])
                c.op("dve", lambda e: e.tensor_tensor(out=X0[:, p, 1:2], in0=G[2][:, SEG - 1:SEG], in1=G[3][:, SEG - 1:SEG], op=ALU.add),
                     reads=[G[2], G[3]], writes=[X0])
                if last:
                    (sre_b, sre_p), (sim_b, sim_p) = bps[2], bps[3]
                    a_ = SM[:, 0:NS]
                    c.op("dve", lambda e: e.tensor_scalar(out=a_, in0=X0S[:, :, p, 0], scalar1=LRE[:, p:p + 1], scalar2=None, op0=ALU.mult),
                         reads=[X0S, LRE], writes=[SM])
                    c.op("dve", lambda e: e.scalar_tensor_tensor(out=a_, in0=X0S[:, :, p, 1], scalar=NLI[:, p:p + 1], in1=a_, op0=ALU.mult, op1=ALU.add),
                         reads=[X0S, NLI, SM], writes=[SM])
                    c.op("dve", lambda e: e.tensor_tensor(out=XSN[:, :, p, 0], in0=a_, in1=sre_p, op=ALU.add), reads=[SM, sre_b], writes=[XSN])
                    b_ = SM[:, 8:8 + NS]
                    c.op("dve", lambda e: e.tensor_scalar(out=b_, in0=X0S[:, :, p, 1], scalar1=LRE[:, p:p + 1], scalar2=None, op0=ALU.mult),
                         reads=[X0S, LRE], writes=[SM])
                    c.op("dve", lambda e: e.scalar_tensor_tensor(out=b_, in0=X0S[:, :, p, 0], scalar=LIM[:, p:p + 1], in1=b_, op0=ALU.mult, op1=ALU.add),
                         reads=[X0S, LIM, SM], writes=[SM])
                    c.op("dve", lambda e: e.tensor_tensor(out=XSN[:, :, p, 1], in0=b_, in1=sim_p, op=ALU.add), reads=[SM, sim_b], writes=[XSN])
                    c.op("act", lambda e: e.activation(out=XR[:, SEG:SEG + NS], in_=XSN[:, :, p, 0], func=AF.Copy), reads=[XSN], writes=[XR])
                    c.op("act", lambda e: e.activation(out=XI[:, SEG:SEG + NS], in_=XSN[:, :, p, 1], func=AF.Copy), reads=[XSN], writes=[XI])
                for ti, (c0, n) in enumerate(tiles):
                    yb_, yap = yps[ti]
                    lastmm = (q == npair - 1)
                    c.op("pe", lambda e, yap=yap, q=q, c0=c0, n=n: e.matmul(yap, CBD[q][0][:, :], XR[:, c0:c0 + n], start=False, stop=False),
                         reads=[CBD[q][0], XR], writes=[yb_])
                    c.op("pe", lambda e, yap=yap, q=q, c0=c0, n=n, lastmm=lastmm: e.matmul(yap, CBD[q][1][:, :], XI[:, c0:c0 + n], start=False, stop=lastmm),
                         reads=[CBD[q][1], XI], writes=[yb_])
            for ti, (c0, n) in enumerate(tiles):
                yb_, yap = yps[ti]
                if "ya" in self.dbg and ci == 1 and ti == 0:
                    self.dump("ya", yb_, yap, [128, 512])
                c.op("act", lambda e, yap=yap, c0=c0, n=n: e.activation(out=ZB[:, ci, c0:c0 + n], in_=yap, func=AF.Gelu_apprx_tanh),
                     reads=[yb_], writes=[ZB])
        if last:
            c.dma("sp", self.o_ssm_p[:, l], X0[:, :, :], reads=[X0], is_output=True)
            c.dma("sp", self.o_ssm_s[:, l], XSN[:, :, :, :], reads=[XSN], is_output=True)
        W = self.next_wa()
        Wg = W[:, :, :, :].rearrange("p a k c -> p (a k c)")[:, 0:2048].rearrange("p (m k c) -> p m k c", m=4, k=4)
        c.dma("pool", Wg, self.wglu[l].rearrange("m p k c -> p m k c"), writes=[W])
        ZS = self.ta([128, 4, NS], F32, "ZS") if last else None
        for mo in range(4):
            for ti, (c0, n) in enumerate(tiles):
                pb = self.PS[mo % 2] if ti == 0 else self.PS[6]
                pap = pb[:, 0:n]
                for k in range(4):
                    c.op("pe", lambda e, k=k, pap=pap, c0=c0, n=n: e.matmul(pap, Wg[:, mo, k, :], ZB[:, k, c0:c0 + n], start=(k == 0), stop=(k == 3)),
                         reads=[W, ZB], writes=[pb], inc=(k == 3))
                gt = self.tmp[self.tmp_i]
                self.tmp_i ^= 1
                c.op("act", lambda e, gt=gt, pap=pap, n=n: e.activation(out=gt[:, 0:n], in_=pap, func=AF.Sigmoid, bias=self.PMX[:, l, 18 + mo:19 + mo]),
                     reads=[pb, self.PMX], writes=[gt])
                dst_b, dst = (G[mo], G[mo][:, 0:n]) if ti == 0 else (ZS, ZS[:, mo, 0:n])
                c.op("dve", lambda e, gt=gt, dst=dst, c0=c0, n=n: e.tensor_tensor(out=dst, in0=ZB[:, mo, c0:c0 + n], in1=gt[:, 0:n], op=ALU.mult),
                     reads=[ZB, gt], writes=[dst_b])

        class _Multi:
            pass
        srcs = list(G) + ([ZS] if last else [])
        self.mix_norm_multi(seg, srcs, lambda k, c0, n: (G[k][:, c0:c0 + n] if c0 < SEG else ZS[:, k, 0:n]), 4, 448, l, 0, 0)
        c.barrier()
        self.ar_off[0] = m_start

    def mix_norm_multi(self, seg, src_bufs, src_fn, nchunk, n_feat, l, gofs, mbase):
        c = self.c
        for (c0, n) in self.tiles(seg):
            ps = self.PS[7]
            for k in range(nchunk):
                sq = self.sq[self.sq_i]
                self.sq_i ^= 1
                c.op("act", lambda e, k=k, sq=sq: e.activation(out=sq[:, 0:n], in_=src_fn(k, c0, n), func=AF.Square),
                     reads=src_bufs, writes=[sq])
                c.op("pe", lambda e, k=k, sq=sq: e.matmul(ps[:, 0:n], self.ones_bf[:, :], sq[:, 0:n], start=(k == 0), stop=(k == nchunk - 1)),
                     reads=[sq, self.ones_bf], writes=[ps])
            c.op("act", lambda e: e.activation(out=self.rstd[:, c0:c0 + n], in_=ps[:, 0:n], func=AF.Sqrt, bias=self.eps_ap(), scale=1.0 / n_feat),
                 reads=[ps, self.epsb], writes=[self.rstd])
            c.op("dve", lambda e: e.reciprocal(out=self.rstd[:, c0:c0 + n], in_=self.rstd[:, c0:c0 + n]), reads=[self.rstd], writes=[self.rstd])
            for k in range(nchunk):
                c.op("dve", lambda e, k=k: e.scalar_tensor_tensor(
                    out=self.MIX[:, mbase + k, c0:c0 + n], in0=src_fn(k, c0, n), scalar=self.PMX[:, l, gofs + k:gofs + k + 1],
                    in1=self.rstd[:, c0:c0 + n], op0=ALU.mult, op1=ALU.mult), reads=src_bufs + [self.PMX, self.rstd], writes=[self.MIX])
    def mix_out(self, seg, l, NC):
        c = self.c
        tiles = self.tiles(seg)
        yb, y = self.Y
        c.barrier()
        for k in range(MIXK):
            self.dump(f"mix{k}_{seg}", self.MIX, self.MIX[:, k, 0:512], [128, 512])
        for mo in range(KT):
            W = self.next_wa()
            Wv = W[:, :, :, :].rearrange("p a k c -> p (a k c)")[:, 0:MIXK * 128].rearrange("p (k c) -> p k c", k=MIXK)
            c.dma("pool", Wv, self.wout[l, mo], writes=[W])
            for ti, (c0, n) in enumerate(tiles):
                pd = self.PS[4 + (mo % 2)] if ti == 0 else self.PS[6]
                od = 0 if ti == 0 else 32
                for k in range(MIXK):
                    c.op("pe", lambda e, k=k, pd=pd, od=od: e.matmul(pd[:, od:od + n], Wv[:, k, :], self.MIX[:, k, c0:c0 + n],
                                                                  start=(k == 0), stop=(k == MIXK - 1)),
                         reads=[W, self.MIX], writes=[pd], inc=(k == MIXK - 1))
                c.op("act", lambda e, pd=pd, od=od, mo=mo: e.activation(out=y[:, mo, c0:c0 + n], in_=pd[:, od:od + n], func=AF.Copy),
                     reads=[pd], writes=[yb])
        self.dump("ymix", yb, y[:, 3, 0:512], [128, 512])
        self.post_norm_add(seg, (l * 6 + 3) * KT)


U_OFF, QB_OFF, KB_OFF, VB_OFF, QC_OFF, KC_OFF, VC_OFF, QD_OFF, FD_OFF, ID_OFF, GD_OFF = (
    0, 448, 960, 1088, 1216, 1792, 1984, 2176, 2688, 3200, 3712)
NPAIR = 22
MIXK = 18


def _prep_ffn_weights(wg, wu, wd):
    nl = wg.shape[0]
    g = wg.reshape(nl, KT, 128, FT, 128).transpose(0, 3, 2, 1, 4)
    u = wu.reshape(nl, KT, 128, FT, 128).transpose(0, 3, 2, 1, 4)
    gu = np.ascontiguousarray(np.stack([g, u], axis=3))
    d = np.ascontiguousarray(wd.reshape(nl, FT, 128, KT, 128).transpose(0, 3, 2, 1, 4))
    return gu, d


def _gain_layout(vs):
    nl = vs[0].shape[0]
    a = np.stack(vs, axis=1)
    a = a.reshape(nl, 6, KT, 128).transpose(3, 0, 1, 2).reshape(128, nl * 6 * KT)
    return np.ascontiguousarray(a)


def _head(off, h):
    return list(range(off + 64 * h, off + 64 * h + 64))


def _swap(cols):
    return cols[32:] + cols[:32]


def _win_chunks():
    Z = [-1] * 64
    ch = []
    for j in range(4):
        ch.append([cc if cc < 448 else -1 for cc in range(128 * j, 128 * j + 128)])
    for j in range(4):
        ch.append(_head(QB_OFF, j) + _head(QB_OFF, j + 4))
        ch.append(_swap(_head(QB_OFF, j)) + _swap(_head(QB_OFF, j + 4)))
    ch.append(_head(KB_OFF, 0) + _head(KB_OFF, 1))
    ch.append(_swap(_head(KB_OFF, 0)) + _swap(_head(KB_OFF, 1)))
    for g in range(3):
        a, b, cc = _head(QC_OFF, 3 * g), _head(QC_OFF, 3 * g + 1), _head(QC_OFF, 3 * g + 2)
        ch.append(a + b)
        ch.append(_swap(a) + _swap(b))
        ch.append(cc + Z)
        ch.append(_swap(cc) + Z)
    for g in range(3):
        k = _head(KC_OFF, g)
        ch.append(k + k)
        ch.append(_swap(k) + _swap(k))
    for h in range(4):
        ch.append(list(range(QD_OFF + 128 * h, QD_OFF + 128 * h + 128)))
        ch.append(list(range(FD_OFF + 128 * h, FD_OFF + 128 * h + 128)))
    for h in range(4):
        ch.append(list(range(GD_OFF + 128 * h, GD_OFF + 128 * h + 128)))
    assert len(ch) == 2 * NPAIR
    return ch


def _gather_cols(w, cols):
    cols = np.asarray(cols)
    out = w[:, :, np.maximum(cols, 0)]
    out[:, :, cols < 0] = 0.0
    return out


def _mix_rows():
    Z = [-1] * 64
    rows = []
    for j in range(4):
        rows += [r if r < 448 else -1 for r in range(128 * j, 128 * j + 128)]
    ob = 448
    for j in range(4):
        rows += _head(ob, j) + _head(ob, j + 4)
    oc = 448 + 512
    for g in range(3):
        rows += _head(oc, 3 * g) + _head(oc, 3 * g + 1)
        rows += _head(oc, 3 * g + 2) + Z
    od = 448 + 512 + 576
    rows += list(range(od, od + 512))
    assert len(rows) == MIXK * 128
    return np.asarray(rows)


PAST_LEN = 16384
ROPE_THETA = 10000.0


def _consts(nseg):
    ntok = nseg * SEG + NS
    cb = np.zeros((128, Prog.NCB), np.float32)
    qi = np.arange(128)[:, None]
    kj = np.arange(256)[None, :]
    dist = 128 + qi - kj
    cb[:, 0:256] = np.where((dist >= 0) & (dist <= 128), 0.0, NEG)
    cb[:, 256] = 0.0
    cb[:, 257] = NEG
    q = np.arange(128)[:, None]
    col = np.arange(512)[None, :]
    rrq, iq = q // 32, q % 32
    jt, rr = 32 * (col // 128) + (col % 32), (col % 128) // 32
    cb[:, Prog.CB_M2:Prog.CB_M2 + 512] = np.where((rr == rrq) & (jt <= 96 + iq), 0.0, NEG)
    s_ = np.arange(128)[:, None]
    t_ = np.arange(128)[None, :]
    cb[:, Prog.CB_BD:Prog.CB_BD + 128] = ((s_ // 32 == t_ // 32) & (s_ <= t_)).astype(np.float32)
    cb[:, Prog.CB_ID:Prog.CB_ID + 128] = np.eye(128, dtype=np.float32)
    row = np.arange(128)[:, None, None]
    qq = np.arange(4)[None, :, None]
    cc = np.arange(128)[None, None, :]
    cb[:, Prog.CB_MB:Prog.CB_MB + 512] = ((row // 16) == 2 * qq + (cc // 64)).astype(np.float32).reshape(128, 512)
    cb[:, Prog.CB_MC:Prog.CB_MC + 512] = ((cc // 16) == 2 * qq + (row // 64)).astype(np.float32).reshape(128, 512)
    cf = np.zeros((128, Prog.NCF), np.float32)
    cf[:, 0:128] = np.eye(128, dtype=np.float32)
    cf[:, Prog.CF_IOTA:Prog.CF_IOTA + 512] = np.arange(1, 513, dtype=np.float32)[None, :]
    cf[:, Prog.CF_RST:Prog.CF_RST + 512] = (np.arange(512) % 32 != 0).astype(np.float32)[None, :]
    pos = np.concatenate([np.arange(nseg * SEG), np.full(NS, PAST_LEN)]).astype(np.float32)
    inv_freq = (np.float32(ROPE_THETA) ** (-(np.arange(32, dtype=np.float32) / np.float32(32)))).astype(np.float32)
    ang = (pos[:, None] * inv_freq[None, :]).astype(np.float32)
    cos = np.cos(ang).astype(np.float32).T
    sin = np.sin(ang).astype(np.float32).T
    rot = np.zeros((128, 2, ntok), np.float32)
    for p in range(128):
        rot[p, 0] = cos[p % 32]
        rot[p, 1] = sin[p % 32] * (-1.0 if (p % 64) < 32 else 1.0)
    return cb, cf, rot


def _prep_shared(inp, nl):
    sh = {}
    for i, nm in ((1, "ffn1"), (2, "ffn2")):
        gu, d = _prep_ffn_weights(inp[nm + "_w_gate"][:nl], inp[nm + "_w_up"][:nl], inp[nm + "_w_down"][:nl])
        sh[f"wgu{i}"], sh[f"wdn{i}"] = gu, d
    sh["gains"] = _gain_layout([inp[k][:nl] for k in ("ffn1_norm_pre", "ffn1_norm_post", "mix_norm_pre", "mix_norm_post",
                                                     "ffn2_norm_pre", "ffn2_norm_post")])
    w_in = inp["w_in"][:nl]
    ch = _win_chunks()
    wf = np.stack([_gather_cols(w_in, cc) for cc in ch], axis=1)
    wf = wf.reshape(nl, NPAIR, 2, KT, 128, 128).transpose(0, 1, 4, 2, 3, 5)
    sh["win_f"] = np.ascontiguousarray(wf)
    Z = [-1] * 64
    tg = [list(range(ID_OFF, ID_OFF + 256)), list(range(ID_OFF + 256, ID_OFF + 512)),
          list(range(VB_OFF, VB_OFF + 128)) + list(range(VC_OFF, VC_OFF + 128)),
          list(range(VC_OFF + 128, VC_OFF + 192)) + Z * 3]
    wt = np.stack([_gather_cols(w_in, cc) for cc in tg], axis=1)
    sh["wt"] = np.ascontiguousarray(wt.reshape(nl, 4, KT, 128, 256).transpose(0, 1, 3, 2, 4))
    rows = _mix_rows()
    wo = inp["w_out"][:nl][:, np.maximum(rows, 0), :].copy()
    wo[:, rows < 0, :] = 0.0
    sh["wout"] = np.ascontiguousarray(wo.reshape(nl, MIXK, 128, KT, 128).transpose(0, 3, 2, 1, 4))
    wg = np.zeros((nl, 512, 512), np.float32)
    wg[:, :448, :448] = inp["ssm_w_glu"][:nl]
    sh["wglu"] = np.ascontiguousarray(wg.reshape(nl, 4, 128, 4, 128).transpose(0, 3, 2, 1, 4))
    a_re, a_im, ldt = inp["ssm_a_re"][:nl], inp["ssm_a_im"][:nl], inp["ssm_log_dt"][:nl]
    plA = np.zeros((128, nl, 46), np.float32)
    plA[:, :, 0:14] = a_re.reshape(nl, 14, 2, 64).transpose(2, 3, 0, 1).reshape(128, nl, 14)
    plA[:, :, 14:28] = a_im.reshape(nl, 14, 2, 64).transpose(2, 3, 0, 1).reshape(128, nl, 14)
    plA[:, :, 28:42] = np.broadcast_to(ldt.reshape(nl, 14, 2, 1), (nl, 14, 2, 64)).transpose(2, 3, 0, 1).reshape(128, nl, 14)
    dsk = np.zeros((nl, 32, 16), np.float32)
    dsk[:, :28] = inp["ssm_d"][:nl]
    plA[:, :, 42:46] = dsk.reshape(nl, 4, 8, 16).transpose(2, 3, 0, 1).reshape(128, nl, 4)
    sh["plA"] = plA
    pm = np.zeros((128, nl, 30), np.float32)
    gcat = np.concatenate([inp["out_norm_a"][:nl], inp["out_norm_b"][:nl], inp["out_norm_c"][:nl], inp["out_norm_d"][:nl]], axis=1)
    gm = gcat[:, np.maximum(rows, 0)].copy()
    gm[:, rows < 0] = 0.0
    pm[:, :, 0:18] = gm.reshape(nl, MIXK, 128).transpose(2, 0, 1)
    bg = np.zeros((nl, 512), np.float32)
    bg[:, :448] = inp["ssm_b_glu"][:nl]
    pm[:, :, 18:22] = bg.reshape(nl, 4, 128).transpose(2, 0, 1)
    pm[:, :, 22:30] = inp["swa_sinks"][:nl][None, :, :]
    sh["pmix"] = pm
    sh["hlb"] = np.ascontiguousarray(inp["hgrn_lower_bounds"][:nl].reshape(nl, 4, 128).transpose(2, 0, 1))
    s5 = np.zeros((128, nl, 1728), np.float32)

    def rowlay(a, fill=0.0):
        z = np.full((nl, 32, 64), fill, np.float32)
        z[:, :28] = a
        z = z.reshape(nl, 4, 8, 1, 64)
        z = np.broadcast_to(z, (nl, 4, 8, 16, 64))
        return z.transpose(2, 3, 0, 1, 4).reshape(128, nl, 256)
    s5[:, :, 0:256] = rowlay(a_re, -1.0)
    s5[:, :, 256:512] = rowlay(a_im, 1.0)
    s5[:, :, 512:768] = rowlay(np.broadcast_to(ldt[:, :, None], (nl, 28, 64)))

    def rowlay_b(b):
        z = np.zeros((nl, 32, 64, 16), np.float32)
        z[:, :28] = b
        return z.reshape(nl, 4, 8, 64, 16).transpose(2, 4, 0, 1, 3).reshape(128, nl, 256)
    s5[:, :, 768:1024] = rowlay_b(inp["ssm_b_re"][:nl])
    s5[:, :, 1024:1280] = rowlay_b(inp["ssm_b_im"][:nl])

    def crow(cm):
        return cm.reshape(nl, 14, 2, 16, 64).transpose(2, 4, 0, 1, 3).reshape(128, nl, 224)
    s5[:, :, 1280:1504] = crow(inp["ssm_c_re"][:nl])
    s5[:, :, 1504:1728] = crow(inp["ssm_c_im"][:nl])
    sh["s5r"] = s5
    return sh


def _prep_core(inp, nl, nseg, seq, sb):
    d = {}
    x = np.concatenate([inp["x_prompt"][seq, :nseg * SEG], inp["x_sample"][sb:sb + NS, 0]], axis=0)
    d["xT"] = np.ascontiguousarray(x.T)
    ss = inp["state_ssm"][:nl, sb:sb + NS]
    d["x0s"] = np.ascontiguousarray(ss.reshape(nl, NS, 14, 2, 64, 2).transpose(3, 4, 0, 1, 2, 5).reshape(128, nl, NS, 14, 2))
    d["hs0"] = np.ascontiguousarray(inp["state_hgrn"][:nl, sb:sb + NS].transpose(3, 0, 1, 2, 4))
    cs = inp["cache_swa_kv"][:nl, sb:sb + NS]
    kct = np.zeros((128, nl, NS, 4, 128), np.float32)
    vct = np.zeros((128, nl, NS, 320), np.float32)
    kct[:, :, :, 0, :] = cs[:, :, :, 0].reshape(nl, NS, 128, 128).transpose(3, 0, 1, 2)
    vct[:, :, :, 0:128] = cs[:, :, :, 1].reshape(nl, NS, 128, 128).transpose(2, 0, 1, 3)
    for g, (nm, st) in enumerate((("cache_dil0_kv", 1), ("cache_dil1_kv", 4), ("cache_dil2_kv", 16))):
        cg = inp[nm][:nl, sb:sb + NS, ::st]
        kk = cg[:, :, :, 0].transpose(3, 0, 1, 2)
        kct[0:64, :, :, 1 + g, :] = kk
        kct[64:128, :, :, 1 + g, :] = kk
        vct[:, :, :, 128 + 64 * g:192 + 64 * g] = cg[:, :, :, 1].transpose(2, 0, 1, 3)
    d["kct"], d["vct"] = kct, vct
    return d


_CACHE = {}


def _build(nseg, nl, **kw):
    key = (nseg, nl, tuple(sorted(kw.items())))
    if key not in _CACHE:
        p = Prog(nseg=nseg, nlayer=nl, **kw)
        p.build()
        _CACHE[key] = p
    return _CACHE[key]


def kernel(**inputs):
    inp = {k: np.asarray(v) for k, v in inputs.items()}
    nl, nseg = L, NSEG
    prog = _build(nseg, nl)
    cb, cf, rot = _consts(nseg)
    sh = _prep_shared(inp, nl)
    sh.update(cbf=cb, cf32=cf, rot=rot)
    in_maps = []
    for core in range(8):
        seq = core // 2
        d = dict(sh)
        d.update(_prep_core(inp, nl, nseg, seq, 8 * seq))
        in_maps.append(d)
    res = run_bass_kernel_spmd(prog.nc, in_maps, core_ids=list(range(8)))
    R = [res.results[2 * s] for s in range(4)]
    return _assemble(R, nl, nseg)


def _assemble(R, nl, nseg):
    nb = len(R)
    T = nseg * SEG
    f32 = np.float32
    y_p = np.stack([r["yT"][:, :T].T for r in R]).astype(f32)
    y_s = np.concatenate([r["yT"][:, T:T + NS].T for r in R])[:, None, :].astype(f32)
    ssm_p = np.stack([r["o_ssm_p"].reshape(2, 64, nl, 14, 2).transpose(2, 3, 0, 1, 4).reshape(nl, 28, 64, 2) for r in R], axis=1)
    ssm_s = np.concatenate([r["o_ssm_s"].reshape(2, 64, nl, NS, 14, 2).transpose(2, 3, 4, 0, 1, 5).reshape(nl, NS, 28, 64, 2)
                            for r in R], axis=1)
    swa_p = np.stack([np.stack([r["o_swa_k"][:, :, 0:128].transpose(0, 2, 1).reshape(nl, 128, 2, 64),
                                r["o_swa_v"].reshape(nl, 128, 2, 64)], axis=2) for r in R], axis=1)
    swa_s = np.concatenate([np.stack([r["o_swa_k"][:, :, 128:128 + NS].transpose(0, 2, 1).reshape(nl, NS, 2, 64),
                                      r["o_sv"][:, :, 0:128].reshape(nl, NS, 2, 64)], axis=2)[:, :, None] for r in R], axis=1)

    def dil(kname, vfun, npr, vcol):
        p = np.stack([np.stack([r[kname][:, :, 0:npr].transpose(0, 2, 1), vfun(r)], axis=2) for r in R], axis=1)
        s = np.concatenate([np.stack([r[kname][:, :, npr:npr + NS].transpose(0, 2, 1),
                                      r["o_sv"][:, :, vcol:vcol + 64]], axis=2)[:, :, None] for r in R], axis=1)
        return p.astype(f32), s.astype(f32)
    d0_p, d0_s = dil("o_d0_k", lambda r: r["o_d0_v"], 128, 128)
    d1_p, d1_s = dil("o_d1_k", lambda r: r["o_d1_v"].transpose(0, 2, 1, 3).reshape(nl, 512, 64), 512, 192)
    d2_p, d2_s = dil("o_d2_k", lambda r: r["o_d2_v"].reshape(nl, nseg, 4, 4, 32, 64).transpose(0, 1, 4, 2, 3, 5).reshape(nl, T, 64),
                     T, 256)
    hg_p = np.stack([r["o_hg_p"] for r in R], axis=1)
    hg_s = np.concatenate([r["o_hg_s"] for r in R], axis=1)
    outs = (y_p, y_s, ssm_p, ssm_s, swa_p, swa_s, d0_p, d0_s, d1_p, d1_s, d2_p, d2_s, hg_p, hg_s)
    return tuple(np.ascontiguousarray(o, dtype=f32) for o in outs)
```

```python
import numpy as np
from contextlib import ExitStack
import concourse.bass as bass
import concourse.mybir as mybir
from concourse.bass_utils import run_bass_kernel_spmd

F32 = mybir.dt.float32
BF16 = mybir.dt.bfloat16
ALU = mybir.AluOpType
AF = mybir.ActivationFunctionType
AX = mybir.AxisListType

D = 2048
KT = D // 128
DFF = 5504
FT = DFF // 128
FH = (22, 21)
L = 2
SEQ = 2048
SEG = 512
NSEG = SEQ // SEG
NS = 8
EPS = 1e-6
NEG = -1e30


class Buf:
    __slots__ = ("t", "name", "w", "r", "dsem", "dcnt", "psum")

    def __init__(self, t, name):
        self.t = t
        self.name = name
        self.psum = False
        self.w = []
        self.r = []
        self.dsem = None
        self.dcnt = 0

    def __getitem__(self, k):
        return self.t[k]


class EngS:
    def __init__(self, name, eng, sem):
        self.name = name
        self.eng = eng
        self.sem = sem
        self.cnt = 0
        self.waited = {}
        self.pend_r = []
        self.pend_w = []


class Ctx:
    def __init__(self, nc, stack):
        self.nc = nc
        self.stack = stack
        self.E = {}
        for name, eng in (("pe", nc.tensor), ("act", nc.scalar), ("dve", nc.vector),
                          ("pool", nc.gpsimd), ("sp", nc.sync)):
            sem = stack.enter_context(nc.semaphore("s_" + name))
            self.E[name] = EngS(name, eng, sem)
        self.nbuf = 0
        self.out_tokens = []
        self.dma_tokens = {}
        self.nins = 0

    def sbuf(self, shape, dt, name=None):
        self.nbuf += 1
        name = f"{name or 'b'}_{self.nbuf}"
        t = self.stack.enter_context(self.nc.sbuf_tensor(name, list(shape), dt))
        return Buf(t, name)

    def psum(self, shape, dt, name=None):
        self.nbuf += 1
        name = f"{name or 'p'}_{self.nbuf}"
        t = self.stack.enter_context(self.nc.psum_tensor(name, list(shape), dt))
        bb = Buf(t, name)
        bb.psum = True
        return bb

    def view(self, buf, name=None):
        self.nbuf += 1
        return Buf(buf.t, f"{name or 'v'}_{self.nbuf}")

    def _wait(self, es, tokens):
        best = {}
        for (sem, val) in tokens:
            k = id(sem)
            if k not in best or best[k][1] < val:
                best[k] = (sem, val)
        for k, (sem, val) in best.items():
            if es.name == "pe" and sem is es.sem:
                continue
            if es.waited.get(k, 0) >= val:
                continue
            es.eng.wait_ge(sem, val)
            es.waited[k] = val

    def _deps(self, reads, writes, en=None):
        toks = []
        for e in self.E.values():
            if e.name == en:
                continue
            for b in writes:
                if any(b is x for x in e.pend_r) or any(b is x for x in e.pend_w):
                    raise RuntimeError(f"hazard: {b.name} is written while engine {e.name} has un-signalled accesses to it")
            for b in reads:
                if any(b is x for x in e.pend_w):
                    raise RuntimeError(f"hazard: {b.name} is read while engine {e.name} has un-signalled writes to it")
        own = self.E[en].sem if en in self.E else None
        for b in reads:
            toks += b.w
            if b.psum:
                toks += [t for t in b.r if t[0] is not own]
        for b in writes:
            toks += b.w
            toks += b.r
        return toks

    def op(self, en, fn, reads=(), writes=(), inc=True):
        es = self.E[en]
        self._wait(es, self._deps(reads, writes, en))
        ins = fn(es.eng)
        self.nins += 1
        es.pend_r += list(reads)
        es.pend_w += list(writes)
        if inc:
            es.cnt += 1
            ins.then_inc(es.sem, 1)
            tok = (es.sem, es.cnt)
            for b in es.pend_r:
                b.r.append(tok)
                if len(b.r) > 12:
                    b.r = self._compact(b.r)
            for b in es.pend_w:
                b.w = [tok]
                b.r = []
            es.pend_r = []
            es.pend_w = []
        return ins

    @staticmethod
    def _compact(toks):
        best = {}
        for (sem, val) in toks:
            k = id(sem)
            if k not in best or best[k][1] < val:
                best[k] = (sem, val)
        return list(best.values())

    def dma(self, qn, out_ap, in_ap, reads=(), writes=(), is_output=False, owner=None):
        es = self.E[qn]
        self._wait(es, self._deps(reads, writes, qn))
        bufs = list(writes) + list(reads)
        owner = owner or bufs[0]
        if owner.dsem is None:
            owner.dsem = self.stack.enter_context(self.nc.semaphore("d_" + owner.name))
        owner.dcnt += 16
        tok = (owner.dsem, owner.dcnt)
        ins = es.eng.dma_start(out=out_ap, in_=in_ap)
        ins.then_inc(owner.dsem, 16)
        self.nins += 1
        for b in reads:
            b.r.append(tok)
        for b in writes:
            b.w = [tok]
            b.r = []
        self.dma_tokens[id(owner.dsem)] = tok
        if is_output:
            self.out_tokens.append(tok)
        return ins

    def barrier(self, engines=("pe", "act", "dve", "sp")):
        toks = [(e.sem, e.cnt) for e in self.E.values() if e.cnt > 0]
        toks += list(self.dma_tokens.values())
        for n in engines:
            self._wait(self.E[n], toks)

    def finish(self):
        es = self.E["sp"]
        toks = list(self.out_tokens) + list(self.dma_tokens.values())
        for n, e in self.E.items():
            if e.cnt > 0:
                toks.append((e.sem, e.cnt))
        self._wait(es, toks)


class Prog:
    def __init__(self, nseg=NSEG, nlayer=L, with_mix=True, dbg=(), parts=("mix",)):
        self.parts = set(parts)
        self.nseg = nseg
        self.nlayer = nlayer
        self.with_mix = with_mix
        self.dbg = set(dbg)
        self.ntok = nseg * SEG + NS
        self.nc = bass.Bass("TRN2", target_bir_lowering=False)
        self.ins = {}
        self.outs = {}

    def din(self, name, shape):
        t = self.nc.dram_tensor(name, list(shape), F32, kind="ExternalInput").ap()
        self.ins[name] = t
        return t

    def dout(self, name, shape):
        t = self.nc.dram_tensor(name, list(shape), F32, kind="ExternalOutput").ap()
        self.outs[name] = t
        return t

    def build(self):
        nc = self.nc
        nl = self.nlayer
        self.xT = self.din("xT", [D, self.ntok])
        self.yT = self.dout("yT", [D, self.ntok])
        self.wgu = [self.din(f"wgu{i}", [nl, FT, 128, 2, KT, 128]) for i in (1, 2)]
        self.wdn = [self.din(f"wdn{i}", [nl, KT, 128, FT, 128]) for i in (1, 2)]
        self.gains = self.din("gains", [128, nl * 6 * KT])
        if self.with_mix:
            self.declare_mix()
        with ExitStack() as st:
            self.c = c = Ctx(nc, st)
            self.st = st
            self.setup_consts()
            if self.with_mix:
                self.setup_mix_consts()
            for seg in range(self.nseg):
                self.run_segment(seg)
            c.finish()
        return nc

    def setup_consts(self):
        c = self.c
        nl = self.nlayer
        self.G = c.sbuf([128, nl * 6 * KT], F32, "gains")
        c.dma("sp", self.G[:], self.gains, writes=[self.G])
        for l in range(nl):
            for n in (1, 5):
                o = (l * 6 + n) * KT
                c.op("dve", lambda e, o=o: e.tensor_scalar(out=self.G[:, o:o + KT], in0=self.G[:, o:o + KT],
                                                           scalar1=0.5, scalar2=None, op0=ALU.mult),
                     reads=[self.G], writes=[self.G])
        self.ones_bf = c.sbuf([128, 128], BF16, "ones")
        c.op("dve", lambda e: e.memset(self.ones_bf[:], 1.0), writes=[self.ones_bf])
        self.NT = SEG + NS
        self.H = c.sbuf([128, KT, self.NT], F32, "H")
        xn = c.sbuf([128, KT, self.NT], BF16, "XN")
        self.XN = (xn, xn.t)
        self.YB = KT * self.NT * 4 + 1024
        self.AB = max(FH[0], 18) * self.NT * 2
        self.ARENA = c.sbuf([128, (self.YB + self.AB) // 4], F32, "ARENA")
        yb = Buf(self.ARENA.t, "Y")
        self.Y = (yb, self.ARENA.t[:, 0:KT * self.NT].rearrange("p (k t) -> p k t", k=KT))
        self.ACTB = Buf(self.ARENA.t[:, self.YB // 4:(self.YB + FH[0] * self.NT * 2) // 4].bitcast(BF16).rearrange(
            "p (k t) -> p k t", k=FH[0]), "ACT")
        self.WA = [c.sbuf([128, 2, KT, 128], BF16, f"WA{i}") for i in range(3)]
        self.WD = [c.sbuf([128, FH[0], 128], BF16, f"WD{i}") for i in range(2)]
        self.WDR = list(self.WD) + [Buf(w.t[:, :, :, :].rearrange("p a k c -> p (a k c)")[:, 0:FH[0] * 128].rearrange(
            "p (k c) -> p k c", k=FH[0]), f"WAd{i}") for i, w in enumerate(self.WA)]
        self.wa_i = 0
        self.wd_i = 0
        self.PS = [c.psum([128, 512], F32, f"PS{i}") for i in range(8)]
        self.tmp = [c.sbuf([128, 512], F32, f"tmp{i}") for i in range(2)]
        self.tmp_i = 0
        self.sq = [c.sbuf([128, 512], BF16, f"sq{i}") for i in range(2)]
        self.sq_i = 0
        self.rstd = c.sbuf([128, self.NT], F32, "rstd")

    def dump(self, name, buf, ap, shape, dt=F32):
        if name not in self.dbg:
            return
        c = self.c
        t = c.sbuf(shape, F32, "dbg_" + name)
        c.op("act", lambda e: e.activation(out=t[:], in_=ap, func=AF.Copy), reads=[buf], writes=[t])
        o = self.dout("dbg_" + name, shape)
        c.dma("sp", o, t[:], reads=[t], is_output=True)
        self.dbg.discard(name)

    def tiles(self, seg):
        t = [(0, SEG)]
        if seg == self.nseg - 1:
            t.append((SEG, NS))
        return t

    def run_segment(self, seg):
        c = self.c
        c.dma("sp", self.H[:, :, 0:SEG], self.xT[:, seg * SEG:(seg + 1) * SEG].rearrange("(k p) t -> p k t", p=128),
              writes=[self.H])
        if seg == self.nseg - 1:
            c.dma("sp", self.H[:, :, SEG:SEG + NS],
                  self.xT[:, self.nseg * SEG:self.nseg * SEG + NS].rearrange("(k p) t -> p k t", p=128),
                  writes=[self.H])
        for l in range(self.nlayer):
            self.ffn(seg, l, 0)
            if self.with_mix:
                self.mixer(seg, l)
            self.ffn(seg, l, 1)
        c.dma("sp", self.yT[:, seg * SEG:(seg + 1) * SEG].rearrange("(k p) t -> p k t", p=128), self.H[:, :, 0:SEG],
              reads=[self.H], is_output=True)
        if seg == self.nseg - 1:
            c.dma("sp", self.yT[:, self.nseg * SEG:self.nseg * SEG + NS].rearrange("(k p) t -> p k t", p=128),
                  self.H[:, :, SEG:SEG + NS], reads=[self.H], is_output=True)

    def sumsq_rstd(self, src_buf, src_ap_fn, nchunk, n_feat, c0, n, ps, eps=EPS):
        c = self.c
        for k in range(nchunk):
            sq = self.sq[self.sq_i]
            self.sq_i ^= 1
            c.op("act", lambda e, k=k, sq=sq: e.activation(out=sq[:, 0:n], in_=src_ap_fn(k), func=AF.Square),
                 reads=[src_buf], writes=[sq])
            c.op("pe", lambda e, k=k, sq=sq: e.matmul(ps[:, 0:n], self.ones_bf[:, :], sq[:, 0:n],
                                                      start=(k == 0), stop=(k == nchunk - 1)),
                 reads=[sq, self.ones_bf], writes=[ps], inc=True)
        c.op("act", lambda e: e.activation(out=self.rstd[:, c0:c0 + n], in_=ps[:, 0:n], func=AF.Sqrt,
                                           bias=self.eps_ap(), scale=1.0 / n_feat),
             reads=[ps, self.epsb], writes=[self.rstd])
        c.op("dve", lambda e: e.reciprocal(out=self.rstd[:, c0:c0 + n], in_=self.rstd[:, c0:c0 + n]),
             reads=[self.rstd], writes=[self.rstd])

    def eps_ap(self):
        if not hasattr(self, "epsb"):
            self.epsb = self.c.sbuf([128, 1], F32, "eps")
            self.c.op("dve", lambda e: e.memset(self.epsb[:], EPS), writes=[self.epsb])
        return self.epsb[:, 0:1]

    def pre_norm(self, seg, gofs):
        c = self.c
        self.eps_ap()
        xnb, xn = self.XN
        for (c0, n) in self.tiles(seg):
            ps = self.PS[7]
            self.sumsq_rstd(self.H, lambda k: self.H[:, k, c0:c0 + n], KT, D, c0, n, ps)
            for k in range(KT):
                c.op("dve", lambda e, k=k: e.scalar_tensor_tensor(
                    out=xn[:, k, c0:c0 + n], in0=self.H[:, k, c0:c0 + n], scalar=self.G[:, gofs + k:gofs + k + 1],
                    in1=self.rstd[:, c0:c0 + n], op0=ALU.mult, op1=ALU.mult),
                    reads=[self.H, self.G, self.rstd], writes=[xnb])

    def post_norm_add(self, seg, gofs):
        c = self.c
        yb, y = self.Y
        for (c0, n) in self.tiles(seg):
            ps = self.PS[7]
            self.sumsq_rstd(yb, lambda k: y[:, k, c0:c0 + n], KT, D, c0, n, ps)
            for k in range(KT):
                t = self.tmp[self.tmp_i]
                self.tmp_i ^= 1
                c.op("dve", lambda e, k=k, t=t: e.scalar_tensor_tensor(
                    out=t[:, 0:n], in0=y[:, k, c0:c0 + n], scalar=self.G[:, gofs + k:gofs + k + 1],
                    in1=self.rstd[:, c0:c0 + n], op0=ALU.mult, op1=ALU.mult),
                    reads=[yb, self.G, self.rstd], writes=[t])
                c.op("dve", lambda e, k=k, t=t: e.tensor_tensor(
                    out=self.H[:, k, c0:c0 + n], in0=self.H[:, k, c0:c0 + n], in1=t[:, 0:n], op=ALU.add),
                    reads=[t, self.H], writes=[self.H])

    def ffn(self, seg, l, which):
        c = self.c
        tiles = self.tiles(seg)
        gbase = (l * 6 + (0 if which == 0 else 4)) * KT
        self.pre_norm(seg, gbase)
        xnb, xn = self.XN
        yb, y = self.Y
        self.dump("rstd", self.rstd, self.rstd[:, 0:512], [128, 512])
        self.dump("xn", xnb, xn[:, 3, 0:512], [128, 512])
        wgu = self.wgu[which]
        wdn = self.wdn[which]
        m0 = 0
        for half in range(2):
            nm = FH[half]
            for mi in range(nm):
                m = m0 + mi
                W = self.WA[self.wa_i]
                self.wa_i = (self.wa_i + 1) % len(self.WA)
                c.dma("pool", W[:, :, :, :], wgu[l, m], writes=[W])
                for ti, (c0, n) in enumerate(tiles):
                    pg = self.PS[0 + (mi % 2)] if ti == 0 else self.PS[6]
                    pu = self.PS[2 + (mi % 2)] if ti == 0 else self.PS[6]
                    og = 0 if ti == 0 else 0
                    ou = 0 if ti == 0 else 16
                    for k in range(KT):
                        c.op("pe", lambda e, k=k, pg=pg, og=og: e.matmul(pg[:, og:og + n], W[:, 0, k, :], xn[:, k, c0:c0 + n],
                                                                      start=(k == 0), stop=(k == KT - 1)),
                             reads=[W, xnb], writes=[pg], inc=(k == KT - 1))
                    for k in range(KT):
                        c.op("pe", lambda e, k=k, pu=pu, ou=ou: e.matmul(pu[:, ou:ou + n], W[:, 1, k, :], xn[:, k, c0:c0 + n],
                                                                      start=(k == 0), stop=(k == KT - 1)),
                             reads=[W, xnb], writes=[pu], inc=(k == KT - 1))
                    t = self.tmp[self.tmp_i]
                    self.tmp_i ^= 1
                    c.op("act", lambda e, t=t, pg=pg, og=og: e.activation(out=t[:, 0:n], in_=pg[:, og:og + n], func=AF.Silu),
                         reads=[pg], writes=[t])
                    c.op("dve", lambda e, t=t, pu=pu, ou=ou, mi=mi: e.tensor_tensor(
                        out=self.ACTB[:, mi, c0:c0 + n], in0=t[:, 0:n], in1=pu[:, ou:ou + n], op=ALU.mult),
                        reads=[t, pu], writes=[self.ACTB])
            self.dump("act", self.ACTB, self.ACTB[:, 1, 0:512], [128, 512])
            for mo in range(KT):
                W = self.WDR[self.wd_i]
                self.wd_i = (self.wd_i + 1) % len(self.WDR)
                wa_alias = self.WA[self.wd_i - 3] if False else None
                if W.name.startswith("WAd"):
                    wa = self.WA[int(W.name[3])]
                    c.dma("pool", W[:, 0:nm, :], wdn[l, mo, :, m0:m0 + nm, :], writes=[wa], owner=wa)
                    W = Buf(W.t, W.name)
                    Wdep = wa
                else:
                    c.dma("pool", W[:, 0:nm, :], wdn[l, mo, :, m0:m0 + nm, :], writes=[W])
                    Wdep = W
                for ti, (c0, n) in enumerate(tiles):
                    pd = self.PS[4 + (mo % 2)] if ti == 0 else self.PS[6]
                    od = 0 if ti == 0 else 32
                    for k in range(nm):
                        c.op("pe", lambda e, k=k, pd=pd, od=od: e.matmul(pd[:, od:od + n], W[:, k, :], self.ACTB[:, k, c0:c0 + n],
                                                                      start=(k == 0), stop=(k == nm - 1)),
                             reads=[Wdep, self.ACTB], writes=[pd], inc=(k == nm - 1))
                    if half == 0:
                        c.op("act", lambda e, pd=pd, od=od, mo=mo: e.activation(out=y[:, mo, c0:c0 + n], in_=pd[:, od:od + n], func=AF.Copy),
                             reads=[pd, xnb], writes=[yb])
                    else:
                        c.op("dve", lambda e, pd=pd, od=od, mo=mo: e.tensor_tensor(out=y[:, mo, c0:c0 + n], in0=y[:, mo, c0:c0 + n],
                                                                                  in1=pd[:, od:od + n], op=ALU.add),
                             reads=[pd, yb], writes=[yb])
            m0 += nm
        self.dump("y", yb, y[:, 2, 0:512], [128, 512])
        self.post_norm_add(seg, gbase + KT)

    NCB = 258 + 512 + 128 + 128 + 512 + 512
    NCF = 128 + 512 + 512
    CB_BAND, CB_M2, CB_BD, CB_ID, CB_MB, CB_MC = 0, 258, 770, 898, 1026, 1538
    CF_ID, CF_IOTA, CF_RST = 0, 128, 640

    def declare_mix(self):
        nl = self.nlayer
        self.win_f = self.din("win_f", [nl, NPAIR, 128, 2, KT, 128])
        self.wt = self.din("wt", [nl, 4, 128, KT, 256])
        self.wout = self.din("wout", [nl, KT, 128, MIXK, 128])
        self.wglu = self.din("wglu", [nl, 4, 128, 4, 128])
        self.rot = self.din("rot", [128, 2, self.ntok])
        self.cbf = self.din("cbf", [128, self.NCB])
        self.cf32 = self.din("cf32", [128, self.NCF])
        self.plA = self.din("plA", [128, nl, 46])
        self.pmix = self.din("pmix", [128, nl, 30])
        self.hlb = self.din("hlb", [128, nl, 4])
        self.s5r = self.din("s5r", [128, nl, 1728])
        self.x0s = self.din("x0s", [128, nl, NS, 14, 2])
        self.hs0 = self.din("hs0", [128, nl, NS, 4, 128])
        self.kct = self.din("kct", [128, nl, NS, 4, 128])
        self.vct = self.din("vct", [128, nl, NS, 320])
        self.o_ssm_p = self.dout("o_ssm_p", [128, nl, 14, 2])
        self.o_ssm_s = self.dout("o_ssm_s", [128, nl, NS, 14, 2])
        self.o_swa_k = self.dout("o_swa_k", [nl, 128, 128 + NS])
        self.o_swa_v = self.dout("o_swa_v", [nl, 128, 128])
        self.o_sv = self.dout("o_sv", [nl, NS, 320])
        self.o_d0_k = self.dout("o_d0_k", [nl, 64, 128 + NS])
        self.o_d0_v = self.dout("o_d0_v", [nl, 128, 64])
        self.o_d1_k = self.dout("o_d1_k", [nl, 64, 512 + NS])
        self.o_d1_v = self.dout("o_d1_v", [nl, 4, 128, 64])
        self.o_d2_k = self.dout("o_d2_k", [nl, 64, self.nseg * SEG + NS])
        self.o_d2_v = self.dout("o_d2_v", [nl, self.nseg, 4, 128, 64])
        self.o_hg_p = self.dout("o_hg_p", [nl, 4, 128, 128])
        self.o_hg_s = self.dout("o_hg_s", [nl, NS, 4, 128, 128])

    def setup_mix_consts(self):
        c = self.c
        nl = self.nlayer
        NT = self.NT
        self.CB = c.sbuf([128, self.NCB], BF16, "CB")
        c.dma("pool", self.CB[:], self.cbf, writes=[self.CB])
        self.CF = c.sbuf([128, self.NCF], F32, "CF")
        c.dma("sp", self.CF[:], self.cf32, writes=[self.CF])
        self.idb = self.CB[:, self.CB_ID:self.CB_ID + 128]
        self.idf = self.CF[:, self.CF_ID:self.CF_ID + 128]
        self.PLA = c.sbuf([128, nl, 46], F32, "PLA")
        c.dma("sp", self.PLA[:], self.plA, writes=[self.PLA])
        self.PMX = c.sbuf([128, nl, 30], F32, "PMX")
        c.dma("sp", self.PMX[:], self.pmix, writes=[self.PMX])
        self.HLB = c.sbuf([128, nl, 4], F32, "HLB")
        c.dma("sp", self.HLB[:], self.hlb, writes=[self.HLB])
        self.SK8 = c.sbuf([1, 8], BF16, "SK8")
        self.LBT = c.sbuf([128, nl, 4, 3], F32, "LBT")
        mx = c.sbuf([128, 4], F32, "lbmx")
        ex = c.sbuf([128, nl, 4], F32, "lbex")
        sm = c.sbuf([128, 4], F32, "lbsm")
        c.op("dve", lambda e: e.tensor_copy(out=mx[:], in_=self.HLB[:, 0, :]), reads=[self.HLB], writes=[mx])
        for l in range(1, nl):
            c.op("dve", lambda e, l=l: e.tensor_tensor(out=mx[:], in0=mx[:], in1=self.HLB[:, l, :], op=ALU.max),
                 reads=[self.HLB, mx], writes=[mx])
        for l in range(nl):
            c.op("dve", lambda e, l=l: e.tensor_tensor(out=ex[:, l, :], in0=self.HLB[:, l, :], in1=mx[:], op=ALU.subtract),
                 reads=[self.HLB, mx], writes=[ex])
        c.op("act", lambda e: e.activation(out=ex[:], in_=ex[:], func=AF.Exp), reads=[ex], writes=[ex])
        c.op("dve", lambda e: e.tensor_copy(out=sm[:], in_=ex[:, 0, :]), reads=[ex], writes=[sm])
        for l in range(1, nl):
            c.op("dve", lambda e, l=l: e.tensor_tensor(out=sm[:], in0=sm[:], in1=ex[:, l, :], op=ALU.add),
                 reads=[ex, sm], writes=[sm])
        c.op("dve", lambda e: e.reciprocal(out=sm[:], in_=sm[:]), reads=[sm], writes=[sm])
        for l in range(nl):
            c.op("dve", lambda e, l=l: e.tensor_tensor(out=ex[:, l, :], in0=ex[:, l, :], in1=sm[:], op=ALU.mult),
                 reads=[ex, sm], writes=[ex])
        cum = c.sbuf([128, 4], F32, "lbcum")
        c.op("dve", lambda e: e.memset(cum[:], 0.0), writes=[cum])
        for l in range(nl):
            if l > 0:
                c.op("dve", lambda e, l=l: e.tensor_tensor(out=cum[:], in0=cum[:], in1=ex[:, l, :], op=ALU.add),
                     reads=[ex, cum], writes=[cum])
            c.op("dve", lambda e, l=l: e.tensor_scalar(out=self.LBT[:, l, :, 0], in0=cum[:], scalar1=-1.0, scalar2=1.0,
                                                       op0=ALU.mult, op1=ALU.add), reads=[cum], writes=[self.LBT])
            c.op("dve", lambda e, l=l: e.tensor_scalar(out=self.LBT[:, l, :, 1], in0=cum[:], scalar1=1e-30, scalar2=None,
                                                       op0=ALU.max), reads=[cum], writes=[self.LBT])
            c.op("dve", lambda e, l=l: e.tensor_scalar(out=self.LBT[:, l, :, 2], in0=cum[:], scalar1=1.0, scalar2=-1.0,
                                                       op0=ALU.mult, op1=ALU.add), reads=[cum], writes=[self.LBT])
        self.KB = [[c.sbuf([128, NT], BF16, f"KB{l}{p}") for p in range(2)] for l in range(nl)]
        self.KC0 = [[c.sbuf([128, NT], BF16, f"KC0{l}{p}") for p in range(2)] for l in range(nl)]
        self.KC1 = [[c.sbuf([128, NT], BF16, f"KC1{l}{p}") for p in range(2)] for l in range(nl)]
        self.KC2 = [c.sbuf([128, self.nseg * SEG + NS], BF16, f"KC2{l}") for l in range(nl)]
        self.VBh = [[c.sbuf([128, 4, 128], BF16, f"VB{l}{p}") for p in range(2)] for l in range(nl)]
        self.VC0h = [[c.sbuf([128, 4, 64], BF16, f"VC0{l}{p}") for p in range(2)] for l in range(nl)]
        self.VC1h = [[c.sbuf([128, 4, 64], BF16, f"VC1{l}{p}") for p in range(2)] for l in range(nl)]
        self.VC2 = [c.sbuf([128, 4, self.nseg, 64], BF16, f"VC2{l}") for l in range(nl)]
        self.X0 = [c.sbuf([128, 14, 2], F32, f"X0{l}") for l in range(nl)]
        self.SH = [[c.sbuf([128, 4, 128], F32, f"SH{l}{p}") for p in range(2)] for l in range(nl)]
        for l in range(nl):
            c.op("dve", lambda e, l=l: e.memset(self.X0[l][:], 0.0), writes=[self.X0[l]])
            c.op("dve", lambda e, l=l: e.memset(self.SH[l][0][:], 0.0), writes=[self.SH[l][0]])
        self.sh_par = [0] * nl
        self.MIX = Buf(self.ARENA.t[:, self.YB // 4:(self.YB + MIXK * NT * 2) // 4].bitcast(BF16).rearrange(
            "p (k t) -> p k t", k=MIXK), "MIX")
        self.stat = [c.sbuf([128, 8], F32, f"stat{i}") for i in range(4)]

    def ar_reset(self):
        self.ar_off = [0, 0]
        self.rings = {}

    def ta(self, shape, dt, name, pool=0):
        esz = 2 if dt == BF16 else 4
        n = 1
        for s in shape[1:]:
            n *= s
        nbytes = (n * esz + 3) // 4 * 4
        off = self.ar_off[pool]
        lim = self.YB if pool == 0 else KT * self.NT * 2
        assert off + nbytes <= lim, f"arena pool {pool} overflow at {name}: {off}+{nbytes} > {lim}"
        self.ar_off[pool] = off + nbytes
        if pool == 0:
            base = self.ARENA.t[:, off // 4:(off + nbytes) // 4]
        else:
            base = self.XN[0].t.rearrange("p k t -> p (k t)")[:, off // 2:(off + nbytes) // 2].bitcast(F32)
        P = shape[0]
        ap = base[0:P, :]
        if dt == BF16:
            ap = ap.bitcast(BF16)
        ap = ap[:, 0:n]
        if len(shape) == 3:
            ap = ap.rearrange("p (a b) -> p a b", a=shape[1])
        elif len(shape) == 4:
            ap = ap.rearrange("p (a b c) -> p a b c", a=shape[1], b=shape[2])
        self.c.nbuf += 1
        return Buf(ap, f"{name}_{self.c.nbuf}")

    def next_wa(self):
        W = self.WA[self.wa_i]
        self.wa_i = (self.wa_i + 1) % len(self.WA)
        return W

    def proj_pair(self, l, pr, tiles, consume, which=(0, 1)):
        c = self.c
        xnb, xn = self.XN
        W = self.next_wa()
        c.dma("pool", W[:, :, :, :], self.win_f[l, pr], writes=[W])
        for ti, (c0, n) in enumerate(tiles):
            res = []
            for s in which:
                if ti == 0:
                    pb = self.PS[2 * s + self.pp[s]]
                    self.pp[s] ^= 1
                    o = 0
                else:
                    pb = self.PS[6]
                    o = 64 * s
                for k in range(KT):
                    c.op("pe", lambda e, k=k, pb=pb, o=o, s=s: e.matmul(pb[:, o:o + n], W[:, s, k, :], xn[:, k, c0:c0 + n],
                                                                      start=(k == 0), stop=(k == KT - 1)),
                         reads=[W, xnb], writes=[pb], inc=(k == KT - 1))
                res.append((pb, pb[:, o:o + n]))
            consume(c0, n, res)

    def rotary(self, c0, n, res, dst_buf, dst_ap, f32_buf=None, f32_ap=None):
        c = self.c
        (b0, x), (b1, xs) = res
        t1 = self.tmp[0]
        t2 = self.tmp[1]
        c.op("dve", lambda e: e.tensor_tensor(out=t1[:, 0:n], in0=x, in1=self.ROT[:, 0, c0:c0 + n], op=ALU.mult),
             reads=[b0, self.ROT], writes=[t1])
        c.op("dve", lambda e: e.tensor_tensor(out=t2[:, 0:n], in0=xs, in1=self.ROT[:, 1, c0:c0 + n], op=ALU.mult),
             reads=[b1, self.ROT], writes=[t2])
        if f32_buf is None:
            c.op("dve", lambda e: e.tensor_tensor(out=dst_ap, in0=t1[:, 0:n], in1=t2[:, 0:n], op=ALU.add),
                 reads=[t1, t2], writes=[dst_buf])
        else:
            c.op("dve", lambda e: e.tensor_tensor(out=f32_ap, in0=t1[:, 0:n], in1=t2[:, 0:n], op=ALU.add),
                 reads=[t1, t2], writes=[f32_buf])
            c.op("act", lambda e: e.activation(out=dst_ap, in_=f32_ap, func=AF.Copy), reads=[f32_buf], writes=[dst_buf])

    def mix_norm(self, seg, src_buf, src_fn, nchunk, n_feat, l, gofs, mbase):
        c = self.c
        for (c0, n) in self.tiles(seg):
            ps = self.PS[7]
            self.sumsq_rstd(src_buf, lambda k: src_fn(k, c0, n), nchunk, n_feat, c0, n, ps)
            for k in range(nchunk):
                c.op("dve", lambda e, k=k: e.scalar_tensor_tensor(
                    out=self.MIX[:, mbase + k, c0:c0 + n], in0=src_fn(k, c0, n),
                    scalar=self.PMX[:, l, gofs + k:gofs + k + 1], in1=self.rstd[:, c0:c0 + n],
                    op0=ALU.mult, op1=ALU.mult), reads=[src_buf, self.PMX, self.rstd], writes=[self.MIX])

    def mixer(self, seg, l):
        c = self.c
        last = (seg == self.nseg - 1)
        NC = SEG + NS if last else SEG
        gm = (l * 6 + 2) * KT
        c.barrier()
        self.ar_reset()
        self.pp = [0, 0]
        self.pre_norm(seg, gm)
        c.op("act", lambda e: e.activation(out=self.SK8[0:1, :], in_=self.PMX[0:1, l, 22:30], func=AF.Copy, scale=8.0),
             reads=[self.PMX], writes=[self.SK8])
        if "mix" in self.parts or "D" in self.parts:
            self.mix_D(seg, l, NC)
        if "mix" in self.parts or "A" in self.parts:
            self.mix_A(seg, l, NC)
        if "mix" in self.parts or "BC" in self.parts:
            self.mix_BC(seg, l, NC)
        if "mix" in self.parts:
            self.mix_out(seg, l, NC)
        c.barrier()
    def set_slots(self, mode):
        if getattr(self, "_slot_mode", None) not in (None, mode):
            self.c.barrier()
        self._slot_mode = mode
        R, T = self.P32R, self.PTR
        sl = []
        if mode == "small":
            for s in range(4):
                P = Buf(R[:, 260 * s:260 * s + 260], f"P32s{s}")
                PT = Buf(T[:, 256 * s:256 * s + 256].rearrange("p (a b) -> p a b", a=2), f"PTs{s}")
                sl.append(dict(ps=self.PS[s], pst=self.PS[4 + s // 2], tc0=256 * (s % 2), pso=self.PS[6 + s // 2], oc0=256 * (s % 2),
                               P=P, PT=PT, st=self.stat[s], eng=("act" if s % 2 == 0 else "dve")))
        else:
            for s in range(2):
                P = Buf(R[:, 520 * s:520 * s + 514], f"P32b{s}")
                PT = Buf(T[:, 512 * s:512 * s + 512].rearrange("p (a b) -> p a b", a=4), f"PTb{s}")
                sl.append(dict(ps=self.PS[3 * s], pst=self.PS[3 * s + 1], tc0=0, pso=self.PS[3 * s + 2], oc0=0,
                               P=P, PT=PT, st=self.stat[s], eng=("act" if s == 0 else "dve")))
        self.AS = sl

    def attn_units(self, units):
        n = len(self.AS)
        for i in range(0, len(units), n):
            batch = units[i:i + n]
            for s, u in enumerate(batch):
                self.au_scores(u, s)
            for s, u in enumerate(batch):
                self.au_softmax(u, s)
            for s, u in enumerate(batch):
                self.au_transpose(u, s)
            for s, u in enumerate(batch):
                self.au_pv(u, s)
            for s, u in enumerate(batch):
                self.au_out(u, s)

    def au_scores(self, u, s):
        c = self.c
        nq, ncols = u["nq"], u["ncols"]
        ps = self.AS[s]["ps"]
        if u.get("sink") is not None:
            hh = u["sink"]
            nkeys = u["nkeys"]
            c.op("pe", lambda e: e.matmul(ps[0:nq, nkeys:nkeys + 1], self.ones_bf[0:1, 0:nq], self.SK8[0:1, hh:hh + 1], start=True, stop=True),
                 reads=[self.ones_bf, self.SK8], writes=[ps], inc=False)
        first = True
        if u["mask"] is not None:
            mb, map_ = u["mask"]
            nkk = u["nkeys"]
            c.op("pe", lambda e: e.matmul(ps[0:nq, 0:nkk], self.idb[0:nq, 0:nq], map_, start=True, stop=False),
                 reads=[self.CB, mb], writes=[ps], inc=False)
            first = False
        col = 0
        nkb = len(u["kblocks"])
        for i, (kb, kap, nk, rr) in enumerate(u["kblocks"]):
            o = ps[0:nq, col:col + nk]
            if rr:
                o = o.rearrange("p (s r j) -> p s r j", s=rr[0], r=rr[1])
            stp = True if first else (i == nkb - 1)
            c.op("pe", lambda e, o=o, kap=kap, stp=stp: e.matmul(o, u["q"][1], kap, start=first, stop=stp),
                 reads=[u["q"][0], kb], writes=[ps], inc=(i == nkb - 1))
            col += nk

    def au_softmax(self, u, s):
        c = self.c
        nq, ncols, nkeys = u["nq"], u["ncols"], u["nkeys"]
        A_ = self.AS[s]
        ps, st, P = A_["ps"], A_["st"], A_["P"]
        c.op("dve", lambda e: e.reduce_max(out=st[0:nq, 0:1], in_=ps[0:nq, 0:ncols], axis=AX.X), reads=[ps], writes=[st])
        c.op("dve", lambda e: e.tensor_scalar(out=st[0:nq, 1:2], in0=st[0:nq, 0:1], scalar1=-0.125, scalar2=None, op0=ALU.mult),
             reads=[st], writes=[st])
        c.op("act", lambda e: e.activation(out=P[0:nq, 0:ncols], in_=ps[0:nq, 0:ncols], func=AF.Exp, bias=st[0:nq, 1:2],
                                           scale=0.125, accum_out=st[0:nq, 2:3]), reads=[ps, st], writes=[P, st])
        c.op("dve", lambda e: e.reciprocal(out=st[0:nq, 3:4], in_=st[0:nq, 2:3]), reads=[st], writes=[st])
        c.op("dve", lambda e: e.tensor_scalar(out=P[0:nq, 0:nkeys], in0=P[0:nq, 0:nkeys], scalar1=st[0:nq, 3:4], scalar2=None,
                                              op0=ALU.mult), reads=[P, st], writes=[P])
        if u.get("lse") is not None:
            c.op("act", lambda e: e.activation(out=st[0:nq, 4:5], in_=st[0:nq, 2:3], func=AF.Ln), reads=[st], writes=[st])
            c.op("dve", lambda e: e.scalar_tensor_tensor(out=st[0:nq, 5:6], in0=st[0:nq, 0:1], scalar=0.125, in1=st[0:nq, 4:5],
                                                         op0=ALU.mult, op1=ALU.add), reads=[st], writes=[st])

    def au_transpose(self, u, s):
        c = self.c
        nq = u["nq"]
        A_ = self.AS[s]
        P, pst, PT, t0 = A_["P"], A_["pst"], A_["PT"], A_["tc0"]
        for i, vb in enumerate(u["vblocks"]):
            if vb["kind"] != "T":
                continue
            nk, k0 = vb["nk"], vb["k0"]
            dst = pst[0:nk, t0 + i * 128:t0 + i * 128 + nq]
            c.op("pe", lambda e, dst=dst, nk=nk, k0=k0: e.transpose(dst, P[0:nq, k0:k0 + nk], self.idf[0:nq, 0:nq]),
                 reads=[P, self.CF], writes=[pst])
            if A_["eng"] == "act":
                c.op("act", lambda e, i=i, nk=nk, dst=dst: e.activation(out=PT[0:nk, i, 0:nq], in_=dst, func=AF.Copy), reads=[pst], writes=[PT])
            else:
                c.op("dve", lambda e, i=i, nk=nk, dst=dst: e.tensor_copy(out=PT[0:nk, i, 0:nq], in_=dst), reads=[pst], writes=[PT])

    def au_pv(self, u, s):
        c = self.c
        nq, hf = u["nq"], u["half"]
        A_ = self.AS[s]
        P, PT, pso, st, o0 = A_["P"], A_["PT"], A_["pso"], A_["st"], A_["oc0"]
        rows = slice(64 * hf, 64 * hf + 64)
        nvb = len(u["vblocks"])
        for i, vb in enumerate(u["vblocks"]):
            if vb["kind"] == "T":
                rhs = PT[0:vb["nk"], i, 0:nq]
                rd = [PT, vb["buf"]]
            else:
                rhs = P[0:1, vb["k0"]:vb["k0"] + 1]
                rd = [P, vb["buf"]]
            c.op("pe", lambda e, vb=vb, rhs=rhs, i=i: e.matmul(pso[rows, o0:o0 + nq], vb["ap"], rhs, start=(i == 0), stop=(i == nvb - 1)),
                 reads=rd, writes=[pso], inc=(i == nvb - 1))
        if u.get("lse") is not None:
            c.op("pe", lambda e: e.matmul(pso[:, o0 + 128:o0 + 128 + nq], st[0:nq, 5:6].to_broadcast([nq, 128]), self.idf[0:nq, 0:nq],
                                          start=True, stop=True), reads=[st, self.CF], writes=[pso])

    def au_out(self, u, s):
        c = self.c
        nq, hf = u["nq"], u["half"]
        pso, o0 = self.AS[s]["pso"], self.AS[s]["oc0"]
        rows = slice(64 * hf, 64 * hf + 64)
        ob, oap = u["out"]
        src = pso[rows, o0:o0 + nq]
        if u.get("orr"):
            src = src.rearrange("p (r j) -> p r j", r=u["orr"])
        c.op("act", lambda e: e.activation(out=oap, in_=src, func=AF.Copy), reads=[pso], writes=[ob])
        if u.get("lse") is not None:
            lb, lap = u["lse"]
            src2 = pso[rows, o0 + 128:o0 + 128 + nq]
            if u.get("orr"):
                src2 = src2.rearrange("p (r j) -> p r j", r=u["orr"])
            c.op("act", lambda e: e.activation(out=lap, in_=src2, func=AF.Copy), reads=[pso], writes=[lb])

    def mix_BC(self, seg, l, NC):
        c = self.c
        last = (seg == self.nseg - 1)
        NT = self.NT
        tiles = self.tiles(seg)
        xnb, xn = self.XN
        par = seg % 2
        KBc, KBp = self.KB[l][par], self.KB[l][1 - par]
        KC0c, KC0p = self.KC0[l][par], self.KC0[l][1 - par]
        KC1c, KC1p = self.KC1[l][par], self.KC1[l][1 - par]
        KC2 = self.KC2[l]
        VBc, VBp = self.VBh[l][par], self.VBh[l][1 - par]
        VC0c, VC0p = self.VC0h[l][par], self.VC0h[l][1 - par]
        VC1c, VC1p = self.VC1h[l][par], self.VC1h[l][1 - par]
        VC2 = self.VC2[l]
        QCT = self.ta([128, 6, NT], BF16, "QCT")
        V8 = self.ta([NS, 320], F32, "V8") if last else None
        self.P32R = self.ta([128, 1040], F32, "P32R")
        self.PTR = self.ta([128, 1024], BF16, "PTR")
        self._slot_mode = None
        if last:
            for nm, shp, dt in (("KCTb", [128, 4, 128], BF16), ("VCTb", [128, 320], BF16), ("VSb", [1, 320], F32)):
                self.ta_ring(nm, shp, dt)
        mark0 = self.ar_off[0]
        QBT = self.ta([128, 4, NT], BF16, "QBT")
        mark1 = self.ar_off[0]
        KF = self.ta([128, NT], F32, "KF")
        VF = self.ta([128, 192], F32, "VF")
        self.ROT = self.ta([128, 2, NT], F32, "ROT")
        c.dma("sp", self.ROT[:, :, 0:SEG], self.rot[:, :, seg * SEG:(seg + 1) * SEG], writes=[self.ROT])
        if last:
            c.dma("sp", self.ROT[:, :, SEG:SEG + NS], self.rot[:, :, self.nseg * SEG:self.nseg * SEG + NS], writes=[self.ROT])
        for j in range(4):
            self.proj_pair(l, 2 + j, tiles, lambda c0, n, res, j=j: self.rotary(c0, n, res, QBT, QBT[:, j, c0:c0 + n]))
        for j in range(6):
            self.proj_pair(l, 7 + j, tiles, lambda c0, n, res, j=j: self.rotary(c0, n, res, QCT, QCT[:, j, c0:c0 + n]))
        import os as _os
        _stopat = _os.environ.get("STOPAT", "")
        if _stopat == "q":
            return
        self.proj_pair(l, 6, tiles, lambda c0, n, res: self.rotary(c0, n, res, KBc, KBc[:, c0:c0 + n], KF, KF[:, c0:c0 + n]))
        if last:
            c.dma("sp", self.o_swa_k[l, :, 0:128], KF[:, SEG - 128:SEG], reads=[KF], is_output=True)
            c.dma("sp", self.o_swa_k[l, :, 128:128 + NS], KF[:, SEG:SEG + NS], reads=[KF], is_output=True)
        self.proj_pair(l, 13, tiles, lambda c0, n, res: self.rotary(c0, n, res, KC0c, KC0c[:, c0:c0 + n], KF, KF[:, c0:c0 + n]))
        if last:
            c.dma("sp", self.o_d0_k[l, :, 0:128], KF[0:64, SEG - 128:SEG], reads=[KF], is_output=True)
            c.dma("sp", self.o_d0_k[l, :, 128:128 + NS], KF[0:64, SEG:SEG + NS], reads=[KF], is_output=True)
        self.proj_pair(l, 14, tiles, lambda c0, n, res: self.rotary(c0, n, res, KC1c, KC1c[:, c0:c0 + n], KF, KF[:, c0:c0 + n]))
        if last:
            c.dma("sp", self.o_d1_k[l, :, 0:SEG + NS], KF[0:64, 0:SEG + NS], reads=[KF], is_output=True)

        def k2(c0, n, res):
            g0 = seg * SEG + c0 if c0 < SEG else self.nseg * SEG + (c0 - SEG)
            self.rotary(c0, n, res, KC2, KC2[:, g0:g0 + n], KF, KF[:, c0:c0 + n])
        self.proj_pair(l, 15, tiles, k2)
        c.dma("sp", self.o_d2_k[l, :, seg * SEG:(seg + 1) * SEG], KF[0:64, 0:SEG], reads=[KF], is_output=True)
        if last:
            c.dma("sp", self.o_d2_k[l, :, self.nseg * SEG:self.nseg * SEG + NS], KF[0:64, SEG:SEG + NS], reads=[KF],
                  is_output=True)
        if _stopat == "k":
            return
        W2 = self.next_wa()
        W2v = W2[:, :, :, :].rearrange("p a k c -> p (a k c)").rearrange("p (k c) -> p k c", k=KT)
        c.dma("pool", W2v, self.wt[l, 2], writes=[W2])
        for blk in range(4):
            ps = self.PS[blk % 2]
            for k in range(KT):
                c.op("pe", lambda e, k=k, ps=ps, blk=blk: e.matmul(ps[:, 0:192], xn[:, k, blk * 128:(blk + 1) * 128], W2v[:, k, 0:192],
                                                                  start=(k == 0), stop=(k == KT - 1)),
                     reads=[W2, xnb], writes=[ps], inc=(k == KT - 1))
            c.op("act", lambda e, ps=ps, blk=blk: e.activation(out=VBc[:, blk, :], in_=ps[:, 0:128], func=AF.Copy), reads=[ps], writes=[VBc])
            c.op("act", lambda e, ps=ps, blk=blk: e.activation(out=VC0c[:, blk, :], in_=ps[:, 128:192], func=AF.Copy), reads=[ps], writes=[VC0c])
            if last and blk == 3 and not _os.environ.get("NOVOUT"):
                c.op("act", lambda e, ps=ps: e.activation(out=VF[:, 0:192], in_=ps[:, 0:192], func=AF.Copy), reads=[ps], writes=[VF])
                c.dma("sp", self.o_swa_v[l], VF[:, 0:128], reads=[VF], is_output=True)
                c.dma("sp", self.o_d0_v[l], VF[:, 128:192], reads=[VF], is_output=True)
        if _stopat == "v1":
            return
        for r in range(4):
            ps = self.PS[r % 2]
            for k in range(KT):
                c.op("pe", lambda e, k=k, ps=ps, r=r: e.matmul(ps[:, 0:64], xn[:, k, r:SEG:4], W2v[:, k, 192:256],
                                                              start=(k == 0), stop=(k == KT - 1)),
                     reads=[W2, xnb], writes=[ps], inc=(k == KT - 1))
            c.op("dve", lambda e, ps=ps, r=r: e.tensor_copy(out=VC1c[:, r, :], in_=ps[:, 0:64]), reads=[ps], writes=[VC1c])
            if last:
                c.op("act", lambda e, ps=ps: e.activation(out=VF[:, 0:64], in_=ps[:, 0:64], func=AF.Copy), reads=[ps], writes=[VF])
                c.dma("sp", self.o_d1_v[l, r], VF[:, 0:64], reads=[VF], is_output=True)
        if last:
            ps8 = self.PS[6]
            for k in range(KT):
                c.op("pe", lambda e, k=k: e.matmul(ps8[0:NS, 0:256], xn[:, k, SEG:SEG + NS], W2v[:, k, 0:256],
                                                   start=(k == 0), stop=(k == KT - 1)),
                     reads=[W2, xnb], writes=[ps8], inc=(k == KT - 1))
        if _stopat == "v2":
            return
        W3 = self.next_wa()
        W3v = W3[:, :, :, :].rearrange("p a k c -> p (a k c)").rearrange("p (k c) -> p k c", k=KT)
        c.dma("pool", W3v, self.wt[l, 3], writes=[W3])
        for r0 in range(4):
            ps = self.PS[2 + r0 % 2]
            for rr in range(4):
                for k in range(KT):
                    c.op("pe", lambda e, k=k, ps=ps, rr=rr, r0=r0: e.matmul(
                        ps[32 * rr:32 * rr + 32, 0:64], xn[:, k, 4 * r0 + rr:SEG:16], W3v[:, k, 0:64],
                        start=(k == 0), stop=(k == KT - 1), tile_position=(0, 32 * rr)),
                        reads=[W3, xnb], writes=[ps], inc=(k == KT - 1))
            c.op("dve", lambda e, ps=ps, r0=r0: e.tensor_copy(out=VC2[:, r0, seg, :], in_=ps[:, 0:64]), reads=[ps], writes=[VC2])
            c.op("act", lambda e, ps=ps: e.activation(out=VF[:, 64:128], in_=ps[:, 0:64], func=AF.Copy), reads=[ps], writes=[VF])
            c.dma("sp", self.o_d2_v[l, seg, r0], VF[:, 64:128], reads=[VF], is_output=True)
        if last:
            for k in range(KT):
                c.op("pe", lambda e, k=k: e.matmul(ps8[0:NS, 256:320], xn[:, k, SEG:SEG + NS], W3v[:, k, 0:64],
                                                   start=(k == 0), stop=(k == KT - 1)),
                     reads=[W3, xnb], writes=[ps8], inc=(k == KT - 1))
            c.op("act", lambda e: e.activation(out=V8[:, :], in_=ps8[0:NS, 0:320], func=AF.Copy), reads=[ps8], writes=[V8])
            c.dma("sp", self.o_sv[l], V8[:, :], reads=[V8], is_output=True)
        if "stop_proj" in self.parts:
            return
        c.barrier()
        self.ar_off[0] = mark1
        OBT = self.ta([128, 4, NT], BF16, "OBT")
        first_seq = (seg == 0)
        units = []
        for h in range(8):
            hf, j = h // 4, h % 4
            rows = slice(64 * hf, 64 * hf + 64)
            for qb in range(4):
                u = dict(nq=128, half=hf, q=(QBT, QBT[rows, j, qb * 128:(qb + 1) * 128]))
                vcol = slice(64 * hf, 64 * hf + 64)
                if qb > 0:
                    u["kblocks"] = [(KBc, KBc[rows, (qb - 1) * 128:(qb + 1) * 128], 256, 0)]
                    u["vblocks"] = [dict(kind="T", buf=VBc, ap=VBc[:, qb - 1, vcol], nk=128, k0=0),
                                    dict(kind="T", buf=VBc, ap=VBc[:, qb, vcol], nk=128, k0=128)]
                    u["mask"] = (self.CB, self.CB[:, 0:256])
                    u["nkeys"], u["ncols"] = 256, 257
                elif not first_seq:
                    u["kblocks"] = [(KBp, KBp[rows, SEG - 128:SEG], 128, 0), (KBc, KBc[rows, 0:128], 128, 0)]
                    u["vblocks"] = [dict(kind="T", buf=VBp, ap=VBp[:, 3, vcol], nk=128, k0=0),
                                    dict(kind="T", buf=VBc, ap=VBc[:, 0, vcol], nk=128, k0=128)]
                    u["mask"] = (self.CB, self.CB[:, 0:256])
                    u["nkeys"], u["ncols"] = 256, 257
                else:
                    u["kblocks"] = [(KBc, KBc[rows, 0:128], 128, 0)]
                    u["vblocks"] = [dict(kind="T", buf=VBc, ap=VBc[:, 0, vcol], nk=128, k0=0)]
                    u["mask"] = (self.CB, self.CB[:, 128:256])
                    u["nkeys"], u["ncols"] = 128, 129
                u["out"] = (OBT, OBT[rows, j, qb * 128:(qb + 1) * 128])
                u["sink"] = h
                units.append(u)
        self.set_slots("small")
        self.attn_units(units)
        if last:
            self.sample_attn(l, QBT, QCT, KBc, (KC0c, KC1c, KC2), V8, OBT, None, None, which="B")
        self.dump(f"obt{seg}", OBT, OBT[:, 1, 0:512], [128, 512])
        if last:
            self.dump("sob", OBT, OBT[:, 1, 512:520], [128, 8])
        self.mix_norm(seg, OBT, lambda k, c0, n: OBT[:, k, c0:c0 + n], 4, 512, l, 4, 4)
        if "stop_B" in self.parts:
            return
        for chn in (4, 5):
            tq = self.tmp[chn % 2]
            c.op("dve", lambda e, chn=chn, tq=tq: e.tensor_copy(
                out=tq[:, 0:SEG].rearrange("p (a r j) -> p a r j", a=4, r=4),
                in_=QCT[:, chn, 0:SEG].rearrange("p (j a r) -> p a r j", a=4, r=4)), reads=[QCT], writes=[tq])
            c.op("dve", lambda e, chn=chn, tq=tq: e.tensor_copy(out=QCT[:, chn, 0:SEG], in_=tq[:, 0:SEG]), reads=[tq], writes=[QCT])
        c.barrier()
        self.ar_off[0] = mark0
        OCT = self.ta([128, 6, NT], F32, "OCT", pool=1)
        LST = self.ta([128, 6, NT], F32, "LST", pool=0)
        c.op("dve", lambda e: e.memset(OCT[:, :, :], 0.0), writes=[OCT])
        c.op("dve", lambda e: e.memset(LST[:, :, :], 0.0), writes=[LST])
        units = []
        band = (self.CB, self.CB[:, 0:256])
        bandf = (self.CB, self.CB[:, 128:256])
        for i in range(3):
            hf = 1 if i == 1 else 0
            rows = slice(64 * hf, 64 * hf + 64)
            ch = 0 + (1 if i == 2 else 0)
            for qb in range(4):
                u = dict(nq=128, half=hf, q=(QCT, QCT[rows, ch, qb * 128:(qb + 1) * 128]))
                if qb > 0:
                    u["kblocks"] = [(KC0c, KC0c[rows, (qb - 1) * 128:(qb + 1) * 128], 256, 0)]
                    u["vblocks"] = [dict(kind="T", buf=VC0c, ap=VC0c[:, qb - 1, :], nk=128, k0=0),
                                    dict(kind="T", buf=VC0c, ap=VC0c[:, qb, :], nk=128, k0=128)]
                    u["mask"], u["nkeys"], u["ncols"] = band, 256, 256
                elif not first_seq:
                    u["kblocks"] = [(KC0p, KC0p[rows, SEG - 128:SEG], 128, 0), (KC0c, KC0c[rows, 0:128], 128, 0)]
                    u["vblocks"] = [dict(kind="T", buf=VC0p, ap=VC0p[:, 3, :], nk=128, k0=0),
                                    dict(kind="T", buf=VC0c, ap=VC0c[:, 0, :], nk=128, k0=128)]
                    u["mask"], u["nkeys"], u["ncols"] = band, 256, 256
                else:
                    u["kblocks"] = [(KC0c, KC0c[rows, 0:128], 128, 0)]
                    u["vblocks"] = [dict(kind="T", buf=VC0c, ap=VC0c[:, 0, :], nk=128, k0=0)]
                    u["mask"], u["nkeys"], u["ncols"] = bandf, 128, 128
                u["out"] = (OCT, OCT[rows, ch, qb * 128:(qb + 1) * 128])
                u["lse"] = (LST, LST[rows, ch, qb * 128:(qb + 1) * 128])
                units.append(u)
            ch = 2 + (1 if i == 2 else 0)
            for r in range(4):
                u = dict(nq=128, half=hf, q=(QCT, QCT[rows, ch, r:SEG:4]))
                if not first_seq:
                    u["kblocks"] = [(KC1p, KC1p[rows, r:SEG:4], 128, 0), (KC1c, KC1c[rows, r:SEG:4], 128, 0)]
                    u["vblocks"] = [dict(kind="T", buf=VC1p, ap=VC1p[:, r, :], nk=128, k0=0),
                                    dict(kind="T", buf=VC1c, ap=VC1c[:, r, :], nk=128, k0=128)]
                    u["mask"], u["nkeys"], u["ncols"] = band, 256, 256
                else:
                    u["kblocks"] = [(KC1c, KC1c[rows, r:SEG:4], 128, 0)]
                    u["vblocks"] = [dict(kind="T", buf=VC1c, ap=VC1c[:, r, :], nk=128, k0=0)]
                    u["mask"], u["nkeys"], u["ncols"] = bandf, 128, 128
                u["out"] = (OCT, OCT[rows, ch, r:SEG:4])
                u["lse"] = (LST, LST[rows, ch, r:SEG:4])
                units.append(u)
            ch = 4 + (1 if i == 2 else 0)
            nkt = 128 * (seg + 1)
            for r0 in range(4):
                u = dict(nq=128, half=hf, q=(QCT, QCT[rows, ch, 128 * r0:128 * r0 + 128]))
                kap = KC2[rows, 0:SEG * (seg + 1)].rearrange("p (s j r) -> p s r j", s=seg + 1, r=16)[:, :, 4 * r0:4 * r0 + 4, :]
                u["kblocks"] = [(KC2, kap, nkt, (seg + 1, 4))]
                u["vblocks"] = [dict(kind="T", buf=VC2, ap=VC2[:, r0, sg, :], nk=128, k0=128 * sg) for sg in range(seg + 1)]
                u["mask"] = (self.CB, self.CB[:, self.CB_M2 + 512 - nkt:self.CB_M2 + 512])
                u["nkeys"], u["ncols"] = nkt, nkt
                u["out"] = (OCT, OCT[rows, ch, 0:SEG].rearrange("p (j r) -> p r j", r=16)[:, 4 * r0:4 * r0 + 4, :])
                u["lse"] = (LST, LST[rows, ch, 0:SEG].rearrange("p (j r) -> p r j", r=16)[:, 4 * r0:4 * r0 + 4, :])
                u["orr"] = 4
                units.append(u)
        self.set_slots("small")
        self.attn_units([u for u in units if not u.get("orr")])
        if last:
            self.sample_attn(l, QBT, QCT, KBc, (KC0c, KC1c, KC2), V8, None, OCT, LST, which="C")
        self.set_slots("big")
        self.attn_units([u for u in units if u.get("orr")])
        self.dump(f"oct_raw{seg}", OCT, OCT[:, 2, 0:512], [128, 512])
        self.dump(f"lst{seg}", LST, LST[:, 2, 0:512], [128, 512])
        ta_, tb_ = self.tmp[0], self.tmp[1]
        E = [self.ta([128, 256], F32, f"E{g}", pool=1) for g in range(3)]
        ctiles = [(0, 256), (256, 256)] + ([(SEG, NS)] if last else [])
        for ls in range(2):
            chs = [2 * g + ls for g in range(3)]
            for (c0, n) in ctiles:
                cs = slice(c0, c0 + n)
                c.op("dve", lambda e, cs=cs, n=n: e.tensor_tensor(out=ta_[:, 0:n], in0=LST[:, chs[0], cs], in1=LST[:, chs[1], cs], op=ALU.max),
                     reads=[LST], writes=[ta_])
                c.op("dve", lambda e, cs=cs, n=n: e.tensor_tensor(out=ta_[:, 0:n], in0=ta_[:, 0:n], in1=LST[:, chs[2], cs], op=ALU.max),
                     reads=[LST, ta_], writes=[ta_])
                for g in range(3):
                    c.op("dve", lambda e, g=g, cs=cs, n=n: e.tensor_tensor(out=E[g][:, 0:n], in0=LST[:, chs[g], cs], in1=ta_[:, 0:n], op=ALU.subtract),
                         reads=[LST, ta_], writes=[E[g]])
                    c.op("act", lambda e, g=g, n=n: e.activation(out=E[g][:, 0:n], in_=E[g][:, 0:n], func=AF.Exp), reads=[E[g]], writes=[E[g]])
                c.op("dve", lambda e, n=n: e.tensor_tensor(out=tb_[:, 0:n], in0=E[0][:, 0:n], in1=E[1][:, 0:n], op=ALU.add),
                     reads=[E[0], E[1]], writes=[tb_])
                c.op("dve", lambda e, n=n: e.tensor_tensor(out=tb_[:, 0:n], in0=tb_[:, 0:n], in1=E[2][:, 0:n], op=ALU.add),
                     reads=[E[2], tb_], writes=[tb_])
                c.op("dve", lambda e, n=n: e.reciprocal(out=tb_[:, 0:n], in_=tb_[:, 0:n]), reads=[tb_], writes=[tb_])
                for g in range(3):
                    c.op("dve", lambda e, g=g, n=n: e.tensor_tensor(out=E[g][:, 0:n], in0=E[g][:, 0:n], in1=tb_[:, 0:n], op=ALU.mult),
                         reads=[E[g], tb_], writes=[E[g]])
                    c.op("dve", lambda e, g=g, cs=cs, n=n: e.tensor_tensor(out=OCT[:, chs[g], cs], in0=OCT[:, chs[g], cs], in1=E[g][:, 0:n], op=ALU.mult),
                         reads=[OCT, E[g]], writes=[OCT])
        self.dump(f"oct{seg}", OCT, OCT[:, 2, 0:512], [128, 512])
        if last:
            self.dump("soc", OCT, OCT[:, 2, 512:520], [128, 8])
        self.mix_norm(seg, OCT, lambda k, c0, n: OCT[:, k, c0:c0 + n], 6, 576, l, 8, 8)

    def sample_attn(self, l, QBT, QCT, KBc, KCs, V8, OBT, OCT, LST, which):
        c = self.c
        for b in range(NS):
            col = SEG + b
            KCTb = self.ta_ring("KCTb", [128, 4, 128], BF16)
            VCTb = self.ta_ring("VCTb", [128, 320], BF16)
            VSb = self.ta_ring("VSb", [1, 320], F32)
            kf, vf = self.tmp[0], self.tmp[1]
            c.dma("sp", kf[:, 0:512].rearrange("p (a b) -> p a b", a=4), self.kct[:, l, b], writes=[kf])
            c.dma("sp", vf[:, 0:320], self.vct[:, l, b], writes=[vf])
            c.op("act", lambda e: e.activation(out=KCTb[:, :, :], in_=kf[:, 0:512].rearrange("p (a b) -> p a b", a=4), func=AF.Copy),
                 reads=[kf], writes=[KCTb])
            c.op("dve", lambda e: e.tensor_copy(out=VCTb[:, :], in_=vf[:, 0:320]), reads=[vf], writes=[VCTb])
            psr = self.PS[6]
            c.op("pe", lambda e, b=b: e.matmul(psr[0:1, 0:320], self.idf[0:NS, b:b + 1], V8[0:NS, 0:320], start=True, stop=True),
                 reads=[self.CF, V8], writes=[psr])
            c.op("act", lambda e: e.activation(out=VSb[0:1, :], in_=psr[0:1, 0:320], func=AF.Copy), reads=[psr], writes=[VSb])
            units = []
            if which == "B":
                for h in range(8):
                    hf, j = h // 4, h % 4
                    rows = slice(64 * hf, 64 * hf + 64)
                    u = dict(nq=1, half=hf, q=(QBT, QBT[rows, j, col:col + 1]))
                    u["kblocks"] = [(KCTb, KCTb[rows, 0, :], 128, 0), (KBc, KBc[rows, col:col + 1], 1, 0)]
                    u["vblocks"] = [dict(kind="T", buf=VCTb, ap=VCTb[:, 64 * hf:64 * hf + 64], nk=128, k0=0),
                                    dict(kind="D", buf=VSb, ap=VSb[0:1, 64 * hf:64 * hf + 64], nk=1, k0=128)]
                    u["mask"] = None
                    u["sink"] = h
                    u["nkeys"], u["ncols"] = 129, 130
                    u["out"] = (OBT, OBT[rows, j, col:col + 1])
                    units.append(u)
            else:
                for g in range(3):
                    Kg = KCs[g]
                    kcol = col if g < 2 else self.nseg * SEG + b
                    for i in range(3):
                        hf = 1 if i == 1 else 0
                        rows = slice(64 * hf, 64 * hf + 64)
                        ch = 2 * g + (1 if i == 2 else 0)
                        u = dict(nq=1, half=hf, q=(QCT, QCT[rows, ch, col:col + 1]))
                        u["kblocks"] = [(KCTb, KCTb[rows, 1 + g, :], 128, 0), (Kg, Kg[rows, kcol:kcol + 1], 1, 0)]
                        u["vblocks"] = [dict(kind="T", buf=VCTb, ap=VCTb[:, 128 + 64 * g:192 + 64 * g], nk=128, k0=0),
                                        dict(kind="D", buf=VSb, ap=VSb[0:1, 128 + 64 * g:192 + 64 * g], nk=1, k0=128)]
                        u["mask"] = None
                        u["nkeys"], u["ncols"] = 129, 129
                        u["out"] = (OCT, OCT[rows, ch, col:col + 1])
                        u["lse"] = (LST, LST[rows, ch, col:col + 1])
                        units.append(u)
            self.attn_units(units)

    def ta_ring(self, name, shape, dt, nbuf=2):
        key = ("ring", name)
        if key not in self.rings:
            self.rings[key] = [[self.ta(shape, dt, f"{name}{i}") for i in range(nbuf)], 0]
        r = self.rings[key]
        b = r[0][r[1]]
        r[1] = (r[1] + 1) % nbuf
        return b
    def mix_D(self, seg, l, NC):
        c = self.c
        last = (seg == self.nseg - 1)
        NT = self.NT
        tiles = self.tiles(seg)
        xnb, xn = self.XN
        m_start = self.ar_off[0]
        VD = self.ta([128, 4, 512], BF16, "VD")
        V8d = self.ta([NS, 512], F32, "V8d") if last else None
        for grp in range(2):
            W = self.next_wa()
            Wv = W[:, :, :, :].rearrange("p a k c -> p (a k c)").rearrange("p (k c) -> p k c", k=KT)
            c.dma("pool", Wv, self.wt[l, grp], writes=[W])
            for blk in range(4):
                ps = self.PS[blk % 2]
                for k in range(KT):
                    c.op("pe", lambda e, k=k, ps=ps, blk=blk: e.matmul(ps[:, 0:256], xn[:, k, blk * 128:(blk + 1) * 128], Wv[:, k, :],
                                                                      start=(k == 0), stop=(k == KT - 1)),
                         reads=[W, xnb], writes=[ps], inc=(k == KT - 1))
                if blk % 2 == 0:
                    c.op("act", lambda e, ps=ps, blk=blk, grp=grp: e.activation(out=VD[:, blk, 256 * grp:256 * grp + 256], in_=ps[:, 0:256],
                                                                             func=AF.Copy), reads=[ps], writes=[VD])
                else:
                    c.op("dve", lambda e, ps=ps, blk=blk, grp=grp: e.tensor_copy(out=VD[:, blk, 256 * grp:256 * grp + 256], in_=ps[:, 0:256]),
                         reads=[ps], writes=[VD])
            if last:
                ps8 = self.PS[6]
                for k in range(KT):
                    c.op("pe", lambda e, k=k: e.matmul(ps8[0:NS, 0:256], xn[:, k, SEG:SEG + NS], Wv[:, k, :],
                                                       start=(k == 0), stop=(k == KT - 1)),
                         reads=[W, xnb], writes=[ps8], inc=(k == KT - 1))
                c.op("act", lambda e, grp=grp: e.activation(out=V8d[:, 256 * grp:256 * grp + 256], in_=ps8[0:NS, 0:256], func=AF.Copy),
                     reads=[ps8], writes=[V8d])
        A = [self.ta([128, NT], F32, f"A{i}") for i in range(6)]
        QTb = self.ta([128, SEG], BF16, "QTb")
        KTb = self.ta([128, SEG], BF16, "KTb")
        KHT = self.ta([128, 4, 128], BF16, "KHT")
        OD = self.ta([128, NT], F32, "OD")
        ATm = [self.ta([128, 128], BF16, f"ATm{i}") for i in range(2)]
        FS = self.ta([128, NS], F32, "FS")
        SGp = self.ta([128, 2, NT], F32, "SGp")
        S0 = self.ta([128, 4, 128], F32, "S0") if last else None
        SN = self.ta([128, 128], F32, "SN") if last else None
        T1 = self.ta([128, 128], F32, "T1") if last else None
        BD = self.CB[:, self.CB_BD:self.CB_BD + 128]
        RST = self.CF[:, self.CF_RST:self.CF_RST + SEG]
        cur = self.sh_par[l]
        for h in range(4):
            if h % 2 == 0:
                def gcons(c0, n, res):
                    for s_, (pb, pap) in enumerate(res):
                        c.op("act", lambda e, s_=s_, pap=pap: e.activation(out=SGp[:, s_, c0:c0 + n], in_=pap, func=AF.Silu),
                             reads=[pb], writes=[SGp])
                self.proj_pair(l, 20 + h // 2, tiles, gcons)
            lb1 = self.LBT[:, l, h, 0:1]
            lbp = self.LBT[:, l, h, 1:2]
            lbn = self.LBT[:, l, h, 2:3]

            def qf(c0, n, res):
                (bq, qp), (bf_, fp) = res
                cs = slice(c0, c0 + n)
                c.op("act", lambda e: e.activation(out=A[0][:, cs], in_=qp, func=AF.Silu), reads=[bq], writes=[A[0]])
                c.op("act", lambda e: e.activation(out=A[1][:, cs], in_=fp, func=AF.Sigmoid), reads=[bf_], writes=[A[1]])
                c.op("dve", lambda e: e.tensor_scalar(out=A[2][:, cs], in0=A[1][:, cs], scalar1=lb1, scalar2=lbp, op0=ALU.mult, op1=ALU.add),
                     reads=[A[1], self.LBT], writes=[A[2]])
                c.op("dve", lambda e: e.tensor_scalar(out=A[3][:, cs], in0=A[1][:, cs], scalar1=lbn, scalar2=lb1, op0=ALU.mult, op1=ALU.add),
                     reads=[A[1], self.LBT], writes=[A[3]])
                if c0 >= SEG:
                    c.op("dve", lambda e: e.tensor_copy(out=FS[:, 0:n], in_=A[2][:, cs]), reads=[A[2]], writes=[FS])
                else:
                    c.op("act", lambda e: e.activation(out=A[2][:, cs], in_=A[2][:, cs], func=AF.Ln), reads=[A[2]], writes=[A[2]])
            self.proj_pair(l, 16 + h, tiles, qf)
            P_ = slice(0, SEG)
            c.op("dve", lambda e: e.tensor_tensor_scan(out=A[1][:, P_], data0=RST, data1=A[2][:, P_], initial=0.0, op0=ALU.mult, op1=ALU.add),
                 reads=[A[2], self.CF], writes=[A[1]])
            c.op("dve", lambda e: e.tensor_scalar(out=A[1][:, P_], in0=A[1][:, P_], scalar1=-80.0, scalar2=None, op0=ALU.max),
                 reads=[A[1]], writes=[A[1]])
            c.op("act", lambda e: e.activation(out=A[2][:, P_], in_=A[1][:, P_], func=AF.Exp), reads=[A[1]], writes=[A[2]])
            c.op("act", lambda e: e.activation(out=A[4][:, P_], in_=A[1][:, P_], func=AF.Exp, scale=-1.0), reads=[A[1]], writes=[A[4]])
            c.op("dve", lambda e: e.tensor_tensor(out=A[5][:, P_], in0=A[0][:, P_], in1=A[2][:, P_], op=ALU.mult),
                 reads=[A[0], A[2]], writes=[A[5]])
            c.op("act", lambda e: e.activation(out=QTb[:, :], in_=A[5][:, P_], func=AF.Copy), reads=[A[5]], writes=[QTb])
            c.op("dve", lambda e: e.tensor_tensor(out=KTb[:, :], in0=A[3][:, P_], in1=A[4][:, P_], op=ALU.mult),
                 reads=[A[3], A[4]], writes=[KTb])
            b3 = A[1][:, P_].rearrange("p (c t) -> p c t", t=32)
            c.op("dve", lambda e: e.tensor_tensor(out=A[4][:, P_].rearrange("p (c t) -> p c t", t=32),
                                                   in0=b3[:, :, 31:32].to_broadcast([128, 16, 32]), in1=b3, op=ALU.subtract),
                 reads=[A[1]], writes=[A[4]])
            c.op("act", lambda e: e.activation(out=A[4][:, P_], in_=A[4][:, P_], func=AF.Exp), reads=[A[4]], writes=[A[4]])
            c.op("dve", lambda e: e.tensor_tensor(out=A[4][:, P_], in0=A[3][:, P_], in1=A[4][:, P_], op=ALU.mult),
                 reads=[A[3], A[4]], writes=[A[4]])
            for blk in range(4):
                pst = self.PS[2 + blk % 2]
                c.op("pe", lambda e, blk=blk, pst=pst: e.transpose(pst[:, 0:128], A[4][:, blk * 128:(blk + 1) * 128], self.idf),
                     reads=[A[4], self.CF], writes=[pst])
                if blk % 2 == 0:
                    c.op("act", lambda e, blk=blk, pst=pst: e.activation(out=KHT[:, blk, :], in_=pst[:, 0:128], func=AF.Copy),
                         reads=[pst], writes=[KHT])
                else:
                    c.op("dve", lambda e, blk=blk, pst=pst: e.tensor_copy(out=KHT[:, blk, :], in_=pst[:, 0:128]), reads=[pst], writes=[KHT])
            hc = slice(128 * h, 128 * h + 128)
            for blk in range(4):
                bs = slice(blk * 128, (blk + 1) * 128)
                pa = self.PS[4]
                po = self.PS[5]
                at = ATm[blk % 2]
                c.op("pe", lambda e, bs=bs: e.matmul(pa[:, 0:128], KTb[:, bs], QTb[:, bs], start=True, stop=True),
                     reads=[KTb, QTb], writes=[pa])
                c.op("dve", lambda e, at=at: e.tensor_tensor(out=at[:, :], in0=pa[:, 0:128], in1=BD, op=ALU.mult),
                     reads=[pa, self.CB], writes=[at])
                c.op("pe", lambda e, at=at, blk=blk: e.matmul(po[:, 0:128], VD[:, blk, hc], at[:, :], start=True, stop=False),
                     reads=[VD, at], writes=[po], inc=True)
                for cc in range(4):
                    pd = self.PS[cc]
                    tp = dict(tile_position=(96, 0)) if cc == 3 else {}
                    c.op("pe", lambda e, cc=cc, blk=blk, pd=pd, tp=tp: e.matmul(
                        pd[:, 0:128], KHT[32 * cc:32 * cc + 32, blk, :], VD[32 * cc:32 * cc + 32, blk, hc],
                        start=True, stop=True, **tp), reads=[KHT, VD], writes=[pd], inc=True)
                for cc in range(4):
                    ch = 4 * blk + cc
                    Sc = self.SH[l][cur]
                    Sn_ = self.SH[l][1 - cur]
                    c.op("pe", lambda e, cc=cc, ch=ch, Sc=Sc: e.matmul(po[:, 32 * cc:32 * cc + 32], Sc[:, h, :], A[5][:, 32 * ch:32 * ch + 32],
                                                                     start=False, stop=(cc == 3)),
                         reads=[Sc, A[5]], writes=[po], inc=True)
                    pd = self.PS[cc]
                    c.op("dve", lambda e, ch=ch, cc=cc, pd=pd, Sc=Sc, Sn_=Sn_: e.scalar_tensor_tensor(
                        out=Sn_[:, h, :], in0=Sc[:, h, :], scalar=A[2][:, 32 * ch + 31:32 * ch + 32], in1=pd[:, 0:128],
                        op0=ALU.mult, op1=ALU.add), reads=[Sc, A[2], pd], writes=[Sn_])
                    cur = 1 - cur
                c.op("act", lambda e, bs=bs: e.activation(out=OD[:, bs], in_=po[:, 0:128], func=AF.Copy), reads=[po], writes=[OD])
            c.op("dve", lambda e, cur=cur: e.tensor_copy(out=self.SH[l][1 - cur][:, h, :], in_=self.SH[l][cur][:, h, :]),
                 reads=[self.SH[l][cur]], writes=[self.SH[l][1 - cur]])
            if last:
                c.dma("sp", self.o_hg_p[l, h], self.SH[l][cur][:, h, :], reads=[self.SH[l][cur]], is_output=True)
                for b in range(NS):
                    col = SEG + b
                    if h == 0 or True:
                        c.dma("sp", S0[:, h, :], self.hs0[:, l, b, h, :], writes=[S0])
                    pv = self.PS[6]
                    c.op("pe", lambda e, b=b: e.matmul(pv[:, 256:384], self.idf[0:NS, b:b + 1].to_broadcast([NS, 128]), V8d[0:NS, hc],
                                                       start=True, stop=True), reads=[self.CF, V8d], writes=[pv])
                    c.op("dve", lambda e, b=b: e.tensor_scalar(out=T1[:, :], in0=S0[:, h, :], scalar1=FS[:, b:b + 1], scalar2=None, op0=ALU.mult),
                         reads=[S0, FS], writes=[T1])
                    c.op("dve", lambda e, col=col: e.scalar_tensor_tensor(out=SN[:, :], in0=pv[:, 256:384], scalar=A[3][:, col:col + 1],
                                                                          in1=T1[:, :], op0=ALU.mult, op1=ALU.add),
                         reads=[pv, A[3], T1], writes=[SN])
                    c.dma("sp", self.o_hg_s[l, b, h], SN[:, :], reads=[SN], is_output=True)
                    c.op("pe", lambda e, col=col: e.matmul(pv[:, 384:385], SN[:, :], A[0][:, col:col + 1], start=True, stop=True),
                         reads=[SN, A[0]], writes=[pv])
                    c.op("act", lambda e, col=col: e.activation(out=OD[:, col:col + 1], in_=pv[:, 384:385], func=AF.Copy),
                         reads=[pv], writes=[OD])
            self.dump(f"od{seg}_{h}", OD, OD[:, 0:512], [128, 512])
            if last:
                self.dump(f"sod{h}", OD, OD[:, 512:520], [128, 8])
            for (c0, n) in tiles:
                cs = slice(c0, c0 + n)
                self.sumsq_rstd(OD, lambda k: OD[:, cs], 1, 128, c0, n, self.PS[7])
                tt = self.tmp[self.tmp_i]
                self.tmp_i ^= 1
                c.op("dve", lambda e, tt=tt, cs=cs, n=n: e.scalar_tensor_tensor(out=tt[:, 0:n], in0=OD[:, cs], scalar=self.PMX[:, l, 14 + h:15 + h],
                                                                        in1=self.rstd[:, cs], op0=ALU.mult, op1=ALU.mult),
                     reads=[OD, self.PMX, self.rstd], writes=[tt])
                c.op("dve", lambda e, tt=tt, cs=cs, n=n: e.tensor_tensor(out=self.MIX[:, 14 + h, cs], in0=tt[:, 0:n], in1=SGp[:, h % 2, cs], op=ALU.mult),
                     reads=[tt, SGp], writes=[self.MIX])
        self.sh_par[l] = cur
        c.barrier()
        self.ar_off[0] = m_start
    MAGIC = 12582912.0
    TWO_PI = 6.283185307179586

    def range_reduce(self, eng, out, src, kt, n=None, reads=(), outb=None, srcb=None, ktb=None):
        c = self.c
        c.op(eng, lambda e: e.tensor_scalar(out=kt, in0=src, scalar1=1.0 / self.TWO_PI, scalar2=self.MAGIC, op0=ALU.mult, op1=ALU.add),
             reads=[srcb], writes=[ktb])
        c.op(eng, lambda e: e.tensor_scalar(out=kt, in0=kt, scalar1=-self.MAGIC, scalar2=None, op0=ALU.add), reads=[ktb], writes=[ktb])
        c.op(eng, lambda e: e.scalar_tensor_tensor(out=out, in0=kt, scalar=-self.TWO_PI, in1=src, op0=ALU.mult, op1=ALU.add),
             reads=[ktb, srcb], writes=[outb])
        c.op(eng, lambda e: e.tensor_scalar(out=out, in0=out, scalar1=-3.14159, scalar2=3.14159, op0=ALU.max, op1=ALU.min),
             reads=[outb], writes=[outb])

    def lam_calc(self, are, aim, ldt, srcb, W, T, want_z):
        c = self.c
        dt, r, th, k, sn, cs, lre, lim = T[:8]
        w = slice(0, W)
        c.op("act", lambda e: e.activation(out=dt[:, w], in_=ldt, func=AF.Exp), reads=[srcb], writes=[dt])
        c.op("dve", lambda e: e.tensor_tensor(out=r[:, w], in0=are, in1=dt[:, w], op=ALU.mult), reads=[srcb, dt], writes=[r])
        c.op("dve", lambda e: e.tensor_tensor(out=th[:, w], in0=aim, in1=dt[:, w], op=ALU.mult), reads=[srcb, dt], writes=[th])
        c.op("act", lambda e: e.activation(out=r[:, w], in_=r[:, w], func=AF.Exp), reads=[r], writes=[r])
        self.range_reduce("dve", th[:, w], th[:, w], k[:, w], outb=th, srcb=th, ktb=k)
        c.op("dve", lambda e: e.tensor_scalar(out=dt[:, w], in0=th[:, w], scalar1=3.141592653589793 / 2, scalar2=None, op0=ALU.add),
             reads=[th], writes=[dt])
        self.range_reduce("dve", dt[:, w], dt[:, w], k[:, w], outb=dt, srcb=dt, ktb=k)
        c.op("act", lambda e: e.activation(out=sn[:, w], in_=th[:, w], func=AF.Sin), reads=[th], writes=[sn])
        c.op("act", lambda e: e.activation(out=cs[:, w], in_=dt[:, w], func=AF.Sin), reads=[dt], writes=[cs])
        c.op("dve", lambda e: e.tensor_tensor(out=lre[:, w], in0=r[:, w], in1=cs[:, w], op=ALU.mult), reads=[r, cs], writes=[lre])
        c.op("dve", lambda e: e.tensor_tensor(out=lim[:, w], in0=r[:, w], in1=sn[:, w], op=ALU.mult), reads=[r, sn], writes=[lim])
        res = dict(r=r, th=th, lre=lre, lim=lim)
        if want_z:
            den, l1, zr, zi = dt, k, sn, cs
            c.op("dve", lambda e: e.tensor_tensor(out=den[:, w], in0=are, in1=are, op=ALU.mult), reads=[srcb], writes=[den])
            c.op("dve", lambda e: e.tensor_tensor(out=l1[:, w], in0=aim, in1=aim, op=ALU.mult), reads=[srcb], writes=[l1])
            c.op("dve", lambda e: e.tensor_tensor(out=den[:, w], in0=den[:, w], in1=l1[:, w], op=ALU.add), reads=[den, l1], writes=[den])
            c.op("dve", lambda e: e.reciprocal(out=den[:, w], in_=den[:, w]), reads=[den], writes=[den])
            c.op("dve", lambda e: e.tensor_scalar(out=l1[:, w], in0=lre[:, w], scalar1=-1.0, scalar2=None, op0=ALU.add), reads=[lre], writes=[l1])
            t1, t2 = T[8], T[9]
            c.op("dve", lambda e: e.tensor_tensor(out=t1[:, w], in0=l1[:, w], in1=are, op=ALU.mult), reads=[l1, srcb], writes=[t1])
            c.op("dve", lambda e: e.tensor_tensor(out=t2[:, w], in0=lim[:, w], in1=aim, op=ALU.mult), reads=[lim, srcb], writes=[t2])
            c.op("dve", lambda e: e.tensor_tensor(out=t1[:, w], in0=t1[:, w], in1=t2[:, w], op=ALU.add), reads=[t1, t2], writes=[t1])
            c.op("dve", lambda e: e.tensor_tensor(out=zr[:, w], in0=t1[:, w], in1=den[:, w], op=ALU.mult), reads=[t1, den], writes=[zr])
            c.op("dve", lambda e: e.tensor_tensor(out=t1[:, w], in0=lim[:, w], in1=are, op=ALU.mult), reads=[lim, srcb], writes=[t1])
            c.op("dve", lambda e: e.tensor_tensor(out=t2[:, w], in0=l1[:, w], in1=aim, op=ALU.mult), reads=[l1, srcb], writes=[t2])
            c.op("dve", lambda e: e.tensor_tensor(out=t1[:, w], in0=t1[:, w], in1=t2[:, w], op=ALU.subtract), reads=[t1, t2], writes=[t1])
            c.op("dve", lambda e: e.tensor_tensor(out=zi[:, w], in0=t1[:, w], in1=den[:, w], op=ALU.mult), reads=[t1, den], writes=[zi])
            res.update(zr=zr, zi=zi)
        return res

    def mix_A(self, seg, l, NC):
        c = self.c
        import os as _os
        self.apool = "pool" if _os.environ.get("A_POOL", "0") == "1" else "dve"
        last = (seg == self.nseg - 1)
        NT = self.NT
        tiles = self.tiles(seg)
        m_start = self.ar_off[0]
        UT = self.ta([128, 4, NT], BF16, "UT")
        ZB = self.ta([128, 4, NT], BF16, "ZB")
        BR = [self.ta([128, 4, 64], F32, f"BR{i}") for i in range(2)]
        CRW = self.ta([128, 448], F32, "CRW")
        LP = [self.ta([128, 14], F32, f"LP{i}") for i in range(10)]
        NLI = self.ta([128, 14], F32, "NLI")
        X0S = self.ta([128, NS, 14, 2], F32, "X0S") if last else None
        XSN = self.ta([128, NS, 14, 2], F32, "XSN") if last else None
        for j in range(2):
            def ucons(c0, n, res, j=j):
                for s_, (pb, pap) in enumerate(res):
                    c.op("act", lambda e, s_=s_, pap=pap: e.activation(out=UT[:, 2 * j + s_, c0:c0 + n], in_=pap, func=AF.Copy),
                         reads=[pb], writes=[UT])
            self.proj_pair(l, j, tiles, ucons)
        m_setup = self.ar_off[0]
        R = self.ta([128, 1728], F32, "R")
        c.dma("sp", R[:, :], self.s5r[:, l, :], writes=[R])
        if last:
            c.dma("sp", X0S[:, :, :, :], self.x0s[:, l], writes=[X0S])
        c.op("dve", lambda e: e.tensor_copy(out=CRW[:, :], in_=R[:, 1280:1728]), reads=[R], writes=[CRW])
        TR = [self.ta([128, 256], F32, f"TR{i}") for i in range(10)]
        zz = self.lam_calc(R[:, 0:256], R[:, 256:512], R[:, 512:768], R, 256, TR, True)
        zr, zi = zz["zr"], zz["zi"]
        t1, t2 = TR[8], TR[9]
        bre, bim = R[:, 768:1024], R[:, 1024:1280]
        br0 = BR[0][:, :, :].rearrange("p a b -> p (a b)")
        br1 = BR[1][:, :, :].rearrange("p a b -> p (a b)")
        c.op("dve", lambda e: e.tensor_tensor(out=t1[:, :], in0=zr[:, :], in1=bre, op=ALU.mult), reads=[zr, R], writes=[t1])
        c.op("dve", lambda e: e.tensor_tensor(out=t2[:, :], in0=zi[:, :], in1=bim, op=ALU.mult), reads=[zi, R], writes=[t2])
        c.op("dve", lambda e: e.tensor_tensor(out=br0, in0=t1[:, :], in1=t2[:, :], op=ALU.subtract), reads=[t1, t2], writes=[BR[0]])
        c.op("dve", lambda e: e.tensor_tensor(out=t1[:, :], in0=zr[:, :], in1=bim, op=ALU.mult), reads=[zr, R], writes=[t1])
        c.op("dve", lambda e: e.tensor_tensor(out=t2[:, :], in0=zi[:, :], in1=bre, op=ALU.mult), reads=[zi, R], writes=[t2])
        c.op("dve", lambda e: e.tensor_tensor(out=br1, in0=t1[:, :], in1=t2[:, :], op=ALU.add), reads=[t1, t2], writes=[BR[1]])
        lp = self.lam_calc(self.PLA[:, l, 0:14], self.PLA[:, l, 14:28], self.PLA[:, l, 28:42], self.PLA, 14, LP, False)
        RR, TH, LRE, LIM = lp["r"], lp["th"], lp["lre"], lp["lim"]
        c.op("dve", lambda e: e.tensor_scalar(out=NLI[:, :], in0=LIM[:, 0:14], scalar1=-1.0, scalar2=None, op0=ALU.mult), reads=[LIM], writes=[NLI])
        c.barrier()
        self.ar_off[0] = m_setup
        CS = self.ta([128, SEG], F32, "CS")
        SNT = self.ta([128, SEG], F32, "SNT")
        G = [self.ta([128, SEG], F32, f"G{i}") for i in range(4)]
        XR = self.ta([128, NT], BF16, "XR")
        XI = self.ta([128, NT], BF16, "XI")
        BBD = [[self.ta([128, 128], BF16, f"BBD{q}{i}") for i in range(2)] for q in range(4)]
        CBD = [[self.ta([128, 128], BF16, f"CBD{q}{i}") for i in range(2)] for q in range(4)]
        DD = self.ta([128, 128], BF16, "DD")
        SM = self.ta([128, 16], F32, "SM")
        IOTA = self.CF[:, self.CF_IOTA:self.CF_IOTA + SEG]
        MB_ = self.CB[:, self.CB_MB:self.CB_MB + 512].rearrange("p (q g s) -> p q g s", q=4, g=2)
        MC_ = self.CB[:, self.CB_MC:self.CB_MC + 512].rearrange("p (q g j) -> p q g j", q=4, g=8)
        X0 = self.X0[l]
        P_ = slice(0, SEG)
        for ci in range(4):
            npair = 4 if ci < 3 else 2
            for q in range(npair):
                p = 4 * ci + q
                for i in range(2):
                    c.op("dve", lambda e, q=q, i=i: e.tensor_tensor(
                        out=BBD[q][i][:, :].rearrange("p (g s) -> p g s", g=2), in0=MB_[:, q, :, :],
                        in1=BR[i][:, ci, :].unsqueeze(1).to_broadcast([128, 2, 64]), op=ALU.mult),
                        reads=[self.CB, BR[i]], writes=[BBD[q][i]])
                crp = CRW[:, 16 * p:16 * p + 16].unsqueeze(1).to_broadcast([128, 8, 16])
                cip = CRW[:, 224 + 16 * p:224 + 16 * p + 16].unsqueeze(1).to_broadcast([128, 8, 16])
                c.op("dve", lambda e, q=q, crp=crp: e.tensor_tensor(out=CBD[q][0][:, :].rearrange("p (g j) -> p g j", g=8),
                                                                    in0=MC_[:, q, :, :], in1=crp, op=ALU.mult),
                     reads=[self.CB, CRW], writes=[CBD[q][0]])
                c.op("dve", lambda e, q=q, cip=cip: e.scalar_tensor_tensor(out=CBD[q][1][:, :].rearrange("p (g j) -> p g j", g=8),
                                                                         in0=cip, scalar=-1.0, in1=MC_[:, q, :, :], op0=ALU.mult, op1=ALU.mult),
                     reads=[self.CB, CRW], writes=[CBD[q][1]])
            c.op("dve", lambda e: e.tensor_scalar(out=DD[:, :], in0=self.idb, scalar1=self.PLA[:, l, 42 + ci:43 + ci], scalar2=None, op0=ALU.mult),
                 reads=[self.CB, self.PLA], writes=[DD])
            yps = [(self.PS[4], self.PS[4][:, 0:SEG])] + ([(self.PS[7], self.PS[7][:, 0:NS])] if last else [])
            for ti, (c0, n) in enumerate(tiles):
                yb_, yap = yps[ti]
                c.op("pe", lambda e, yap=yap, c0=c0, n=n: e.matmul(yap, DD[:, :], UT[:, ci, c0:c0 + n], start=True, stop=False),
                     reads=[DD, UT], writes=[yb_])
            for q in range(npair):
                p = 4 * ci + q
                bps = []
                for ti, (c0, n) in enumerate(tiles):
                    for i in range(2):
                        pb = self.PS[i] if ti == 0 else self.PS[6]
                        pap = pb[:, 0:n] if ti == 0 else pb[:, 16 * i:16 * i + n]
                        c.op("pe", lambda e, pap=pap, i=i, q=q, c0=c0, n=n: e.matmul(pap, BBD[q][i][:, :], UT[:, ci, c0:c0 + n], start=True, stop=True),
                             reads=[BBD[q][i], UT], writes=[pb])
                        bps.append((pb, pap))
                (bre_b, bre_p), (bim_b, bim_p) = bps[0], bps[1]
                thp = TH[:, p:p + 1]
                c.op("dve", lambda e: e.tensor_scalar(out=G[0][:, :], in0=IOTA, scalar1=thp, scalar2=None, op0=ALU.mult), reads=[self.CF, TH], writes=[G[0]])
                self.range_reduce("dve", G[2][:, :], G[0][:, :], G[1][:, :], outb=G[2], srcb=G[0], ktb=G[1])
                c.op("act", lambda e: e.activation(out=SNT[:, :], in_=G[2][:, :], func=AF.Sin), reads=[G[2]], writes=[SNT])
                c.op("dve", lambda e: e.tensor_scalar(out=G[2][:, :], in0=G[2][:, :], scalar1=3.141592653589793 / 2, scalar2=None, op0=ALU.add),
                     reads=[G[2]], writes=[G[2]])
                self.range_reduce("dve", G[3][:, :], G[2][:, :], G[1][:, :], outb=G[3], srcb=G[2], ktb=G[1])
                c.op("act", lambda e: e.activation(out=CS[:, :], in_=G[3][:, :], func=AF.Sin), reads=[G[3]], writes=[CS])
                c.op("dve", lambda e: e.tensor_tensor(out=G[0][:, :], in0=bre_p, in1=CS[:, :], op=ALU.mult), reads=[bre_b, CS], writes=[G[0]])
                c.op("dve", lambda e: e.tensor_tensor(out=G[1][:, :], in0=bim_p, in1=SNT[:, :], op=ALU.mult), reads=[bim_b, SNT], writes=[G[1]])
                c.op(self.apool, lambda e: e.tensor_tensor(out=G[2][:, :], in0=G[0][:, :], in1=G[1][:, :], op=ALU.add), reads=[G[0], G[1]], writes=[G[2]])
                c.op("dve", lambda e: e.tensor_tensor(out=G[0][:, :], in0=bim_p, in1=CS[:, :], op=ALU.mult), reads=[bim_b, CS], writes=[G[0]])
                c.op("dve", lambda e: e.tensor_tensor(out=G[1][:, :], in0=bre_p, in1=SNT[:, :], op=ALU.mult), reads=[bre_b, SNT], writes=[G[1]])
                c.op(self.apool, lambda e: e.tensor_tensor(out=G[3][:, :], in0=G[0][:, :], in1=G[1][:, :], op=ALU.subtract), reads=[G[0], G[1]], writes=[G[3]])
                rb = RR[:, p:p + 1].to_broadcast([128, SEG])
                c.op("dve", lambda e: e.tensor_tensor_scan(out=G[0][:, :], data0=rb, data1=G[2][:, :], initial=X0[:, p, 0:1], op0=ALU.mult, op1=ALU.add),
                     reads=[RR, G[2], X0], writes=[G[0]])
                c.op("dve", lambda e: e.tensor_tensor_scan(out=G[1][:, :], data0=rb, data1=G[3][:, :], initial=X0[:, p, 1:2], op0=ALU.mult, op1=ALU.add),
                     reads=[RR, G[3], X0], writes=[G[1]])
                c.op("dve", lambda e: e.tensor_tensor(out=G[2][:, :], in0=G[0][:, :], in1=CS[:, :], op=ALU.mult), reads=[G[0], CS], writes=[G[2]])
                c.op(self.apool, lambda e: e.tensor_tensor(out=G[3][:, :], in0=G[1][:, :], in1=SNT[:, :], op=ALU.mult), reads=[G[1], SNT], writes=[G[3]])
                c.op(self.apool, lambda e: e.tensor_tensor(out=XR[:, P_], in0=G[2][:, :], in1=G[3][:, :], op=ALU.subtract), reads=[G[2], G[3]], writes=[XR])
                c.op("dve", lambda e: e.tensor_tensor(out=X0[:, p, 0:1], in0=G[2][:, SEG - 1:SEG], in1=G[3][:, SEG - 1:SEG], op=ALU.subtract),
                     reads=[G[2], G[3]], writes=[X0])
                c.op("dve", lambda e: e.tensor_tensor(out=G[2][:, :], in0=G[0][:, :], in1=SNT[:, :], op=ALU.mult), reads=[G[0], SNT], writes=[G[2]])
                c.op(self.apool, lambda e: e.tensor_tensor(out=G[3][:, :], in0=G[1][:, :], in1=CS[:, :], op=ALU.mult), reads=[G[1], CS], writes=[G[3]])
                c.op(self.apool, lambda e: e.tensor_tensor(out=XI[:, P_], in0=G[2][:, :], in1=G[3][:, :], op=ALU.add), reads=[G[2], G[3]], writes=[XI])
                c.op("dve", lambda e: e.tensor_tensor(out=X0[:, p, 1:2], in0=G[2][:, SEG - 1:SEG], in1=G[3][:, SEG - 1:SEG], op=ALU.add),
                     reads=[G[2], G[3]], writes=[X0])
                if last:
                    (sre_b, sre_p), (sim_b, sim_p) = bps[2], bps[3]
                    a_ = SM[:, 0:NS]
                    c.op("dve", lambda e: e.tensor_scalar(out=a_, in0=X0S[:, :, p, 0], scalar1=LRE[:, p:p + 1], scalar2=None, op0=ALU.mult),
                         reads=[X0S, LRE], writes=[SM])
                    c.op("dve", lambda e: e.scalar_tensor_tensor(out=a_, in0=X0S[:, :, p, 1], scalar=NLI[:, p:p + 1], in1=a_, op0=ALU.mult, op1=ALU.add),
                         reads=[X0S, NLI, SM], writes=[SM])
                    c.op("dve", lambda e: e.tensor_tensor(out=XSN[:, :, p, 0], in0=a_, in1=sre_p, op=ALU.add), reads=[SM, sre_b], writes=[XSN])
                    b_ = SM[:, 8:8 + NS]
                    c.op("dve", lambda e: e.tensor_scalar(out=b_, in0=X0S[:, :, p, 1], scalar1=LRE[:, p:p + 1], scalar2=None, op0=ALU.mult),
                         reads=[X0S, LRE], writes=[SM])
                    c.op("dve", lambda e: e.scalar_tensor_tensor(out=b_, in0=X0S[:, :, p, 0], scalar=LIM[:, p:p + 1], in1=b_, op0=ALU.mult, op1=ALU.add),
                         reads=[X0S, LIM, SM], writes=[SM])
                    c.op("dve", lambda e: e.tensor_tensor(out=XSN[:, :, p, 1], in0=b_, in1=sim_p, op=ALU.add), reads=[SM, sim_b], writes=[XSN])
                    c.op("act", lambda e: e.activation(out=XR[:, SEG:SEG + NS], in_=XSN[:, :, p, 0], func=AF.Copy), reads=[XSN], writes=[XR])
                    c.op("act", lambda e: e.activation(out=XI[:, SEG:SEG + NS], in_=XSN[:, :, p, 1], func=AF.Copy), reads=[XSN], writes=[XI])
                for ti, (c0, n) in enumerate(tiles):
                    yb_, yap = yps[ti]
                    lastmm = (q == npair - 1)
                    c.op("pe", lambda e, yap=yap, q=q, c0=c0, n=n: e.matmul(yap, CBD[q][0][:, :], XR[:, c0:c0 + n], start=False, stop=False),
                         reads=[CBD[q][0], XR], writes=[yb_])
                    c.op("pe", lambda e, yap=yap, q=q, c0=c0, n=n, lastmm=lastmm: e.matmul(yap, CBD[q][1][:, :], XI[:, c0:c0 + n], start=False, stop=lastmm),
                         reads=[CBD[q][1], XI], writes=[yb_])
            for ti, (c0, n) in enumerate(tiles):
                yb_, yap = yps[ti]
                if "ya" in self.dbg and ci == 1 and ti == 0:
                    self.dump("ya", yb_, yap, [128, 512])
                c.op("act", lambda e, yap=yap, c0=c0, n=n: e.activation(out=ZB[:, ci, c0:c0 + n], in_=yap, func=AF.Gelu_apprx_tanh),
                     reads=[yb_], writes=[ZB])
        if last:
            c.dma("sp", self.o_ssm_p[:, l], X0[:, :, :], reads=[X0], is_output=True)
            c.dma("sp", self.o_ssm_s[:, l], XSN[:, :, :, :], reads=[XSN], is_output=True)
        W = self.next_wa()
        Wg = W[:, :, :, :].rearrange("p a k c -> p (a k c)")[:, 0:2048].rearrange("p (m k c) -> p m k c", m=4, k=4)
        c.dma("pool", Wg, self.wglu[l].rearrange("m p k c -> p m k c"), writes=[W])
        ZS = self.ta([128, 4, NS], F32, "ZS") if last else None
        for mo in range(4):
            for ti, (c0, n) in enumerate(tiles):
                pb = self.PS[mo % 2] if ti == 0 else self.PS[6]
                pap = pb[:, 0:n]
                for k in range(4):
                    c.op("pe", lambda e, k=k, pap=pap, c0=c0, n=n: e.matmul(pap, Wg[:, mo, k, :], ZB[:, k, c0:c0 + n], start=(k == 0), stop=(k == 3)),
                         reads=[W, ZB], writes=[pb], inc=(k == 3))
                gt = self.tmp[self.tmp_i]
                self.tmp_i ^= 1
                c.op("act", lambda e, gt=gt, pap=pap, n=n: e.activation(out=gt[:, 0:n], in_=pap, func=AF.Sigmoid, bias=self.PMX[:, l, 18 + mo:19 + mo]),
                     reads=[pb, self.PMX], writes=[gt])
                dst_b, dst = (G[mo], G[mo][:, 0:n]) if ti == 0 else (ZS, ZS[:, mo, 0:n])
                c.op("dve", lambda e, gt=gt, dst=dst, c0=c0, n=n: e.tensor_tensor(out=dst, in0=ZB[:, mo, c0:c0 + n], in1=gt[:, 0:n], op=ALU.mult),
                     reads=[ZB, gt], writes=[dst_b])

        class _Multi:
            pass
        srcs = list(G) + ([ZS] if last else [])
        self.mix_norm_multi(seg, srcs, lambda k, c0, n: (G[k][:, c0:c0 + n] if c0 < SEG else ZS[:, k, 0:n]), 4, 448, l, 0, 0)
        c.barrier()
        self.ar_off[0] = m_start

    def mix_norm_multi(self, seg, src_bufs, src_fn, nchunk, n_feat, l, gofs, mbase):
        c = self.c
        for (c0, n) in self.tiles(seg):
            ps = self.PS[7]
            for k in range(nchunk):
                sq = self.sq[self.sq_i]
                self.sq_i ^= 1
                c.op("act", lambda e, k=k, sq=sq: e.activation(out=sq[:, 0:n], in_=src_fn(k, c0, n), func=AF.Square),
                     reads=src_bufs, writes=[sq])
                c.op("pe", lambda e, k=k, sq=sq: e.matmul(ps[:, 0:n], self.ones_bf[:, :], sq[:, 0:n], start=(k == 0), stop=(k == nchunk - 1)),
                     reads=[sq, self.ones_bf], writes=[ps])
            c.op("act", lambda e: e.activation(out=self.rstd[:, c0:c0 + n], in_=ps[:, 0:n], func=AF.Sqrt, bias=self.eps_ap(), scale=1.0 / n_feat),
                 reads=[ps, self.epsb], writes=[self.rstd])
            c.op("dve", lambda e: e.reciprocal(out=self.rstd[:, c0:c0 + n], in_=self.rstd[:, c0:c0 + n]), reads=[self.rstd], writes=[self.rstd])
            for k in range(nchunk):
                c.op("dve", lambda e, k=k: e.scalar_tensor_tensor(
                    out=self.MIX[:, mbase + k, c0:c0 + n], in0=src_fn(k, c0, n), scalar=self.PMX[:, l, gofs + k:gofs + k + 1],
                    in1=self.rstd[:, c0:c0 + n], op0=ALU.mult, op1=ALU.mult), reads=src_bufs + [self.PMX, self.rstd], writes=[self.MIX])
    def mix_out(self, seg, l, NC):
        c = self.c
        tiles = self.tiles(seg)
        yb, y = self.Y
        c.barrier()
        for k in range(MIXK):
            self.dump(f"mix{k}_{seg}", self.MIX, self.MIX[:, k, 0:512], [128, 512])
        for mo in range(KT):
            W = self.next_wa()
            Wv = W[:, :, :, :].rearrange("p a k c -> p (a k c)")[:, 0:MIXK * 128].rearrange("p (k c) -> p k c", k=MIXK)
            c.dma("pool", Wv, self.wout[l, mo], writes=[W])
            for ti, (c0, n) in enumerate(tiles):
                pd = self.PS[4 + (mo % 2)] if ti == 0 else self.PS[6]
                od = 0 if ti == 0 else 32
                for k in range(MIXK):
                    c.op("pe", lambda e, k=k, pd=pd, od=od: e.matmul(pd[:, od:od + n], Wv[:, k, :], self.MIX[:, k, c0:c0 + n],
                                                                  start=(k == 0), stop=(k == MIXK - 1)),
                         reads=[W, self.MIX], writes=[pd], inc=(k == MIXK - 1))
                c.op("act", lambda e, pd=pd, od=od, mo=mo: e.activation(out=y[:, mo, c0:c0 + n], in_=pd[:, od:od + n], func=AF.Copy),
                     reads=[pd], writes=[yb])
        self.dump("ymix", yb, y[:, 3, 0:512], [128, 512])
        self.post_norm_add(seg, (l * 6 + 3) * KT)


U_OFF, QB_OFF, KB_OFF, VB_OFF, QC_OFF, KC_OFF, VC_OFF, QD_OFF, FD_OFF, ID_OFF, GD_OFF = (
    0, 448, 960, 1088, 1216, 1792, 1984, 2176, 2688, 3200, 3712)
NPAIR = 22
MIXK = 18


def _prep_ffn_weights(wg, wu, wd):
    nl = wg.shape[0]
    g = wg.reshape(nl, KT, 128, FT, 128).transpose(0, 3, 2, 1, 4)
    u = wu.reshape(nl, KT, 128, FT, 128).transpose(0, 3, 2, 1, 4)
    gu = np.ascontiguousarray(np.stack([g, u], axis=3))
    d = np.ascontiguousarray(wd.reshape(nl, FT, 128, KT, 128).transpose(0, 3, 2, 1, 4))
    return gu, d


def _gain_layout(vs):
    nl = vs[0].shape[0]
    a = np.stack(vs, axis=1)
    a = a.reshape(nl, 6, KT, 128).transpose(3, 0, 1, 2).reshape(128, nl * 6 * KT)
    return np.ascontiguousarray(a)


def _head(off, h):
    return list(range(off + 64 * h, off + 64 * h + 64))


def _swap(cols):
    return cols[32:] + cols[:32]


def _win_chunks():
    Z = [-1] * 64
    ch = []
    for j in range(4):
        ch.append([cc if cc < 448 else -1 for cc in range(128 * j, 128 * j + 128)])
    for j in range(4):
        ch.append(_head(QB_OFF, j) + _head(QB_OFF, j + 4))
        ch.append(_swap(_head(QB_OFF, j)) + _swap(_head(QB_OFF, j + 4)))
    ch.append(_head(KB_OFF, 0) + _head(KB_OFF, 1))
    ch.append(_swap(_head(KB_OFF, 0)) + _swap(_head(KB_OFF, 1)))
    for g in range(3):
        a, b, cc = _head(QC_OFF, 3 * g), _head(QC_OFF, 3 * g + 1), _head(QC_OFF, 3 * g + 2)
        ch.append(a + b)
        ch.append(_swap(a) + _swap(b))
        ch.append(cc + Z)
        ch.append(_swap(cc) + Z)
    for g in range(3):
        k = _head(KC_OFF, g)
        ch.append(k + k)
        ch.append(_swap(k) + _swap(k))
    for h in range(4):
        ch.append(list(range(QD_OFF + 128 * h, QD_OFF + 128 * h + 128)))
        ch.append(list(range(FD_OFF + 128 * h, FD_OFF + 128 * h + 128)))
    for h in range(4):
        ch.append(list(range(GD_OFF + 128 * h, GD_OFF + 128 * h + 128)))
    assert len(ch) == 2 * NPAIR
    return ch


def _gather_cols(w, cols):
    cols = np.asarray(cols)
    out = w[:, :, np.maximum(cols, 0)]
    out[:, :, cols < 0] = 0.0
    return out


def _mix_rows():
    Z = [-1] * 64
    rows = []
    for j in range(4):
        rows += [r if r < 448 else -1 for r in range(128 * j, 128 * j + 128)]
    ob = 448
    for j in range(4):
        rows += _head(ob, j) + _head(ob, j + 4)
    oc = 448 + 512
    for g in range(3):
        rows += _head(oc, 3 * g) + _head(oc, 3 * g + 1)
        rows += _head(oc, 3 * g + 2) + Z
    od = 448 + 512 + 576
    rows += list(range(od, od + 512))
    assert len(rows) == MIXK * 128
    return np.asarray(rows)


PAST_LEN = 16384
ROPE_THETA = 10000.0


def _consts(nseg):
    ntok = nseg * SEG + NS
    cb = np.zeros((128, Prog.NCB), np.float32)
    qi = np.arange(128)[:, None]
    kj = np.arange(256)[None, :]
    dist = 128 + qi - kj
    cb[:, 0:256] = np.where((dist >= 0) & (dist <= 128), 0.0, NEG)
    cb[:, 256] = 0.0
    cb[:, 257] = NEG
    q = np.arange(128)[:, None]
    col = np.arange(512)[None, :]
    rrq, iq = q // 32, q % 32
    jt, rr = 32 * (col // 128) + (col % 32), (col % 128) // 32
    cb[:, Prog.CB_M2:Prog.CB_M2 + 512] = np.where((rr == rrq) & (jt <= 96 + iq), 0.0, NEG)
    s_ = np.arange(128)[:, None]
    t_ = np.arange(128)[None, :]
    cb[:, Prog.CB_BD:Prog.CB_BD + 128] = ((s_ // 32 == t_ // 32) & (s_ <= t_)).astype(np.float32)
    cb[:, Prog.CB_ID:Prog.CB_ID + 128] = np.eye(128, dtype=np.float32)
    row = np.arange(128)[:, None, None]
    qq = np.arange(4)[None, :, None]
    cc = np.arange(128)[None, None, :]
    cb[:, Prog.CB_MB:Prog.CB_MB + 512] = ((row // 16) == 2 * qq + (cc // 64)).astype(np.float32).reshape(128, 512)
    cb[:, Prog.CB_MC:Prog.CB_MC + 512] = ((cc // 16) == 2 * qq + (row // 64)).astype(np.float32).reshape(128, 512)
    cf = np.zeros((128, Prog.NCF), np.float32)
    cf[:, 0:128] = np.eye(128, dtype=np.float32)
    cf[:, Prog.CF_IOTA:Prog.CF_IOTA + 512] = np.arange(1, 513, dtype=np.float32)[None, :]
    cf[:, Prog.CF_RST:Prog.CF_RST + 512] = (np.arange(512) % 32 != 0).astype(np.float32)[None, :]
    pos = np.concatenate([np.arange(nseg * SEG), np.full(NS, PAST_LEN)]).astype(np.float32)
    inv_freq = (np.float32(ROPE_THETA) ** (-(np.arange(32, dtype=np.float32) / np.float32(32)))).astype(np.float32)
    ang = (pos[:, None] * inv_freq[None, :]).astype(np.float32)
    cos = np.cos(ang).astype(np.float32).T
    sin = np.sin(ang).astype(np.float32).T
    rot = np.zeros((128, 2, ntok), np.float32)
    for p in range(128):
        rot[p, 0] = cos[p % 32]
        rot[p, 1] = sin[p % 32] * (-1.0 if (p % 64) < 32 else 1.0)
    return cb, cf, rot


def _prep_shared(inp, nl):
    sh = {}
    for i, nm in ((1, "ffn1"), (2, "ffn2")):
        gu, d = _prep_ffn_weights(inp[nm + "_w_gate"][:nl], inp[nm + "_w_up"][:nl], inp[nm + "_w_down"][:nl])
        sh[f"wgu{i}"], sh[f"wdn{i}"] = gu, d
    sh["gains"] = _gain_layout([inp[k][:nl] for k in ("ffn1_norm_pre", "ffn1_norm_post", "mix_norm_pre", "mix_norm_post",
                                                     "ffn2_norm_pre", "ffn2_norm_post")])
    w_in = inp["w_in"][:nl]
    ch = _win_chunks()
    wf = np.stack([_gather_cols(w_in, cc) for cc in ch], axis=1)
    wf = wf.reshape(nl, NPAIR, 2, KT, 128, 128).transpose(0, 1, 4, 2, 3, 5)
    sh["win_f"] = np.ascontiguousarray(wf)
    Z = [-1] * 64
    tg = [list(range(ID_OFF, ID_OFF + 256)), list(range(ID_OFF + 256, ID_OFF + 512)),
          list(range(VB_OFF, VB_OFF + 128)) + list(range(VC_OFF, VC_OFF + 128)),
          list(range(VC_OFF + 128, VC_OFF + 192)) + Z * 3]
    wt = np.stack([_gather_cols(w_in, cc) for cc in tg], axis=1)
    sh["wt"] = np.ascontiguousarray(wt.reshape(nl, 4, KT, 128, 256).transpose(0, 1, 3, 2, 4))
    rows = _mix_rows()
    wo = inp["w_out"][:nl][:, np.maximum(rows, 0), :].copy()
    wo[:, rows < 0, :] = 0.0
    sh["wout"] = np.ascontiguousarray(wo.reshape(nl, MIXK, 128, KT, 128).transpose(0, 3, 2, 1, 4))
    wg = np.zeros((nl, 512, 512), np.float32)
    wg[:, :448, :448] = inp["ssm_w_glu"][:nl]
    sh["wglu"] = np.ascontiguousarray(wg.reshape(nl, 4, 128, 4, 128).transpose(0, 3, 2, 1, 4))
    a_re, a_im, ldt = inp["ssm_a_re"][:nl], inp["ssm_a_im"][:nl], inp["ssm_log_dt"][:nl]
    plA = np.zeros((128, nl, 46), np.float32)
    plA[:, :, 0:14] = a_re.reshape(nl, 14, 2, 64).transpose(2, 3, 0, 1).reshape(128, nl, 14)
    plA[:, :, 14:28] = a_im.reshape(nl, 14, 2, 64).transpose(2, 3, 0, 1).reshape(128, nl, 14)
    plA[:, :, 28:42] = np.broadcast_to(ldt.reshape(nl, 14, 2, 1), (nl, 14, 2, 64)).transpose(2, 3, 0, 1).reshape(128, nl, 14)
    dsk = np.zeros((nl, 32, 16), np.float32)
    dsk[:, :28] = inp["ssm_d"][:nl]
    plA[:, :, 42:46] = dsk.reshape(nl, 4, 8, 16).transpose(2, 3, 0, 1).reshape(128, nl, 4)
    sh["plA"] = plA
    pm = np.zeros((128, nl, 30), np.float32)
    gcat = np.concatenate([inp["out_norm_a"][:nl], inp["out_norm_b"][:nl], inp["out_norm_c"][:nl], inp["out_norm_d"][:nl]], axis=1)
    gm = gcat[:, np.maximum(rows, 0)].copy()
    gm[:, rows < 0] = 0.0
    pm[:, :, 0:18] = gm.reshape(nl, MIXK, 128).transpose(2, 0, 1)
    bg = np.zeros((nl, 512), np.float32)
    bg[:, :448] = inp["ssm_b_glu"][:nl]
    pm[:, :, 18:22] = bg.reshape(nl, 4, 128).transpose(2, 0, 1)
    pm[:, :, 22:30] = inp["swa_sinks"][:nl][None, :, :]
    sh["pmix"] = pm
    sh["hlb"] = np.ascontiguousarray(inp["hgrn_lower_bounds"][:nl].reshape(nl, 4, 128).transpose(2, 0, 1))
    s5 = np.zeros((128, nl, 1728), np.float32)

    def rowlay(a, fill=0.0):
        z = np.full((nl, 32, 64), fill, np.float32)
        z[:, :28] = a
        z = z.reshape(nl, 4, 8, 1, 64)
        z = np.broadcast_to(z, (nl, 4, 8, 16, 64))
        return z.transpose(2, 3, 0, 1, 4).reshape(128, nl, 256)
    s5[:, :, 0:256] = rowlay(a_re, -1.0)
    s5[:, :, 256:512] = rowlay(a_im, 1.0)
    s5[:, :, 512:768] = rowlay(np.broadcast_to(ldt[:, :, None], (nl, 28, 64)))

    def rowlay_b(b):
        z = np.zeros((nl, 32, 64, 16), np.float32)
        z[:, :28] = b
        return z.reshape(nl, 4, 8, 64, 16).transpose(2, 4, 0, 1, 3).reshape(128, nl, 256)
    s5[:, :, 768:1024] = rowlay_b(inp["ssm_b_re"][:nl])
    s5[:, :, 1024:1280] = rowlay_b(inp["ssm_b_im"][:nl])

    def crow(cm):
        return cm.reshape(nl, 14, 2, 16, 64).transpose(2, 4, 0, 1, 3).reshape(128, nl, 224)
    s5[:, :, 1280:1504] = crow(inp["ssm_c_re"][:nl])
    s5[:, :, 1504:1728] = crow(inp["ssm_c_im"][:nl])
    sh["s5r"] = s5
    return sh


def _prep_core(inp, nl, nseg, seq, sb):
    d = {}
    x = np.concatenate([inp["x_prompt"][seq, :nseg * SEG], inp["x_sample"][sb:sb + NS, 0]], axis=0)
    d["xT"] = np.ascontiguousarray(x.T)
    ss = inp["state_ssm"][:nl, sb:sb + NS]
    d["x0s"] = np.ascontiguousarray(ss.reshape(nl, NS, 14, 2, 64, 2).transpose(3, 4, 0, 1, 2, 5).reshape(128, nl, NS, 14, 2))
    d["hs0"] = np.ascontiguousarray(inp["state_hgrn"][:nl, sb:sb + NS].transpose(3, 0, 1, 2, 4))
    cs = inp["cache_swa_kv"][:nl, sb:sb + NS]
    kct = np.zeros((128, nl, NS, 4, 128), np.float32)
    vct = np.zeros((128, nl, NS, 320), np.float32)
    kct[:, :, :, 0, :] = cs[:, :, :, 0].reshape(nl, NS, 128, 128).transpose(3, 0, 1, 2)
    vct[:, :, :, 0:128] = cs[:, :, :, 1].reshape(nl, NS, 128, 128).transpose(2, 0, 1, 3)
    for g, (nm, st) in enumerate((("cache_dil0_kv", 1), ("cache_dil1_kv", 4), ("cache_dil2_kv", 16))):
        cg = inp[nm][:nl, sb:sb + NS, ::st]
        kk = cg[:, :, :, 0].transpose(3, 0, 1, 2)
        kct[0:64, :, :, 1 + g, :] = kk
        kct[64:128, :, :, 1 + g, :] = kk
        vct[:, :, :, 128 + 64 * g:192 + 64 * g] = cg[:, :, :, 1].transpose(2, 0, 1, 3)
    d["kct"], d["vct"] = kct, vct
    return d


_CACHE = {}


def _build(nseg, nl, **kw):
    key = (nseg, nl, tuple(sorted(kw.items())))
    if key not in _CACHE:
        p = Prog(nseg=nseg, nlayer=nl, **kw)
        p.build()
        _CACHE[key] = p
    return _CACHE[key]


def kernel(**inputs):
    inp = {k: np.asarray(v) for k, v in inputs.items()}
    nl, nseg = L, NSEG
    prog = _build(nseg, nl)
    cb, cf, rot = _consts(nseg)
    sh = _prep_shared(inp, nl)
    sh.update(cbf=cb, cf32=cf, rot=rot)
    in_maps = []
    for core in range(8):
        seq = core // 2
        d = dict(sh)
        d.update(_prep_core(inp, nl, nseg, seq, 8 * seq))
        in_maps.append(d)
    res = run_bass_kernel_spmd(prog.nc, in_maps, core_ids=list(range(8)))
    R = [res.results[2 * s] for s in range(4)]
    return _assemble(R, nl, nseg)


def _assemble(R, nl, nseg):
    nb = len(R)
    T = nseg * SEG
    f32 = np.float32
    y_p = np.stack([r["yT"][:, :T].T for r in R]).astype(f32)
    y_s = np.concatenate([r["yT"][:, T:T + NS].T for r in R])[:, None, :].astype(f32)
    ssm_p = np.stack([r["o_ssm_p"].reshape(2, 64, nl, 14, 2).transpose(2, 3, 0, 1, 4).reshape(nl, 28, 64, 2) for r in R], axis=1)
    ssm_s = np.concatenate([r["o_ssm_s"].reshape(2, 64, nl, NS, 14, 2).transpose(2, 3, 4, 0, 1, 5).reshape(nl, NS, 28, 64, 2)
                            for r in R], axis=1)
    swa_p = np.stack([np.stack([r["o_swa_k"][:, :, 0:128].transpose(0, 2, 1).reshape(nl, 128, 2, 64),
                                r["o_swa_v"].reshape(nl, 128, 2, 64)], axis=2) for r in R], axis=1)
    swa_s = np.concatenate([np.stack([r["o_swa_k"][:, :, 128:128 + NS].transpose(0, 2, 1).reshape(nl, NS, 2, 64),
                                      r["o_sv"][:, :, 0:128].reshape(nl, NS, 2, 64)], axis=2)[:, :, None] for r in R], axis=1)

    def dil(kname, vfun, npr, vcol):
        p = np.stack([np.stack([r[kname][:, :, 0:npr].transpose(0, 2, 1), vfun(r)], axis=2) for r in R], axis=1)
        s = np.concatenate([np.stack([r[kname][:, :, npr:npr + NS].transpose(0, 2, 1),
                                      r["o_sv"][:, :, vcol:vcol + 64]], axis=2)[:, :, None] for r in R], axis=1)
        return p.astype(f32), s.astype(f32)
    d0_p, d0_s = dil("o_d0_k", lambda r: r["o_d0_v"], 128, 128)
    d1_p, d1_s = dil("o_d1_k", lambda r: r["o_d1_v"].transpose(0, 2, 1, 3).reshape(nl, 512, 64), 512, 192)
    d2_p, d2_s = dil("o_d2_k", lambda r: r["o_d2_v"].reshape(nl, nseg, 4, 4, 32, 64).transpose(0, 1, 4, 2, 3, 5).reshape(nl, T, 64),
                     T, 256)
    hg_p = np.stack([r["o_hg_p"] for r in R], axis=1)
    hg_s = np.concatenate([r["o_hg_s"] for r in R], axis=1)
    outs = (y_p, y_s, ssm_p, ssm_s, swa_p, swa_s, d0_p, d0_s, d1_p, d1_s, d2_p, d2_s, hg_p, hg_s)
    return tuple(np.ascontiguousarray(o, dtype=f32) for o in outs)
```

```python
import numpy as np
from contextlib import ExitStack
import concourse.bass as bass
import concourse.mybir as mybir
from concourse.bass_utils import run_bass_kernel_spmd

F32 = mybir.dt.float32
BF16 = mybir.dt.bfloat16
ALU = mybir.AluOpType
AF = mybir.ActivationFunctionType
AX = mybir.AxisListType

D = 2048
KT = D // 128
DFF = 5504
FT = DFF // 128
FH = (22, 21)
L = 2
SEQ = 2048
SEG = 512
NSEG = SEQ // SEG
NS = 8
EPS = 1e-6
NEG = -1e30


class Buf:
    __slots__ = ("t", "name", "w", "r", "dsem", "dcnt", "psum")

    def __init__(self, t, name):
        self.t = t
        self.name = name
        self.psum = False
        self.w = []
        self.r = []
        self.dsem = None
        self.dcnt = 0

    def __getitem__(self, k):
        return self.t[k]


class EngS:
    def __init__(self, name, eng, sem):
        self.name = name
        self.eng = eng
        self.sem = sem
        self.cnt = 0
        self.waited = {}
        self.pend_r = []
        self.pend_w = []


class Ctx:
    def __init__(self, nc, stack):
        self.nc = nc
        self.stack = stack
        self.E = {}
        for name, eng in (("pe", nc.tensor), ("act", nc.scalar), ("dve", nc.vector),
                          ("pool", nc.gpsimd), ("sp", nc.sync)):
            sem = stack.enter_context(nc.semaphore("s_" + name))
            self.E[name] = EngS(name, eng, sem)
        self.nbuf = 0
        self.out_tokens = []
        self.dma_tokens = {}
        self.nins = 0

    def sbuf(self, shape, dt, name=None):
        self.nbuf += 1
        name = f"{name or 'b'}_{self.nbuf}"
        t = self.stack.enter_context(self.nc.sbuf_tensor(name, list(shape), dt))
        return Buf(t, name)

    def psum(self, shape, dt, name=None):
        self.nbuf += 1
        name = f"{name or 'p'}_{self.nbuf}"
        t = self.stack.enter_context(self.nc.psum_tensor(name, list(shape), dt))
        bb = Buf(t, name)
        bb.psum = True
        return bb

    def view(self, buf, name=None):
        self.nbuf += 1
        return Buf(buf.t, f"{name or 'v'}_{self.nbuf}")

    def _wait(self, es, tokens):
        best = {}
        for (sem, val) in tokens:
            k = id(sem)
            if k not in best or best[k][1] < val:
                best[k] = (sem, val)
        for k, (sem, val) in best.items():
            if es.name == "pe" and sem is es.sem:
                continue
            if es.waited.get(k, 0) >= val:
                continue
            es.eng.wait_ge(sem, val)
            es.waited[k] = val

    def _deps(self, reads, writes, en=None):
        toks = []
        for e in self.E.values():
            if e.name == en:
                continue
            for b in writes:
                if any(b is x for x in e.pend_r) or any(b is x for x in e.pend_w):
                    raise RuntimeError(f"hazard: {b.name} is written while engine {e.name} has un-signalled accesses to it")
            for b in reads:
                if any(b is x for x in e.pend_w):
                    raise RuntimeError(f"hazard: {b.name} is read while engine {e.name} has un-signalled writes to it")
        own = self.E[en].sem if en in self.E else None
        for b in reads:
            toks += b.w
            if b.psum:
                toks += [t for t in b.r if t[0] is not own]
        for b in writes:
            toks += b.w
            toks += b.r
        return toks

    def op(self, en, fn, reads=(), writes=(), inc=True):
        es = self.E[en]
        self._wait(es, self._deps(reads, writes, en))
        ins = fn(es.eng)
        self.nins += 1
        es.pend_r += list(reads)
        es.pend_w += list(writes)
        if inc:
            es.cnt += 1
            ins.then_inc(es.sem, 1)
            tok = (es.sem, es.cnt)
            for b in es.pend_r:
                b.r.append(tok)
                if len(b.r) > 12:
                    b.r = self._compact(b.r)
            for b in es.pend_w:
                b.w = [tok]
                b.r = []
            es.pend_r = []
            es.pend_w = []
        return ins

    @staticmethod
    def _compact(toks):
        best = {}
        for (sem, val) in toks:
            k = id(sem)
            if k not in best or best[k][1] < val:
                best[k] = (sem, val)
        return list(best.values())

    def dma(self, qn, out_ap, in_ap, reads=(), writes=(), is_output=False, owner=None):
        es = self.E[qn]
        self._wait(es, self._deps(reads, writes, qn))
        bufs = list(writes) + list(reads)
        owner = owner or bufs[0]
        if owner.dsem is None:
            owner.dsem = self.stack.enter_context(self.nc.semaphore("d_" + owner.name))
        owner.dcnt += 16
        tok = (owner.dsem, owner.dcnt)
        ins = es.eng.dma_start(out=out_ap, in_=in_ap)
        ins.then_inc(owner.dsem, 16)
        self.nins += 1
        for b in reads:
            b.r.append(tok)
        for b in writes:
            b.w = [tok]
            b.r = []
        self.dma_tokens[id(owner.dsem)] = tok
        if is_output:
            self.out_tokens.append(tok)
        return ins

    def barrier(self, engines=("pe", "act", "dve", "sp")):
        toks = [(e.sem, e.cnt) for e in self.E.values() if e.cnt > 0]
        toks += list(self.dma_tokens.values())
        for n in engines:
            self._wait(self.E[n], toks)

    def finish(self):
        es = self.E["sp"]
        toks = list(self.out_tokens) + list(self.dma_tokens.values())
        for n, e in self.E.items():
            if e.cnt > 0:
                toks.append((e.sem, e.cnt))
        self._wait(es, toks)


class Prog:
    def __init__(self, nseg=NSEG, nlayer=L, with_mix=True, dbg=(), parts=("mix",)):
        self.parts = set(parts)
        self.nseg = nseg
        self.nlayer = nlayer
        self.with_mix = with_mix
        self.dbg = set(dbg)
        self.ntok = nseg * SEG + NS
        self.nc = bass.Bass("TRN2", target_bir_lowering=False)
        self.ins = {}
        self.outs = {}

    def din(self, name, shape):
        t = self.nc.dram_tensor(name, list(shape), F32, kind="ExternalInput").ap()
        self.ins[name] = t
        return t

    def dout(self, name, shape):
        t = self.nc.dram_tensor(name, list(shape), F32, kind="ExternalOutput").ap()
        self.outs[name] = t
        return t

    def build(self):
        nc = self.nc
        nl = self.nlayer
        self.xT = self.din("xT", [D, self.ntok])
        self.yT = self.dout("yT", [D, self.ntok])
        self.wgu = [self.din(f"wgu{i}", [nl, FT, 128, 2, KT, 128]) for i in (1, 2)]
        self.wdn = [self.din(f"wdn{i}", [nl, KT, 128, FT, 128]) for i in (1, 2)]
        self.gains = self.din("gains", [128, nl * 6 * KT])
        if self.with_mix:
            self.declare_mix()
        with ExitStack() as st:
            self.c = c = Ctx(nc, st)
            self.st = st
            self.setup_consts()
            if self.with_mix:
                self.setup_mix_consts()
            for seg in range(self.nseg):
                self.run_segment(seg)
            c.finish()
        return nc

    def setup_consts(self):
        c = self.c
        nl = self.nlayer
        self.G = c.sbuf([128, nl * 6 * KT], F32, "gains")
        c.dma("sp", self.G[:], self.gains, writes=[self.G])
        for l in range(nl):
            for n in (1, 5):
                o = (l * 6 + n) * KT
                c.op("dve", lambda e, o=o: e.tensor_scalar(out=self.G[:, o:o + KT], in0=self.G[:, o:o + KT],
                                                           scalar1=0.5, scalar2=None, op0=ALU.mult),
                     reads=[self.G], writes=[self.G])
        self.ones_bf = c.sbuf([128, 128], BF16, "ones")
        c.op("dve", lambda e: e.memset(self.ones_bf[:], 1.0), writes=[self.ones_bf])
        self.NT = SEG + NS
        self.H = c.sbuf([128, KT, self.NT], F32, "H")
        xn = c.sbuf([128, KT, self.NT], BF16, "XN")
        self.XN = (xn, xn.t)
        self.YB = KT * self.NT * 4 + 1024
        self.AB = max(FH[0], 18) * self.NT * 2
        self.ARENA = c.sbuf([128, (self.YB + self.AB) // 4], F32, "ARENA")
        yb = Buf(self.ARENA.t, "Y")
        self.Y = (yb, self.ARENA.t[:, 0:KT * self.NT].rearrange("p (k t) -> p k t", k=KT))
        self.ACTB = Buf(self.ARENA.t[:, self.YB // 4:(self.YB + FH[0] * self.NT * 2) // 4].bitcast(BF16).rearrange(
            "p (k t) -> p k t", k=FH[0]), "ACT")
        self.WA = [c.sbuf([128, 2, KT, 128], BF16, f"WA{i}") for i in range(3)]
        self.WD = [c.sbuf([128, FH[0], 128], BF16, f"WD{i}") for i in range(2)]
        self.WDR = list(self.WD) + [Buf(w.t[:, :, :, :].rearrange("p a k c -> p (a k c)")[:, 0:FH[0] * 128].rearrange(
            "p (k c) -> p k c", k=FH[0]), f"WAd{i}") for i, w in enumerate(self.WA)]
        self.wa_i = 0
        self.wd_i = 0
        self.PS = [c.psum([128, 512], F32, f"PS{i}") for i in range(8)]
        self.tmp = [c.sbuf([128, 512], F32, f"tmp{i}") for i in range(2)]
        self.tmp_i = 0
        self.sq = [c.sbuf([128, 512], BF16, f"sq{i}") for i in range(2)]
        self.sq_i = 0
        self.rstd = c.sbuf([128, self.NT], F32, "rstd")

    def dump(self, name, buf, ap, shape, dt=F32):
        if name not in self.dbg:
            return
        c = self.c
        t = c.sbuf(shape, F32, "dbg_" + name)
        c.op("act", lambda e: e.activation(out=t[:], in_=ap, func=AF.Copy), reads=[buf], writes=[t])
        o = self.dout("dbg_" + name, shape)
        c.dma("sp", o, t[:], reads=[t], is_output=True)
        self.dbg.discard(name)

    def tiles(self, seg):
        t = [(0, SEG)]
        if seg == self.nseg - 1:
            t.append((SEG, NS))
        return t

    def run_segment(self, seg):
        c = self.c
        c.dma("sp", self.H[:, :, 0:SEG], self.xT[:, seg * SEG:(seg + 1) * SEG].rearrange("(k p) t -> p k t", p=128),
              writes=[self.H])
        if seg == self.nseg - 1:
            c.dma("sp", self.H[:, :, SEG:SEG + NS],
                  self.xT[:, self.nseg * SEG:self.nseg * SEG + NS].rearrange("(k p) t -> p k t", p=128),
                  writes=[self.H])
        for l in range(self.nlayer):
            self.ffn(seg, l, 0)
            if self.with_mix:
                self.mixer(seg, l)
            self.ffn(seg, l, 1)
        c.dma("sp", self.yT[:, seg * SEG:(seg + 1) * SEG].rearrange("(k p) t -> p k t", p=128), self.H[:, :, 0:SEG],
              reads=[self.H], is_output=True)
        if seg == self.nseg - 1:
            c.dma("sp", self.yT[:, self.nseg * SEG:self.nseg * SEG + NS].rearrange("(k p) t -> p k t", p=128),
                  self.H[:, :, SEG:SEG + NS], reads=[self.H], is_output=True)

    def sumsq_rstd(self, src_buf, src_ap_fn, nchunk, n_feat, c0, n, ps, eps=EPS):
        c = self.c
        for k in range(nchunk):
            sq = self.sq[self.sq_i]
            self.sq_i ^= 1
            c.op("act", lambda e, k=k, sq=sq: e.activation(out=sq[:, 0:n], in_=src_ap_fn(k), func=AF.Square),
                 reads=[src_buf], writes=[sq])
            c.op("pe", lambda e, k=k, sq=sq: e.matmul(ps[:, 0:n], self.ones_bf[:, :], sq[:, 0:n],
                                                      start=(k == 0), stop=(k == nchunk - 1)),
                 reads=[sq, self.ones_bf], writes=[ps], inc=True)
        c.op("act", lambda e: e.activation(out=self.rstd[:, c0:c0 + n], in_=ps[:, 0:n], func=AF.Sqrt,
                                           bias=self.eps_ap(), scale=1.0 / n_feat),
             reads=[ps, self.epsb], writes=[self.rstd])
        c.op("dve", lambda e: e.reciprocal(out=self.rstd[:, c0:c0 + n], in_=self.rstd[:, c0:c0 + n]),
             reads=[self.rstd], writes=[self.rstd])

    def eps_ap(self):
        if not hasattr(self, "epsb"):
            self.epsb = self.c.sbuf([128, 1], F32, "eps")
            self.c.op("dve", lambda e: e.memset(self.epsb[:], EPS), writes=[self.epsb])
        return self.epsb[:, 0:1]

    def pre_norm(self, seg, gofs):
        c = self.c
        self.eps_ap()
        xnb, xn = self.XN
        for (c0, n) in self.tiles(seg):
            ps = self.PS[7]
            self.sumsq_rstd(self.H, lambda k: self.H[:, k, c0:c0 + n], KT, D, c0, n, ps)
            for k in range(KT):
                c.op("dve", lambda e, k=k: e.scalar_tensor_tensor(
                    out=xn[:, k, c0:c0 + n], in0=self.H[:, k, c0:c0 + n], scalar=self.G[:, gofs + k:gofs + k + 1],
                    in1=self.rstd[:, c0:c0 + n], op0=ALU.mult, op1=ALU.mult),
                    reads=[self.H, self.G, self.rstd], writes=[xnb])

    def post_norm_add(self, seg, gofs):
        c = self.c
        yb, y = self.Y
        for (c0, n) in self.tiles(seg):
            ps = self.PS[7]
            self.sumsq_rstd(yb, lambda k: y[:, k, c0:c0 + n], KT, D, c0, n, ps)
            for k in range(KT):
                t = self.tmp[self.tmp_i]
                self.tmp_i ^= 1
                c.op("dve", lambda e, k=k, t=t: e.scalar_tensor_tensor(
                    out=t[:, 0:n], in0=y[:, k, c0:c0 + n], scalar=self.G[:, gofs + k:gofs + k + 1],
                    in1=self.rstd[:, c0:c0 + n], op0=ALU.mult, op1=ALU.mult),
                    reads=[yb, self.G, self.rstd], writes=[t])
                c.op("dve", lambda e, k=k, t=t: e.tensor_tensor(
                    out=self.H[:, k, c0:c0 + n], in0=self.H[:, k, c0:c0 + n], in1=t[:, 0:n], op=ALU.add),
                    reads=[t, self.H], writes=[self.H])

    def ffn(self, seg, l, which):
        c = self.c
        tiles = self.tiles(seg)
        gbase = (l * 6 + (0 if which == 0 else 4)) * KT
        self.pre_norm(seg, gbase)
        xnb, xn = self.XN
        yb, y = self.Y
        self.dump("rstd", self.rstd, self.rstd[:, 0:512], [128, 512])
        self.dump("xn", xnb, xn[:, 3, 0:512], [128, 512])
        wgu = self.wgu[which]
        wdn = self.wdn[which]
        m0 = 0
        for half in range(2):
            nm = FH[half]
            for mi in range(nm):
                m = m0 + mi
                W = self.WA[self.wa_i]
                self.wa_i = (self.wa_i + 1) % len(self.WA)
                c.dma("pool", W[:, :, :, :], wgu[l, m], writes=[W])
                for ti, (c0, n) in enumerate(tiles):
                    pg = self.PS[0 + (mi % 2)] if ti == 0 else self.PS[6]
                    pu = self.PS[2 + (mi % 2)] if ti == 0 else self.PS[6]
                    og = 0 if ti == 0 else 0
                    ou = 0 if ti == 0 else 16
                    for k in range(KT):
                        c.op("pe", lambda e, k=k, pg=pg, og=og: e.matmul(pg[:, og:og + n], W[:, 0, k, :], xn[:, k, c0:c0 + n],
                                                                      start=(k == 0), stop=(k == KT - 1)),
                             reads=[W, xnb], writes=[pg], inc=(k == KT - 1))
                    for k in range(KT):
                        c.op("pe", lambda e, k=k, pu=pu, ou=ou: e.matmul(pu[:, ou:ou + n], W[:, 1, k, :], xn[:, k, c0:c0 + n],
                                                                      start=(k == 0), stop=(k == KT - 1)),
                             reads=[W, xnb], writes=[pu], inc=(k == KT - 1))
                    t = self.tmp[self.tmp_i]
                    self.tmp_i ^= 1
                    c.op("act", lambda e, t=t, pg=pg, og=og: e.activation(out=t[:, 0:n], in_=pg[:, og:og + n], func=AF.Silu),
                         reads=[pg], writes=[t])
                    c.op("dve", lambda e, t=t, pu=pu, ou=ou, mi=mi: e.tensor_tensor(
                        out=self.ACTB[:, mi, c0:c0 + n], in0=t[:, 0:n], in1=pu[:, ou:ou + n], op=ALU.mult),
                        reads=[t, pu], writes=[self.ACTB])
            self.dump("act", self.ACTB, self.ACTB[:, 1, 0:512], [128, 512])
            for mo in range(KT):
                W = self.WDR[self.wd_i]
                self.wd_i = (self.wd_i + 1) % len(self.WDR)
                wa_alias = self.WA[self.wd_i - 3] if False else None
                if W.name.startswith("WAd"):
                    wa = self.WA[int(W.name[3])]
                    c.dma("pool", W[:, 0:nm, :], wdn[l, mo, :, m0:m0 + nm, :], writes=[wa], owner=wa)
                    W = Buf(W.t, W.name)
                    Wdep = wa
                else:
                    c.dma("pool", W[:, 0:nm, :], wdn[l, mo, :, m0:m0 + nm, :], writes=[W])
                    Wdep = W
                for ti, (c0, n) in enumerate(tiles):
                    pd = self.PS[4 + (mo % 2)] if ti == 0 else self.PS[6]
                    od = 0 if ti == 0 else 32
                    for k in range(nm):
                        c.op("pe", lambda e, k=k, pd=pd, od=od: e.matmul(pd[:, od:od + n], W[:, k, :], self.ACTB[:, k, c0:c0 + n],
                                                                      start=(k == 0), stop=(k == nm - 1)),
                             reads=[Wdep, self.ACTB], writes=[pd], inc=(k == nm - 1))
                    if half == 0:
                        c.op("act", lambda e, pd=pd, od=od, mo=mo: e.activation(out=y[:, mo, c0:c0 + n], in_=pd[:, od:od + n], func=AF.Copy),
                             reads=[pd, xnb], writes=[yb])
                    else:
                        c.op("dve", lambda e, pd=pd, od=od, mo=mo: e.tensor_tensor(out=y[:, mo, c0:c0 + n], in0=y[:, mo, c0:c0 + n],
                                                                                  in1=pd[:, od:od + n], op=ALU.add),
                             reads=[pd, yb], writes=[yb])
            m0 += nm
        self.dump("y", yb, y[:, 2, 0:512], [128, 512])
        self.post_norm_add(seg, gbase + KT)

    NCB = 258 + 512 + 128 + 128 + 512 + 512
    NCF = 128 + 512 + 512
    CB_BAND, CB_M2, CB_BD, CB_ID, CB_MB, CB_MC = 0, 258, 770, 898, 1026, 1538
    CF_ID, CF_IOTA, CF_RST = 0, 128, 640

    def declare_mix(self):
        nl = self.nlayer
        self.win_f = self.din("win_f", [nl, NPAIR, 128, 2, KT, 128])
        self.wt = self.din("wt", [nl, 4, 128, KT, 256])
        self.wout = self.din("wout", [nl, KT, 128, MIXK, 128])
        self.wglu = self.din("wglu", [nl, 4, 128, 4, 128])
        self.rot = self.din("rot", [128, 2, self.ntok])
        self.cbf = self.din("cbf", [128, self.NCB])
        self.cf32 = self.din("cf32", [128, self.NCF])
        self.plA = self.din("plA", [128, nl, 46])
        self.pmix = self.din("pmix", [128, nl, 30])
        self.hlb = self.din("hlb", [128, nl, 4])
        self.s5r = self.din("s5r", [128, nl, 1728])
        self.x0s = self.din("x0s", [128, nl, NS, 14, 2])
        self.hs0 = self.din("hs0", [128, nl, NS, 4, 128])
        self.kct = self.din("kct", [128, nl, NS, 4, 128])
        self.vct = self.din("vct", [128, nl, NS, 320])
        self.o_ssm_p = self.dout("o_ssm_p", [128, nl, 14, 2])
        self.o_ssm_s = self.dout("o_ssm_s", [128, nl, NS, 14, 2])
        self.o_swa_k = self.dout("o_swa_k", [nl, 128, 128 + NS])
        self.o_swa_v = self.dout("o_swa_v", [nl, 128, 128])
        self.o_sv = self.dout("o_sv", [nl, NS, 320])
        self.o_d0_k = self.dout("o_d0_k", [nl, 64, 128 + NS])
        self.o_d0_v = self.dout("o_d0_v", [nl, 128, 64])
        self.o_d1_k = self.dout("o_d1_k", [nl, 64, 512 + NS])
        self.o_d1_v = self.dout("o_d1_v", [nl, 4, 128, 64])
        self.o_d2_k = self.dout("o_d2_k", [nl, 64, self.nseg * SEG + NS])
        self.o_d2_v = self.dout("o_d2_v", [nl, self.nseg, 4, 128, 64])
        self.o_hg_p = self.dout("o_hg_p", [nl, 4, 128, 128])
        self.o_hg_s = self.dout("o_hg_s", [nl, NS, 4, 128, 128])

    def setup_mix_consts(self):
        c = self.c
        nl = self.nlayer
        NT = self.NT
        self.CB = c.sbuf([128, self.NCB], BF16, "CB")
        c.dma("pool", self.CB[:], self.cbf, writes=[self.CB])
        self.CF = c.sbuf([128, self.NCF], F32, "CF")
        c.dma("sp", self.CF[:], self.cf32, writes=[self.CF])
        self.idb = self.CB[:, self.CB_ID:self.CB_ID + 128]
        self.idf = self.CF[:, self.CF_ID:self.CF_ID + 128]
        self.PLA = c.sbuf([128, nl, 46], F32, "PLA")
        c.dma("sp", self.PLA[:], self.plA, writes=[self.PLA])
        self.PMX = c.sbuf([128, nl, 30], F32, "PMX")
        c.dma("sp", self.PMX[:], self.pmix, writes=[self.PMX])
        self.HLB = c.sbuf([128, nl, 4], F32, "HLB")
        c.dma("sp", self.HLB[:], self.hlb, writes=[self.HLB])
        self.SK8 = c.sbuf([1, 8], BF16, "SK8")
        self.LBT = c.sbuf([128, nl, 4, 3], F32, "LBT")
        mx = c.sbuf([128, 4], F32, "lbmx")
        ex = c.sbuf([128, nl, 4], F32, "lbex")
        sm = c.sbuf([128, 4], F32, "lbsm")
        c.op("dve", lambda e: e.tensor_copy(out=mx[:], in_=self.HLB[:, 0, :]), reads=[self.HLB], writes=[mx])
        for l in range(1, nl):
            c.op("dve", lambda e, l=l: e.tensor_tensor(out=mx[:], in0=mx[:], in1=self.HLB[:, l, :], op=ALU.max),
                 reads=[self.HLB, mx], writes=[mx])
        for l in range(nl):
            c.op("dve", lambda e, l=l: e.tensor_tensor(out=ex[:, l, :], in0=self.HLB[:, l, :], in1=mx[:], op=ALU.subtract),
                 reads=[self.HLB, mx], writes=[ex])
        c.op("act", lambda e: e.activation(out=ex[:], in_=ex[:], func=AF.Exp), reads=[ex], writes=[ex])
        c.op("dve", lambda e: e.tensor_copy(out=sm[:], in_=ex[:, 0, :]), reads=[ex], writes=[sm])
        for l in range(1, nl):
            c.op("dve", lambda e, l=l: e.tensor_tensor(out=sm[:], in0=sm[:], in1=ex[:, l, :], op=ALU.add),
                 reads=[ex, sm], writes=[sm])
        c.op("dve", lambda e: e.reciprocal(out=sm[:], in_=sm[:]), reads=[sm], writes=[sm])
        for l in range(nl):
            c.op("dve", lambda e, l=l: e.tensor_tensor(out=ex[:, l, :], in0=ex[:, l, :], in1=sm[:], op=ALU.mult),
                 reads=[ex, sm], writes=[ex])
        cum = c.sbuf([128, 4], F32, "lbcum")
        c.op("dve", lambda e: e.memset(cum[:], 0.0), writes=[cum])
        for l in range(nl):
            if l > 0:
                c.op("dve", lambda e, l=l: e.tensor_tensor(out=cum[:], in0=cum[:], in1=ex[:, l, :], op=ALU.add),
                     reads=[ex, cum], writes=[cum])
            c.op("dve", lambda e, l=l: e.tensor_scalar(out=self.LBT[:, l, :, 0], in0=cum[:], scalar1=-1.0, scalar2=1.0,
                                                       op0=ALU.mult, op1=ALU.add), reads=[cum], writes=[self.LBT])
            c.op("dve", lambda e, l=l: e.tensor_scalar(out=self.LBT[:, l, :, 1], in0=cum[:], scalar1=1e-30, scalar2=None,
                                                       op0=ALU.max), reads=[cum], writes=[self.LBT])
            c.op("dve", lambda e, l=l: e.tensor_scalar(out=self.LBT[:, l, :, 2], in0=cum[:], scalar1=1.0, scalar2=-1.0,
                                                       op0=ALU.mult, op1=ALU.add), reads=[cum], writes=[self.LBT])
        self.KB = [[c.sbuf([128, NT], BF16, f"KB{l}{p}") for p in range(2)] for l in range(nl)]
        self.KC0 = [[c.sbuf([128, NT], BF16, f"KC0{l}{p}") for p in range(2)] for l in range(nl)]
        self.KC1 = [[c.sbuf([128, NT], BF16, f"KC1{l}{p}") for p in range(2)] for l in range(nl)]
        self.KC2 = [c.sbuf([128, self.nseg * SEG + NS], BF16, f"KC2{l}") for l in range(nl)]
        self.VBh = [[c.sbuf([128, 4, 128], BF16, f"VB{l}{p}") for p in range(2)] for l in range(nl)]
        self.VC0h = [[c.sbuf([128, 4, 64], BF16, f"VC0{l}{p}") for p in range(2)] for l in range(nl)]
        self.VC1h = [[c.sbuf([128, 4, 64], BF16, f"VC1{l}{p}") for p in range(2)] for l in range(nl)]
        self.VC2 = [c.sbuf([128, 4, self.nseg, 64], BF16, f"VC2{l}") for l in range(nl)]
        self.X0 = [c.sbuf([128, 14, 2], F32, f"X0{l}") for l in range(nl)]
        self.SH = [[c.sbuf([128, 4, 128], F32, f"SH{l}{p}") for p in range(2)] for l in range(nl)]
        for l in range(nl):
            c.op("dve", lambda e, l=l: e.memset(self.X0[l][:], 0.0), writes=[self.X0[l]])
            c.op("dve", lambda e, l=l: e.memset(self.SH[l][0][:], 0.0), writes=[self.SH[l][0]])
        self.sh_par = [0] * nl
        self.MIX = Buf(self.ARENA.t[:, self.YB // 4:(self.YB + MIXK * NT * 2) // 4].bitcast(BF16).rearrange(
            "p (k t) -> p k t", k=MIXK), "MIX")
        self.stat = [c.sbuf([128, 8], F32, f"stat{i}") for i in range(4)]

    def ar_reset(self):
        self.ar_off = [0, 0]
        self.rings = {}

    def ta(self, shape, dt, name, pool=0):
        esz = 2 if dt == BF16 else 4
        n = 1
        for s in shape[1:]:
            n *= s
        nbytes = (n * esz + 3) // 4 * 4
        off = self.ar_off[pool]
        lim = self.YB if pool == 0 else KT * self.NT * 2
        assert off + nbytes <= lim, f"arena pool {pool} overflow at {name}: {off}+{nbytes} > {lim}"
        self.ar_off[pool] = off + nbytes
        if pool == 0:
            base = self.ARENA.t[:, off // 4:(off + nbytes) // 4]
        else:
            base = self.XN[0].t.rearrange("p k t -> p (k t)")[:, off // 2:(off + nbytes) // 2].bitcast(F32)
        P = shape[0]
        ap = base[0:P, :]
        if dt == BF16:
            ap = ap.bitcast(BF16)
        ap = ap[:, 0:n]
        if len(shape) == 3:
            ap = ap.rearrange("p (a b) -> p a b", a=shape[1])
        elif len(shape) == 4:
            ap = ap.rearrange("p (a b c) -> p a b c", a=shape[1], b=shape[2])
        self.c.nbuf += 1
        return Buf(ap, f"{name}_{self.c.nbuf}")

    def next_wa(self):
        W = self.WA[self.wa_i]
        self.wa_i = (self.wa_i + 1) % len(self.WA)
        return W

    def proj_pair(self, l, pr, tiles, consume, which=(0, 1)):
        c = self.c
        xnb, xn = self.XN
        W = self.next_wa()
        c.dma("pool", W[:, :, :, :], self.win_f[l, pr], writes=[W])
        for ti, (c0, n) in enumerate(tiles):
            res = []
            for s in which:
                if ti == 0:
                    pb = self.PS[2 * s + self.pp[s]]
                    self.pp[s] ^= 1
                    o = 0
                else:
                    pb = self.PS[6]
                    o = 64 * s
                for k in range(KT):
                    c.op("pe", lambda e, k=k, pb=pb, o=o, s=s: e.matmul(pb[:, o:o + n], W[:, s, k, :], xn[:, k, c0:c0 + n],
                                                                      start=(k == 0), stop=(k == KT - 1)),
                         reads=[W, xnb], writes=[pb], inc=(k == KT - 1))
                res.append((pb, pb[:, o:o + n]))
            consume(c0, n, res)

    def rotary(self, c0, n, res, dst_buf, dst_ap, f32_buf=None, f32_ap=None):
        c = self.c
        (b0, x), (b1, xs) = res
        t1 = self.tmp[0]
        t2 = self.tmp[1]
        c.op("dve", lambda e: e.tensor_tensor(out=t1[:, 0:n], in0=x, in1=self.ROT[:, 0, c0:c0 + n], op=ALU.mult),
             reads=[b0, self.ROT], writes=[t1])
        c.op("dve", lambda e: e.tensor_tensor(out=t2[:, 0:n], in0=xs, in1=self.ROT[:, 1, c0:c0 + n], op=ALU.mult),
             reads=[b1, self.ROT], writes=[t2])
        if f32_buf is None:
            c.op("dve", lambda e: e.tensor_tensor(out=dst_ap, in0=t1[:, 0:n], in1=t2[:, 0:n], op=ALU.add),
                 reads=[t1, t2], writes=[dst_buf])
        else:
            c.op("dve", lambda e: e.tensor_tensor(out=f32_ap, in0=t1[:, 0:n], in1=t2[:, 0:n], op=ALU.add),
                 reads=[t1, t2], writes=[f32_buf])
            c.op("act", lambda e: e.activation(out=dst_ap, in_=f32_ap, func=AF.Copy), reads=[f32_buf], writes=[dst_buf])

    def mix_norm(self, seg, src_buf, src_fn, nchunk, n_feat, l, gofs, mbase):
        c = self.c
        for (c0, n) in self.tiles(seg):
            ps = self.PS[7]
            self.sumsq_rstd(src_buf, lambda k: src_fn(k, c0, n), nchunk, n_feat, c0, n, ps)
            for k in range(nchunk):
                c.op("dve", lambda e, k=k: e.scalar_tensor_tensor(
                    out=self.MIX[:, mbase + k, c0:c0 + n], in0=src_fn(k, c0, n),
                    scalar=self.PMX[:, l, gofs + k:gofs + k + 1], in1=self.rstd[:, c0:c0 + n],
                    op0=ALU.mult, op1=ALU.mult), reads=[src_buf, self.PMX, self.rstd], writes=[self.MIX])

    def mixer(self, seg, l):
        c = self.c
        last = (seg == self.nseg - 1)
        NC = SEG + NS if last else SEG
        gm = (l * 6 + 2) * KT
        c.barrier()
        self.ar_reset()
        self.pp = [0, 0]
        self.pre_norm(seg, gm)
        c.op("act", lambda e: e.activation(out=self.SK8[0:1, :], in_=self.PMX[0:1, l, 22:30], func=AF.Copy, scale=8.0),
             reads=[self.PMX], writes=[self.SK8])
        if "mix" in self.parts or "D" in self.parts:
            self.mix_D(seg, l, NC)
        if "mix" in self.parts or "A" in self.parts:
            self.mix_A(seg, l, NC)
        if "mix" in self.parts or "BC" in self.parts:
            self.mix_BC(seg, l, NC)
        if "mix" in self.parts:
            self.mix_out(seg, l, NC)
        c.barrier()
    def set_slots(self, mode):
        if getattr(self, "_slot_mode", None) not in (None, mode):
            self.c.barrier()
        self._slot_mode = mode
        R, T = self.P32R, self.PTR
        sl = []
        if mode == "small":
            for s in range(4):
                P = Buf(R[:, 260 * s:260 * s + 260], f"P32s{s}")
                PT = Buf(T[:, 256 * s:256 * s + 256].rearrange("p (a b) -> p a b", a=2), f"PTs{s}")
                sl.append(dict(ps=self.PS[s], pst=self.PS[4 + s // 2], tc0=256 * (s % 2), pso=self.PS[6 + s // 2], oc0=256 * (s % 2),
                               P=P, PT=PT, st=self.stat[s], eng=("act" if s % 2 == 0 else "dve")))
        else:
            for s in range(2):
                P = Buf(R[:, 520 * s:520 * s + 514], f"P32b{s}")
                PT = Buf(T[:, 512 * s:512 * s + 512].rearrange("p (a b) -> p a b", a=4), f"PTb{s}")
                sl.append(dict(ps=self.PS[3 * s], pst=self.PS[3 * s + 1], tc0=0, pso=self.PS[3 * s + 2], oc0=0,
                               P=P, PT=PT, st=self.stat[s], eng=("act" if s == 0 else "dve")))
        self.AS = sl

    def attn_units(self, units):
        n = len(self.AS)
        for i in range(0, len(units), n):
            batch = units[i:i + n]
            for s, u in enumerate(batch):
                self.au_scores(u, s)
            for stage in range(7):
                for s, u in enumerate(batch):
                    self.au_softmax(u, s, stage)
            for s, u in enumerate(batch):
                self.au_transpose(u, s)
            for s, u in enumerate(batch):
                self.au_pv(u, s)
            for s, u in enumerate(batch):
                self.au_out(u, s)

    def au_scores(self, u, s):
        c = self.c
        nq, ncols = u["nq"], u["ncols"]
        ps = self.AS[s]["ps"]
        if u.get("sink") is not None:
            hh = u["sink"]
            nkeys = u["nkeys"]
            c.op("pe", lambda e: e.matmul(ps[0:nq, nkeys:nkeys + 1], self.ones_bf[0:1, 0:nq], self.SK8[0:1, hh:hh + 1], start=True, stop=True),
                 reads=[self.ones_bf, self.SK8], writes=[ps], inc=False)
        first = True
        if u["mask"] is not None:
            mb, map_ = u["mask"]
            nkk = u["nkeys"]
            c.op("pe", lambda e: e.matmul(ps[0:nq, 0:nkk], self.idb[0:nq, 0:nq], map_, start=True, stop=False),
                 reads=[self.CB, mb], writes=[ps], inc=False)
            first = False
        col = 0
        nkb = len(u["kblocks"])
        for i, (kb, kap, nk, rr) in enumerate(u["kblocks"]):
            o = ps[0:nq, col:col + nk]
            if rr:
                o = o.rearrange("p (s r j) -> p s r j", s=rr[0], r=rr[1])
            stp = True if first else (i == nkb - 1)
            c.op("pe", lambda e, o=o, kap=kap, stp=stp: e.matmul(o, u["q"][1], kap, start=first, stop=stp),
                 reads=[u["q"][0], kb], writes=[ps], inc=(i == nkb - 1))
            col += nk

    def au_softmax(self, u, s, stage):
        c = self.c
        nq, ncols, nkeys = u["nq"], u["ncols"], u["nkeys"]
        A_ = self.AS[s]
        ps, st, P = A_["ps"], A_["st"], A_["P"]
        if stage == 0:
            c.op("dve", lambda e: e.reduce_max(out=st[0:nq, 0:1], in_=ps[0:nq, 0:ncols], axis=AX.X), reads=[ps], writes=[st])
        elif stage == 1:
            c.op("dve", lambda e: e.tensor_scalar(out=st[0:nq, 1:2], in0=st[0:nq, 0:1], scalar1=-0.125, scalar2=None, op0=ALU.mult),
                 reads=[st], writes=[st])
        elif stage == 2:
            c.op("act", lambda e: e.activation(out=P[0:nq, 0:ncols], in_=ps[0:nq, 0:ncols], func=AF.Exp, bias=st[0:nq, 1:2],
                                               scale=0.125, accum_out=st[0:nq, 2:3]), reads=[ps, st], writes=[P, st])
        elif stage == 3:
            c.op("dve", lambda e: e.reciprocal(out=st[0:nq, 3:4], in_=st[0:nq, 2:3]), reads=[st], writes=[st])
        elif stage == 4:
            c.op("dve", lambda e: e.tensor_scalar(out=P[0:nq, 0:nkeys], in0=P[0:nq, 0:nkeys], scalar1=st[0:nq, 3:4], scalar2=None,
                                                  op0=ALU.mult), reads=[P, st], writes=[P])
        elif stage == 5 and u.get("lse") is not None:
            c.op("act", lambda e: e.activation(out=st[0:nq, 4:5], in_=st[0:nq, 2:3], func=AF.Ln), reads=[st], writes=[st])
        elif stage == 6 and u.get("lse") is not None:
            c.op("dve", lambda e: e.scalar_tensor_tensor(out=st[0:nq, 5:6], in0=st[0:nq, 0:1], scalar=0.125, in1=st[0:nq, 4:5],
                                                         op0=ALU.mult, op1=ALU.add), reads=[st], writes=[st])

    def au_transpose(self, u, s):
        c = self.c
        nq = u["nq"]
        A_ = self.AS[s]
        P, pst, PT, t0 = A_["P"], A_["pst"], A_["PT"], A_["tc0"]
        for i, vb in enumerate(u["vblocks"]):
            if vb["kind"] != "T":
                continue
            nk, k0 = vb["nk"], vb["k0"]
            dst = pst[0:nk, t0 + i * 128:t0 + i * 128 + nq]
            c.op("pe", lambda e, dst=dst, nk=nk, k0=k0: e.transpose(dst, P[0:nq, k0:k0 + nk], self.idf[0:nq, 0:nq]),
                 reads=[P, self.CF], writes=[pst])
            if A_["eng"] == "act":
                c.op("act", lambda e, i=i, nk=nk, dst=dst: e.activation(out=PT[0:nk, i, 0:nq], in_=dst, func=AF.Copy), reads=[pst], writes=[PT])
            else:
                c.op("dve", lambda e, i=i, nk=nk, dst=dst: e.tensor_copy(out=PT[0:nk, i, 0:nq], in_=dst), reads=[pst], writes=[PT])

    def au_pv(self, u, s):
        c = self.c
        nq, hf = u["nq"], u["half"]
        A_ = self.AS[s]
        P, PT, pso, st, o0 = A_["P"], A_["PT"], A_["pso"], A_["st"], A_["oc0"]
        rows = slice(64 * hf, 64 * hf + 64)
        nvb = len(u["vblocks"])
        for i, vb in enumerate(u["vblocks"]):
            if vb["kind"] == "T":
                rhs = PT[0:vb["nk"], i, 0:nq]
                rd = [PT, vb["buf"]]
            else:
                rhs = P[0:1, vb["k0"]:vb["k0"] + 1]
                rd = [P, vb["buf"]]
            c.op("pe", lambda e, vb=vb, rhs=rhs, i=i: e.matmul(pso[rows, o0:o0 + nq], vb["ap"], rhs, start=(i == 0), stop=(i == nvb - 1)),
                 reads=rd, writes=[pso], inc=(i == nvb - 1))
        if u.get("lse") is not None:
            c.op("pe", lambda e: e.matmul(pso[:, o0 + 128:o0 + 128 + nq], st[0:nq, 5:6].to_broadcast([nq, 128]), self.idf[0:nq, 0:nq],
                                          start=True, stop=True), reads=[st, self.CF], writes=[pso])

    def au_out(self, u, s):
        c = self.c
        nq, hf = u["nq"], u["half"]
        pso, o0 = self.AS[s]["pso"], self.AS[s]["oc0"]
        rows = slice(64 * hf, 64 * hf + 64)
        ob, oap = u["out"]
        src = pso[rows, o0:o0 + nq]
        if u.get("orr"):
            src = src.rearrange("p (r j) -> p r j", r=u["orr"])
        c.op("act", lambda e: e.activation(out=oap, in_=src, func=AF.Copy), reads=[pso], writes=[ob])
        if u.get("lse") is not None:
            lb, lap = u["lse"]
            src2 = pso[rows, o0 + 128:o0 + 128 + nq]
            if u.get("orr"):
                src2 = src2.rearrange("p (r j) -> p r j", r=u["orr"])
            c.op("act", lambda e: e.activation(out=lap, in_=src2, func=AF.Copy), reads=[pso], writes=[lb])

    def mix_BC(self, seg, l, NC):
        c = self.c
        last = (seg == self.nseg - 1)
        NT = self.NT
        tiles = self.tiles(seg)
        xnb, xn = self.XN
        par = seg % 2
        KBc, KBp = self.KB[l][par], self.KB[l][1 - par]
        KC0c, KC0p = self.KC0[l][par], self.KC0[l][1 - par]
        KC1c, KC1p = self.KC1[l][par], self.KC1[l][1 - par]
        KC2 = self.KC2[l]
        VBc, VBp = self.VBh[l][par], self.VBh[l][1 - par]
        VC0c, VC0p = self.VC0h[l][par], self.VC0h[l][1 - par]
        VC1c, VC1p = self.VC1h[l][par], self.VC1h[l][1 - par]
        VC2 = self.VC2[l]
        QCT = self.ta([128, 6, NT], BF16, "QCT")
        V8 = self.ta([NS, 320], F32, "V8") if last else None
        self.P32R = self.ta([128, 1040], F32, "P32R")
        self.PTR = self.ta([128, 1024], BF16, "PTR")
        self._slot_mode = None
        if last:
            for nm, shp, dt in (("KCTb", [128, 4, 128], BF16), ("VCTb", [128, 320], BF16), ("VSb", [1, 320], F32)):
                self.ta_ring(nm, shp, dt)
        mark0 = self.ar_off[0]
        QBT = self.ta([128, 4, NT], BF16, "QBT")
        mark1 = self.ar_off[0]
        KF = self.ta([128, NT], F32, "KF")
        VF = self.ta([128, 192], F32, "VF")
        self.ROT = self.ta([128, 2, NT], F32, "ROT")
        c.dma("sp", self.ROT[:, :, 0:SEG], self.rot[:, :, seg * SEG:(seg + 1) * SEG], writes=[self.ROT])
        if last:
            c.dma("sp", self.ROT[:, :, SEG:SEG + NS], self.rot[:, :, self.nseg * SEG:self.nseg * SEG + NS], writes=[self.ROT])
        for j in range(4):
            self.proj_pair(l, 2 + j, tiles, lambda c0, n, res, j=j: self.rotary(c0, n, res, QBT, QBT[:, j, c0:c0 + n]))
        for j in range(6):
            self.proj_pair(l, 7 + j, tiles, lambda c0, n, res, j=j: self.rotary(c0, n, res, QCT, QCT[:, j, c0:c0 + n]))
        import os as _os
        _stopat = _os.environ.get("STOPAT", "")
        if _stopat == "q":
            return
        self.proj_pair(l, 6, tiles, lambda c0, n, res: self.rotary(c0, n, res, KBc, KBc[:, c0:c0 + n], KF, KF[:, c0:c0 + n]))
        if last:
            c.dma("sp", self.o_swa_k[l, :, 0:128], KF[:, SEG - 128:SEG], reads=[KF], is_output=True)
            c.dma("sp", self.o_swa_k[l, :, 128:128 + NS], KF[:, SEG:SEG + NS], reads=[KF], is_output=True)
        self.proj_pair(l, 13, tiles, lambda c0, n, res: self.rotary(c0, n, res, KC0c, KC0c[:, c0:c0 + n], KF, KF[:, c0:c0 + n]))
        if last:
            c.dma("sp", self.o_d0_k[l, :, 0:128], KF[0:64, SEG - 128:SEG], reads=[KF], is_output=True)
            c.dma("sp", self.o_d0_k[l, :, 128:128 + NS], KF[0:64, SEG:SEG + NS], reads=[KF], is_output=True)
        self.proj_pair(l, 14, tiles, lambda c0, n, res: self.rotary(c0, n, res, KC1c, KC1c[:, c0:c0 + n], KF, KF[:, c0:c0 + n]))
        if last:
            c.dma("sp", self.o_d1_k[l, :, 0:SEG + NS], KF[0:64, 0:SEG + NS], reads=[KF], is_output=True)

        def k2(c0, n, res):
            g0 = seg * SEG + c0 if c0 < SEG else self.nseg * SEG + (c0 - SEG)
            self.rotary(c0, n, res, KC2, KC2[:, g0:g0 + n], KF, KF[:, c0:c0 + n])
        self.proj_pair(l, 15, tiles, k2)
        c.dma("sp", self.o_d2_k[l, :, seg * SEG:(seg + 1) * SEG], KF[0:64, 0:SEG], reads=[KF], is_output=True)
        if last:
            c.dma("sp", self.o_d2_k[l, :, self.nseg * SEG:self.nseg * SEG + NS], KF[0:64, SEG:SEG + NS], reads=[KF],
                  is_output=True)
        if _stopat == "k":
            return
        W2 = self.next_wa()
        W2v = W2[:, :, :, :].rearrange("p a k c -> p (a k c)").rearrange("p (k c) -> p k c", k=KT)
        c.dma("pool", W2v, self.wt[l, 2], writes=[W2])
        for blk in range(4):
            ps = self.PS[blk % 2]
            for k in range(KT):
                c.op("pe", lambda e, k=k, ps=ps, blk=blk: e.matmul(ps[:, 0:192], xn[:, k, blk * 128:(blk + 1) * 128], W2v[:, k, 0:192],
                                                                  start=(k == 0), stop=(k == KT - 1)),
                     reads=[W2, xnb], writes=[ps], inc=(k == KT - 1))
            c.op("act", lambda e, ps=ps, blk=blk: e.activation(out=VBc[:, blk, :], in_=ps[:, 0:128], func=AF.Copy), reads=[ps], writes=[VBc])
            c.op("act", lambda e, ps=ps, blk=blk: e.activation(out=VC0c[:, blk, :], in_=ps[:, 128:192], func=AF.Copy), reads=[ps], writes=[VC0c])
            if last and blk == 3 and not _os.environ.get("NOVOUT"):
                c.op("act", lambda e, ps=ps: e.activation(out=VF[:, 0:192], in_=ps[:, 0:192], func=AF.Copy), reads=[ps], writes=[VF])
                c.dma("sp", self.o_swa_v[l], VF[:, 0:128], reads=[VF], is_output=True)
                c.dma("sp", self.o_d0_v[l], VF[:, 128:192], reads=[VF], is_output=True)
        if _stopat == "v1":
            return
        for r in range(4):
            ps = self.PS[r % 2]
            for k in range(KT):
                c.op("pe", lambda e, k=k, ps=ps, r=r: e.matmul(ps[:, 0:64], xn[:, k, r:SEG:4], W2v[:, k, 192:256],
                                                              start=(k == 0), stop=(k == KT - 1)),
                     reads=[W2, xnb], writes=[ps], inc=(k == KT - 1))
            c.op("dve", lambda e, ps=ps, r=r: e.tensor_copy(out=VC1c[:, r, :], in_=ps[:, 0:64]), reads=[ps], writes=[VC1c])
            if last:
                c.op("act", lambda e, ps=ps: e.activation(out=VF[:, 0:64], in_=ps[:, 0:64], func=AF.Copy), reads=[ps], writes=[VF])
                c.dma("sp", self.o_d1_v[l, r], VF[:, 0:64], reads=[VF], is_output=True)
        if last:
            ps8 = self.PS[6]
            for k in range(KT):
                c.op("pe", lambda e, k=k: e.matmul(ps8[0:NS, 0:256], xn[:, k, SEG:SEG + NS], W2v[:, k, 0:256],
                                                   start=(k == 0), stop=(k == KT - 1)),
                     reads=[W2, xnb], writes=[ps8], inc=(k == KT - 1))
        if _stopat == "v2":
            return
        W3 = self.next_wa()
        W3v = W3[:, :, :, :].rearrange("p a k c -> p (a k c)").rearrange("p (k c) -> p k c", k=KT)
        c.dma("pool", W3v, self.wt[l, 3], writes=[W3])
        for r0 in range(4):
            ps = self.PS[2 + r0 % 2]
            for rr in range(4):
                for k in range(KT):
                    c.op("pe", lambda e, k=k, ps=ps, rr=rr, r0=r0: e.matmul(
                        ps[32 * rr:32 * rr + 32, 0:64], xn[:, k, 4 * r0 + rr:SEG:16], W3v[:, k, 0:64],
                        start=(k == 0), stop=(k == KT - 1), tile_position=(0, 32 * rr)),
                        reads=[W3, xnb], writes=[ps], inc=(k == KT - 1))
            c.op("dve", lambda e, ps=ps, r0=r0: e.tensor_copy(out=VC2[:, r0, seg, :], in_=ps[:, 0:64]), reads=[ps], writes=[VC2])
            c.op("act", lambda e, ps=ps: e.activation(out=VF[:, 64:128], in_=ps[:, 0:64], func=AF.Copy), reads=[ps], writes=[VF])
            c.dma("sp", self.o_d2_v[l, seg, r0], VF[:, 64:128], reads=[VF], is_output=True)
        if last:
            for k in range(KT):
                c.op("pe", lambda e, k=k: e.matmul(ps8[0:NS, 256:320], xn[:, k, SEG:SEG + NS], W3v[:, k, 0:64],
                                                   start=(k == 0), stop=(k == KT - 1)),
                     reads=[W3, xnb], writes=[ps8], inc=(k == KT - 1))
            c.op("act", lambda e: e.activation(out=V8[:, :], in_=ps8[0:NS, 0:320], func=AF.Copy), reads=[ps8], writes=[V8])
            c.dma("sp", self.o_sv[l], V8[:, :], reads=[V8], is_output=True)
        if "stop_proj" in self.parts:
            return
        c.barrier()
        self.ar_off[0] = mark1
        OBT = self.ta([128, 4, NT], BF16, "OBT")
        first_seq = (seg == 0)
        units = []
        for h in range(8):
            hf, j = h // 4, h % 4
            rows = slice(64 * hf, 64 * hf + 64)
            for qb in range(4):
                u = dict(nq=128, half=hf, q=(QBT, QBT[rows, j, qb * 128:(qb + 1) * 128]))
                vcol = slice(64 * hf, 64 * hf + 64)
                if qb > 0:
                    u["kblocks"] = [(KBc, KBc[rows, (qb - 1) * 128:(qb + 1) * 128], 256, 0)]
                    u["vblocks"] = [dict(kind="T", buf=VBc, ap=VBc[:, qb - 1, vcol], nk=128, k0=0),
                                    dict(kind="T", buf=VBc, ap=VBc[:, qb, vcol], nk=128, k0=128)]
                    u["mask"] = (self.CB, self.CB[:, 0:256])
                    u["nkeys"], u["ncols"] = 256, 257
                elif not first_seq:
                    u["kblocks"] = [(KBp, KBp[rows, SEG - 128:SEG], 128, 0), (KBc, KBc[rows, 0:128], 128, 0)]
                    u["vblocks"] = [dict(kind="T", buf=VBp, ap=VBp[:, 3, vcol], nk=128, k0=0),
                                    dict(kind="T", buf=VBc, ap=VBc[:, 0, vcol], nk=128, k0=128)]
                    u["mask"] = (self.CB, self.CB[:, 0:256])
                    u["nkeys"], u["ncols"] = 256, 257
                else:
                    u["kblocks"] = [(KBc, KBc[rows, 0:128], 128, 0)]
                    u["vblocks"] = [dict(kind="T", buf=VBc, ap=VBc[:, 0, vcol], nk=128, k0=0)]
                    u["mask"] = (self.CB, self.CB[:, 128:256])
                    u["nkeys"], u["ncols"] = 128, 129
                u["out"] = (OBT, OBT[rows, j, qb * 128:(qb + 1) * 128])
                u["sink"] = h
                units.append(u)
        self.set_slots("small")
        self.attn_units(units)
        if last:
            self.sample_attn(l, QBT, QCT, KBc, (KC0c, KC1c, KC2), V8, OBT, None, None, which="B")
        self.dump(f"obt{seg}", OBT, OBT[:, 1, 0:512], [128, 512])
        if last:
            self.dump("sob", OBT, OBT[:, 1, 512:520], [128, 8])
        self.mix_norm(seg, OBT, lambda k, c0, n: OBT[:, k, c0:c0 + n], 4, 512, l, 4, 4)
        if "stop_B" in self.parts:
            return
        for chn in (4, 5):
            tq = self.tmp[chn % 2]
            c.op("dve", lambda e, chn=chn, tq=tq: e.tensor_copy(
                out=tq[:, 0:SEG].rearrange("p (a r j) -> p a r j", a=4, r=4),
                in_=QCT[:, chn, 0:SEG].rearrange("p (j a r) -> p a r j", a=4, r=4)), reads=[QCT], writes=[tq])
            c.op("dve", lambda e, chn=chn, tq=tq: e.tensor_copy(out=QCT[:, chn, 0:SEG], in_=tq[:, 0:SEG]), reads=[tq], writes=[QCT])
        c.barrier()
        self.ar_off[0] = mark0
        OCT = self.ta([128, 6, NT], F32, "OCT", pool=1)
        LST = self.ta([128, 6, NT], F32, "LST", pool=0)
        c.op("dve", lambda e: e.memset(OCT[:, :, :], 0.0), writes=[OCT])
        c.op("dve", lambda e: e.memset(LST[:, :, :], 0.0), writes=[LST])
        units = []
        band = (self.CB, self.CB[:, 0:256])
        bandf = (self.CB, self.CB[:, 128:256])
        for i in range(3):
            hf = 1 if i == 1 else 0
            rows = slice(64 * hf, 64 * hf + 64)
            ch = 0 + (1 if i == 2 else 0)
            for qb in range(4):
                u = dict(nq=128, half=hf, q=(QCT, QCT[rows, ch, qb * 128:(qb + 1) * 128]))
                if qb > 0:
                    u["kblocks"] = [(KC0c, KC0c[rows, (qb - 1) * 128:(qb + 1) * 128], 256, 0)]
                    u["vblocks"] = [dict(kind="T", buf=VC0c, ap=VC0c[:, qb - 1, :], nk=128, k0=0),
                                    dict(kind="T", buf=VC0c, ap=VC0c[:, qb, :], nk=128, k0=128)]
                    u["mask"], u["nkeys"], u["ncols"] = band, 256, 256
                elif not first_seq:
                    u["kblocks"] = [(KC0p, KC0p[rows, SEG - 128:SEG], 128, 0), (KC0c, KC0c[rows, 0:128], 128, 0)]
                    u["vblocks"] = [dict(kind="T", buf=VC0p, ap=VC0p[:, 3, :], nk=128, k0=0),
                                    dict(kind="T", buf=VC0c, ap=VC0c[:, 0, :], nk=128, k0=128)]
                    u["mask"], u["nkeys"], u["ncols"] = band, 256, 256
                else:
                    u["kblocks"] = [(KC0c, KC0c[rows, 0:128], 128, 0)]
                    u["vblocks"] = [dict(kind="T", buf=VC0c, ap=VC0c[:, 0, :], nk=128, k0=0)]
                    u["mask"], u["nkeys"], u["ncols"] = bandf, 128, 128
                u["out"] = (OCT, OCT[rows, ch, qb * 128:(qb + 1) * 128])
                u["lse"] = (LST, LST[rows, ch, qb * 128:(qb + 1) * 128])
                units.append(u)
            ch = 2 + (1 if i == 2 else 0)
            for r in range(4):
                u = dict(nq=128, half=hf, q=(QCT, QCT[rows, ch, r:SEG:4]))
                if not first_seq:
                    u["kblocks"] = [(KC1p, KC1p[rows, r:SEG:4], 128, 0), (KC1c, KC1c[rows, r:SEG:4], 128, 0)]
                    u["vblocks"] = [dict(kind="T", buf=VC1p, ap=VC1p[:, r, :], nk=128, k0=0),
                                    dict(kind="T", buf=VC1c, ap=VC1c[:, r, :], nk=128, k0=128)]
                    u["mask"], u["nkeys"], u["ncols"] = band, 256, 256
                else:
                    u["kblocks"] = [(KC1c, KC1c[rows, r:SEG:4], 128, 0)]
                    u["vblocks"] = [dict(kind="T", buf=VC1c, ap=VC1c[:, r, :], nk=128, k0=0)]
                    u["mask"], u["nkeys"], u["ncols"] = bandf, 128, 128
                u["out"] = (OCT, OCT[rows, ch, r:SEG:4])
                u["lse"] = (LST, LST[rows, ch, r:SEG:4])
                units.append(u)
            ch = 4 + (1 if i == 2 else 0)
            nkt = 128 * (seg + 1)
            for r0 in range(4):
                u = dict(nq=128, half=hf, q=(QCT, QCT[rows, ch, 128 * r0:128 * r0 + 128]))
                kap = KC2[rows, 0:SEG * (seg + 1)].rearrange("p (s j r) -> p s r j", s=seg + 1, r=16)[:, :, 4 * r0:4 * r0 + 4, :]
                u["kblocks"] = [(KC2, kap, nkt, (seg + 1, 4))]
                u["vblocks"] = [dict(kind="T", buf=VC2, ap=VC2[:, r0, sg, :], nk=128, k0=128 * sg) for sg in range(seg + 1)]
                u["mask"] = (self.CB, self.CB[:, self.CB_M2 + 512 - nkt:self.CB_M2 + 512])
                u["nkeys"], u["ncols"] = nkt, nkt
                u["out"] = (OCT, OCT[rows, ch, 0:SEG].rearrange("p (j r) -> p r j", r=16)[:, 4 * r0:4 * r0 + 4, :])
                u["lse"] = (LST, LST[rows, ch, 0:SEG].rearrange("p (j r) -> p r j", r=16)[:, 4 * r0:4 * r0 + 4, :])
                u["orr"] = 4
                units.append(u)
        self.set_slots("small")
        self.attn_units([u for u in units if not u.get("orr")])
        if last:
            self.sample_attn(l, QBT, QCT, KBc, (KC0c, KC1c, KC2), V8, None, OCT, LST, which="C")
        self.set_slots("big")
        self.attn_units([u for u in units if u.get("orr")])
        self.dump(f"oct_raw{seg}", OCT, OCT[:, 2, 0:512], [128, 512])
        self.dump(f"lst{seg}", LST, LST[:, 2, 0:512], [128, 512])
        ta_, tb_ = self.tmp[0], self.tmp[1]
        E = [self.ta([128, 256], F32, f"E{g}", pool=1) for g in range(3)]
        ctiles = [(0, 256), (256, 256)] + ([(SEG, NS)] if last else [])
        for ls in range(2):
            chs = [2 * g + ls for g in range(3)]
            for (c0, n) in ctiles:
                cs = slice(c0, c0 + n)
                c.op("dve", lambda e, cs=cs, n=n: e.tensor_tensor(out=ta_[:, 0:n], in0=LST[:, chs[0], cs], in1=LST[:, chs[1], cs], op=ALU.max),
                     reads=[LST], writes=[ta_])
                c.op("dve", lambda e, cs=cs, n=n: e.tensor_tensor(out=ta_[:, 0:n], in0=ta_[:, 0:n], in1=LST[:, chs[2], cs], op=ALU.max),
                     reads=[LST, ta_], writes=[ta_])
                for g in range(3):
                    c.op("dve", lambda e, g=g, cs=cs, n=n: e.tensor_tensor(out=E[g][:, 0:n], in0=LST[:, chs[g], cs], in1=ta_[:, 0:n], op=ALU.subtract),
                         reads=[LST, ta_], writes=[E[g]])
                    c.op("act", lambda e, g=g, n=n: e.activation(out=E[g][:, 0:n], in_=E[g][:, 0:n], func=AF.Exp), reads=[E[g]], writes=[E[g]])
                c.op("dve", lambda e, n=n: e.tensor_tensor(out=tb_[:, 0:n], in0=E[0][:, 0:n], in1=E[1][:, 0:n], op=ALU.add),
                     reads=[E[0], E[1]], writes=[tb_])
                c.op("dve", lambda e, n=n: e.tensor_tensor(out=tb_[:, 0:n], in0=tb_[:, 0:n], in1=E[2][:, 0:n], op=ALU.add),
                     reads=[E[2], tb_], writes=[tb_])
                c.op("dve", lambda e, n=n: e.reciprocal(out=tb_[:, 0:n], in_=tb_[:, 0:n]), reads=[tb_], writes=[tb_])
                for g in range(3):
                    c.op("dve", lambda e, g=g, n=n: e.tensor_tensor(out=E[g][:, 0:n], in0=E[g][:, 0:n], in1=tb_[:, 0:n], op=ALU.mult),
                         reads=[E[g], tb_], writes=[E[g]])
                    c.op("dve", lambda e, g=g, cs=cs, n=n: e.tensor_tensor(out=OCT[:, chs[g], cs], in0=OCT[:, chs[g], cs], in1=E[g][:, 0:n], op=ALU.mult),
                         reads=[OCT, E[g]], writes=[OCT])
        self.dump(f"oct{seg}", OCT, OCT[:, 2, 0:512], [128, 512])
        if last:
            self.dump("soc", OCT, OCT[:, 2, 512:520], [128, 8])
        self.mix_norm(seg, OCT, lambda k, c0, n: OCT[:, k, c0:c0 + n], 6, 576, l, 8, 8)

    def sample_attn(self, l, QBT, QCT, KBc, KCs, V8, OBT, OCT, LST, which):
        c = self.c
        for b in range(NS):
            col = SEG + b
            KCTb = self.ta_ring("KCTb", [128, 4, 128], BF16)
            VCTb = self.ta_ring("VCTb", [128, 320], BF16)
            VSb = self.ta_ring("VSb", [1, 320], F32)
            kf, vf = self.tmp[0], self.tmp[1]
            c.dma("sp", kf[:, 0:512].rearrange("p (a b) -> p a b", a=4), self.kct[:, l, b], writes=[kf])
            c.dma("sp", vf[:, 0:320], self.vct[:, l, b], writes=[vf])
            c.op("act", lambda e: e.activation(out=KCTb[:, :, :], in_=kf[:, 0:512].rearrange("p (a b) -> p a b", a=4), func=AF.Copy),
                 reads=[kf], writes=[KCTb])
            c.op("dve", lambda e: e.tensor_copy(out=VCTb[:, :], in_=vf[:, 0:320]), reads=[vf], writes=[VCTb])
            psr = self.PS[6]
            c.op("pe", lambda e, b=b: e.matmul(psr[0:1, 0:320], self.idf[0:NS, b:b + 1], V8[0:NS, 0:320], start=True, stop=True),
                 reads=[self.CF, V8], writes=[psr])
            c.op("act", lambda e: e.activation(out=VSb[0:1, :], in_=psr[0:1, 0:320], func=AF.Copy), reads=[psr], writes=[VSb])
            units = []
            if which == "B":
                for h in range(8):
                    hf, j = h // 4, h % 4
                    rows = slice(64 * hf, 64 * hf + 64)
                    u = dict(nq=1, half=hf, q=(QBT, QBT[rows, j, col:col + 1]))
                    u["kblocks"] = [(KCTb, KCTb[rows, 0, :], 128, 0), (KBc, KBc[rows, col:col + 1], 1, 0)]
                    u["vblocks"] = [dict(kind="T", buf=VCTb, ap=VCTb[:, 64 * hf:64 * hf + 64], nk=128, k0=0),
                                    dict(kind="D", buf=VSb, ap=VSb[0:1, 64 * hf:64 * hf + 64], nk=1, k0=128)]
                    u["mask"] = None
                    u["sink"] = h
                    u["nkeys"], u["ncols"] = 129, 130
                    u["out"] = (OBT, OBT[rows, j, col:col + 1])
                    units.append(u)
            else:
                for g in range(3):
                    Kg = KCs[g]
                    kcol = col if g < 2 else self.nseg * SEG + b
                    for i in range(3):
                        hf = 1 if i == 1 else 0
                        rows = slice(64 * hf, 64 * hf + 64)
                        ch = 2 * g + (1 if i == 2 else 0)
                        u = dict(nq=1, half=hf, q=(QCT, QCT[rows, ch, col:col + 1]))
                        u["kblocks"] = [(KCTb, KCTb[rows, 1 + g, :], 128, 0), (Kg, Kg[rows, kcol:kcol + 1], 1, 0)]
                        u["vblocks"] = [dict(kind="T", buf=VCTb, ap=VCTb[:, 128 + 64 * g:192 + 64 * g], nk=128, k0=0),
                                        dict(kind="D", buf=VSb, ap=VSb[0:1, 128 + 64 * g:192 + 64 * g], nk=1, k0=128)]
                        u["mask"] = None
                        u["nkeys"], u["ncols"] = 129, 129
                        u["out"] = (OCT, OCT[rows, ch, col:col + 1])
                        u["lse"] = (LST, LST[rows, ch, col:col + 1])
                        units.append(u)
            self.attn_units(units)

    def ta_ring(self, name, shape, dt, nbuf=2):
        key = ("ring", name)
        if key not in self.rings:
            self.rings[key] = [[self.ta(shape, dt, f"{name}{i}") for i in range(nbuf)], 0]
        r = self.rings[key]
        b = r[0][r[1]]
        r[1] = (r[1] + 1) % nbuf
        return b
    def mix_D(self, seg, l, NC):
        c = self.c
        last = (seg == self.nseg - 1)
        NT = self.NT
        tiles = self.tiles(seg)
        xnb, xn = self.XN
        m_start = self.ar_off[0]
        VD = self.ta([128, 4, 512], BF16, "VD")
        V8d = self.ta([NS, 512], F32, "V8d") if last else None
        for grp in range(2):
            W = self.next_wa()
            Wv = W[:, :, :, :].rearrange("p a k c -> p (a k c)").rearrange("p (k c) -> p k c", k=KT)
            c.dma("pool", Wv, self.wt[l, grp], writes=[W])
            for blk in range(4):
                ps = self.PS[blk % 2]
                for k in range(KT):
                    c.op("pe", lambda e, k=k, ps=ps, blk=blk: e.matmul(ps[:, 0:256], xn[:, k, blk * 128:(blk + 1) * 128], Wv[:, k, :],
                                                                      start=(k == 0), stop=(k == KT - 1)),
                         reads=[W, xnb], writes=[ps], inc=(k == KT - 1))
                if blk % 2 == 0:
                    c.op("act", lambda e, ps=ps, blk=blk, grp=grp: e.activation(out=VD[:, blk, 256 * grp:256 * grp + 256], in_=ps[:, 0:256],
                                                                             func=AF.Copy), reads=[ps], writes=[VD])
                else:
                    c.op("dve", lambda e, ps=ps, blk=blk, grp=grp: e.tensor_copy(out=VD[:, blk, 256 * grp:256 * grp + 256], in_=ps[:, 0:256]),
                         reads=[ps], writes=[VD])
            if last:
                ps8 = self.PS[6]
                for k in range(KT):
                    c.op("pe", lambda e, k=k: e.matmul(ps8[0:NS, 0:256], xn[:, k, SEG:SEG + NS], Wv[:, k, :],
                                                       start=(k == 0), stop=(k == KT - 1)),
                         reads=[W, xnb], writes=[ps8], inc=(k == KT - 1))
                c.op("act", lambda e, grp=grp: e.activation(out=V8d[:, 256 * grp:256 * grp + 256], in_=ps8[0:NS, 0:256], func=AF.Copy),
                     reads=[ps8], writes=[V8d])
        A = [self.ta([128, NT], F32, f"A{i}") for i in range(6)]
        QTb = self.ta([128, SEG], BF16, "QTb")
        KTb = self.ta([128, SEG], BF16, "KTb")
        KHT = self.ta([128, 4, 128], BF16, "KHT")
        OD = self.ta([128, NT], F32, "OD")
        ATm = [self.ta([128, 128], BF16, f"ATm{i}") for i in range(2)]
        FS = self.ta([128, NS], F32, "FS")
        SGp = self.ta([128, 2, NT], F32, "SGp")
        S0 = self.ta([128, 4, 128], F32, "S0") if last else None
        SN = self.ta([128, 128], F32, "SN") if last else None
        T1 = self.ta([128, 128], F32, "T1") if last else None
        BD = self.CB[:, self.CB_BD:self.CB_BD + 128]
        RST = self.CF[:, self.CF_RST:self.CF_RST + SEG]
        cur = self.sh_par[l]
        for h in range(4):
            if h % 2 == 0:
                def gcons(c0, n, res):
                    for s_, (pb, pap) in enumerate(res):
                        c.op("act", lambda e, s_=s_, pap=pap: e.activation(out=SGp[:, s_, c0:c0 + n], in_=pap, func=AF.Silu),
                             reads=[pb], writes=[SGp])
                self.proj_pair(l, 20 + h // 2, tiles, gcons)
            lb1 = self.LBT[:, l, h, 0:1]
            lbp = self.LBT[:, l, h, 1:2]
            lbn = self.LBT[:, l, h, 2:3]

            def qf(c0, n, res):
                (bq, qp), (bf_, fp) = res
                cs = slice(c0, c0 + n)
                c.op("act", lambda e: e.activation(out=A[0][:, cs], in_=qp, func=AF.Silu), reads=[bq], writes=[A[0]])
                c.op("act", lambda e: e.activation(out=A[1][:, cs], in_=fp, func=AF.Sigmoid), reads=[bf_], writes=[A[1]])
                c.op("dve", lambda e: e.tensor_scalar(out=A[2][:, cs], in0=A[1][:, cs], scalar1=lb1, scalar2=lbp, op0=ALU.mult, op1=ALU.add),
                     reads=[A[1], self.LBT], writes=[A[2]])
                c.op("dve", lambda e: e.tensor_scalar(out=A[3][:, cs], in0=A[1][:, cs], scalar1=lbn, scalar2=lb1, op0=ALU.mult, op1=ALU.add),
                     reads=[A[1], self.LBT], writes=[A[3]])
                if c0 >= SEG:
                    c.op("dve", lambda e: e.tensor_copy(out=FS[:, 0:n], in_=A[2][:, cs]), reads=[A[2]], writes=[FS])
                else:
                    c.op("act", lambda e: e.activation(out=A[2][:, cs], in_=A[2][:, cs], func=AF.Ln), reads=[A[2]], writes=[A[2]])
            self.proj_pair(l, 16 + h, tiles, qf)
            P_ = slice(0, SEG)
            c.op("dve", lambda e: e.tensor_tensor_scan(out=A[1][:, P_], data0=RST, data1=A[2][:, P_], initial=0.0, op0=ALU.mult, op1=ALU.add),
                 reads=[A[2], self.CF], writes=[A[1]])
            c.op("dve", lambda e: e.tensor_scalar(out=A[1][:, P_], in0=A[1][:, P_], scalar1=-80.0, scalar2=None, op0=ALU.max),
                 reads=[A[1]], writes=[A[1]])
            c.op("act", lambda e: e.activation(out=A[2][:, P_], in_=A[1][:, P_], func=AF.Exp), reads=[A[1]], writes=[A[2]])
            c.op("act", lambda e: e.activation(out=A[4][:, P_], in_=A[1][:, P_], func=AF.Exp, scale=-1.0), reads=[A[1]], writes=[A[4]])
            c.op("dve", lambda e: e.tensor_tensor(out=A[5][:, P_], in0=A[0][:, P_], in1=A[2][:, P_], op=ALU.mult),
                 reads=[A[0], A[2]], writes=[A[5]])
            c.op("act", lambda e: e.activation(out=QTb[:, :], in_=A[5][:, P_], func=AF.Copy), reads=[A[5]], writes=[QTb])
            c.op("dve", lambda e: e.tensor_tensor(out=KTb[:, :], in0=A[3][:, P_], in1=A[4][:, P_], op=ALU.mult),
                 reads=[A[3], A[4]], writes=[KTb])
            b3 = A[1][:, P_].rearrange("p (c t) -> p c t", t=32)
            c.op("dve", lambda e: e.tensor_tensor(out=A[4][:, P_].rearrange("p (c t) -> p c t", t=32),
                                                   in0=b3[:, :, 31:32].to_broadcast([128, 16, 32]), in1=b3, op=ALU.subtract),
                 reads=[A[1]], writes=[A[4]])
            c.op("act", lambda e: e.activation(out=A[4][:, P_], in_=A[4][:, P_], func=AF.Exp), reads=[A[4]], writes=[A[4]])
            c.op("dve", lambda e: e.tensor_tensor(out=A[4][:, P_], in0=A[3][:, P_], in1=A[4][:, P_], op=ALU.mult),
                 reads=[A[3], A[4]], writes=[A[4]])
            for blk in range(4):
                pst = self.PS[2 + blk % 2]
                c.op("pe", lambda e, blk=blk, pst=pst: e.transpose(pst[:, 0:128], A[4][:, blk * 128:(blk + 1) * 128], self.idf),
                     reads=[A[4], self.CF], writes=[pst])
                if blk % 2 == 0:
                    c.op("act", lambda e, blk=blk, pst=pst: e.activation(out=KHT[:, blk, :], in_=pst[:, 0:128], func=AF.Copy),
                         reads=[pst], writes=[KHT])
                else:
                    c.op("dve", lambda e, blk=blk, pst=pst: e.tensor_copy(out=KHT[:, blk, :], in_=pst[:, 0:128]), reads=[pst], writes=[KHT])
            hc = slice(128 * h, 128 * h + 128)
            for blk in range(4):
                bs = slice(blk * 128, (blk + 1) * 128)
                pa = self.PS[4]
                po = self.PS[5]
                at = ATm[blk % 2]
                c.op("pe", lambda e, bs=bs: e.matmul(pa[:, 0:128], KTb[:, bs], QTb[:, bs], start=True, stop=True),
                     reads=[KTb, QTb], writes=[pa])
                c.op("dve", lambda e, at=at: e.tensor_tensor(out=at[:, :], in0=pa[:, 0:128], in1=BD, op=ALU.mult),
                     reads=[pa, self.CB], writes=[at])
                c.op("pe", lambda e, at=at, blk=blk: e.matmul(po[:, 0:128], VD[:, blk, hc], at[:, :], start=True, stop=False),
                     reads=[VD, at], writes=[po], inc=True)
                for cc in range(4):
                    pd = self.PS[cc]
                    tp = dict(tile_position=(96, 0)) if cc == 3 else {}
                    c.op("pe", lambda e, cc=cc, blk=blk, pd=pd, tp=tp: e.matmul(
                        pd[:, 0:128], KHT[32 * cc:32 * cc + 32, blk, :], VD[32 * cc:32 * cc + 32, blk, hc],
                        start=True, stop=True, **tp), reads=[KHT, VD], writes=[pd], inc=True)
                for cc in range(4):
                    ch = 4 * blk + cc
                    Sc = self.SH[l][cur]
                    Sn_ = self.SH[l][1 - cur]
                    c.op("pe", lambda e, cc=cc, ch=ch, Sc=Sc: e.matmul(po[:, 32 * cc:32 * cc + 32], Sc[:, h, :], A[5][:, 32 * ch:32 * ch + 32],
                                                                     start=False, stop=(cc == 3)),
                         reads=[Sc, A[5]], writes=[po], inc=True)
                    pd = self.PS[cc]
                    c.op("dve", lambda e, ch=ch, cc=cc, pd=pd, Sc=Sc, Sn_=Sn_: e.scalar_tensor_tensor(
                        out=Sn_[:, h, :], in0=Sc[:, h, :], scalar=A[2][:, 32 * ch + 31:32 * ch + 32], in1=pd[:, 0:128],
                        op0=ALU.mult, op1=ALU.add), reads=[Sc, A[2], pd], writes=[Sn_])
                    cur = 1 - cur
                c.op("act", lambda e, bs=bs: e.activation(out=OD[:, bs], in_=po[:, 0:128], func=AF.Copy), reads=[po], writes=[OD])
            c.op("dve", lambda e, cur=cur: e.tensor_copy(out=self.SH[l][1 - cur][:, h, :], in_=self.SH[l][cur][:, h, :]),
                 reads=[self.SH[l][cur]], writes=[self.SH[l][1 - cur]])
            if last:
                c.dma("sp", self.o_hg_p[l, h], self.SH[l][cur][:, h, :], reads=[self.SH[l][cur]], is_output=True)
                for b in range(NS):
                    col = SEG + b
                    if h == 0 or True:
                        c.dma("sp", S0[:, h, :], self.hs0[:, l, b, h, :], writes=[S0])
                    pv = self.PS[6]
                    c.op("pe", lambda e, b=b: e.matmul(pv[:, 256:384], self.idf[0:NS, b:b + 1].to_broadcast([NS, 128]), V8d[0:NS, hc],
                                                       start=True, stop=True), reads=[self.CF, V8d], writes=[pv])
                    c.op("dve", lambda e, b=b: e.tensor_scalar(out=T1[:, :], in0=S0[:, h, :], scalar1=FS[:, b:b + 1], scalar2=None, op0=ALU.mult),
                         reads=[S0, FS], writes=[T1])
                    c.op("dve", lambda e, col=col: e.scalar_tensor_tensor(out=SN[:, :], in0=pv[:, 256:384], scalar=A[3][:, col:col + 1],
                                                                          in1=T1[:, :], op0=ALU.mult, op1=ALU.add),
                         reads=[pv, A[3], T1], writes=[SN])
                    c.dma("sp", self.o_hg_s[l, b, h], SN[:, :], reads=[SN], is_output=True)
                    c.op("pe", lambda e, col=col: e.matmul(pv[:, 384:385], SN[:, :], A[0][:, col:col + 1], start=True, stop=True),
                         reads=[SN, A[0]], writes=[pv])
                    c.op("act", lambda e, col=col: e.activation(out=OD[:, col:col + 1], in_=pv[:, 384:385], func=AF.Copy),
                         reads=[pv], writes=[OD])
            self.dump(f"od{seg}_{h}", OD, OD[:, 0:512], [128, 512])
            if last:
                self.dump(f"sod{h}", OD, OD[:, 512:520], [128, 8])
            for (c0, n) in tiles:
                cs = slice(c0, c0 + n)
                self.sumsq_rstd(OD, lambda k: OD[:, cs], 1, 128, c0, n, self.PS[7])
                tt = self.tmp[self.tmp_i]
                self.tmp_i ^= 1
                c.op("dve", lambda e, tt=tt, cs=cs, n=n: e.scalar_tensor_tensor(out=tt[:, 0:n], in0=OD[:, cs], scalar=self.PMX[:, l, 14 + h:15 + h],
                                                                        in1=self.rstd[:, cs], op0=ALU.mult, op1=ALU.mult),
                     reads=[OD, self.PMX, self.rstd], writes=[tt])
                c.op("dve", lambda e, tt=tt, cs=cs, n=n: e.tensor_tensor(out=self.MIX[:, 14 + h, cs], in0=tt[:, 0:n], in1=SGp[:, h % 2, cs], op=ALU.mult),
                     reads=[tt, SGp], writes=[self.MIX])
        self.sh_par[l] = cur
        c.barrier()
        self.ar_off[0] = m_start
    MAGIC = 12582912.0
    TWO_PI = 6.283185307179586

    def range_reduce(self, eng, out, src, kt, n=None, reads=(), outb=None, srcb=None, ktb=None):
        c = self.c
        c.op(eng, lambda e: e.tensor_scalar(out=kt, in0=src, scalar1=1.0 / self.TWO_PI, scalar2=self.MAGIC, op0=ALU.mult, op1=ALU.add),
             reads=[srcb], writes=[ktb])
        c.op(eng, lambda e: e.tensor_scalar(out=kt, in0=kt, scalar1=-self.MAGIC, scalar2=None, op0=ALU.add), reads=[ktb], writes=[ktb])
        c.op(eng, lambda e: e.scalar_tensor_tensor(out=out, in0=kt, scalar=-self.TWO_PI, in1=src, op0=ALU.mult, op1=ALU.add),
             reads=[ktb, srcb], writes=[outb])
        c.op(eng, lambda e: e.tensor_scalar(out=out, in0=out, scalar1=-3.14159, scalar2=3.14159, op0=ALU.max, op1=ALU.min),
             reads=[outb], writes=[outb])

    def lam_calc(self, are, aim, ldt, srcb, W, T, want_z):
        c = self.c
        dt, r, th, k, sn, cs, lre, lim = T[:8]
        w = slice(0, W)
        c.op("act", lambda e: e.activation(out=dt[:, w], in_=ldt, func=AF.Exp), reads=[srcb], writes=[dt])
        c.op("dve", lambda e: e.tensor_tensor(out=r[:, w], in0=are, in1=dt[:, w], op=ALU.mult), reads=[srcb, dt], writes=[r])
        c.op("dve", lambda e: e.tensor_tensor(out=th[:, w], in0=aim, in1=dt[:, w], op=ALU.mult), reads=[srcb, dt], writes=[th])
        c.op("act", lambda e: e.activation(out=r[:, w], in_=r[:, w], func=AF.Exp), reads=[r], writes=[r])
        self.range_reduce("dve", th[:, w], th[:, w], k[:, w], outb=th, srcb=th, ktb=k)
        c.op("dve", lambda e: e.tensor_scalar(out=dt[:, w], in0=th[:, w], scalar1=3.141592653589793 / 2, scalar2=None, op0=ALU.add),
             reads=[th], writes=[dt])
        self.range_reduce("dve", dt[:, w], dt[:, w], k[:, w], outb=dt, srcb=dt, ktb=k)
        c.op("act", lambda e: e.activation(out=sn[:, w], in_=th[:, w], func=AF.Sin), reads=[th], writes=[sn])
        c.op("act", lambda e: e.activation(out=cs[:, w], in_=dt[:, w], func=AF.Sin), reads=[dt], writes=[cs])
        c.op("dve", lambda e: e.tensor_tensor(out=lre[:, w], in0=r[:, w], in1=cs[:, w], op=ALU.mult), reads=[r, cs], writes=[lre])
        c.op("dve", lambda e: e.tensor_tensor(out=lim[:, w], in0=r[:, w], in1=sn[:, w], op=ALU.mult), reads=[r, sn], writes=[lim])
        res = dict(r=r, th=th, lre=lre, lim=lim)
        if want_z:
            den, l1, zr, zi = dt, k, sn, cs
            c.op("dve", lambda e: e.tensor_tensor(out=den[:, w], in0=are, in1=are, op=ALU.mult), reads=[srcb], writes=[den])
            c.op("dve", lambda e: e.tensor_tensor(out=l1[:, w], in0=aim, in1=aim, op=ALU.mult), reads=[srcb], writes=[l1])
            c.op("dve", lambda e: e.tensor_tensor(out=den[:, w], in0=den[:, w], in1=l1[:, w], op=ALU.add), reads=[den, l1], writes=[den])
            c.op("dve", lambda e: e.reciprocal(out=den[:, w], in_=den[:, w]), reads=[den], writes=[den])
            c.op("dve", lambda e: e.tensor_scalar(out=l1[:, w], in0=lre[:, w], scalar1=-1.0, scalar2=None, op0=ALU.add), reads=[lre], writes=[l1])
            t1, t2 = T[8], T[9]
            c.op("dve", lambda e: e.tensor_tensor(out=t1[:, w], in0=l1[:, w], in1=are, op=ALU.mult), reads=[l1, srcb], writes=[t1])
            c.op("dve", lambda e: e.tensor_tensor(out=t2[:, w], in0=lim[:, w], in1=aim, op=ALU.mult), reads=[lim, srcb], writes=[t2])
            c.op("dve", lambda e: e.tensor_tensor(out=t1[:, w], in0=t1[:, w], in1=t2[:, w], op=ALU.add), reads=[t1, t2], writes=[t1])
            c.op("dve", lambda e: e.tensor_tensor(out=zr[:, w], in0=t1[:, w], in1=den[:, w], op=ALU.mult), reads=[t1, den], writes=[zr])
            c.op("dve", lambda e: e.tensor_tensor(out=t1[:, w], in0=lim[:, w], in1=are, op=ALU.mult), reads=[lim, srcb], writes=[t1])
            c.op("dve", lambda e: e.tensor_tensor(out=t2[:, w], in0=l1[:, w], in1=aim, op=ALU.mult), reads=[l1, srcb], writes=[t2])
            c.op("dve", lambda e: e.tensor_tensor(out=t1[:, w], in0=t1[:, w], in1=t2[:, w], op=ALU.subtract), reads=[t1, t2], writes=[t1])
            c.op("dve", lambda e: e.tensor_tensor(out=zi[:, w], in0=t1[:, w], in1=den[:, w], op=ALU.mult), reads=[t1, den], writes=[zi])
            res.update(zr=zr, zi=zi)
        return res

    def mix_A(self, seg, l, NC):
        c = self.c
        import os as _os
        self.apool = "pool" if _os.environ.get("A_POOL", "0") == "1" else "dve"
        last = (seg == self.nseg - 1)
        NT = self.NT
        tiles = self.tiles(seg)
        m_start = self.ar_off[0]
        UT = self.ta([128, 4, NT], BF16, "UT")
        ZB = self.ta([128, 4, NT], BF16, "ZB")
        BR = [self.ta([128, 4, 64], F32, f"BR{i}") for i in range(2)]
        CRW = self.ta([128, 448], F32, "CRW")
        LP = [self.ta([128, 14], F32, f"LP{i}") for i in range(10)]
        NLI = self.ta([128, 14], F32, "NLI")
        X0S = self.ta([128, NS, 14, 2], F32, "X0S") if last else None
        XSN = self.ta([128, NS, 14, 2], F32, "XSN") if last else None
        for j in range(2):
            def ucons(c0, n, res, j=j):
                for s_, (pb, pap) in enumerate(res):
                    c.op("act", lambda e, s_=s_, pap=pap: e.activation(out=UT[:, 2 * j + s_, c0:c0 + n], in_=pap, func=AF.Copy),
                         reads=[pb], writes=[UT])
            self.proj_pair(l, j, tiles, ucons)
        m_setup = self.ar_off[0]
        R = self.ta([128, 1728], F32, "R")
        c.dma("sp", R[:, :], self.s5r[:, l, :], writes=[R])
        if last:
            c.dma("sp", X0S[:, :, :, :], self.x0s[:, l], writes=[X0S])
        c.op("dve", lambda e: e.tensor_copy(out=CRW[:, :], in_=R[:, 1280:1728]), reads=[R], writes=[CRW])
        TR = [self.ta([128, 256], F32, f"TR{i}") for i in range(10)]
        zz = self.lam_calc(R[:, 0:256], R[:, 256:512], R[:, 512:768], R, 256, TR, True)
        zr, zi = zz["zr"], zz["zi"]
        t1, t2 = TR[8], TR[9]
        bre, bim = R[:, 768:1024], R[:, 1024:1280]
        br0 = BR[0][:, :, :].rearrange("p a b -> p (a b)")
        br1 = BR[1][:, :, :].rearrange("p a b -> p (a b)")
        c.op("dve", lambda e: e.tensor_tensor(out=t1[:, :], in0=zr[:, :], in1=bre, op=ALU.mult), reads=[zr, R], writes=[t1])
        c.op("dve", lambda e: e.tensor_tensor(out=t2[:, :], in0=zi[:, :], in1=bim, op=ALU.mult), reads=[zi, R], writes=[t2])
        c.op("dve", lambda e: e.tensor_tensor(out=br0, in0=t1[:, :], in1=t2[:, :], op=ALU.subtract), reads=[t1, t2], writes=[BR[0]])
        c.op("dve", lambda e: e.tensor_tensor(out=t1[:, :], in0=zr[:, :], in1=bim, op=ALU.mult), reads=[zr, R], writes=[t1])
        c.op("dve", lambda e: e.tensor_tensor(out=t2[:, :], in0=zi[:, :], in1=bre, op=ALU.mult), reads=[zi, R], writes=[t2])
        c.op("dve", lambda e: e.tensor_tensor(out=br1, in0=t1[:, :], in1=t2[:, :], op=ALU.add), reads=[t1, t2], writes=[BR[1]])
        lp = self.lam_calc(self.PLA[:, l, 0:14], self.PLA[:, l, 14:28], self.PLA[:, l, 28:42], self.PLA, 14, LP, False)
        RR, TH, LRE, LIM = lp["r"], lp["th"], lp["lre"], lp["lim"]
        c.op("dve", lambda e: e.tensor_scalar(out=NLI[:, :], in0=LIM[:, 0:14], scalar1=-1.0, scalar2=None, op0=ALU.mult), reads=[LIM], writes=[NLI])
        c.barrier()
        self.ar_off[0] = m_setup
        CS = self.ta([128, SEG], F32, "CS")
        SNT = self.ta([128, SEG], F32, "SNT")
        G = [self.ta([128, SEG], F32, f"G{i}") for i in range(4)]
        XR = self.ta([128, NT], BF16, "XR")
        XI = self.ta([128, NT], BF16, "XI")
        BBD = [[self.ta([128, 128], BF16, f"BBD{q}{i}") for i in range(2)] for q in range(4)]
        CBD = [[self.ta([128, 128], BF16, f"CBD{q}{i}") for i in range(2)] for q in range(4)]
        DD = self.ta([128, 128], BF16, "DD")
        SM = self.ta([128, 16], F32, "SM")
        IOTA = self.CF[:, self.CF_IOTA:self.CF_IOTA + SEG]
        MB_ = self.CB[:, self.CB_MB:self.CB_MB + 512].rearrange("p (q g s) -> p q g s", q=4, g=2)
        MC_ = self.CB[:, self.CB_MC:self.CB_MC + 512].rearrange("p (q g j) -> p q g j", q=4, g=8)
        X0 = self.X0[l]
        P_ = slice(0, SEG)
        for ci in range(4):
            npair = 4 if ci < 3 else 2
            for q in range(npair):
                p = 4 * ci + q
                for i in range(2):
                    c.op("dve", lambda e, q=q, i=i: e.tensor_tensor(
                        out=BBD[q][i][:, :].rearrange("p (g s) -> p g s", g=2), in0=MB_[:, q, :, :],
                        in1=BR[i][:, ci, :].unsqueeze(1).to_broadcast([128, 2, 64]), op=ALU.mult),
                        reads=[self.CB, BR[i]], writes=[BBD[q][i]])
                crp = CRW[:, 16 * p:16 * p + 16].unsqueeze(1).to_broadcast([128, 8, 16])
                cip = CRW[:, 224 + 16 * p:224 + 16 * p + 16].unsqueeze(1).to_broadcast([128, 8, 16])
                c.op("dve", lambda e, q=q, crp=crp: e.tensor_tensor(out=CBD[q][0][:, :].rearrange("p (g j) -> p g j", g=8),
                                                                    in0=MC_[:, q, :, :], in1=crp, op=ALU.mult),
                     reads=[self.CB, CRW], writes=[CBD[q][0]])
                c.op("dve", lambda e, q=q, cip=cip: e.scalar_tensor_tensor(out=CBD[q][1][:, :].rearrange("p (g j) -> p g j", g=8),
                                                                         in0=cip, scalar=-1.0, in1=MC_[:, q, :, :], op0=ALU.mult, op1=ALU.mult),
                     reads=[self.CB, CRW], writes=[CBD[q][1]])
            c.op("dve", lambda e: e.tensor_scalar(out=DD[:, :], in0=self.idb, scalar1=self.PLA[:, l, 42 + ci:43 + ci], scalar2=None, op0=ALU.mult),
                 reads=[self.CB, self.PLA], writes=[DD])
            yps = [(self.PS[4], self.PS[4][:, 0:SEG])] + ([(self.PS[7], self.PS[7][:, 0:NS])] if last else [])
            for ti, (c0, n) in enumerate(tiles):
                yb_, yap = yps[ti]
                c.op("pe", lambda e, yap=yap, c0=c0, n=n: e.matmul(yap, DD[:, :], UT[:, ci, c0:c0 + n], start=True, stop=False),
                     reads=[DD, UT], writes=[yb_])
            for q in range(npair):
                p = 4 * ci + q
                bps = []
                for ti, (c0, n) in enumerate(tiles):
                    for i in range(2):
                        pb = self.PS[i] if ti == 0 else self.PS[6]
                        pap = pb[:, 0:n] if ti == 0 else pb[:, 16 * i:16 * i + n]
                        c.op("pe", lambda e, pap=pap, i=i, q=q, c0=c0, n=n: e.matmul(pap, BBD[q][i][:, :], UT[:, ci, c0:c0 + n], start=True, stop=True),
                             reads=[BBD[q][i], UT], writes=[pb])
                        bps.append((pb, pap))
                (bre_b, bre_p), (bim_b, bim_p) = bps[0], bps[1]
                thp = TH[:, p:p + 1]
                c.op("dve", lambda e: e.tensor_scalar(out=G[0][:, :], in0=IOTA, scalar1=thp, scalar2=None, op0=ALU.mult), reads=[self.CF, TH], writes=[G[0]])
                self.range_reduce("dve", G[2][:, :], G[0][:, :], G[1][:, :], outb=G[2], srcb=G[0], ktb=G[1])
                c.op("act", lambda e: e.activation(out=SNT[:, :], in_=G[2][:, :], func=AF.Sin), reads=[G[2]], writes=[SNT])
                c.op("dve", lambda e: e.tensor_scalar(out=G[2][:, :], in0=G[2][:, :], scalar1=3.141592653589793 / 2, scalar2=None, op0=ALU.add),
                     reads=[G[2]], writes=[G[2]])
                self.range_reduce("dve", G[3][:, :], G[2][:, :], G[1][:, :], outb=G[3], srcb=G[2], ktb=G[1])
                c.op("act", lambda e: e.activation(out=CS[:, :], in_=G[3][:, :], func=AF.Sin), reads=[G[3]], writes=[CS])
                c.op("dve", lambda e: e.tensor_tensor(out=G[0][:, :], in0=bre_p, in1=CS[:, :], op=ALU.mult), reads=[bre_b, CS], writes=[G[0]])
                c.op("dve", lambda e: e.tensor_tensor(out=G[1][:, :], in0=bim_p, in1=SNT[:, :], op=ALU.mult), reads=[bim_b, SNT], writes=[G[1]])
                c.op(self.apool, lambda e: e.tensor_tensor(out=G[2][:, :], in0=G[0][:, :], in1=G[1][:, :], op=ALU.add), reads=[G[0], G[1]], writes=[G[2]])
                c.op("dve", lambda e: e.tensor_tensor(out=G[0][:, :], in0=bim_p, in1=CS[:, :], op=ALU.mult), reads=[bim_b, CS], writes=[G[0]])
                c.op("dve", lambda e: e.tensor_tensor(out=G[1][:, :], in0=bre_p, in1=SNT[:, :], op=ALU.mult), reads=[bre_b, SNT], writes=[G[1]])
                c.op(self.apool, lambda e: e.tensor_tensor(out=G[3][:, :], in0=G[0][:, :], in1=G[1][:, :], op=ALU.subtract), reads=[G[0], G[1]], writes=[G[3]])
                rb = RR[:, p:p + 1].to_broadcast([128, SEG])
                c.op("dve", lambda e: e.tensor_tensor_scan(out=G[0][:, :], data0=rb, data1=G[2][:, :], initial=X0[:, p, 0:1], op0=ALU.mult, op1=ALU.add),
                     reads=[RR, G[2], X0], writes=[G[0]])
                c.op("dve", lambda e: e.tensor_tensor_scan(out=G[1][:, :], data0=rb, data1=G[3][:, :], initial=X0[:, p, 1:2], op0=ALU.mult, op1=ALU.add),
                     reads=[RR, G[3], X0], writes=[G[1]])
                c.op("dve", lambda e: e.tensor_tensor(out=G[2][:, :], in0=G[0][:, :], in1=CS[:, :], op=ALU.mult), reads=[G[0], CS], writes=[G[2]])
                c.op(self.apool, lambda e: e.tensor_tensor(out=G[3][:, :], in0=G[1][:, :], in1=SNT[:, :], op=ALU.mult), reads=[G[1], SNT], writes=[G[3]])
                c.op(self.apool, lambda e: e.tensor_tensor(out=XR[:, P_], in0=G[2][:, :], in1=G[3][:, :], op=ALU.subtract), reads=[G[2], G[3]], writes=[XR])
                c.op("dve", lambda e: e.tensor_tensor(out=X0[:, p, 0:1], in0=G[2][:, SEG - 1:SEG], in1=G[3][:, SEG - 1:SEG], op=ALU.subtract),
                     reads=[G[2], G[3]], writes=[X0])
                c.op("dve", lambda e: e.tensor_tensor(out=G[2][:, :], in0=G[0][:, :], in1=SNT[:, :], op=ALU.mult), reads=[G[0], SNT], writes=[G[2]])
                c.op(self.apool, lambda e: e.tensor_tensor(out=G[3][:, :], in0=G[1][:, :], in1=CS[:, :], op=ALU.mult), reads=[G[1], CS], writes=[G[3]])
                c.op(self.apool, lambda e: e.tensor_tensor(out=XI[:, P_], in0=G[2][:, :], in1=G[3][:, :], op=ALU.add), reads=[G[2], G[3]], writes=[XI])
                c.op("dve", lambda e: e.tensor_tensor(out=X0[:, p, 1:2], in0=G[2][:, SEG - 1:SEG], in1=G[3][:, SEG - 1:SEG], op=ALU.add),
                     reads=[G[2], G[3]], writes=[X0])
                if last:
                    (sre_b, sre_p), (sim_b, sim_p) = bps[2], bps[3]
                    a_ = SM[:, 0:NS]
                    c.op("dve", lambda e: e.tensor_scalar(out=a_, in0=X0S[:, :, p, 0], scalar1=LRE[:, p:p + 1], scalar2=None, op0=ALU.mult),
                         reads=[X0S, LRE], writes=[SM])
                    c.op("dve", lambda e: e.scalar_tensor_tensor(out=a_, in0=X0S[:, :, p, 1], scalar=NLI[:, p:p + 1], in1=a_, op0=ALU.mult, op1=ALU.add),
                         reads=[X0S, NLI, SM], writes=[SM])
                    c.op("dve", lambda e: e.tensor_tensor(out=XSN[:, :, p, 0], in0=a_, in1=sre_p, op=ALU.add), reads=[SM, sre_b], writes=[XSN])
                    b_ = SM[:, 8:8 + NS]
                    c.op("dve", lambda e: e.tensor_scalar(out=b_, in0=X0S[:, :, p, 1], scalar1=LRE[:, p:p + 1], scalar2=None, op0=ALU.mult),
                         reads=[X0S, LRE], writes=[SM])
                    c.op("dve", lambda e: e.scalar_tensor_tensor(out=b_, in0=X0S[:, :, p, 0], scalar=LIM[:, p:p + 1], in1=b_, op0=ALU.mult, op1=ALU.add),
                         reads=[X0S, LIM, SM], writes=[SM])
                    c.op("dve", lambda e: e.tensor_tensor(out=XSN[:, :, p, 1], in0=b_, in1=sim_p, op=ALU.add), reads=[SM, sim_b], writes=[XSN])
                    c.op("act", lambda e: e.activation(out=XR[:, SEG:SEG + NS], in_=XSN[:, :, p, 0], func=AF.Copy), reads=[XSN], writes=[XR])
                    c.op("act", lambda e: e.activation(out=XI[:, SEG:SEG + NS], in_=XSN[:, :, p, 1], func=AF.Copy), reads=[XSN], writes=[XI])
                for ti, (c0, n) in enumerate(tiles):
                    yb_, yap = yps[ti]
                    lastmm = (q == npair - 1)
                    c.op("pe", lambda e, yap=yap, q=q, c0=c0, n=n: e.matmul(yap, CBD[q][0][:, :], XR[:, c0:c0 + n], start=False, stop=False),
                         reads=[CBD[q][0], XR], writes=[yb_])
                    c.op("pe", lambda e, yap=yap, q=q, c0=c0, n=n, lastmm=lastmm: e.matmul(yap, CBD[q][1][:, :], XI[:, c0:c0 + n], start=False, stop=lastmm),
                         reads=[CBD[q][1], XI], writes=[yb_])
            for ti, (c0, n) in enumerate(tiles):
                yb_, yap = yps[ti]
                if "ya" in self.dbg and ci == 1 and ti == 0:
                    self.dump("ya", yb_, yap, [128, 512])
                c.op("act", lambda e, yap=yap, c0=c0, n=n: e.activation(out=ZB[:, ci, c0:c0 + n], in_=yap, func=AF.Gelu_apprx_tanh),
                     reads=[yb_], writes=[ZB])
        if last:
            c.dma("sp", self.o_ssm_p[:, l], X0[:, :, :], reads=[X0], is_output=True)
            c.dma("sp", self.o_ssm_s[:, l], XSN[:, :, :, :], reads=[XSN], is_output=True)
        W = self.next_wa()
        Wg = W[:, :, :, :].rearrange("p a k c -> p (a k c)")[:, 0:2048].rearrange("p (m k c) -> p m k c", m=4, k=4)
        c.dma("pool", Wg, self.wglu[l].rearrange("m p k c -> p m k c"), writes=[W])
        ZS = self.ta([128, 4, NS], F32, "ZS") if last else None
        for mo in range(4):
            for ti, (c0, n) in enumerate(tiles):
                pb = self.PS[mo % 2] if ti == 0 else self.PS[6]
                pap = pb[:, 0:n]
                for k in range(4):
                    c.op("pe", lambda e, k=k, pap=pap, c0=c0, n=n: e.matmul(pap, Wg[:, mo, k, :], ZB[:, k, c0:c0 + n], start=(k == 0), stop=(k == 3)),
                         reads=[W, ZB], writes=[pb], inc=(k == 3))
                gt = self.tmp[self.tmp_i]
                self.tmp_i ^= 1
                c.op("act", lambda e, gt=gt, pap=pap, n=n: e.activation(out=gt[:, 0:n], in_=pap, func=AF.Sigmoid, bias=self.PMX[:, l, 18 + mo:19 + mo]),
                     reads=[pb, self.PMX], writes=[gt])
                dst_b, dst = (G[mo], G[mo][:, 0:n]) if ti == 0 else (ZS, ZS[:, mo, 0:n])
                c.op("dve", lambda e, gt=gt, dst=dst, c0=c0, n=n: e.tensor_tensor(out=dst, in0=ZB[:, mo, c0:c0 + n], in1=gt[:, 0:n], op=ALU.mult),
                     reads=[ZB, gt], writes=[dst_b])

        class _Multi:
            pass
        srcs = list(G) + ([ZS] if last else [])
        self.mix_norm_multi(seg, srcs, lambda k, c0, n: (G[k][:, c0:c0 + n] if c0 < SEG else ZS[:, k, 0:n]), 4, 448, l, 0, 0)
        c.barrier()
        self.ar_off[0] = m_start

    def mix_norm_multi(self, seg, src_bufs, src_fn, nchunk, n_feat, l, gofs, mbase):
        c = self.c
        for (c0, n) in self.tiles(seg):
            ps = self.PS[7]
            for k in range(nchunk):
                sq = self.sq[self.sq_i]
                self.sq_i ^= 1
                c.op("act", lambda e, k=k, sq=sq: e.activation(out=sq[:, 0:n], in_=src_fn(k, c0, n), func=AF.Square),
                     reads=src_bufs, writes=[sq])
                c.op("pe", lambda e, k=k, sq=sq: e.matmul(ps[:, 0:n], self.ones_bf[:, :], sq[:, 0:n], start=(k == 0), stop=(k == nchunk - 1)),
                     reads=[sq, self.ones_bf], writes=[ps])
            c.op("act", lambda e: e.activation(out=self.rstd[:, c0:c0 + n], in_=ps[:, 0:n], func=AF.Sqrt, bias=self.eps_ap(), scale=1.0 / n_feat),
                 reads=[ps, self.epsb], writes=[self.rstd])
            c.op("dve", lambda e: e.reciprocal(out=self.rstd[:, c0:c0 + n], in_=self.rstd[:, c0:c0 + n]), reads=[self.rstd], writes=[self.rstd])
            for k in range(nchunk):
                c.op("dve", lambda e, k=k: e.scalar_tensor_tensor(
                    out=self.MIX[:, mbase + k, c0:c0 + n], in0=src_fn(k, c0, n), scalar=self.PMX[:, l, gofs + k:gofs + k + 1],
                    in1=self.rstd[:, c0:c0 + n], op0=ALU.mult, op1=ALU.mult), reads=src_bufs + [self.PMX, self.rstd], writes=[self.MIX])
    def mix_out(self, seg, l, NC):
        c = self.c
        tiles = self.tiles(seg)
        yb, y = self.Y
        c.barrier()
        for k in range(MIXK):
            self.dump(f"mix{k}_{seg}", self.MIX, self.MIX[:, k, 0:512], [128, 512])
        for mo in range(KT):
            W = self.next_wa()
            Wv = W[:, :, :, :].rearrange("p a k c -> p (a k c)")[:, 0:MIXK * 128].rearrange("p (k c) -> p k c", k=MIXK)
            c.dma("pool", Wv, self.wout[l, mo], writes=[W])
            for ti, (c0, n) in enumerate(tiles):
                pd = self.PS[4 + (mo % 2)] if ti == 0 else self.PS[6]
                od = 0 if ti == 0 else 32
                for k in range(MIXK):
                    c.op("pe", lambda e, k=k, pd=pd, od=od: e.matmul(pd[:, od:od + n], Wv[:, k, :], self.MIX[:, k, c0:c0 + n],
                                                                  start=(k == 0), stop=(k == MIXK - 1)),
                         reads=[W, self.MIX], writes=[pd], inc=(k == MIXK - 1))
                c.op("act", lambda e, pd=pd, od=od, mo=mo: e.activation(out=y[:, mo, c0:c0 + n], in_=pd[:, od:od + n], func=AF.Copy),
                     reads=[pd], writes=[yb])
        self.dump("ymix", yb, y[:, 3, 0:512], [128, 512])
        self.post_norm_add(seg, (l * 6 + 3) * KT)


U_OFF, QB_OFF, KB_OFF, VB_OFF, QC_OFF, KC_OFF, VC_OFF, QD_OFF, FD_OFF, ID_OFF, GD_OFF = (
    0, 448, 960, 1088, 1216, 1792, 1984, 2176, 2688, 3200, 3712)
NPAIR = 22
MIXK = 18


def _prep_ffn_weights(wg, wu, wd):
    nl = wg.shape[0]
    g = wg.reshape(nl, KT, 128, FT, 128).transpose(0, 3, 2, 1, 4)
    u = wu.reshape(nl, KT, 128, FT, 128).transpose(0, 3, 2, 1, 4)
    gu = np.ascontiguousarray(np.stack([g, u], axis=3))
    d = np.ascontiguousarray(wd.reshape(nl, FT, 128, KT, 128).transpose(0, 3, 2, 1, 4))
    return gu, d


def _gain_layout(vs):
    nl = vs[0].shape[0]
    a = np.stack(vs, axis=1)
    a = a.reshape(nl, 6, KT, 128).transpose(3, 0, 1, 2).reshape(128, nl * 6 * KT)
    return np.ascontiguousarray(a)


def _head(off, h):
    return list(range(off + 64 * h, off + 64 * h + 64))


def _swap(cols):
    return cols[32:] + cols[:32]


def _win_chunks():
    Z = [-1] * 64
    ch = []
    for j in range(4):
        ch.append([cc if cc < 448 else -1 for cc in range(128 * j, 128 * j + 128)])
    for j in range(4):
        ch.append(_head(QB_OFF, j) + _head(QB_OFF, j + 4))
        ch.append(_swap(_head(QB_OFF, j)) + _swap(_head(QB_OFF, j + 4)))
    ch.append(_head(KB_OFF, 0) + _head(KB_OFF, 1))
    ch.append(_swap(_head(KB_OFF, 0)) + _swap(_head(KB_OFF, 1)))
    for g in range(3):
        a, b, cc = _head(QC_OFF, 3 * g), _head(QC_OFF, 3 * g + 1), _head(QC_OFF, 3 * g + 2)
        ch.append(a + b)
        ch.append(_swap(a) + _swap(b))
        ch.append(cc + Z)
        ch.append(_swap(cc) + Z)
    for g in range(3):
        k = _head(KC_OFF, g)
        ch.append(k + k)
        ch.append(_swap(k) + _swap(k))
    for h in range(4):
        ch.append(list(range(QD_OFF + 128 * h, QD_OFF + 128 * h + 128)))
        ch.append(list(range(FD_OFF + 128 * h, FD_OFF + 128 * h + 128)))
    for h in range(4):
        ch.append(list(range(GD_OFF + 128 * h, GD_OFF + 128 * h + 128)))
    assert len(ch) == 2 * NPAIR
    return ch


def _gather_cols(w, cols):
    cols = np.asarray(cols)
    out = w[:, :, np.maximum(cols, 0)]
    out[:, :, cols < 0] = 0.0
    return out


def _mix_rows():
    Z = [-1] * 64
    rows = []
    for j in range(4):
        rows += [r if r < 448 else -1 for r in range(128 * j, 128 * j + 128)]
    ob = 448
    for j in range(4):
        rows += _head(ob, j) + _head(ob, j + 4)
    oc = 448 + 512
    for g in range(3):
        rows += _head(oc, 3 * g) + _head(oc, 3 * g + 1)
        rows += _head(oc, 3 * g + 2) + Z
    od = 448 + 512 + 576
    rows += list(range(od, od + 512))
    assert len(rows) == MIXK * 128
    return np.asarray(rows)


PAST_LEN = 16384
ROPE_THETA = 10000.0


def _consts(nseg):
    ntok = nseg * SEG + NS
    cb = np.zeros((128, Prog.NCB), np.float32)
    qi = np.arange(128)[:, None]
    kj = np.arange(256)[None, :]
    dist = 128 + qi - kj
    cb[:, 0:256] = np.where((dist >= 0) & (dist <= 128), 0.0, NEG)
    cb[:, 256] = 0.0
    cb[:, 257] = NEG
    q = np.arange(128)[:, None]
    col = np.arange(512)[None, :]
    rrq, iq = q // 32, q % 32
    jt, rr = 32 * (col // 128) + (col % 32), (col % 128) // 32
    cb[:, Prog.CB_M2:Prog.CB_M2 + 512] = np.where((rr == rrq) & (jt <= 96 + iq), 0.0, NEG)
    s_ = np.arange(128)[:, None]
    t_ = np.arange(128)[None, :]
    cb[:, Prog.CB_BD:Prog.CB_BD + 128] = ((s_ // 32 == t_ // 32) & (s_ <= t_)).astype(np.float32)
    cb[:, Prog.CB_ID:Prog.CB_ID + 128] = np.eye(128, dtype=np.float32)
    row = np.arange(128)[:, None, None]
    qq = np.arange(4)[None, :, None]
    cc = np.arange(128)[None, None, :]
    cb[:, Prog.CB_MB:Prog.CB_MB + 512] = ((row // 16) == 2 * qq + (cc // 64)).astype(np.float32).reshape(128, 512)
    cb[:, Prog.CB_MC:Prog.CB_MC + 512] = ((cc // 16) == 2 * qq + (row // 64)).astype(np.float32).reshape(128, 512)
    cf = np.zeros((128, Prog.NCF), np.float32)
    cf[:, 0:128] = np.eye(128, dtype=np.float32)
    cf[:, Prog.CF_IOTA:Prog.CF_IOTA + 512] = np.arange(1, 513, dtype=np.float32)[None, :]
    cf[:, Prog.CF_RST:Prog.CF_RST + 512] = (np.arange(512) % 32 != 0).astype(np.float32)[None, :]
    pos = np.concatenate([np.arange(nseg * SEG), np.full(NS, PAST_LEN)]).astype(np.float32)
    inv_freq = (np.float32(ROPE_THETA) ** (-(np.arange(32, dtype=np.float32) / np.float32(32)))).astype(np.float32)
    ang = (pos[:, None] * inv_freq[None, :]).astype(np.float32)
    cos = np.cos(ang).astype(np.float32).T
    sin = np.sin(ang).astype(np.float32).T
    rot = np.zeros((128, 2, ntok), np.float32)
    for p in range(128):
        rot[p, 0] = cos[p % 32]
        rot[p, 1] = sin[p % 32] * (-1.0 if (p % 64) < 32 else 1.0)
    return cb, cf, rot


def _prep_shared(inp, nl):
    sh = {}
    for i, nm in ((1, "ffn1"), (2, "ffn2")):
        gu, d = _prep_ffn_weights(inp[nm + "_w_gate"][:nl], inp[nm + "_w_up"][:nl], inp[nm + "_w_down"][:nl])
        sh[f"wgu{i}"], sh[f"wdn{i}"] = gu, d
    sh["gains"] = _gain_layout([inp[k][:nl] for k in ("ffn1_norm_pre", "ffn1_norm_post", "mix_norm_pre", "mix_norm_post",
                                                     "ffn2_norm_pre", "ffn2_norm_post")])
    w_in = inp["w_in"][:nl]
    ch = _win_chunks()
    wf = np.stack([_gather_cols(w_in, cc) for cc in ch], axis=1)
    wf = wf.reshape(nl, NPAIR, 2, KT, 128, 128).transpose(0, 1, 4, 2, 3, 5)
    sh["win_f"] = np.ascontiguousarray(wf)
    Z = [-1] * 64
    tg = [list(range(ID_OFF, ID_OFF + 256)), list(range(ID_OFF + 256, ID_OFF + 512)),
          list(range(VB_OFF, VB_OFF + 128)) + list(range(VC_OFF, VC_OFF + 128)),
          list(range(VC_OFF + 128, VC_OFF + 192)) + Z * 3]
    wt = np.stack([_gather_cols(w_in, cc) for cc in tg], axis=1)
    sh["wt"] = np.ascontiguousarray(wt.reshape(nl, 4, KT, 128, 256).transpose(0, 1, 3, 2, 4))
    rows = _mix_rows()
    wo = inp["w_out"][:nl][:, np.maximum(rows, 0), :].copy()
    wo[:, rows < 0, :] = 0.0
    sh["wout"] = np.ascontiguousarray(wo.reshape(nl, MIXK, 128, KT, 128).transpose(0, 3, 2, 1, 4))
    wg = np.zeros((nl, 512, 512), np.float32)
    wg[:, :448, :448] = inp["ssm_w_glu"][:nl]
    sh["wglu"] = np.ascontiguousarray(wg.reshape(nl, 4, 128, 4, 128).transpose(0, 3, 2, 1, 4))
    a_re, a_im, ldt = inp["ssm_a_re"][:nl], inp["ssm_a_im"][:nl], inp["ssm_log_dt"][:nl]
    plA = np.zeros((128, nl, 46), np.float32)
    plA[:, :, 0:14] = a_re.reshape(nl, 14, 2, 64).transpose(2, 3, 0, 1).reshape(128, nl, 14)
    plA[:, :, 14:28] = a_im.reshape(nl, 14, 2, 64).transpose(2, 3, 0, 1).reshape(128, nl, 14)
    plA[:, :, 28:42] = np.broadcast_to(ldt.reshape(nl, 14, 2, 1), (nl, 14, 2, 64)).transpose(2, 3, 0, 1).reshape(128, nl, 14)
    dsk = np.zeros((nl, 32, 16), np.float32)
    dsk[:, :28] = inp["ssm_d"][:nl]
    plA[:, :, 42:46] = dsk.reshape(nl, 4, 8, 16).transpose(2, 3, 0, 1).reshape(128, nl, 4)
    sh["plA"] = plA
    pm = np.zeros((128, nl, 30), np.float32)
    gcat = np.concatenate([inp["out_norm_a"][:nl], inp["out_norm_b"][:nl], inp["out_norm_c"][:nl], inp["out_norm_d"][:nl]], axis=1)
    gm = gcat[:, np.maximum(rows, 0)].copy()
    gm[:, rows < 0] = 0.0
    pm[:, :, 0:18] = gm.reshape(nl, MIXK, 128).transpose(2, 0, 1)
    bg = np.zeros((nl, 512), np.float32)
    bg[:, :448] = inp["ssm_b_glu"][:nl]
    pm[:, :, 18:22] = bg.reshape(nl, 4, 128).transpose(2, 0, 1)
    pm[:, :, 22:30] = inp["swa_sinks"][:nl][None, :, :]
    sh["pmix"] = pm
    sh["hlb"] = np.ascontiguousarray(inp["hgrn_lower_bounds"][:nl].reshape(nl, 4, 128).transpose(2, 0, 1))
    s5 = np.zeros((128, nl, 1728), np.float32)

    def rowlay(a, fill=0.0):
        z = np.full((nl, 32, 64), fill, np.float32)
        z[:, :28] = a
        z = z.reshape(nl, 4, 8, 1, 64)
        z = np.broadcast_to(z, (nl, 4, 8, 16, 64))
        return z.transpose(2, 3, 0, 1, 4).reshape(128, nl, 256)
    s5[:, :, 0:256] = rowlay(a_re, -1.0)
    s5[:, :, 256:512] = rowlay(a_im, 1.0)
    s5[:, :, 512:768] = rowlay(np.broadcast_to(ldt[:, :, None], (nl, 28, 64)))

    def rowlay_b(b):
        z = np.zeros((nl, 32, 64, 16), np.float32)
        z[:, :28] = b
        return z.reshape(nl, 4, 8, 64, 16).transpose(2, 4, 0, 1, 3).reshape(128, nl, 256)
    s5[:, :, 768:1024] = rowlay_b(inp["ssm_b_re"][:nl])
    s5[:, :, 1024:1280] = rowlay_b(inp["ssm_b_im"][:nl])

    def crow(cm):
        return cm.reshape(nl, 14, 2, 16, 64).transpose(2, 4, 0, 1, 3).reshape(128, nl, 224)
    s5[:, :, 1280:1504] = crow(inp["ssm_c_re"][:nl])
    s5[:, :, 1504:1728] = crow(inp["ssm_c_im"][:nl])
    sh["s5r"] = s5
    return sh


def _prep_core(inp, nl, nseg, seq, sb):
    d = {}
    x = np.concatenate([inp["x_prompt"][seq, :nseg * SEG], inp["x_sample"][sb:sb + NS, 0]], axis=0)
    d["xT"] = np.ascontiguousarray(x.T)
    ss = inp["state_ssm"][:nl, sb:sb + NS]
    d["x0s"] = np.ascontiguousarray(ss.reshape(nl, NS, 14, 2, 64, 2).transpose(3, 4, 0, 1, 2, 5).reshape(128, nl, NS, 14, 2))
    d["hs0"] = np.ascontiguousarray(inp["state_hgrn"][:nl, sb:sb + NS].transpose(3, 0, 1, 2, 4))
    cs = inp["cache_swa_kv"][:nl, sb:sb + NS]
    kct = np.zeros((128, nl, NS, 4, 128), np.float32)
    vct = np.zeros((128, nl, NS, 320), np.float32)
    kct[:, :, :, 0, :] = cs[:, :, :, 0].reshape(nl, NS, 128, 128).transpose(3, 0, 1, 2)
    vct[:, :, :, 0:128] = cs[:, :, :, 1].reshape(nl, NS, 128, 128).transpose(2, 0, 1, 3)
    for g, (nm, st) in enumerate((("cache_dil0_kv", 1), ("cache_dil1_kv", 4), ("cache_dil2_kv", 16))):
        cg = inp[nm][:nl, sb:sb + NS, ::st]
        kk = cg[:, :, :, 0].transpose(3, 0, 1, 2)
        kct[0:64, :, :, 1 + g, :] = kk
        kct[64:128, :, :, 1 + g, :] = kk
        vct[:, :, :, 128 + 64 * g:192 + 64 * g] = cg[:, :, :, 1].transpose(2, 0, 1, 3)
    d["kct"], d["vct"] = kct, vct
    return d


_CACHE = {}


def _build(nseg, nl, **kw):
    key = (nseg, nl, tuple(sorted(kw.items())))
    if key not in _CACHE:
        p = Prog(nseg=nseg, nlayer=nl, **kw)
        p.build()
        _CACHE[key] = p
    return _CACHE[key]


def kernel(**inputs):
    inp = {k: np.asarray(v) for k, v in inputs.items()}
    nl, nseg = L, NSEG
    prog = _build(nseg, nl)
    cb, cf, rot = _consts(nseg)
    sh = _prep_shared(inp, nl)
    sh.update(cbf=cb, cf32=cf, rot=rot)
    in_maps = []
    for core in range(8):
        seq = core // 2
        d = dict(sh)
        d.update(_prep_core(inp, nl, nseg, seq, 8 * seq))
        in_maps.append(d)
    res = run_bass_kernel_spmd(prog.nc, in_maps, core_ids=list(range(8)))
    R = [res.results[2 * s] for s in range(4)]
    return _assemble(R, nl, nseg)


def _assemble(R, nl, nseg):
    nb = len(R)
    T = nseg * SEG
    f32 = np.float32
    y_p = np.stack([r["yT"][:, :T].T for r in R]).astype(f32)
    y_s = np.concatenate([r["yT"][:, T:T + NS].T for r in R])[:, None, :].astype(f32)
    ssm_p = np.stack([r["o_ssm_p"].reshape(2, 64, nl, 14, 2).transpose(2, 3, 0, 1, 4).reshape(nl, 28, 64, 2) for r in R], axis=1)
    ssm_s = np.concatenate([r["o_ssm_s"].reshape(2, 64, nl, NS, 14, 2).transpose(2, 3, 4, 0, 1, 5).reshape(nl, NS, 28, 64, 2)
                            for r in R], axis=1)
    swa_p = np.stack([np.stack([r["o_swa_k"][:, :, 0:128].transpose(0, 2, 1).reshape(nl, 128, 2, 64),
                                r["o_swa_v"].reshape(nl, 128, 2, 64)], axis=2) for r in R], axis=1)
    swa_s = np.concatenate([np.stack([r["o_swa_k"][:, :, 128:128 + NS].transpose(0, 2, 1).reshape(nl, NS, 2, 64),
                                      r["o_sv"][:, :, 0:128].reshape(nl, NS, 2, 64)], axis=2)[:, :, None] for r in R], axis=1)

    def dil(kname, vfun, npr, vcol):
        p = np.stack([np.stack([r[kname][:, :, 0:npr].transpose(0, 2, 1), vfun(r)], axis=2) for r in R], axis=1)
        s = np.concatenate([np.stack([r[kname][:, :, npr:npr + NS].transpose(0, 2, 1),
                                      r["o_sv"][:, :, vcol:vcol + 64]], axis=2)[:, :, None] for r in R], axis=1)
        return p.astype(f32), s.astype(f32)
    d0_p, d0_s = dil("o_d0_k", lambda r: r["o_d0_v"], 128, 128)
    d1_p, d1_s = dil("o_d1_k", lambda r: r["o_d1_v"].transpose(0, 2, 1, 3).reshape(nl, 512, 64), 512, 192)
    d2_p, d2_s = dil("o_d2_k", lambda r: r["o_d2_v"].reshape(nl, nseg, 4, 4, 32, 64).transpose(0, 1, 4, 2, 3, 5).reshape(nl, T, 64),
                     T, 256)
    hg_p = np.stack([r["o_hg_p"] for r in R], axis=1)
    hg_s = np.concatenate([r["o_hg_s"] for r in R], axis=1)
    outs = (y_p, y_s, ssm_p, ssm_s, swa_p, swa_s, d0_p, d0_s, d1_p, d1_s, d2_p, d2_s, hg_p, hg_s)
    return tuple(np.ascontiguousarray(o, dtype=f32) for o in outs)
```
